# Optimizing a Trainium2 kernel written in Bass

```python
import functools
import jax, jax.numpy as jnp
from jax import lax
import numpy as np

D_MODEL = 2048
BATCH = 4
SEQ = 2048
DEPTH = 2
DEC_BATCH = 128
DEC_SEQ = 1
PAST_LEN = 8192
PAGE_SIZE = 128

D_MIX = D_MODEL
D_ATTN = D_MIX // 2
D_LRU = D_MIX // 4
D_SC = D_MIX - D_ATTN - D_LRU
HEAD_DIM = 64
N_HEADS = D_ATTN // HEAD_DIM
N_KV_HEADS = 4
N_GROUP = N_HEADS // N_KV_HEADS
D_KV = N_KV_HEADS * HEAD_DIM
WINDOW = 128
ATTN_BLOCK = WINDOW
ROPE_THETA = 10000.0
N_LRU_HEADS = 8
LRU_BLK = D_LRU // N_LRU_HEADS
LRU_CONV_W = 4
LRU_C = 8.0
SC_CONV_W = 3
D_FF = (-(-8 * D_MODEL // 3) + 255) // 256 * 256
D_IN = D_ATTN + 2 * D_KV + 2 * D_LRU + 3 * D_SC
RMS_EPS = 1e-6

kernel_name = 'hymba_rglru_swa_shortconv_step'


def rms_norm(x, g):
    xf = x.astype(jnp.float32)
    y = xf * lax.rsqrt(jnp.mean(xf * xf, axis=-1, keepdims=True) + RMS_EPS)
    return (y * g.astype(jnp.float32)).astype(x.dtype)


def rope(x, pos):
    half = HEAD_DIM // 2
    inv = ROPE_THETA ** (-jnp.arange(half, dtype=jnp.float32) / half)
    ang = pos.astype(jnp.float32)[:, None] * inv[None, :]
    cos = jnp.cos(ang)[None, :, None, :]
    sin = jnp.sin(ang)[None, :, None, :]
    xf = x.astype(jnp.float32)
    x1, x2 = xf[..., :half], xf[..., half:]
    return jnp.concatenate([x1 * cos - x2 * sin, x2 * cos + x1 * sin], axis=-1).astype(x.dtype)


def causal_dwconv(u, buf, w):
    K = w.shape[0]
    T = u.shape[1]
    ext = jnp.concatenate([buf.astype(u.dtype), u], axis=1)
    y = ext[:, 0:T] * w[0]
    for k in range(1, K):
        y = y + ext[:, k:k + T] * w[k]
    return y, ext[:, T:]


def linear_scan(a, b, h0):
    b = b.at[:, 0].add(a[:, 0] * h0)
    def combine(l, r):
        al, bl = l
        ar, br = r
        return al * ar, ar * bl + br
    _, h = lax.associative_scan(combine, (a, b), axis=1)
    return h


def rglru_mixer(u_x, u_gate, conv_buf, h0, pos, conv_w, conv_b, w_a, b_a, w_i, b_i, lam):
    xc, new_buf = causal_dwconv(u_x, conv_buf, conv_w)
    xc = xc + conv_b
    N, T, _ = xc.shape
    xf = xc.astype(jnp.float32)
    xh = xf.reshape(N, T, N_LRU_HEADS, LRU_BLK)
    r = jax.nn.sigmoid(jnp.einsum('nthi,hij->nthj', xh, w_a.astype(jnp.float32)).reshape(N, T, D_LRU) + b_a.astype(jnp.float32))
    i = jax.nn.sigmoid(jnp.einsum('nthi,hij->nthj', xh, w_i.astype(jnp.float32)).reshape(N, T, D_LRU) + b_i.astype(jnp.float32))
    log_a = -LRU_C * r * jax.nn.softplus(-lam.astype(jnp.float32))
    a = jnp.exp(log_a)
    mult = jnp.sqrt(-jnp.expm1(2.0 * log_a))
    mult = jnp.where((pos == 0)[None, :, None], 1.0, mult)
    h = linear_scan(a, mult * i * xf, h0.astype(jnp.float32))
    y = h.astype(u_x.dtype) * jax.nn.gelu(u_gate)
    return y, new_buf, h[:, -1].astype(u_x.dtype)


def short_conv_mixer(u_b, u_c, u_h, buf, w):
    y, new_buf = causal_dwconv(u_c * u_h, buf, w)
    return u_b * y, new_buf


def sink_probs(s, mask, sink):
    s = jnp.where(mask, s, -jnp.inf)
    m = jnp.maximum(jnp.max(s, axis=-1, keepdims=True), sink)
    p = jnp.exp(s - m)
    return p / (jnp.sum(p, axis=-1, keepdims=True) + jnp.exp(sink - m))


def swa_prompt(q, k, v, sinks):
    N, T = q.shape[0], q.shape[1]
    L = ATTN_BLOCK
    nb = T // L
    qb = q.reshape(N, nb, L, N_KV_HEADS, N_GROUP, HEAD_DIM)
    def band(t):
        tp = jnp.concatenate([jnp.zeros_like(t[:, :L]), t], axis=1).reshape(N, nb + 1, L, N_KV_HEADS, HEAD_DIM)
        return jnp.concatenate([tp[:, :-1], tp[:, 1:]], axis=2)
    kb, vb = band(k), band(v)
    s = jnp.einsum('nbqkgd,nbskd->nbkgqs', qb, kb, preferred_element_type=jnp.float32) * (HEAD_DIM ** -0.5)
    qi = jnp.arange(L)[:, None]
    sj = jnp.arange(2 * L)[None, :]
    diff = L + qi - sj
    k_pos = (jnp.arange(nb)[:, None, None] - 1) * L + sj[None]
    mask = (diff >= 0) & (diff < WINDOW) & (k_pos >= 0)
    mask = mask[None, :, None, None]
    sink = sinks.astype(jnp.float32).reshape(N_KV_HEADS, N_GROUP)[None, None, :, :, None, None]
    p = sink_probs(s, mask, sink)
    o = jnp.einsum('nbkgqs,nbskd->nbqkgd', p.astype(vb.dtype), vb).reshape(N, T, D_ATTN)
    wb = min(WINDOW, T)
    return o, k[:, T - wb:], v[:, T - wb:]


def swa_decode(q, k, v, k_buf, v_buf, sinks, pos0):
    N, T = q.shape[0], q.shape[1]
    wb = k_buf.shape[1]
    kk = jnp.concatenate([k_buf.astype(k.dtype), k], axis=1)
    vv = jnp.concatenate([v_buf.astype(v.dtype), v], axis=1)
    qg = q.reshape(N, T, N_KV_HEADS, N_GROUP, HEAD_DIM)
    s = jnp.einsum('ntkgd,nskd->nkgts', qg, kk, preferred_element_type=jnp.float32) * (HEAD_DIM ** -0.5)
    q_pos = pos0 + jnp.arange(T)
    k_pos = pos0 - wb + jnp.arange(wb + T)
    diff = q_pos[:, None] - k_pos[None, :]
    mask = ((diff >= 0) & (diff < WINDOW))[None, None, None]
    sink = sinks.astype(jnp.float32).reshape(N_KV_HEADS, N_GROUP)[None, :, :, None, None]
    p = sink_probs(s, mask, sink)
    o = jnp.einsum('nkgts,nskd->ntkgd', p.astype(vv.dtype), vv).reshape(N, T, D_ATTN)
    return o, kk[:, T:], vv[:, T:]


def trunk_layer(x, pos, lru_h0, lru_buf, sc_buf, attend, norm_mix, w_in, norm_grp, w_out,
                lru_conv_w, lru_conv_b, lru_w_a, lru_b_a, lru_w_i, lru_b_i, lru_lambda,
                sc_conv_w, norm_ffn, ffn_w_gu, ffn_w_down):
    N, T, _ = x.shape
    h = rms_norm(x, norm_mix)
    z = h @ w_in
    sizes = [D_ATTN, D_KV, D_KV, D_LRU, D_LRU, D_SC, D_SC, D_SC]
    offs = []
    acc = 0
    for sz in sizes[:-1]:
        acc += sz
        offs.append(acc)
    q, k, v, u_x, u_gate, u_b, u_c, u_h = jnp.split(z, offs, axis=-1)
    q = rope(q.reshape(N, T, N_HEADS, HEAD_DIM), pos)
    k = rope(k.reshape(N, T, N_KV_HEADS, HEAD_DIM), pos)
    v = v.reshape(N, T, N_KV_HEADS, HEAD_DIM)
    o_attn, k_state, v_state = attend(q, k, v)
    o_lru, lru_buf_new, lru_h_new = rglru_mixer(u_x, u_gate, lru_buf, lru_h0, pos, lru_conv_w, lru_conv_b,
                                                lru_w_a, lru_b_a, lru_w_i, lru_b_i, lru_lambda)
    o_sc, sc_buf_new = short_conv_mixer(u_b, u_c, u_h, sc_buf, sc_conv_w)
    g_attn, g_lru, g_sc = jnp.split(norm_grp, [D_ATTN, D_ATTN + D_LRU])
    mixed = jnp.concatenate([rms_norm(o_attn, g_attn), rms_norm(o_lru, g_lru), rms_norm(o_sc, g_sc)], axis=-1)
    x = x + mixed @ w_out
    hf = rms_norm(x, norm_ffn)
    gate, up = jnp.split(hf @ ffn_w_gu, 2, axis=-1)
    x = x + (jax.nn.silu(gate) * up) @ ffn_w_down
    return x, (lru_h_new, lru_buf_new, k_state, v_state, sc_buf_new)


def setup_inputs(seed: int = 0) -> dict:
    key = jax.random.key(seed)
    ks = jax.random.split(key, 24)
    f32 = jnp.float32
    def nrm(k, shape, scale):
        return jax.random.normal(k, shape, f32) * scale
    def gain(k, shape):
        return 1.0 + 0.02 * jax.random.normal(k, shape, f32)
    wb = min(WINDOW, PAST_LEN)
    a0 = jax.random.uniform(ks[17], (DEPTH, D_LRU), f32, 0.9, 0.999)
    s0 = a0 ** (1.0 / LRU_C)
    return {
        'x_prompt': nrm(ks[0], (BATCH, SEQ, D_MODEL), 1.0),
        'x_sample': nrm(ks[1], (DEC_BATCH, DEC_SEQ, D_MODEL), 1.0),
        'state_lru_h': nrm(ks[2], (DEPTH, DEC_BATCH, D_LRU), 0.5),
        'state_lru_conv': nrm(ks[3], (DEPTH, DEC_BATCH, LRU_CONV_W - 1, D_LRU), 0.5),
        'cache_swa_k': nrm(ks[4], (DEPTH, DEC_BATCH, wb, N_KV_HEADS, HEAD_DIM), 1.0),
        'cache_swa_v': nrm(ks[5], (DEPTH, DEC_BATCH, wb, N_KV_HEADS, HEAD_DIM), 1.0),
        'state_sconv': nrm(ks[6], (DEPTH, DEC_BATCH, SC_CONV_W - 1, D_SC), 0.5),
        'norm_mix': gain(ks[7], (DEPTH, D_MODEL)),
        'w_in': nrm(ks[8], (DEPTH, D_MODEL, D_IN), D_MODEL ** -0.5),
        'norm_grp': gain(ks[9], (DEPTH, D_MIX)),
        'w_out': nrm(ks[10], (DEPTH, D_MIX, D_MODEL), D_MIX ** -0.5),
        'lru_conv_w': nrm(ks[11], (DEPTH, LRU_CONV_W, D_LRU), LRU_CONV_W ** -0.5),
        'lru_conv_b': nrm(ks[12], (DEPTH, D_LRU), 0.01),
        'lru_w_a': nrm(ks[13], (DEPTH, N_LRU_HEADS, LRU_BLK, LRU_BLK), LRU_BLK ** -0.5),
        'lru_b_a': nrm(ks[14], (DEPTH, D_LRU), 0.01),
        'lru_w_i': nrm(ks[15], (DEPTH, N_LRU_HEADS, LRU_BLK, LRU_BLK), LRU_BLK ** -0.5),
        'lru_b_i': nrm(ks[16], (DEPTH, D_LRU), 0.01),
        'lru_lambda': jnp.log(s0) - jnp.log1p(-s0),
        'sc_conv_w': nrm(ks[18], (DEPTH, SC_CONV_W, D_SC), SC_CONV_W ** -0.5),
        'attn_sinks': nrm(ks[19], (DEPTH, N_HEADS), 0.5),
        'norm_ffn': gain(ks[20], (DEPTH, D_MODEL)),
        'ffn_w_gu': nrm(ks[21], (DEPTH, D_MODEL, 2 * D_FF), D_MODEL ** -0.5),
        'ffn_w_down': nrm(ks[22], (DEPTH, D_FF, D_MODEL), D_FF ** -0.5),
        'norm_final': gain(ks[23], (D_MODEL,)),
    }


def reference(x_prompt, x_sample, state_lru_h, state_lru_conv, cache_swa_k, cache_swa_v, state_sconv,
              norm_mix, w_in, norm_grp, w_out, lru_conv_w, lru_conv_b, lru_w_a, lru_b_a, lru_w_i, lru_b_i,
              lru_lambda, sc_conv_w, attn_sinks, norm_ffn, ffn_w_gu, ffn_w_down, norm_final):
    nb_p, t_p = x_prompt.shape[0], x_prompt.shape[1]
    t_s = x_sample.shape[1]
    pos_p = jnp.arange(t_p, dtype=jnp.int32)
    pos_s = PAST_LEN + jnp.arange(t_s, dtype=jnp.int32)
    xp, xs = x_prompt, x_sample
    p_states, s_states = [], []
    for l in range(DEPTH):
        lw = dict(norm_mix=norm_mix[l], w_in=w_in[l], norm_grp=norm_grp[l], w_out=w_out[l],
                  lru_conv_w=lru_conv_w[l], lru_conv_b=lru_conv_b[l], lru_w_a=lru_w_a[l], lru_b_a=lru_b_a[l],
                  lru_w_i=lru_w_i[l], lru_b_i=lru_b_i[l], lru_lambda=lru_lambda[l], sc_conv_w=sc_conv_w[l],
                  norm_ffn=norm_ffn[l], ffn_w_gu=ffn_w_gu[l], ffn_w_down=ffn_w_down[l])
        xp, st_p = trunk_layer(
            xp, pos_p,
            jnp.zeros((nb_p, D_LRU), xp.dtype),
            jnp.zeros((nb_p, LRU_CONV_W - 1, D_LRU), xp.dtype),
            jnp.zeros((nb_p, SC_CONV_W - 1, D_SC), xp.dtype),
            functools.partial(swa_prompt, sinks=attn_sinks[l]),
            **lw)
        p_states.append(st_p)
        xs, st_s = trunk_layer(
            xs, pos_s, state_lru_h[l], state_lru_conv[l], state_sconv[l],
            functools.partial(swa_decode, k_buf=cache_swa_k[l], v_buf=cache_swa_v[l],
                              sinks=attn_sinks[l], pos0=PAST_LEN),
            **lw)
        s_states.append(st_s)
    y_prompt = rms_norm(xp, norm_final)
    y_sample = rms_norm(xs, norm_final)
    p_lru_h = jnp.stack([st[0] for st in p_states])
    p_lru_conv = jnp.stack([st[1] for st in p_states])
    p_swa_k = jnp.stack([st[2] for st in p_states])
    p_swa_v = jnp.stack([st[3] for st in p_states])
    p_sconv = jnp.stack([st[4] for st in p_states])
    s_lru_h = jnp.stack([st[0] for st in s_states])
    s_lru_conv = jnp.stack([st[1] for st in s_states])
    s_swa_k = jnp.stack([st[2] for st in s_states])
    s_swa_v = jnp.stack([st[3] for st in s_states])
    s_sconv = jnp.stack([st[4] for st in s_states])
    return (y_prompt, y_sample, p_lru_h, p_lru_conv, p_swa_k, p_swa_v, p_sconv,
            s_lru_h, s_lru_conv, s_swa_k, s_swa_v, s_sconv)
```

```python
import numpy as np
from contextlib import ExitStack
import concourse.bass as bass
import concourse.mybir as mybir
from concourse.bass_utils import run_bass_kernel_spmd

F32 = mybir.dt.float32
BF16 = mybir.dt.bfloat16
AF = mybir.ActivationFunctionType
ALU = mybir.AluOpType

D = 2048
KC = 16
T = 512
NS = 16
DFF = 5632
FC = 44
EPS = 1e-6
PAST = 8192
NCORES = 8
SEQ = 2048


def pv_layout():
    lay = {}
    off = 0
    for l in range(2):
        for name, n in (("nmix", 16), ("nffn", 16), ("ngrp", 16), ("lcw", 16), ("lcb", 4), ("ba", 4), ("bi", 4),
                        ("lam", 4), ("scw", 12), ("sink", 8), ("sinkall", 16)):
            lay[(name, l)] = (off, n)
            off += n
    lay[("nfin", 0)] = (off, 16)
    off += 16
    return lay, off


class _Stop(Exception):
    pass


class Trk:
    SEM_MAX = 3800

    def __init__(self, nc, es):
        self.nc = nc
        self.es = es
        self.E = {"pe": nc.tensor, "act": nc.scalar, "dve": nc.vector, "pool": nc.gpsimd, "sp": nc.sync}
        self.gen = {k: 0 for k in self.E}
        self.sem = {k: es.enter_context(nc.semaphore("pg_" + k + "_0")) for k in self.E}
        self.semname = {k: "pg_" + k + "_0" for k in self.E}
        self.cnt = {k: 0 for k in self.E}
        self.seen = {k: {} for k in self.E}
        self.lastw = {}
        self.readers = {}
        self.dsems = {}
        self.all_dsems = []
        self.pending_noinc = {k: False for k in self.E}

    def _wait(self, eng, ev):
        name, semh, val = ev
        if eng == "pe" and name.startswith("pg_pe_"):
            return
        if self.seen[eng].get(name, 0) >= val:
            return
        self.E[eng].wait_ge(semh, val)
        self.seen[eng][name] = val

    def _deps(self, eng, R, W):
        best = {}

        def add(ev):
            if ev[0] not in best or best[ev[0]][2] < ev[2]:
                best[ev[0]] = ev
        for r in R:
            if r in self.lastw:
                add(self.lastw[r])
        for w in W:
            if w in self.lastw:
                add(self.lastw[w])
            for ev in self.readers.get(w, {}).values():
                add(ev)
        for ev in best.values():
            self._wait(eng, ev)

    def _commit(self, ev, R, W):
        for r in R:
            d = self.readers.setdefault(r, {})
            if ev[0] not in d or d[ev[0]][2] < ev[2]:
                d[ev[0]] = ev
        for w in W:
            self.lastw[w] = ev
            self.readers[w] = {}

    def _roll(self, eng):
        if self.cnt[eng] >= self.SEM_MAX and not self.pending_noinc[eng]:
            self.gen[eng] += 1
            nm = f"pg_{eng}_{self.gen[eng]}"
            self.sem[eng] = self.es.enter_context(self.nc.semaphore(nm))
            self.semname[eng] = nm
            self.cnt[eng] = 0

    def op(self, eng, fn, R=(), W=(), inc=True):
        self._roll(eng)
        self._deps(eng, R, W)
        ins = fn(self.E[eng])
        if inc:
            self.cnt[eng] += 1
            ins.then_inc(self.sem[eng], 1)
            ev = (self.semname[eng], self.sem[eng], self.cnt[eng])
            self.pending_noinc[eng] = False
        else:
            ev = (self.semname[eng], self.sem[eng], self.cnt[eng] + 1)
            self.pending_noinc[eng] = True
        self._commit(ev, R, W)
        return ins

    def fence(self, eng, others):
        for o in others:
            self._wait(eng, (self.semname[o], self.sem[o], self.cnt[o] + (1 if self.pending_noinc[o] else 0)))

    def reserve(self, sem, n):
        if sem not in self.dsems or self.dsems[sem][1] + 16 * n > self.SEM_MAX:
            g = self.dsems[sem][3] + 1 if sem in self.dsems else 0
            nm = f"d_{sem}_{g}"
            self.dsems[sem] = [self.es.enter_context(self.nc.semaphore(nm)), 0, nm, g]
            self.all_dsems.append(self.dsems[sem])

    def dma(self, q, out, in_, R=(), W=(), sem="g", skip_deps=False, **kw):
        if not skip_deps:
            self._deps(q, R, W)
        self.reserve(sem, 1)
        ds = self.dsems[sem]
        ins = self.E[q].dma_start(out=out, in_=in_, **kw)
        ds[1] += 16
        ins.then_inc(ds[0], 16)
        ev = (ds[2], ds[0], ds[1])
        self._commit(ev, R, W)
        return ins

    def wait_all_dma(self, eng):
        for ds in self.all_dsems:
            if ds[1]:
                self.E[eng].wait_ge(ds[0], ds[1])


def build(NPP):
    lay, NV = pv_layout()
    TOK = NPP * T
    nc = bass.Bass("TRN2", target_bir_lowering=False)

    def din(name, shape, dt=F32):
        return nc.dram_tensor(name, list(shape), dt, kind="ExternalInput").ap()

    def dout(name, shape, dt=F32):
        return nc.dram_tensor(name, list(shape), dt, kind="ExternalOutput").ap()

    xp = din("xp", [TOK, D]); xs = din("xs", [NS, D])
    st_h = din("st_h", [2, NS, 512]); st_conv = din("st_conv", [2, NS, 3, 512]); st_sc = din("st_sc", [2, NS, 2, 512])
    ck = din("ck", [2, NS, 128, 256]); cv = din("cv", [2, NS, 128, 256])
    w_in = din("w_in", [2, D, 4096]); w_out = din("w_out", [2, D, D]); w_gu = din("w_gu", [2, D, 2 * DFF]); w_dn = din("w_dn", [2, DFF, D])
    bd = din("bd", [2, 128, 8, 128])
    pvd = din("pv", [128, NV])
    csd = din("cs", [NPP + 1, 128, 2, T])
    mskd = din("msk", [128, 2, 128])
    idnd = din("idn", [128, 128])

    yp = dout("yp", [TOK, D]); ys = dout("ys", [NS, D])
    p_h = dout("p_h", [2, 512]); p_conv = dout("p_conv", [2, 3, 512]); p_k = dout("p_k", [2, 128, 256]); p_v = dout("p_v", [2, 128, 256]); p_sc = dout("p_sc", [2, 2, 512])
    s_h = dout("s_h", [2, NS, 512]); s_conv = dout("s_conv", [2, NS, 3, 512]); s_k = dout("s_k", [2, NS, 128, 256]); s_v = dout("s_v", [2, NS, 128, 256]); s_sc = dout("s_sc", [2, NS, 2, 512])

    es = ExitStack()
    with es:
        trk = Trk(nc, es)
        op = trk.op

        def sb(name, shape, dt):
            return es.enter_context(nc.sbuf_tensor(name, list(shape), dt))

        def ps(name, shape, dt):
            return es.enter_context(nc.psum_tensor(name, list(shape), dt))

        X = sb("X", [128, KC, T], F32)
        XN = sb("XN", [128, KC, T], BF16)
        H = sb("H", [128, FC, T], BF16)
        MX = H[:, 0:16, :]
        QK = H[:, 16:26, :]
        KS = H[:, 26:34, :].rearrange("p c (a f) -> p (c a) f", f=256)
        VS2 = H[:, 34:42, :].rearrange("p c (a f) -> p (c a) f", f=256)
        NWS = 2
        WS = [sb(f"WS{i}", [128, 8192], BF16) for i in range(NWS)]
        NTMP = 10
        TMP = [sb(f"TMP{i}", [128, T + 4], F32) for i in range(NTMP)]
        SQ = [sb(f"SQ{i}", [128, T], BF16) for i in range(3)]
        PT = [sb(f"PT{i}", [128, 512], BF16) for i in range(4)]
        STG = XN[:, :, :].rearrange("p c t -> p (c t)").bitcast(F32)
        PV = sb("PV", [128, NV], F32)
        CS = sb("CS", [128, 2, T], F32)
        MSKF = sb("MSKF", [128, 2, 128], F32)
        MSK = sb("MSK", [128, 2, 4, 128], BF16)
        IDN = sb("IDN", [128, 128], F32)
        IDNB = sb("IDNB", [128, 128], BF16)
        ONES = sb("ONES", [128, 128], BF16)
        BD = sb("BD", [128, 8, 128], F32)
        RS = [sb(f"RS{i}", [128, T], F32) for i in range(2)]
        SCL = sb("SCL", [128, 2, 2, 4], F32)
        SNK = sb("SNK", [128, 2, 4, 128], F32)
        EXS = sb("EXS", [128, 2, 8], F32)
        SNKS = sb("SNKS", [128, 2, NS, 16], F32)
        HC = sb("HC", [128, 2, 4], F32)
        CUX = sb("CUX", [128, 2, 4, 3], F32)
        CSC = sb("CSC", [128, 2, 4, 2], F32)
        KH = sb("KH", [128, 2, 2, 128 + T], BF16)
        VH = sb("VH", [128, 2, 5, 256], BF16)
        KF = [sb(f"KF{i}", [128, T], F32) for i in range(2)]
        ROW = sb("ROW", [128, 512], F32)
        UXS = sb("UXS", [128, 4, NS, 4], F32)
        GCS = sb("GCS", [128, 4, NS, 3], F32)
        H0S = sb("H0S", [128, 4, NS], F32)
        STS = sb("STS", [48, 512], F32)
        KTS = [sb(f"KTS{i}", [128, 2, 128], BF16) for i in range(2)]
        PSS = sb("PSS", [128, 256], BF16)
        RD = sb("RD", [128, 256], F32)

        PA = [ps(f"PA{i}", [128, 512], F32) for i in range(4)]
        PB = [ps(f"PB{i}", [128, 512], F32) for i in range(4)]

        wstate = {"i": 0}

        def wload(pieces, nm):
            i = wstate["i"]; wstate["i"] += 1
            slot = i % NWS
            trk.reserve(f"w{slot}", len(pieces))
            for pi_, (dst, src) in enumerate(pieces):
                trk.dma("pool", dst(WS[slot]), src, R=(), W=(f"W{slot}",), sem=f"w{slot}", skip_deps=(pi_ > 0))
            return WS[slot], f"W{slot}"

        def wv(slot, kc, n):
            return slot[:, 0:kc * n].rearrange("p (k n) -> p k n", n=n)

        pa_i = {"i": 0}

        def next_pa():
            i = pa_i["i"] % 4; pa_i["i"] += 1
            return PA[i], f"PA{i}"

        tmp_i = {"i": 0}

        def next_tmp():
            i = tmp_i["i"] % NTMP; tmp_i["i"] += 1
            return TMP[i], f"TMP{i}"

        sq_i = {"i": 0}

        def next_sq():
            i = sq_i["i"] % 3; sq_i["i"] += 1
            return SQ[i], f"SQ{i}"

        def colp(l, name, c=0):
            o, n = lay[(name, l)]
            return PV[:, o + c:o + c + 1]

        trk.dma("sp", PV[:], pvd, W=("PV",), sem="c_pv")
        trk.dma("sp", IDN[:], idnd, W=("IDN",), sem="c_idn")
        trk.dma("sp", MSKF[:], mskd, W=("MSKF",), sem="c_msk")
        op("dve", lambda e: e.tensor_copy(out=IDNB[:], in_=IDN[:]), R=("IDN",), W=("IDNB",))
        op("dve", lambda e: e.memset(ONES[:], 1.0), W=("ONES",))
        for m in range(2):
            for j in range(4):
                op("dve", lambda e, m=m, j=j: e.tensor_copy(out=MSK[:, m, j, :], in_=MSKF[:, m, :]), R=("MSKF",), W=("MSK",))
        for t_, nm in ((HC, "HC"), (CUX, "CUX"), (CSC, "CSC"), (KH, "KH"), (VH, "VH")):
            op("dve", lambda e, t_=t_: e.memset(t_[:], 0.0), W=(nm,))
        for l in range(2):
            o, n = lay[("lam", l)]
            tt, tn = next_tmp()
            op("act", lambda e: e.activation(out=tt[:, 0:4], in_=PV[:, o:o + 4], func=AF.Exp, scale=-1.0), R=("PV",), W=(tn,))
            op("act", lambda e: e.activation(out=tt[:, 4:8], in_=tt[:, 0:4], func=AF.Ln, bias=1.0), R=(tn,), W=(tn,))
            op("dve", lambda e: e.tensor_scalar(out=SCL[:, l, 0, :], in0=tt[:, 4:8], scalar1=-8.0, scalar2=None, op0=ALU.mult), R=(tn,), W=("SCL",))
            op("dve", lambda e: e.tensor_scalar(out=SCL[:, l, 1, :], in0=tt[:, 4:8], scalar1=-16.0, scalar2=None, op0=ALU.mult), R=(tn,), W=("SCL",))
            o, n = lay[("sink", l)]
            op("act", lambda e: e.activation(out=EXS[:, l, :], in_=PV[:, o:o + 8], func=AF.Exp), R=("PV",), W=("EXS",))
            o, n = lay[("sinkall", l)]
            op("act", lambda e: e.activation(out=tt[:, 16:32], in_=PV[:, o:o + 16], func=AF.Exp), R=("PV",), W=(tn,))
            for n_ in range(NS):
                op("dve", lambda e, n_=n_: e.tensor_copy(out=SNKS[:, l, n_, :], in_=tt[:, 16:32]), R=(tn,), W=("SNKS",))

        def rmsnorm_to_xn(l, gname, N):
            pb, pbn = PB[0], "PB0"
            for c in range(KC):
                sq, sqn = next_sq()
                op("act", lambda e: e.activation(out=sq[:, 0:N], in_=X[:, c, 0:N], func=AF.Square), R=(f"X{c}",), W=(sqn,))
                op("pe", lambda e: e.matmul(pb[:, 0:N], lhsT=ONES[:, :], rhs=sq[:, 0:N], start=(c == 0), stop=(c == KC - 1)),
                   R=(sqn, "ONES"), W=(pbn,), inc=True)
            rs = RS[0]
            op("dve", lambda e: e.tensor_scalar(out=rs[:, 0:N], in0=pb[:, 0:N], scalar1=1.0 / D, scalar2=EPS, op0=ALU.mult, op1=ALU.add), R=(pbn,), W=("RS0",))
            op("act", lambda e: e.activation(out=rs[:, 0:N], in_=rs[:, 0:N], func=AF.Sqrt), R=("RS0",), W=("RS0",))
            op("dve", lambda e: e.reciprocal(out=rs[:, 0:N], in_=rs[:, 0:N]), R=("RS0",), W=("RS0",))
            for c in range(KC):
                op("dve", lambda e: e.scalar_tensor_tensor(out=XN[:, c, 0:N], in0=X[:, c, 0:N], scalar=colp(l, gname, c), in1=rs[:, 0:N], op0=ALU.mult, op1=ALU.mult),
                   R=(f"X{c}", "RS0", "PV"), W=(f"XN{c}",))

        def gemm_chunk(lhsT_of_k, wtok, nk, rhs_of_k, rtok_of_k, N, last_in_block=False):
            pa, pan = next_pa()
            for k in range(nk):
                op("pe", lambda e: e.matmul(pa[:, 0:N], lhsT=lhsT_of_k(k), rhs=rhs_of_k(k), start=(k == 0), stop=(k == nk - 1)),
                   R=(wtok, rtok_of_k(k)), W=(pan,), inc=(k == nk - 1))
            return pa, pan

        def group_norm_finish(pb, pbn, l, c0, nch, N, rsi):
            rs = RS[rsi]; rsn = f"RS{rsi}"
            op("dve", lambda e: e.tensor_scalar(out=rs[:, 0:N], in0=pb[:, 0:N], scalar1=1.0 / (nch * 128), scalar2=EPS, op0=ALU.mult, op1=ALU.add), R=(pbn,), W=(rsn,))
            op("act", lambda e: e.activation(out=rs[:, 0:N], in_=rs[:, 0:N], func=AF.Sqrt), R=(rsn,), W=(rsn,))
            op("dve", lambda e: e.reciprocal(out=rs[:, 0:N], in_=rs[:, 0:N]), R=(rsn,), W=(rsn,))
            for c in range(c0, c0 + nch):
                op("dve", lambda e: e.scalar_tensor_tensor(out=MX[:, c, 0:N], in0=MX[:, c, 0:N], scalar=colp(l, "ngrp", c), in1=rs[:, 0:N], op0=ALU.mult, op1=ALU.mult),
                   R=(f"MX{c}", rsn, "PV"), W=(f"MX{c}",))

        def sq_accum(pb, pbn, c, first, last, N):
            sq, sqn = next_sq()
            op("act", lambda e: e.activation(out=sq[:, 0:N], in_=MX[:, c, 0:N], func=AF.Square), R=(f"MX{c}",), W=(sqn,))
            op("pe", lambda e: e.matmul(pb[:, 0:N], lhsT=ONES[:, :], rhs=sq[:, 0:N], start=first, stop=last), R=(sqn, "ONES"), W=(pbn,), inc=True)

        def transpose_out(src_ap, rows, cols, dst_dram, rtoks, tag, view=None, multi=None):
            op("pe", lambda e: e.transpose(PB[3][0:cols, 0:rows], src_ap, IDN[0:rows, 0:rows]), R=tuple(rtoks) + ("IDN",), W=("PB3",))
            op("act", lambda e: e.activation(out=ROW[0:cols, 0:rows], in_=PB[3][0:cols, 0:rows], func=AF.Copy), R=("PB3",), W=("ROW",))
            if multi is not None:
                trk.reserve("oROW", len(multi))
                for (d_ap, r0, r1) in multi:
                    trk.dma("sp", d_ap, ROW[r0:r1, 0:rows], R=("ROW",), W=(tag,), sem="oROW")
                return
            srcv = ROW[0:cols, 0:rows]
            if view is not None:
                srcv = view(srcv)
            trk.dma("sp", dst_dram, srcv, R=("ROW",), W=(tag,), sem="oROW")

        def layer(l, pi, kind, last_pass):
            N = T if kind == "p" else NS
            first_pass = (kind == "p" and pi == 0)
            if kind == "p":
                for kc in range(2):
                    for j in range(4):
                        op("dve", lambda e: e.tensor_scalar(out=SNK[:, kc, j, :], in0=IDN[:, :], scalar1=0.0, scalar2=EXS[:, l, kc * 4 + j:kc * 4 + j + 1], op0=ALU.mult, op1=ALU.add), R=("EXS", "IDN"), W=("SNK",))
            rmsnorm_to_xn(l, "nmix", N)
            xr = lambda k: XN[:, k, 0:N]
            xt = lambda k: f"XN{k}"
            if kind == "s":
                trk.dma("sp", STS[0:48, :], st_conv[l].rearrange("n k f -> (n k) f"), W=("STS",), sem="st")
                for c in range(4):
                    op("pe", lambda e: e.transpose(PB[3][:, 0:48], STS[0:48, c * 128:(c + 1) * 128], IDN[0:48, 0:48]), R=("STS", "IDN"), W=("PB3",))
                    op("act", lambda e: e.activation(out=UXS[:, c, :, 0:3], in_=PB[3][:, 0:48].rearrange("p (n k) -> p n k", k=3), func=AF.Copy), R=("PB3",), W=("UXS",))
                trk.dma("sp", STS[0:32, :], st_sc[l].rearrange("n k f -> (n k) f"), R=(), W=("STS",), sem="st")
                for c in range(4):
                    op("pe", lambda e: e.transpose(PB[3][:, 0:32], STS[0:32, c * 128:(c + 1) * 128], IDN[0:32, 0:32]), R=("STS", "IDN"), W=("PB3",))
                    op("act", lambda e: e.activation(out=GCS[:, c, :, 0:2], in_=PB[3][:, 0:32].rearrange("p (n k) -> p n k", k=2), func=AF.Copy), R=("PB3",), W=("GCS",))
                trk.dma("sp", STS[0:16, :], st_h[l], W=("STS",), sem="st")
                for c in range(4):
                    op("pe", lambda e: e.transpose(PB[3][:, 0:16], STS[0:16, c * 128:(c + 1) * 128], IDN[0:16, 0:16]), R=("STS", "IDN"), W=("PB3",))
                    op("act", lambda e: e.activation(out=H0S[:, c, :], in_=PB[3][:, 0:16], func=AF.Copy), R=("PB3",), W=("H0S",))
                trk.dma("sp", s_k[l][:, 0:127, :], ck[l][:, 1:128, :], W=(f"skA{l}",), sem=f"o2a{l}")
                trk.dma("sp", s_v[l][:, 0:127, :], cv[l][:, 1:128, :], W=(f"svA{l}",), sem=f"o2b{l}")
                trk.dma("sp", s_conv[l][:, 0:2, :], st_conv[l][:, 1:3, :], W=(f"scvA{l}",), sem=f"o2c{l}")
                trk.dma("sp", s_sc[l][:, 0:1, :], st_sc[l][:, 1:2, :], W=(f"sscA{l}",), sem=f"o2d{l}")

            if (_DBG.get("stop") == "norm" and kind == _DBG.get("stop_kind", "p") and l == _DBG.get("stop_layer", 0)):
                raise _Stop()
            trk.dma("sp", BD[:], bd[l], W=("BD",), sem="bd")
            pbl, pbln = PB[1], "PB1"
            for blk in range(2):
                c0 = blk * 2
                slot, wtok = wload([
                    (lambda s: wv(s, KC, 512)[:, :, 0:256], w_in[l][:, 1536 + c0 * 128:1536 + c0 * 128 + 256].rearrange("(k p) m -> p k m", p=128)),
                    (lambda s: wv(s, KC, 512)[:, :, 256:512], w_in[l][:, 2048 + c0 * 128:2048 + c0 * 128 + 256].rearrange("(k p) m -> p k m", p=128)),
                ], "lru")
                sv = wv(slot, KC, 512)
                for cc in range(2):
                    c = c0 + cc
                    pa, pan = gemm_chunk(lambda k: sv[:, k, cc * 128:(cc + 1) * 128], wtok, KC, xr, xt, N)
                    ux, uxn = next_tmp()
                    xc, xcn = next_tmp()
                    if kind == "p":
                        op("dve", lambda e: e.tensor_copy(out=ux[:, 0:3], in_=CUX[:, l, c, :]), R=("CUX",), W=(uxn,))
                        op("act", lambda e: e.activation(out=ux[:, 3:3 + T], in_=pa[:, 0:T], func=AF.Copy), R=(pan,), W=(uxn,))
                        op("dve", lambda e: e.tensor_copy(out=CUX[:, l, c, :], in_=ux[:, T:T + 3]), R=(uxn,), W=("CUX",))
                        tap = lambda k: ux[:, k:k + T]
                    else:
                        op("act", lambda e: e.activation(out=UXS[:, c, :, 3], in_=pa[:, 0:NS], func=AF.Copy), R=(pan,), W=("UXS",))
                        uxn = "UXS"
                        tap = lambda k: UXS[:, c, :, k]
                    o_w, _ = lay[("lcw", l)]
                    op("dve", lambda e: e.tensor_scalar(out=xc[:, 0:N], in0=tap(3), scalar1=PV[:, o_w + 12 + c:o_w + 13 + c], scalar2=colp(l, "lcb", c), op0=ALU.mult, op1=ALU.add), R=(uxn, "PV"), W=(xcn,))
                    for k in (2, 1, 0):
                        op("dve", lambda e, k=k: e.scalar_tensor_tensor(out=xc[:, 0:N], in0=tap(k), scalar=PV[:, o_w + k * 4 + c:o_w + k * 4 + c + 1], in1=xc[:, 0:N], op0=ALU.mult, op1=ALU.add), R=(uxn, "PV", xcn), W=(xcn,))
                    if kind == "s":
                        transpose_out(UXS[:, c, :, 3], 128, NS, s_conv[l][:, 2, c * 128:(c + 1) * 128], ("UXS",), f"scvB{l}")
                    gates = []
                    for gi, bname in ((0, "ba"), (1, "bi")):
                        pg, pgn = next_pa()
                        op("pe", lambda e: e.matmul(pg[:, 0:N], lhsT=BD[:, gi * 4 + c, :], rhs=xc[:, 0:N], start=True, stop=True), R=("BD", xcn), W=(pgn,))
                        gt, gtn = next_tmp()
                        op("act", lambda e: e.activation(out=gt[:, 0:N], in_=pg[:, 0:N], func=AF.Sigmoid, bias=colp(l, bname, c)), R=(pgn, "PV"), W=(gtn,))
                        gates.append((gt, gtn))
                    (rg, rgn), (ig, ign) = gates
                    a_, an = next_tmp()
                    m_, mn = next_tmp()
                    op("act", lambda e: e.activation(out=a_[:, 0:N], in_=rg[:, 0:N], func=AF.Exp, scale=SCL[:, l, 0, c:c + 1]), R=(rgn, "SCL"), W=(an,))
                    op("act", lambda e: e.activation(out=m_[:, 0:N], in_=rg[:, 0:N], func=AF.Exp, scale=SCL[:, l, 1, c:c + 1]), R=(rgn, "SCL"), W=(mn,))
                    op("dve", lambda e: e.tensor_scalar(out=m_[:, 0:N], in0=m_[:, 0:N], scalar1=-1.0, scalar2=1.0, op0=ALU.mult, op1=ALU.add), R=(mn,), W=(mn,))
                    op("act", lambda e: e.activation(out=m_[:, 0:N], in_=m_[:, 0:N], func=AF.Sqrt), R=(mn,), W=(mn,))
                    if first_pass:
                        op("dve", lambda e: e.memset(m_[:, 0:1], 1.0), R=(mn,), W=(mn,))
                    op("dve", lambda e: e.tensor_tensor(out=ig[:, 0:N], in0=ig[:, 0:N], in1=xc[:, 0:N], op=ALU.mult), R=(ign, xcn), W=(ign,))
                    op("dve", lambda e: e.tensor_tensor(out=ig[:, 0:N], in0=ig[:, 0:N], in1=m_[:, 0:N], op=ALU.mult), R=(ign, mn), W=(ign,))
                    hs, hsn = rg, rgn
                    if kind == "p":
                        op("dve", lambda e: e.tensor_tensor_scan(out=hs[:, 0:T], data0=a_[:, 0:T], data1=ig[:, 0:T], initial=HC[:, l, c:c + 1], op0=ALU.mult, op1=ALU.add), R=(an, ign, "HC", rgn), W=(hsn,))
                        op("dve", lambda e: e.tensor_copy(out=HC[:, l, c:c + 1], in_=hs[:, T - 1:T]), R=(hsn,), W=("HC",))
                    else:
                        op("dve", lambda e: e.tensor_tensor(out=hs[:, 0:N], in0=a_[:, 0:N], in1=H0S[:, c, :], op=ALU.mult), R=(an, "H0S", rgn), W=(hsn,))
                        op("dve", lambda e: e.tensor_tensor(out=hs[:, 0:N], in0=hs[:, 0:N], in1=ig[:, 0:N], op=ALU.add), R=(hsn, ign), W=(hsn,))
                        transpose_out(hs[:, 0:NS], 128, NS, s_h[l][:, c * 128:(c + 1) * 128], (hsn,), f"shB{l}")
                    pa2, pa2n = gemm_chunk(lambda k: sv[:, k, 256 + cc * 128:256 + (cc + 1) * 128], wtok, KC, xr, xt, N)
                    ug, ugn = next_tmp()
                    g2, g2n = next_tmp()
                    op("act", lambda e: e.activation(out=ug[:, 0:N], in_=pa2[:, 0:N], func=AF.Copy), R=(pa2n,), W=(ugn,))
                    op("dve", lambda e: e.tensor_tensor(out=g2[:, 0:N], in0=ug[:, 0:N], in1=ug[:, 0:N], op=ALU.mult), R=(ugn,), W=(g2n,))
                    op("dve", lambda e: e.tensor_scalar(out=g2[:, 0:N], in0=g2[:, 0:N], scalar1=0.044715, scalar2=1.0, op0=ALU.mult, op1=ALU.add), R=(g2n,), W=(g2n,))
                    op("dve", lambda e: e.tensor_tensor(out=g2[:, 0:N], in0=g2[:, 0:N], in1=ug[:, 0:N], op=ALU.mult), R=(g2n, ugn), W=(g2n,))
                    op("act", lambda e: e.activation(out=g2[:, 0:N], in_=g2[:, 0:N], func=AF.Sigmoid, scale=1.5957691216057308), R=(g2n,), W=(g2n,))
                    op("dve", lambda e: e.tensor_tensor(out=g2[:, 0:N], in0=g2[:, 0:N], in1=ug[:, 0:N], op=ALU.mult), R=(g2n, ugn), W=(g2n,))
                    op("dve", lambda e: e.tensor_tensor(out=MX[:, 8 + c, 0:N], in0=g2[:, 0:N], in1=hs[:, 0:N], op=ALU.mult), R=(g2n, hsn), W=(f"MX{8 + c}",))
                    sq_accum(pbl, pbln, 8 + c, c == 0, c == 3, N)
            group_norm_finish(pbl, pbln, l, 8, 4, N, 1)
            if kind == "p" and last_pass:
                transpose_out(HC[:, l, :], 128, 4, p_h[l].rearrange("(c f) -> c f", f=128), ("HC",), "ph")
                transpose_out(CUX[:, l, :, :], 128, 12, None, ("CUX",), "pconv", multi=[(p_conv[l][:, c * 128:(c + 1) * 128], c * 3, c * 3 + 3) for c in range(4)])

            if (_DBG.get("stop") == "lru" and kind == _DBG.get("stop_kind", "p") and l == _DBG.get("stop_layer", 0)):
                raise _Stop()
            pbs, pbsn = PB[2], "PB2"
            for c in range(4):
                slot, wtok = wload([
                    (lambda s: wv(s, KC, 384)[:, :, 0:128], w_in[l][:, 3072 + c * 128:3072 + (c + 1) * 128].rearrange("(k p) m -> p k m", p=128)),
                    (lambda s: wv(s, KC, 384)[:, :, 128:256], w_in[l][:, 3584 + c * 128:3584 + (c + 1) * 128].rearrange("(k p) m -> p k m", p=128)),
                    (lambda s: wv(s, KC, 384)[:, :, 256:384], w_in[l][:, 2560 + c * 128:2560 + (c + 1) * 128].rearrange("(k p) m -> p k m", p=128)),
                ], "sc")
                sv = wv(slot, KC, 384)
                pa, pan = gemm_chunk(lambda k: sv[:, k, 0:128], wtok, KC, xr, xt, N)
                uc, ucn = next_tmp()
                op("act", lambda e: e.activation(out=uc[:, 0:N], in_=pa[:, 0:N], func=AF.Copy), R=(pan,), W=(ucn,))
                pa, pan = gemm_chunk(lambda k: sv[:, k, 128:256], wtok, KC, xr, xt, N)
                gc, gcn = next_tmp()
                if kind == "p":
                    op("dve", lambda e: e.tensor_copy(out=gc[:, 0:2], in_=CSC[:, l, c, :]), R=("CSC",), W=(gcn,))
                    op("dve", lambda e: e.tensor_tensor(out=gc[:, 2:2 + T], in0=pa[:, 0:T], in1=uc[:, 0:T], op=ALU.mult), R=(pan, ucn), W=(gcn,))
                    op("dve", lambda e: e.tensor_copy(out=CSC[:, l, c, :], in_=gc[:, T:T + 2]), R=(gcn,), W=("CSC",))
                    tap = lambda k: gc[:, k:k + T]
                else:
                    op("dve", lambda e: e.tensor_tensor(out=GCS[:, c, :, 2], in0=pa[:, 0:NS], in1=uc[:, 0:NS], op=ALU.mult), R=(pan, ucn), W=("GCS",))
                    gcn = "GCS"
                    tap = lambda k: GCS[:, c, :, k]
                    transpose_out(GCS[:, c, :, 2], 128, NS, s_sc[l][:, 1, c * 128:(c + 1) * 128], ("GCS",), f"sscB{l}")
                y_, yn = next_tmp()
                o_w, _ = lay[("scw", l)]
                op("dve", lambda e: e.tensor_scalar(out=y_[:, 0:N], in0=tap(2), scalar1=PV[:, o_w + 8 + c:o_w + 9 + c], scalar2=None, op0=ALU.mult), R=(gcn, "PV"), W=(yn,))
                for k in (1, 0):
                    op("dve", lambda e, k=k: e.scalar_tensor_tensor(out=y_[:, 0:N], in0=tap(k), scalar=PV[:, o_w + k * 4 + c:o_w + k * 4 + c + 1], in1=y_[:, 0:N], op0=ALU.mult, op1=ALU.add), R=(gcn, "PV", yn), W=(yn,))
                pa, pan = gemm_chunk(lambda k: sv[:, k, 256:384], wtok, KC, xr, xt, N)
                op("dve", lambda e: e.tensor_tensor(out=MX[:, 12 + c, 0:N], in0=pa[:, 0:N], in1=y_[:, 0:N], op=ALU.mult), R=(pan, yn), W=(f"MX{12 + c}",))
                sq_accum(pbs, pbsn, 12 + c, c == 0, c == 3, N)
            group_norm_finish(pbs, pbsn, l, 12, 4, N, 0)
            if kind == "p" and last_pass:
                transpose_out(CSC[:, l, :, :], 128, 8, None, ("CSC",), "psc", multi=[(p_sc[l][:, c * 128:(c + 1) * 128], c * 2, c * 2 + 2) for c in range(4)])

            if (_DBG.get("stop") == "sc" and kind == _DBG.get("stop_kind", "p") and l == _DBG.get("stop_layer", 0)):
                raise _Stop()
            pieces = [(lambda s: wv(s, KC, 512)[:, :, 256:512], w_in[l][:, 1280:1536].rearrange("(k p) m -> p k m", p=128))]
            for h_ in range(2):
                for g_ in range(4):
                    c_src = 1024 + g_ * 64 + h_ * 32
                    c_dst = h_ * 128 + g_ * 32
                    pieces.append((lambda s, c_dst=c_dst: wv(s, KC, 512)[:, :, c_dst:c_dst + 32], w_in[l][:, c_src:c_src + 32].rearrange("(k p) m -> p k m", p=128)))
            slot, wtok = wload(pieces, "kv")
            sv = wv(slot, KC, 512)
            if kind == "p":
                for pl in range(2):
                    op("dve", lambda e, pl=pl: e.tensor_copy(out=KH[:, l, pl, 0:128], in_=KH[:, l, pl, T:T + 128]), R=("KH",), W=("KH",))
                op("dve", lambda e: e.tensor_copy(out=VH[:, l, 0, :], in_=VH[:, l, 4, :]), R=("VH",), W=("VH",))
            kf = []
            for pl in range(2):
                pa, pan = gemm_chunk(lambda k: sv[:, k, pl * 128:(pl + 1) * 128], wtok, KC, xr, xt, N)
                t_, tn = next_tmp()
                op("act", lambda e: e.activation(out=t_[:, 0:N], in_=pa[:, 0:N], func=AF.Copy), R=(pan,), W=(tn,))
                kf.append((t_, tn))
            rope(kf, [(KF[0], "KF0"), (KF[1], "KF1")], N)
            if kind == "p":
                for pl in range(2):
                    op("act", lambda e, pl=pl: e.activation(out=KH[:, l, pl, 128:128 + T], in_=KF[pl][:, 0:T], func=AF.Copy), R=(f"KF{pl}",), W=("KH",))
                if last_pass:
                    for pl in range(2):
                        transpose_out(KF[pl][:, T - 128:T], 128, 128, p_k[l].rearrange("t (g h i) -> t h g i", h=2, i=32)[:, pl, :, :], (f"KF{pl}",), "pk", view=lambda a: a.rearrange("t (g i) -> t g i", i=32))
            else:
                for pl in range(2):
                    op("act", lambda e, pl=pl: e.activation(out=QK[:, 8 + pl, 0:NS], in_=KF[pl][:, 0:NS], func=AF.Copy), R=(f"KF{pl}",), W=(f"QK{8 + pl}",))
                    transpose_out(KF[pl][:, 0:NS], 128, NS, s_k[l].rearrange("n s (g h i) -> n s h g i", h=2, i=32)[:, 127, pl, :, :], (f"KF{pl}",), f"skB{l}", view=lambda a: a.rearrange("t (g i) -> t g i", i=32))
            ntb = 4 if kind == "p" else 1
            for tb in range(ntb):
                M = 128 if kind == "p" else NS
                pa, pan = next_pa()
                for k in range(KC):
                    op("pe", lambda e: e.matmul(pa[0:M, 0:256], lhsT=XN[:, k, tb * 128:tb * 128 + M], rhs=sv[:, k, 256:512], start=(k == 0), stop=(k == KC - 1)),
                       R=(wtok, f"XN{k}"), W=(pan,), inc=(k == KC - 1))
                if kind == "p":
                    op("act", lambda e: e.activation(out=VH[:, l, 1 + tb, :], in_=pa[:, 0:256], func=AF.Copy), R=(pan,), W=("VH",))
                    if last_pass and tb == 3:
                        op("act", lambda e: e.activation(out=ROW[:, 0:256], in_=pa[:, 0:256], func=AF.Copy), R=(pan,), W=("ROW",))
                        trk.dma("sp", p_v[l], ROW[:, 0:256], R=("ROW",), W=("pv",), sem="oROW")
                else:
                    op("act", lambda e: e.activation(out=ROW[0:NS, 0:256], in_=pa[0:NS, 0:256], func=AF.Copy), R=(pan,), W=("ROW",))
                    trk.dma("sp", s_v[l][:, 127, :], ROW[0:NS, 0:256], R=("ROW",), W=(f"svB{l}",), sem="oROW")

            if (_DBG.get("stop") == "kv" and kind == _DBG.get("stop_kind", "p") and l == _DBG.get("stop_layer", 0)):
                raise _Stop()
            for jb in range(2):
                pieces = []
                for jj_ in range(2):
                    for h_ in range(2):
                        for g_ in range(4):
                            c_src = (g_ * 4 + jb * 2 + jj_) * 64 + h_ * 32
                            c_dst = (jj_ * 2 + h_) * 128 + g_ * 32
                            pieces.append((lambda s, c_dst=c_dst: wv(s, KC, 512)[:, :, c_dst:c_dst + 32], w_in[l][:, c_src:c_src + 32].rearrange("(k p) m -> p k m", p=128)))
                slot, wtok = wload(pieces, "q")
                sv = wv(slot, KC, 512)
                for jj in range(2):
                    j = jb * 2 + jj
                    qf = []
                    for pl in range(2):
                        pa, pan = gemm_chunk(lambda k: sv[:, k, (jj * 2 + pl) * 128:(jj * 2 + pl + 1) * 128], wtok, KC, xr, xt, N)
                        t_, tn = next_tmp()
                        op("act", lambda e: e.activation(out=t_[:, 0:N], in_=pa[:, 0:N], func=AF.Copy), R=(pan,), W=(tn,))
                        qf.append((t_, tn))
                    rope(qf, [(QK[:, j * 2, :], f"QK{j * 2}"), (QK[:, j * 2 + 1, :], f"QK{j * 2 + 1}")], N)

            if (_DBG.get("stop") == "q" and kind == _DBG.get("stop_kind", "p") and l == _DBG.get("stop_layer", 0)):
                raise _Stop()
            pba, pban = PB[1], "PB1"
            if kind == "p":
                attention_prompt(l, first_pass)
            else:
                attention_sample(l)
            for c in range(8):
                sq_accum(pba, pban, c, c == 0, c == 7, N)
            group_norm_finish(pba, pban, l, 0, 8, N, 1)

            if (_DBG.get("stop") == "attn" and kind == _DBG.get("stop_kind", "p") and l == _DBG.get("stop_layer", 0)):
                raise _Stop()
            for cb in range(4):
                cs_ = slice(cb * 512, (cb + 1) * 512)
                pieces = []
                for half in range(2):
                    for kc2 in range(2):
                        r0 = kc2 * 512 + half * 256
                        pieces.append((lambda s, half=half, kc2=kc2: wv(s, KC, 512)[half * 64:(half + 1) * 64, kc2 * 4:(kc2 + 1) * 4, :],
                                       w_out[l][r0:r0 + 256, cs_].rearrange("(j d) m -> d j m", d=64)))
                pieces.append((lambda s: wv(s, KC, 512)[:, 8:16, :], w_out[l][1024:2048, cs_].rearrange("(k p) m -> p k m", p=128)))
                slot, wtok = wload(pieces, "out")
                sv = wv(slot, KC, 512)
                for mm in range(4):
                    m = cb * 4 + mm
                    pa, pan = gemm_chunk(lambda k: sv[:, k, mm * 128:(mm + 1) * 128], wtok, KC, lambda k: MX[:, k, 0:N], lambda k: f"MX{k}", N)
                    op("dve", lambda e: e.tensor_tensor(out=X[:, m, 0:N], in0=pa[:, 0:N], in1=X[:, m, 0:N], op=ALU.add), R=(pan, f"X{m}"), W=(f"X{m}",))

            if (_DBG.get("stop") == "out" and kind == _DBG.get("stop_kind", "p") and l == _DBG.get("stop_layer", 0)):
                raise _Stop()
            rmsnorm_to_xn(l, "nffn", N)
            trk.fence("dve", ["pe", "act"])
            for fb in range(22):
                slot, wtok = wload([
                    (lambda s: wv(s, KC, 512)[:, :, 0:256], w_gu[l][:, fb * 256:(fb + 1) * 256].rearrange("(k p) m -> p k m", p=128)),
                    (lambda s: wv(s, KC, 512)[:, :, 256:512], w_gu[l][:, DFF + fb * 256:DFF + (fb + 1) * 256].rearrange("(k p) m -> p k m", p=128)),
                ], "gu")
                sv = wv(slot, KC, 512)
                for cc in range(2):
                    fc = fb * 2 + cc
                    pa, pan = gemm_chunk(lambda k: sv[:, k, cc * 128:(cc + 1) * 128], wtok, KC, xr, xt, N)
                    sg, sgn = next_tmp()
                    op("act", lambda e: e.activation(out=sg[:, 0:N], in_=pa[:, 0:N], func=AF.Silu), R=(pan,), W=(sgn,))
                    pa, pan = gemm_chunk(lambda k: sv[:, k, 256 + cc * 128:256 + (cc + 1) * 128], wtok, KC, xr, xt, N)
                    op("dve", lambda e: e.tensor_tensor(out=H[:, fc, 0:N], in0=pa[:, 0:N], in1=sg[:, 0:N], op=ALU.mult), R=(pan, sgn), W=(f"H{fc}",))
            for cg in range(8):
                accs = None
                for kh in range(2):
                    slot, wtok = wload([(lambda s: wv(s, 22, 256), w_dn[l][kh * 2816:(kh + 1) * 2816, cg * 256:(cg + 1) * 256].rearrange("(k p) m -> p k m", p=128))], "dn")
                    sv = wv(slot, 22, 256)
                    if kh == 0:
                        accs = [next_pa(), next_pa()]
                    for mm in range(2):
                        pa, pan = accs[mm]
                        for k in range(22):
                            kk = kh * 22 + k
                            op("pe", lambda e: e.matmul(pa[:, 0:N], lhsT=sv[:, k, mm * 128:(mm + 1) * 128], rhs=H[:, kk, 0:N], start=(kk == 0), stop=(kk == FC - 1)),
                               R=(wtok, f"H{kk}"), W=(pan,), inc=(k == 21))
                for mm in range(2):
                    m = cg * 2 + mm
                    pa, pan = accs[mm]
                    op("dve", lambda e: e.tensor_tensor(out=X[:, m, 0:N], in0=pa[:, 0:N], in1=X[:, m, 0:N], op=ALU.add), R=(pan, f"X{m}"), W=(f"X{m}",))

        def rope(src, dst, N):
            (a_, an), (b_, bn) = src
            (da, dan), (db, dbn) = dst
            t1, t1n = next_tmp(); t2, t2n = next_tmp()
            cos = CS[:, 0, 0:N]; sin = CS[:, 1, 0:N]
            op("dve", lambda e: e.tensor_tensor(out=t1[:, 0:N], in0=a_[:, 0:N], in1=cos, op=ALU.mult), R=(an, "CS"), W=(t1n,))
            op("dve", lambda e: e.tensor_tensor(out=t2[:, 0:N], in0=b_[:, 0:N], in1=sin, op=ALU.mult), R=(bn, "CS"), W=(t2n,))
            op("dve", lambda e: e.tensor_tensor(out=da[:, 0:N], in0=t1[:, 0:N], in1=t2[:, 0:N], op=ALU.subtract), R=(t1n, t2n), W=(dan,))
            op("dve", lambda e: e.tensor_tensor(out=t1[:, 0:N], in0=b_[:, 0:N], in1=cos, op=ALU.mult), R=(bn, "CS"), W=(t1n,))
            op("dve", lambda e: e.tensor_tensor(out=t2[:, 0:N], in0=a_[:, 0:N], in1=sin, op=ALU.mult), R=(an, "CS"), W=(t2n,))
            op("dve", lambda e: e.tensor_tensor(out=db[:, 0:N], in0=t1[:, 0:N], in1=t2[:, 0:N], op=ALU.add), R=(t1n, t2n), W=(dbn,))

        pt_i = {"i": 0}

        def attention_prompt(l, first_pass):
            QKv = QK[:, 0:8, :].rearrange("p (j pl) t -> p pl j t", pl=2)
            for qb in range(4):
                for kc in range(2):
                    num, numn = PB[2], "PB2"
                    den, denn = PB[3], "PB3"
                    for half in range(2):
                        g = kc * 2 + half
                        blocks = []
                        if not (first_pass and qb == 0):
                            blocks.append((qb, 1))
                        blocks.append((qb + 1, 0))
                        pts = []
                        for bi, (kb, mk) in enumerate(blocks):
                            sps = PB[bi]; spsn = f"PB{bi}"
                            for pl in range(2):
                                op("pe", lambda e: e.matmul(sps[:, :].rearrange("p (j q) -> p j q", q=128), lhsT=KH[32 * g:32 * g + 32, l, pl, kb * 128:(kb + 1) * 128],
                                                            rhs=QKv[32 * g:32 * g + 32, pl, :, qb * 128:(qb + 1) * 128], start=(pl == 0), stop=(pl == 1), tile_position=(32 * g, 0)),
                                   R=("KH",) + tuple(f"QK{j * 2 + pl}" for j in range(4)), W=(spsn,), inc=(pl == 1))
                            i = pt_i["i"] % 4; pt_i["i"] += 1
                            pt, ptn = PT[i], f"PT{i}"
                            op("act", lambda e: e.activation(out=pt[:, :], in_=sps[:, :], func=AF.Exp, scale=0.125), R=(spsn,), W=(ptn,))
                            op("dve", lambda e: e.tensor_tensor(out=pt[:, :], in0=pt[:, :], in1=MSK[:, mk, :, :].rearrange("p j q -> p (j q)"), op=ALU.mult), R=(ptn, "MSK"), W=(ptn,))
                            pts.append((pt, ptn, kb))
                        for bi, (pt, ptn, kb) in enumerate(pts):
                            st = (bi == 0); sp_ = (bi == len(pts) - 1)
                            op("pe", lambda e: e.matmul(num[half * 64:(half + 1) * 64, :], lhsT=VH[:, l, kb, g * 64:(g + 1) * 64], rhs=pt[:, :], start=st, stop=sp_),
                               R=(ptn, "VH"), W=(numn,), inc=False)
                            op("pe", lambda e: e.matmul(den[half * 64:(half + 1) * 64, :], lhsT=ONES[:, 0:64], rhs=pt[:, :], start=st, stop=sp_),
                               R=(ptn, "ONES"), W=(denn,), inc=True)
                    dn, dnn = next_tmp()
                    op("dve", lambda e: e.tensor_tensor(out=dn[:, 0:512], in0=den[:, :], in1=SNK[:, kc, :, :].rearrange("p j q -> p (j q)"), op=ALU.add), R=(denn, "SNK"), W=(dnn,))
                    op("dve", lambda e: e.reciprocal(out=dn[:, 0:512], in_=dn[:, 0:512]), R=(dnn,), W=(dnn,))
                    op("dve", lambda e: e.tensor_tensor(out=MX[:, kc * 4:kc * 4 + 4, qb * 128:(qb + 1) * 128], in0=num[:, :].rearrange("p (j q) -> p j q", q=128),
                                                        in1=dn[:, 0:512].rearrange("p (j q) -> p j q", q=128), op=ALU.mult),
                       R=(numn, dnn), W=tuple(f"MX{kc * 4 + j}" for j in range(4)))

        def attention_sample(l):
            QKv = QK[:, 0:8, :].rearrange("p (j pl) t -> p pl j t", pl=2)
            first = True
            trk.reserve("ks", 8)
            for h_ in range(2):
                for g_ in range(4):
                    trk.dma("pool", KS[:, :, h_ * 128 + g_ * 32:h_ * 128 + g_ * 32 + 32], s_k[l][:, :, g_ * 64 + h_ * 32:g_ * 64 + h_ * 32 + 32].rearrange("n s i -> s n i"),
                            R=(f"skA{l}", f"skB{l}"), W=("KS",), sem="ks", skip_deps=not first)
                    first = False
            trk.dma("pool", VS2[:], s_v[l].rearrange("n s f -> s n f"), R=(f"svA{l}", f"svB{l}"), W=("VS2",), sem="vs")
            for n in range(NS):
                kt, ktn = KTS[n % 2], f"KTS{n % 2}"
                ptb = PB[3][:, 0:128].bitcast(BF16).rearrange("p (pl s) -> p pl s", pl=2)
                for pl in range(2):
                    op("pe", lambda e: e.transpose(ptb[:, pl, :], KS[:, n, pl * 128:(pl + 1) * 128], IDNB[:, :]), R=("KS", "IDNB"), W=("PB3",))
                op("act", lambda e: e.activation(out=kt[:, :, :], in_=ptb, func=AF.Copy), R=("PB3",), W=(ktn,))
                for g in range(4):
                    for pl in range(2):
                        op("pe", lambda e: e.matmul(PA[g][:, n * 4:n * 4 + 4], lhsT=kt[32 * g:32 * g + 32, pl, :], rhs=QKv[32 * g:32 * g + 32, pl, :, n],
                                                    start=(pl == 0), stop=(pl == 1), tile_position=(32 * g, 0)),
                           R=(ktn,) + tuple(f"QK{j * 2 + pl}" for j in range(4)), W=(f"PA{g}",), inc=(pl == 1))
            pssv = PSS[:, :].rearrange("p (n g j) -> p n g j", g=4, j=4)
            for g in range(4):
                op("act", lambda e: e.activation(out=pssv[:, :, g, :], in_=PA[g][:, 0:64].rearrange("p (n j) -> p n j", j=4), func=AF.Exp, scale=0.125), R=(f"PA{g}",), W=("PSS",))
            den, denn = PB[1], "PB1"
            op("pe", lambda e: e.matmul(den[:, 0:256], lhsT=ONES[:, :], rhs=PSS[:, :], start=True, stop=True), R=("PSS", "ONES"), W=(denn,))
            op("dve", lambda e: e.tensor_tensor(out=RD[:, :], in0=den[:, 0:256], in1=SNKS[:, l, :, :].rearrange("p n h -> p (n h)"), op=ALU.add), R=(denn, "SNKS"), W=("RD",))
            op("dve", lambda e: e.reciprocal(out=RD[:, :], in_=RD[:, :]), R=("RD",), W=("RD",))
            num, numn = PB[2], "PB2"
            numv = num[:, 0:128].rearrange("p (kc j n) -> p kc j n", kc=2, j=4)
            for n in range(NS):
                for g in range(4):
                    hf = g % 2
                    op("pe", lambda e: e.matmul(numv[hf * 64:(hf + 1) * 64, g // 2, :, n], lhsT=VS2[:, n, g * 64:(g + 1) * 64], rhs=PSS[:, n * 16 + g * 4:n * 16 + g * 4 + 4], start=True, stop=True),
                       R=("VS2", "PSS"), W=(numn,), inc=(n == NS - 1 and g == 3))
            rdv = RD[:, :].rearrange("p (n g j) -> p g j n", g=4, j=4)
            for hf in range(2):
                op("dve", lambda e: e.tensor_tensor(out=MX[hf * 64:(hf + 1) * 64, 0:8, 0:NS].rearrange("p (kc j) n -> p kc j n", kc=2),
                                                    in0=numv[hf * 64:(hf + 1) * 64, :, :, :],
                                                    in1=rdv[hf * 64:(hf + 1) * 64, hf::2, :, :], op=ALU.mult),
                   R=(numn, "RD"), W=tuple(f"MX{c}" for c in range(8)))

        def run_pass(pi, kind):
            N = T if kind == "p" else NS
            last_pass = (kind == "p" and pi == NPP - 1)
            ci = pi if kind == "p" else NPP
            trk.dma("sp", CS[:], csd[ci], W=("CS",), sem="cs")
            ntb = 4 if kind == "p" else 1
            for tb in range(ntb):
                rows = 128 if kind == "p" else NS
                src = xp[pi * T + tb * 128: pi * T + tb * 128 + 128, :] if kind == "p" else xs
                trk.dma("sp", STG[0:rows, 0:D], src, W=("STG",), sem="xin")
                for c4 in range(4):
                    pb = PB[c4 % 2]; pbn = f"PB{c4 % 2}"
                    for cc in range(4):
                        c = c4 * 4 + cc
                        op("pe", lambda e: e.transpose(pb[:, cc * 128:cc * 128 + rows], STG[0:rows, c * 128:(c + 1) * 128], IDN[0:rows, 0:rows]), R=("STG", "IDN"), W=(pbn,), inc=(cc == 3))
                    op("act", lambda e: e.activation(out=X[:, c4 * 4:c4 * 4 + 4, tb * 128:tb * 128 + rows], in_=pb[:, :].rearrange("p (c t) -> p c t", t=128)[:, :, 0:rows], func=AF.Copy),
                       R=(pbn,), W=tuple(f"X{c4 * 4 + cc}" for cc in range(4)))
            if (_DBG.get("stop") == "load" and kind == _DBG.get("stop_kind", "p")):
                raise _Stop()
            for l in range(2):
                if l == 1 and (_DBG.get("stop") == "l1" and kind == _DBG.get("stop_kind", "p")):
                    raise _Stop()
                layer(l, pi, kind, last_pass)
            if (_DBG.get("stop") == "final" and kind == _DBG.get("stop_kind", "p")):
                raise _Stop()
            pb, pbn = PB[0], "PB0"
            for c in range(KC):
                sq, sqn = next_sq()
                op("act", lambda e: e.activation(out=sq[:, 0:N], in_=X[:, c, 0:N], func=AF.Square), R=(f"X{c}",), W=(sqn,))
                op("pe", lambda e: e.matmul(pb[:, 0:N], lhsT=ONES[:, :], rhs=sq[:, 0:N], start=(c == 0), stop=(c == KC - 1)), R=(sqn, "ONES"), W=(pbn,), inc=True)
            rs = RS[0]
            op("dve", lambda e: e.tensor_scalar(out=rs[:, 0:N], in0=pb[:, 0:N], scalar1=1.0 / D, scalar2=EPS, op0=ALU.mult, op1=ALU.add), R=(pbn,), W=("RS0",))
            op("act", lambda e: e.activation(out=rs[:, 0:N], in_=rs[:, 0:N], func=AF.Sqrt), R=("RS0",), W=("RS0",))
            op("dve", lambda e: e.reciprocal(out=rs[:, 0:N], in_=rs[:, 0:N]), R=("RS0",), W=("RS0",))
            o_f, _ = lay[("nfin", 0)]
            for c in range(KC):
                op("dve", lambda e: e.scalar_tensor_tensor(out=X[:, c, 0:N], in0=X[:, c, 0:N], scalar=PV[:, o_f + c:o_f + c + 1], in1=rs[:, 0:N], op0=ALU.mult, op1=ALU.mult),
                   R=(f"X{c}", "RS0", "PV"), W=(f"X{c}",))
            for tb in range(ntb):
                rows = 128 if kind == "p" else NS
                for c4 in range(4):
                    pb = PB[c4 % 2]; pbn = f"PB{c4 % 2}"
                    for cc in range(4):
                        c = c4 * 4 + cc
                        op("pe", lambda e: e.transpose(pb[0:rows, cc * 128:(cc + 1) * 128], X[:, c, tb * 128:tb * 128 + rows], IDN[:, :]), R=(f"X{c}", "IDN"), W=(pbn,), inc=(cc == 3))
                    op("act", lambda e: e.activation(out=STG[0:rows, c4 * 512:(c4 + 1) * 512], in_=pb[0:rows, :], func=AF.Copy), R=(pbn,), W=("STG",))
                dst = yp[pi * T + tb * 128: pi * T + tb * 128 + 128, :] if kind == "p" else ys
                trk.dma("sp", dst, STG[0:rows, 0:D], R=("STG",), W=("yout",), sem="oSTG")

        try:
            for pi in range(NPP):
                run_pass(pi, "p")
            if not _DBG.get("nosample"):
                run_pass(0, "s")
        except _Stop:
            pass
        trk.wait_all_dma("sp")
    return nc


_CACHE = {}
_DBG = {}


def kernel(x_prompt, x_sample, state_lru_h, state_lru_conv, cache_swa_k, cache_swa_v, state_sconv,
           norm_mix, w_in, norm_grp, w_out, lru_conv_w, lru_conv_b, lru_w_a, lru_b_a, lru_w_i, lru_b_i,
           lru_lambda, sc_conv_w, attn_sinks, norm_ffn, ffn_w_gu, ffn_w_down, norm_final):
    f = lambda a: np.ascontiguousarray(np.asarray(a, dtype=np.float32))
    x_prompt = f(x_prompt); x_sample = f(x_sample)
    NPP = _DBG.get('npp', SEQ // T)
    lay, NV = pv_layout()
    pv = np.zeros((128, NV), np.float32)

    def put(name, l, arr2d):
        o, n = lay[(name, l)]
        assert arr2d.shape == (128, n), (name, arr2d.shape)
        pv[:, o:o + n] = arr2d
    fm = lambda v: f(v).reshape(-1, 128).T
    for l in range(2):
        put("nmix", l, fm(norm_mix[l])); put("nffn", l, fm(norm_ffn[l]))
        g = f(norm_grp[l])
        ga = g[:1024].reshape(2, 2, 4, 64)
        ga = ga.transpose(1, 3, 0, 2).reshape(128, 8)
        put("ngrp", l, np.concatenate([ga, fm(g[1024:1536]), fm(g[1536:2048])], axis=1))
        cw = f(lru_conv_w[l])
        put("lcw", l, np.concatenate([fm(cw[k]) for k in range(4)], axis=1))
        put("lcb", l, fm(lru_conv_b[l])); put("ba", l, fm(lru_b_a[l])); put("bi", l, fm(lru_b_i[l])); put("lam", l, fm(lru_lambda[l]))
        sw = f(sc_conv_w[l])
        put("scw", l, np.concatenate([fm(sw[k]) for k in range(3)], axis=1))
        sk = f(attn_sinks[l]).reshape(2, 2, 4)
        sk2 = np.repeat(sk.transpose(1, 0, 2).reshape(2, 1, 8), 64, axis=1).reshape(128, 8)
        put("sink", l, sk2)
        put("sinkall", l, np.repeat(f(attn_sinks[l])[None, :], 128, axis=0))
    put("nfin", 0, fm(norm_final))
    bdh = np.zeros((2, 128, 8, 128), np.float32)
    for l in range(2):
        for gi, wsrc in enumerate((lru_w_a, lru_w_i)):
            wl = f(wsrc[l])
            for c in range(4):
                bdh[l, 0:64, gi * 4 + c, 0:64] = wl[2 * c]
                bdh[l, 64:128, gi * 4 + c, 64:128] = wl[2 * c + 1]
    half = 32
    inv = (10000.0 ** (-np.arange(half, dtype=np.float32) / half)).astype(np.float32)
    cs = np.zeros((NPP + 1, 128, 2, T), np.float32)
    for pi in range(NPP + 1):
        pos = (np.arange(T) + pi * T).astype(np.float32) if pi < NPP else np.full(T, float(PAST), np.float32)
        ang = pos[None, :] * inv[:, None]
        cs[pi, :, 0, :] = np.tile(np.cos(ang).astype(np.float32), (4, 1))
        cs[pi, :, 1, :] = np.tile(np.sin(ang).astype(np.float32), (4, 1))
    sidx = np.arange(128)[:, None]; qidx = np.arange(128)[None, :]
    msk = np.stack([(sidx <= qidx), (sidx > qidx)], axis=1).astype(np.float32)
    idn = np.eye(128, dtype=np.float32)
    w_in = f(w_in); w_out = f(w_out); ffn_w_gu = f(ffn_w_gu); ffn_w_down = f(ffn_w_down)
    sh = f(state_lru_h); scv = f(state_lru_conv); ssc = f(state_sconv)
    ckk = f(cache_swa_k).reshape(2, 128, 128, 256); cvv = f(cache_swa_v).reshape(2, 128, 128, 256)
    in_maps = []
    for c in range(NCORES):
        b = c % 4
        ns = slice(c * NS, (c + 1) * NS)
        in_maps.append({
            "xp": np.ascontiguousarray(x_prompt[b][:NPP * T]), "xs": np.ascontiguousarray(x_sample[ns, 0, :]),
            "st_h": np.ascontiguousarray(sh[:, ns]), "st_conv": np.ascontiguousarray(scv[:, ns]), "st_sc": np.ascontiguousarray(ssc[:, ns]),
            "ck": np.ascontiguousarray(ckk[:, ns]), "cv": np.ascontiguousarray(cvv[:, ns]),
            "w_in": w_in, "w_out": w_out, "w_gu": ffn_w_gu, "w_dn": ffn_w_down,
            "bd": bdh, "pv": pv, "cs": cs, "msk": msk, "idn": idn,
        })
    if NPP not in _CACHE:
        _CACHE[NPP] = build(NPP)
    nc = _CACHE[NPP]
    if 'ncores' in _DBG:
        n_ = _DBG['ncores']
        res = run_bass_kernel_spmd(nc, in_maps[:n_], core_ids=list(range(n_)), trace=_DBG.get('trace', False))
        _DBG['res'] = res
        return res.results
    res = run_bass_kernel_spmd(nc, in_maps, core_ids=list(range(NCORES)))
    R = res.results
    y_prompt = np.stack([R[b]["yp"] for b in range(4)]).reshape(4, SEQ, D)
    y_sample = np.concatenate([R[c]["ys"] for c in range(NCORES)], axis=0).reshape(128, 1, D)
    stk = lambda name, shp: np.stack([R[b][name] for b in range(4)], axis=1).reshape(shp)
    p_lru_h = stk("p_h", (2, 4, 512))
    p_lru_conv = stk("p_conv", (2, 4, 3, 512))
    p_swa_k = stk("p_k", (2, 4, 128, 4, 64))
    p_swa_v = stk("p_v", (2, 4, 128, 4, 64))
    p_sconv = stk("p_sc", (2, 4, 2, 512))
    cat = lambda name, shp: np.concatenate([R[c][name] for c in range(NCORES)], axis=1).reshape(shp)
    s_lru_h = cat("s_h", (2, 128, 512))
    s_lru_conv = cat("s_conv", (2, 128, 3, 512))
    s_swa_k = cat("s_k", (2, 128, 128, 4, 64))
    s_swa_v = cat("s_v", (2, 128, 128, 4, 64))
    s_sconv = cat("s_sc", (2, 128, 2, 512))
    outs = (y_prompt, y_sample, p_lru_h, p_lru_conv, p_swa_k, p_swa_v, p_sconv, s_lru_h, s_lru_conv, s_swa_k, s_swa_v, s_sconv)
    return tuple(np.ascontiguousarray(o, dtype=np.float32) for o in outs)
```

```python
import numpy as np
from contextlib import ExitStack
import concourse.bass as bass
import concourse.mybir as mybir
from concourse.bass_utils import run_bass_kernel_spmd

F32 = mybir.dt.float32
BF16 = mybir.dt.bfloat16
AF = mybir.ActivationFunctionType
ALU = mybir.AluOpType

D = 2048
KC = 16
T = 512
NS = 16
DFF = 5632
FC = 44
EPS = 1e-6
PAST = 8192
NCORES = 8
SEQ = 2048


def pv_layout():
    lay = {}
    off = 0
    for l in range(2):
        for name, n in (("nmix", 16), ("nffn", 16), ("ngrp", 16), ("lcw", 16), ("lcb", 4), ("ba", 4), ("bi", 4),
                        ("lam", 4), ("scw", 12), ("sink", 8), ("sinkall", 16)):
            lay[(name, l)] = (off, n)
            off += n
    lay[("nfin", 0)] = (off, 16)
    off += 16
    return lay, off


class _Stop(Exception):
    pass


class Trk:
    SEM_MAX = 3800

    def __init__(self, nc, es):
        self.nc = nc
        self.es = es
        self.E = {"pe": nc.tensor, "act": nc.scalar, "dve": nc.vector, "pool": nc.gpsimd, "sp": nc.sync}
        self.gen = {k: 0 for k in self.E}
        self.sem = {k: es.enter_context(nc.semaphore("pg_" + k + "_0")) for k in self.E}
        self.semname = {k: "pg_" + k + "_0" for k in self.E}
        self.cnt = {k: 0 for k in self.E}
        self.seen = {k: {} for k in self.E}
        self.lastw = {}
        self.readers = {}
        self.dsems = {}
        self.all_dsems = []
        self.pending_noinc = {k: False for k in self.E}

    def _wait(self, eng, ev):
        name, semh, val = ev
        if eng == "pe" and name.startswith("pg_pe_"):
            return
        if self.seen[eng].get(name, 0) >= val:
            return
        self.E[eng].wait_ge(semh, val)
        self.seen[eng][name] = val

    def _deps(self, eng, R, W):
        best = {}

        def add(ev):
            if ev[0] not in best or best[ev[0]][2] < ev[2]:
                best[ev[0]] = ev
        for r in R:
            if r in self.lastw:
                add(self.lastw[r])
        for w in W:
            if w in self.lastw:
                add(self.lastw[w])
            for ev in self.readers.get(w, {}).values():
                add(ev)
        for ev in best.values():
            self._wait(eng, ev)

    def _commit(self, ev, R, W):
        for r in R:
            d = self.readers.setdefault(r, {})
            if ev[0] not in d or d[ev[0]][2] < ev[2]:
                d[ev[0]] = ev
        for w in W:
            self.lastw[w] = ev
            self.readers[w] = {}

    def _roll(self, eng):
        if self.cnt[eng] >= self.SEM_MAX and not self.pending_noinc[eng]:
            self.gen[eng] += 1
            nm = f"pg_{eng}_{self.gen[eng]}"
            self.sem[eng] = self.es.enter_context(self.nc.semaphore(nm))
            self.semname[eng] = nm
            self.cnt[eng] = 0

    def op(self, eng, fn, R=(), W=(), inc=True):
        self._roll(eng)
        self._deps(eng, R, W)
        ins = fn(self.E[eng])
        if inc:
            self.cnt[eng] += 1
            ins.then_inc(self.sem[eng], 1)
            ev = (self.semname[eng], self.sem[eng], self.cnt[eng])
            self.pending_noinc[eng] = False
        else:
            ev = (self.semname[eng], self.sem[eng], self.cnt[eng] + 1)
            self.pending_noinc[eng] = True
        self._commit(ev, R, W)
        return ins

    def fence(self, eng, others):
        for o in others:
            self._wait(eng, (self.semname[o], self.sem[o], self.cnt[o] + (1 if self.pending_noinc[o] else 0)))

    def reserve(self, sem, n):
        if sem not in self.dsems or self.dsems[sem][1] + 16 * n > self.SEM_MAX:
            g = self.dsems[sem][3] + 1 if sem in self.dsems else 0
            nm = f"d_{sem}_{g}"
            self.dsems[sem] = [self.es.enter_context(self.nc.semaphore(nm)), 0, nm, g]
            self.all_dsems.append(self.dsems[sem])

    def dma(self, q, out, in_, R=(), W=(), sem="g", skip_deps=False, **kw):
        if not skip_deps:
            self._deps(q, R, W)
        self.reserve(sem, 1)
        ds = self.dsems[sem]
        ins = self.E[q].dma_start(out=out, in_=in_, **kw)
        ds[1] += 16
        ins.then_inc(ds[0], 16)
        ev = (ds[2], ds[0], ds[1])
        self._commit(ev, R, W)
        return ins

    def wait_all_dma(self, eng):
        for ds in self.all_dsems:
            if ds[1]:
                self.E[eng].wait_ge(ds[0], ds[1])


def build(NPP):
    lay, NV = pv_layout()
    TOK = NPP * T
    nc = bass.Bass("TRN2", target_bir_lowering=False)

    def din(name, shape, dt=F32):
        return nc.dram_tensor(name, list(shape), dt, kind="ExternalInput").ap()

    def dout(name, shape, dt=F32):
        return nc.dram_tensor(name, list(shape), dt, kind="ExternalOutput").ap()

    xp = din("xp", [TOK, D]); xs = din("xs", [NS, D])
    st_h = din("st_h", [2, NS, 512]); st_conv = din("st_conv", [2, NS, 3, 512]); st_sc = din("st_sc", [2, NS, 2, 512])
    ck = din("ck", [2, NS, 128, 256]); cv = din("cv", [2, NS, 128, 256])
    w_in = din("w_in", [2, D, 4096]); w_out = din("w_out", [2, D, D]); w_gu = din("w_gu", [2, D, 2 * DFF]); w_dn = din("w_dn", [2, DFF, D])
    bd = din("bd", [2, 128, 8, 128])
    pvd = din("pv", [128, NV])
    csd = din("cs", [NPP + 1, 128, 2, T])
    mskd = din("msk", [128, 2, 128])
    idnd = din("idn", [128, 128])

    yp = dout("yp", [TOK, D]); ys = dout("ys", [NS, D])
    p_h = dout("p_h", [2, 512]); p_conv = dout("p_conv", [2, 3, 512]); p_k = dout("p_k", [2, 128, 256]); p_v = dout("p_v", [2, 128, 256]); p_sc = dout("p_sc", [2, 2, 512])
    s_h = dout("s_h", [2, NS, 512]); s_conv = dout("s_conv", [2, NS, 3, 512]); s_k = dout("s_k", [2, NS, 128, 256]); s_v = dout("s_v", [2, NS, 128, 256]); s_sc = dout("s_sc", [2, NS, 2, 512])

    es = ExitStack()
    with es:
        trk = Trk(nc, es)
        op = trk.op

        def sb(name, shape, dt):
            return es.enter_context(nc.sbuf_tensor(name, list(shape), dt))

        def ps(name, shape, dt):
            return es.enter_context(nc.psum_tensor(name, list(shape), dt))

        X = sb("X", [128, KC, T], F32)
        XN = sb("XN", [128, KC, T], BF16)
        H = sb("H", [128, FC, T], BF16)
        MX = H[:, 0:16, :]
        QK = H[:, 16:26, :]
        KS = H[:, 26:34, :].rearrange("p c (a f) -> p (c a) f", f=256)
        VS2 = H[:, 34:42, :].rearrange("p c (a f) -> p (c a) f", f=256)
        NWS = 2
        WS = [sb(f"WS{i}", [128, 8192], BF16) for i in range(NWS)]
        NTMP = 10
        TMP = [sb(f"TMP{i}", [128, T + 4], F32) for i in range(NTMP)]
        SQ = [sb(f"SQ{i}", [128, T], BF16) for i in range(3)]
        PT = [sb(f"PT{i}", [128, 512], BF16) for i in range(4)]
        STG = XN[:, :, :].rearrange("p c t -> p (c t)").bitcast(F32)
        PV = sb("PV", [128, NV], F32)
        CS = sb("CS", [128, 2, T], F32)
        MSKF = sb("MSKF", [128, 2, 128], F32)
        MSK = sb("MSK", [128, 2, 4, 128], BF16)
        IDN = sb("IDN", [128, 128], F32)
        IDNB = sb("IDNB", [128, 128], BF16)
        ONES = sb("ONES", [128, 128], BF16)
        BD = sb("BD", [128, 8, 128], F32)
        RS = [sb(f"RS{i}", [128, T], F32) for i in range(2)]
        SCL = sb("SCL", [128, 2, 2, 4], F32)
        SNK = sb("SNK", [128, 2, 4, 128], F32)
        EXS = sb("EXS", [128, 2, 8], F32)
        SNKS = sb("SNKS", [128, 2, NS, 16], F32)
        HC = sb("HC", [128, 2, 4], F32)
        CUX = sb("CUX", [128, 2, 4, 3], F32)
        CSC = sb("CSC", [128, 2, 4, 2], F32)
        KH = sb("KH", [128, 2, 2, 128 + T], BF16)
        VH = sb("VH", [128, 2, 5, 256], BF16)
        KF = [sb(f"KF{i}", [128, T], F32) for i in range(2)]
        ROW = sb("ROW", [128, 512], F32)
        UXS = sb("UXS", [128, 4, NS, 4], F32)
        GCS = sb("GCS", [128, 4, NS, 3], F32)
        H0S = sb("H0S", [128, 4, NS], F32)
        STS = sb("STS", [48, 512], F32)
        KTS = [sb(f"KTS{i}", [128, 2, 128], BF16) for i in range(2)]
        PSS = sb("PSS", [128, 256], BF16)
        RD = sb("RD", [128, 256], F32)

        PA = [ps(f"PA{i}", [128, 512], F32) for i in range(4)]
        PB = [ps(f"PB{i}", [128, 512], F32) for i in range(4)]

        wstate = {"i": 0}

        def wload(pieces, nm):
            i = wstate["i"]; wstate["i"] += 1
            slot = i % NWS
            trk.reserve(f"w{slot}", len(pieces))
            for pi_, (dst, src) in enumerate(pieces):
                trk.dma("pool", dst(WS[slot]), src, R=(), W=(f"W{slot}",), sem=f"w{slot}", skip_deps=(pi_ > 0))
            return WS[slot], f"W{slot}"

        def wv(slot, kc, n):
            return slot[:, 0:kc * n].rearrange("p (k n) -> p k n", n=n)

        pa_i = {"i": 0}

        def next_pa():
            i = pa_i["i"] % 4; pa_i["i"] += 1
            return PA[i], f"PA{i}"

        tmp_i = {"i": 0}

        def next_tmp():
            i = tmp_i["i"] % NTMP; tmp_i["i"] += 1
            return TMP[i], f"TMP{i}"

        sq_i = {"i": 0}

        def next_sq():
            i = sq_i["i"] % 3; sq_i["i"] += 1
            return SQ[i], f"SQ{i}"

        def colp(l, name, c=0):
            o, n = lay[(name, l)]
            return PV[:, o + c:o + c + 1]

        trk.dma("sp", PV[:], pvd, W=("PV",), sem="c_pv")
        trk.dma("sp", IDN[:], idnd, W=("IDN",), sem="c_idn")
        trk.dma("sp", MSKF[:], mskd, W=("MSKF",), sem="c_msk")
        op("dve", lambda e: e.tensor_copy(out=IDNB[:], in_=IDN[:]), R=("IDN",), W=("IDNB",))
        op("dve", lambda e: e.memset(ONES[:], 1.0), W=("ONES",))
        for m in range(2):
            for j in range(4):
                op("dve", lambda e, m=m, j=j: e.tensor_copy(out=MSK[:, m, j, :], in_=MSKF[:, m, :]), R=("MSKF",), W=("MSK",))
        for t_, nm in ((HC, "HC"), (CUX, "CUX"), (CSC, "CSC"), (KH, "KH"), (VH, "VH")):
            op("dve", lambda e, t_=t_: e.memset(t_[:], 0.0), W=(nm,))
        for l in range(2):
            o, n = lay[("lam", l)]
            tt, tn = next_tmp()
            op("act", lambda e: e.activation(out=tt[:, 0:4], in_=PV[:, o:o + 4], func=AF.Exp, scale=-1.0), R=("PV",), W=(tn,))
            op("act", lambda e: e.activation(out=tt[:, 4:8], in_=tt[:, 0:4], func=AF.Ln, bias=1.0), R=(tn,), W=(tn,))
            op("dve", lambda e: e.tensor_scalar(out=SCL[:, l, 0, :], in0=tt[:, 4:8], scalar1=-8.0, scalar2=None, op0=ALU.mult), R=(tn,), W=("SCL",))
            op("dve", lambda e: e.tensor_scalar(out=SCL[:, l, 1, :], in0=tt[:, 4:8], scalar1=-16.0, scalar2=None, op0=ALU.mult), R=(tn,), W=("SCL",))
            o, n = lay[("sink", l)]
            op("act", lambda e: e.activation(out=EXS[:, l, :], in_=PV[:, o:o + 8], func=AF.Exp), R=("PV",), W=("EXS",))
            o, n = lay[("sinkall", l)]
            op("act", lambda e: e.activation(out=tt[:, 16:32], in_=PV[:, o:o + 16], func=AF.Exp), R=("PV",), W=(tn,))
            for n_ in range(NS):
                op("dve", lambda e, n_=n_: e.tensor_copy(out=SNKS[:, l, n_, :], in_=tt[:, 16:32]), R=(tn,), W=("SNKS",))

        def rmsnorm_to_xn(l, gname, N):
            pb, pbn = PB[0], "PB0"
            for c in range(KC):
                sq, sqn = next_sq()
                op("act", lambda e: e.activation(out=sq[:, 0:N], in_=X[:, c, 0:N], func=AF.Square), R=(f"X{c}",), W=(sqn,))
                op("pe", lambda e: e.matmul(pb[:, 0:N], lhsT=ONES[:, :], rhs=sq[:, 0:N], start=(c == 0), stop=(c == KC - 1)),
                   R=(sqn, "ONES"), W=(pbn,), inc=True)
            rs = RS[0]
            op("dve", lambda e: e.tensor_scalar(out=rs[:, 0:N], in0=pb[:, 0:N], scalar1=1.0 / D, scalar2=EPS, op0=ALU.mult, op1=ALU.add), R=(pbn,), W=("RS0",))
            op("act", lambda e: e.activation(out=rs[:, 0:N], in_=rs[:, 0:N], func=AF.Sqrt), R=("RS0",), W=("RS0",))
            op("dve", lambda e: e.reciprocal(out=rs[:, 0:N], in_=rs[:, 0:N]), R=("RS0",), W=("RS0",))
            for c in range(KC):
                op("dve", lambda e: e.scalar_tensor_tensor(out=XN[:, c, 0:N], in0=X[:, c, 0:N], scalar=colp(l, gname, c), in1=rs[:, 0:N], op0=ALU.mult, op1=ALU.mult),
                   R=(f"X{c}", "RS0", "PV"), W=(f"XN{c}",))

        def gemm_chunk(lhsT_of_k, wtok, nk, rhs_of_k, rtok_of_k, N, last_in_block=False):
            pa, pan = next_pa()
            for k in range(nk):
                op("pe", lambda e: e.matmul(pa[:, 0:N], lhsT=lhsT_of_k(k), rhs=rhs_of_k(k), start=(k == 0), stop=(k == nk - 1)),
                   R=(wtok, rtok_of_k(k)), W=(pan,), inc=(k == nk - 1))
            return pa, pan

        def group_norm_finish(pb, pbn, l, c0, nch, N, rsi):
            rs = RS[rsi]; rsn = f"RS{rsi}"
            op("dve", lambda e: e.tensor_scalar(out=rs[:, 0:N], in0=pb[:, 0:N], scalar1=1.0 / (nch * 128), scalar2=EPS, op0=ALU.mult, op1=ALU.add), R=(pbn,), W=(rsn,))
            op("act", lambda e: e.activation(out=rs[:, 0:N], in_=rs[:, 0:N], func=AF.Sqrt), R=(rsn,), W=(rsn,))
            op("dve", lambda e: e.reciprocal(out=rs[:, 0:N], in_=rs[:, 0:N]), R=(rsn,), W=(rsn,))
            for c in range(c0, c0 + nch):
                op("dve", lambda e: e.scalar_tensor_tensor(out=MX[:, c, 0:N], in0=MX[:, c, 0:N], scalar=colp(l, "ngrp", c), in1=rs[:, 0:N], op0=ALU.mult, op1=ALU.mult),
                   R=(f"MX{c}", rsn, "PV"), W=(f"MX{c}",))

        def sq_accum(pb, pbn, c, first, last, N):
            sq, sqn = next_sq()
            op("act", lambda e: e.activation(out=sq[:, 0:N], in_=MX[:, c, 0:N], func=AF.Square), R=(f"MX{c}",), W=(sqn,))
            op("pe", lambda e: e.matmul(pb[:, 0:N], lhsT=ONES[:, :], rhs=sq[:, 0:N], start=first, stop=last), R=(sqn, "ONES"), W=(pbn,), inc=True)

        def transpose_out(src_ap, rows, cols, dst_dram, rtoks, tag, view=None, multi=None):
            op("pe", lambda e: e.transpose(PB[3][0:cols, 0:rows], src_ap, IDN[0:rows, 0:rows]), R=tuple(rtoks) + ("IDN",), W=("PB3",))
            op("act", lambda e: e.activation(out=ROW[0:cols, 0:rows], in_=PB[3][0:cols, 0:rows], func=AF.Copy), R=("PB3",), W=("ROW",))
            if multi is not None:
                trk.reserve("oROW", len(multi))
                for (d_ap, r0, r1) in multi:
                    trk.dma("sp", d_ap, ROW[r0:r1, 0:rows], R=("ROW",), W=(tag,), sem="oROW")
                return
            srcv = ROW[0:cols, 0:rows]
            if view is not None:
                srcv = view(srcv)
            trk.dma("sp", dst_dram, srcv, R=("ROW",), W=(tag,), sem="oROW")

        def layer(l, pi, kind, last_pass):
            N = T if kind == "p" else NS
            first_pass = (kind == "p" and pi == 0)
            if kind == "p":
                for kc in range(2):
                    for j in range(4):
                        op("dve", lambda e: e.tensor_scalar(out=SNK[:, kc, j, :], in0=IDN[:, :], scalar1=0.0, scalar2=EXS[:, l, kc * 4 + j:kc * 4 + j + 1], op0=ALU.mult, op1=ALU.add), R=("EXS", "IDN"), W=("SNK",))
            rmsnorm_to_xn(l, "nmix", N)
            xr = lambda k: XN[:, k, 0:N]
            xt = lambda k: f"XN{k}"
            if kind == "s":
                trk.dma("sp", STS[0:48, :], st_conv[l].rearrange("n k f -> (n k) f"), W=("STS",), sem="st")
                for c in range(4):
                    op("pe", lambda e: e.transpose(PB[3][:, 0:48], STS[0:48, c * 128:(c + 1) * 128], IDN[0:48, 0:48]), R=("STS", "IDN"), W=("PB3",))
                    op("act", lambda e: e.activation(out=UXS[:, c, :, 0:3], in_=PB[3][:, 0:48].rearrange("p (n k) -> p n k", k=3), func=AF.Copy), R=("PB3",), W=("UXS",))
                trk.dma("sp", STS[0:32, :], st_sc[l].rearrange("n k f -> (n k) f"), R=(), W=("STS",), sem="st")
                for c in range(4):
                    op("pe", lambda e: e.transpose(PB[3][:, 0:32], STS[0:32, c * 128:(c + 1) * 128], IDN[0:32, 0:32]), R=("STS", "IDN"), W=("PB3",))
                    op("act", lambda e: e.activation(out=GCS[:, c, :, 0:2], in_=PB[3][:, 0:32].rearrange("p (n k) -> p n k", k=2), func=AF.Copy), R=("PB3",), W=("GCS",))
                trk.dma("sp", STS[0:16, :], st_h[l], W=("STS",), sem="st")
                for c in range(4):
                    op("pe", lambda e: e.transpose(PB[3][:, 0:16], STS[0:16, c * 128:(c + 1) * 128], IDN[0:16, 0:16]), R=("STS", "IDN"), W=("PB3",))
                    op("act", lambda e: e.activation(out=H0S[:, c, :], in_=PB[3][:, 0:16], func=AF.Copy), R=("PB3",), W=("H0S",))
                trk.dma("sp", s_k[l][:, 0:127, :], ck[l][:, 1:128, :], W=(f"skA{l}",), sem=f"o2a{l}")
                trk.dma("sp", s_v[l][:, 0:127, :], cv[l][:, 1:128, :], W=(f"svA{l}",), sem=f"o2b{l}")
                trk.dma("sp", s_conv[l][:, 0:2, :], st_conv[l][:, 1:3, :], W=(f"scvA{l}",), sem=f"o2c{l}")
                trk.dma("sp", s_sc[l][:, 0:1, :], st_sc[l][:, 1:2, :], W=(f"sscA{l}",), sem=f"o2d{l}")

            if (_DBG.get("stop") == "norm" and kind == _DBG.get("stop_kind", "p") and l == _DBG.get("stop_layer", 0)):
                raise _Stop()
            trk.dma("sp", BD[:], bd[l], W=("BD",), sem="bd")
            pbl, pbln = PB[1], "PB1"
            pbs, pbsn = PB[2], "PB2"
            deferred = []

            def flush_deferred():
                while deferred:
                    deferred.pop(0)()

            def lru_chunk(c):
                cc = 0
                slot, wtok = wload([
                    (lambda s: wv(s, KC, 512)[:, :, 0:128], w_in[l][:, 1536 + c * 128:1536 + (c + 1) * 128].rearrange("(k p) m -> p k m", p=128)),
                    (lambda s: wv(s, KC, 512)[:, :, 256:384], w_in[l][:, 2048 + c * 128:2048 + (c + 1) * 128].rearrange("(k p) m -> p k m", p=128)),
                ], "lru")
                sv = wv(slot, KC, 512)
                pa, pan = gemm_chunk(lambda k: sv[:, k, cc * 128:(cc + 1) * 128], wtok, KC, xr, xt, N)
                ux, uxn = next_tmp()
                xc, xcn = next_tmp()
                if kind == "p":
                    op("dve", lambda e: e.tensor_copy(out=ux[:, 0:3], in_=CUX[:, l, c, :]), R=("CUX",), W=(uxn,))
                    op("act", lambda e: e.activation(out=ux[:, 3:3 + T], in_=pa[:, 0:T], func=AF.Copy), R=(pan,), W=(uxn,))
                    op("dve", lambda e: e.tensor_copy(out=CUX[:, l, c, :], in_=ux[:, T:T + 3]), R=(uxn,), W=("CUX",))
                    tap = lambda k: ux[:, k:k + T]
                else:
                    op("act", lambda e: e.activation(out=UXS[:, c, :, 3], in_=pa[:, 0:NS], func=AF.Copy), R=(pan,), W=("UXS",))
                    uxn = "UXS"
                    tap = lambda k: UXS[:, c, :, k]
                o_w, _ = lay[("lcw", l)]
                op("dve", lambda e: e.tensor_scalar(out=xc[:, 0:N], in0=tap(3), scalar1=PV[:, o_w + 12 + c:o_w + 13 + c], scalar2=colp(l, "lcb", c), op0=ALU.mult, op1=ALU.add), R=(uxn, "PV"), W=(xcn,))
                for k in (2, 1, 0):
                    op("dve", lambda e, k=k: e.scalar_tensor_tensor(out=xc[:, 0:N], in0=tap(k), scalar=PV[:, o_w + k * 4 + c:o_w + k * 4 + c + 1], in1=xc[:, 0:N], op0=ALU.mult, op1=ALU.add), R=(uxn, "PV", xcn), W=(xcn,))
                if kind == "s":
                    transpose_out(UXS[:, c, :, 3], 128, NS, s_conv[l][:, 2, c * 128:(c + 1) * 128], ("UXS",), f"scvB{l}")
                pa2, pa2n = gemm_chunk(lambda k: sv[:, k, 256:384], wtok, KC, xr, xt, N)
                ug, ugn = next_tmp()
                op("act", lambda e: e.activation(out=ug[:, 0:N], in_=pa2[:, 0:N], func=AF.Copy), R=(pa2n,), W=(ugn,))
                gates = []
                for gi, bname in ((0, "ba"), (1, "bi")):
                    pg, pgn = next_pa()
                    op("pe", lambda e: e.matmul(pg[:, 0:N], lhsT=BD[:, gi * 4 + c, :], rhs=xc[:, 0:N], start=True, stop=True), R=("BD", xcn), W=(pgn,))
                    gt, gtn = next_tmp()
                    op("act", lambda e: e.activation(out=gt[:, 0:N], in_=pg[:, 0:N], func=AF.Sigmoid, bias=colp(l, bname, c)), R=(pgn, "PV"), W=(gtn,))
                    gates.append((gt, gtn))
                (rg, rgn), (ig, ign) = gates
                a_, an = next_tmp()
                m_, mn = next_tmp()
                op("act", lambda e: e.activation(out=a_[:, 0:N], in_=rg[:, 0:N], func=AF.Exp, scale=SCL[:, l, 0, c:c + 1]), R=(rgn, "SCL"), W=(an,))
                op("act", lambda e: e.activation(out=m_[:, 0:N], in_=rg[:, 0:N], func=AF.Exp, scale=SCL[:, l, 1, c:c + 1]), R=(rgn, "SCL"), W=(mn,))
                op("dve", lambda e: e.tensor_scalar(out=m_[:, 0:N], in0=m_[:, 0:N], scalar1=-1.0, scalar2=1.0, op0=ALU.mult, op1=ALU.add), R=(mn,), W=(mn,))
                op("act", lambda e: e.activation(out=m_[:, 0:N], in_=m_[:, 0:N], func=AF.Sqrt), R=(mn,), W=(mn,))
                if first_pass:
                    op("dve", lambda e: e.memset(m_[:, 0:1], 1.0), R=(mn,), W=(mn,))
                op("dve", lambda e: e.tensor_tensor(out=ig[:, 0:N], in0=ig[:, 0:N], in1=xc[:, 0:N], op=ALU.mult), R=(ign, xcn), W=(ign,))
                op("dve", lambda e: e.tensor_tensor(out=ig[:, 0:N], in0=ig[:, 0:N], in1=m_[:, 0:N], op=ALU.mult), R=(ign, mn), W=(ign,))
                hs, hsn = rg, rgn
                if kind == "p":
                    op("dve", lambda e: e.tensor_tensor_scan(out=hs[:, 0:T], data0=a_[:, 0:T], data1=ig[:, 0:T], initial=HC[:, l, c:c + 1], op0=ALU.mult, op1=ALU.add), R=(an, ign, "HC", rgn), W=(hsn,))
                    op("dve", lambda e: e.tensor_copy(out=HC[:, l, c:c + 1], in_=hs[:, T - 1:T]), R=(hsn,), W=("HC",))
                else:
                    op("dve", lambda e: e.tensor_tensor(out=hs[:, 0:N], in0=a_[:, 0:N], in1=H0S[:, c, :], op=ALU.mult), R=(an, "H0S", rgn), W=(hsn,))
                    op("dve", lambda e: e.tensor_tensor(out=hs[:, 0:N], in0=hs[:, 0:N], in1=ig[:, 0:N], op=ALU.add), R=(hsn, ign), W=(hsn,))
                    transpose_out(hs[:, 0:NS], 128, NS, s_h[l][:, c * 128:(c + 1) * 128], (hsn,), f"shB{l}")
                flush_deferred()
                g2, g2n = next_tmp()
                op("dve", lambda e: e.tensor_tensor(out=g2[:, 0:N], in0=ug[:, 0:N], in1=ug[:, 0:N], op=ALU.mult), R=(ugn,), W=(g2n,))
                op("dve", lambda e: e.tensor_scalar(out=g2[:, 0:N], in0=g2[:, 0:N], scalar1=0.044715, scalar2=1.0, op0=ALU.mult, op1=ALU.add), R=(g2n,), W=(g2n,))
                op("dve", lambda e: e.tensor_tensor(out=g2[:, 0:N], in0=g2[:, 0:N], in1=ug[:, 0:N], op=ALU.mult), R=(g2n, ugn), W=(g2n,))
                op("act", lambda e: e.activation(out=g2[:, 0:N], in_=g2[:, 0:N], func=AF.Sigmoid, scale=1.5957691216057308), R=(g2n,), W=(g2n,))
                op("dve", lambda e: e.tensor_tensor(out=g2[:, 0:N], in0=g2[:, 0:N], in1=ug[:, 0:N], op=ALU.mult), R=(g2n, ugn), W=(g2n,))
                op("dve", lambda e: e.tensor_tensor(out=MX[:, 8 + c, 0:N], in0=g2[:, 0:N], in1=hs[:, 0:N], op=ALU.mult), R=(g2n, hsn), W=(f"MX{8 + c}",))
                deferred.append(lambda: sq_accum(pbl, pbln, 8 + c, c == 0, c == 3, N))

            def sc_chunk(c):
                slot, wtok = wload([
                    (lambda s: wv(s, KC, 384)[:, :, 0:128], w_in[l][:, 3072 + c * 128:3072 + (c + 1) * 128].rearrange("(k p) m -> p k m", p=128)),
                    (lambda s: wv(s, KC, 384)[:, :, 128:256], w_in[l][:, 3584 + c * 128:3584 + (c + 1) * 128].rearrange("(k p) m -> p k m", p=128)),
                    (lambda s: wv(s, KC, 384)[:, :, 256:384], w_in[l][:, 2560 + c * 128:2560 + (c + 1) * 128].rearrange("(k p) m -> p k m", p=128)),
                ], "sc")
                sv = wv(slot, KC, 384)
                pa, pan = gemm_chunk(lambda k: sv[:, k, 0:128], wtok, KC, xr, xt, N)
                uc, ucn = next_tmp()
                op("act", lambda e: e.activation(out=uc[:, 0:N], in_=pa[:, 0:N], func=AF.Copy), R=(pan,), W=(ucn,))
                flush_deferred()
                pa, pan = gemm_chunk(lambda k: sv[:, k, 128:256], wtok, KC, xr, xt, N)
                gc, gcn = next_tmp()
                if kind == "p":
                    op("dve", lambda e: e.tensor_copy(out=gc[:, 0:2], in_=CSC[:, l, c, :]), R=("CSC",), W=(gcn,))
                    op("dve", lambda e: e.tensor_tensor(out=gc[:, 2:2 + T], in0=pa[:, 0:T], in1=uc[:, 0:T], op=ALU.mult), R=(pan, ucn), W=(gcn,))
                    op("dve", lambda e: e.tensor_copy(out=CSC[:, l, c, :], in_=gc[:, T:T + 2]), R=(gcn,), W=("CSC",))
                    tap = lambda k: gc[:, k:k + T]
                else:
                    op("dve", lambda e: e.tensor_tensor(out=GCS[:, c, :, 2], in0=pa[:, 0:NS], in1=uc[:, 0:NS], op=ALU.mult), R=(pan, ucn), W=("GCS",))
                    gcn = "GCS"
                    tap = lambda k: GCS[:, c, :, k]
                    transpose_out(GCS[:, c, :, 2], 128, NS, s_sc[l][:, 1, c * 128:(c + 1) * 128], ("GCS",), f"sscB{l}")
                y_, yn = next_tmp()
                o_w, _ = lay[("scw", l)]
                op("dve", lambda e: e.tensor_scalar(out=y_[:, 0:N], in0=tap(2), scalar1=PV[:, o_w + 8 + c:o_w + 9 + c], scalar2=None, op0=ALU.mult), R=(gcn, "PV"), W=(yn,))
                for k in (1, 0):
                    op("dve", lambda e, k=k: e.scalar_tensor_tensor(out=y_[:, 0:N], in0=tap(k), scalar=PV[:, o_w + k * 4 + c:o_w + k * 4 + c + 1], in1=y_[:, 0:N], op0=ALU.mult, op1=ALU.add), R=(gcn, "PV", yn), W=(yn,))
                pa, pan = gemm_chunk(lambda k: sv[:, k, 256:384], wtok, KC, xr, xt, N)
                op("dve", lambda e: e.tensor_tensor(out=MX[:, 12 + c, 0:N], in0=pa[:, 0:N], in1=y_[:, 0:N], op=ALU.mult), R=(pan, yn), W=(f"MX{12 + c}",))
                deferred.append(lambda: sq_accum(pbs, pbsn, 12 + c, c == 0, c == 3, N))

            for c in range(4):
                lru_chunk(c)
                sc_chunk(c)
            flush_deferred()
            group_norm_finish(pbl, pbln, l, 8, 4, N, 1)
            group_norm_finish(pbs, pbsn, l, 12, 4, N, 0)
            if kind == "p" and last_pass:
                transpose_out(HC[:, l, :], 128, 4, p_h[l].rearrange("(c f) -> c f", f=128), ("HC",), "ph")
                transpose_out(CUX[:, l, :, :], 128, 12, None, ("CUX",), "pconv", multi=[(p_conv[l][:, c * 128:(c + 1) * 128], c * 3, c * 3 + 3) for c in range(4)])
                transpose_out(CSC[:, l, :, :], 128, 8, None, ("CSC",), "psc", multi=[(p_sc[l][:, c * 128:(c + 1) * 128], c * 2, c * 2 + 2) for c in range(4)])

            if (_DBG.get("stop") == "sc" and kind == _DBG.get("stop_kind", "p") and l == _DBG.get("stop_layer", 0)):
                raise _Stop()
            pieces = [(lambda s: wv(s, KC, 512)[:, :, 256:512], w_in[l][:, 1280:1536].rearrange("(k p) m -> p k m", p=128))]
            for h_ in range(2):
                for g_ in range(4):
                    c_src = 1024 + g_ * 64 + h_ * 32
                    c_dst = h_ * 128 + g_ * 32
                    pieces.append((lambda s, c_dst=c_dst: wv(s, KC, 512)[:, :, c_dst:c_dst + 32], w_in[l][:, c_src:c_src + 32].rearrange("(k p) m -> p k m", p=128)))
            slot, wtok = wload(pieces, "kv")
            sv = wv(slot, KC, 512)
            if kind == "p":
                for pl in range(2):
                    op("dve", lambda e, pl=pl: e.tensor_copy(out=KH[:, l, pl, 0:128], in_=KH[:, l, pl, T:T + 128]), R=("KH",), W=("KH",))
                op("dve", lambda e: e.tensor_copy(out=VH[:, l, 0, :], in_=VH[:, l, 4, :]), R=("VH",), W=("VH",))
            kf = []
            for pl in range(2):
                pa, pan = gemm_chunk(lambda k: sv[:, k, pl * 128:(pl + 1) * 128], wtok, KC, xr, xt, N)
                t_, tn = next_tmp()
                op("act", lambda e: e.activation(out=t_[:, 0:N], in_=pa[:, 0:N], func=AF.Copy), R=(pan,), W=(tn,))
                kf.append((t_, tn))
            rope(kf, [(KF[0], "KF0"), (KF[1], "KF1")], N)
            if kind == "p":
                for pl in range(2):
                    op("act", lambda e, pl=pl: e.activation(out=KH[:, l, pl, 128:128 + T], in_=KF[pl][:, 0:T], func=AF.Copy), R=(f"KF{pl}",), W=("KH",))
                if last_pass:
                    for pl in range(2):
                        transpose_out(KF[pl][:, T - 128:T], 128, 128, p_k[l].rearrange("t (g h i) -> t h g i", h=2, i=32)[:, pl, :, :], (f"KF{pl}",), "pk", view=lambda a: a.rearrange("t (g i) -> t g i", i=32))
            else:
                for pl in range(2):
                    op("act", lambda e, pl=pl: e.activation(out=QK[:, 8 + pl, 0:NS], in_=KF[pl][:, 0:NS], func=AF.Copy), R=(f"KF{pl}",), W=(f"QK{8 + pl}",))
                    transpose_out(KF[pl][:, 0:NS], 128, NS, s_k[l].rearrange("n s (g h i) -> n s h g i", h=2, i=32)[:, 127, pl, :, :], (f"KF{pl}",), f"skB{l}", view=lambda a: a.rearrange("t (g i) -> t g i", i=32))
            ntb = 4 if kind == "p" else 1
            for tb in range(ntb):
                M = 128 if kind == "p" else NS
                pa, pan = next_pa()
                for k in range(KC):
                    op("pe", lambda e: e.matmul(pa[0:M, 0:256], lhsT=XN[:, k, tb * 128:tb * 128 + M], rhs=sv[:, k, 256:512], start=(k == 0), stop=(k == KC - 1)),
                       R=(wtok, f"XN{k}"), W=(pan,), inc=(k == KC - 1))
                if kind == "p":
                    op("act", lambda e: e.activation(out=VH[:, l, 1 + tb, :], in_=pa[:, 0:256], func=AF.Copy), R=(pan,), W=("VH",))
                    if last_pass and tb == 3:
                        op("act", lambda e: e.activation(out=ROW[:, 0:256], in_=pa[:, 0:256], func=AF.Copy), R=(pan,), W=("ROW",))
                        trk.dma("sp", p_v[l], ROW[:, 0:256], R=("ROW",), W=("pv",), sem="oROW")
                else:
                    op("act", lambda e: e.activation(out=ROW[0:NS, 0:256], in_=pa[0:NS, 0:256], func=AF.Copy), R=(pan,), W=("ROW",))
                    trk.dma("sp", s_v[l][:, 127, :], ROW[0:NS, 0:256], R=("ROW",), W=(f"svB{l}",), sem="oROW")

            if (_DBG.get("stop") == "kv" and kind == _DBG.get("stop_kind", "p") and l == _DBG.get("stop_layer", 0)):
                raise _Stop()
            for jb in range(2):
                pieces = []
                for jj_ in range(2):
                    for h_ in range(2):
                        for g_ in range(4):
                            c_src = (g_ * 4 + jb * 2 + jj_) * 64 + h_ * 32
                            c_dst = (jj_ * 2 + h_) * 128 + g_ * 32
                            pieces.append((lambda s, c_dst=c_dst: wv(s, KC, 512)[:, :, c_dst:c_dst + 32], w_in[l][:, c_src:c_src + 32].rearrange("(k p) m -> p k m", p=128)))
                slot, wtok = wload(pieces, "q")
                sv = wv(slot, KC, 512)
                for jj in range(2):
                    j = jb * 2 + jj
                    qf = []
                    for pl in range(2):
                        pa, pan = gemm_chunk(lambda k: sv[:, k, (jj * 2 + pl) * 128:(jj * 2 + pl + 1) * 128], wtok, KC, xr, xt, N)
                        t_, tn = next_tmp()
                        op("act", lambda e: e.activation(out=t_[:, 0:N], in_=pa[:, 0:N], func=AF.Copy), R=(pan,), W=(tn,))
                        qf.append((t_, tn))
                    rope(qf, [(QK[:, j * 2, :], f"QK{j * 2}"), (QK[:, j * 2 + 1, :], f"QK{j * 2 + 1}")], N)

            if (_DBG.get("stop") == "q" and kind == _DBG.get("stop_kind", "p") and l == _DBG.get("stop_layer", 0)):
                raise _Stop()
            pba, pban = PB[1], "PB1"
            if kind == "p":
                attention_prompt(l, first_pass)
            else:
                attention_sample(l)
            for c in range(8):
                sq_accum(pba, pban, c, c == 0, c == 7, N)
            group_norm_finish(pba, pban, l, 0, 8, N, 1)

            if (_DBG.get("stop") == "attn" and kind == _DBG.get("stop_kind", "p") and l == _DBG.get("stop_layer", 0)):
                raise _Stop()
            for cb in range(4):
                cs_ = slice(cb * 512, (cb + 1) * 512)
                pieces = []
                for half in range(2):
                    for kc2 in range(2):
                        r0 = kc2 * 512 + half * 256
                        pieces.append((lambda s, half=half, kc2=kc2: wv(s, KC, 512)[half * 64:(half + 1) * 64, kc2 * 4:(kc2 + 1) * 4, :],
                                       w_out[l][r0:r0 + 256, cs_].rearrange("(j d) m -> d j m", d=64)))
                pieces.append((lambda s: wv(s, KC, 512)[:, 8:16, :], w_out[l][1024:2048, cs_].rearrange("(k p) m -> p k m", p=128)))
                slot, wtok = wload(pieces, "out")
                sv = wv(slot, KC, 512)
                for mm in range(4):
                    m = cb * 4 + mm
                    ko = lambda k: (k + 8) % 16
                    pa, pan = gemm_chunk(lambda k: sv[:, ko(k), mm * 128:(mm + 1) * 128], wtok, KC, lambda k: MX[:, ko(k), 0:N], lambda k: f"MX{ko(k)}", N)
                    op("dve", lambda e: e.tensor_tensor(out=X[:, m, 0:N], in0=pa[:, 0:N], in1=X[:, m, 0:N], op=ALU.add), R=(pan, f"X{m}"), W=(f"X{m}",))

            if (_DBG.get("stop") == "out" and kind == _DBG.get("stop_kind", "p") and l == _DBG.get("stop_layer", 0)):
                raise _Stop()
            rmsnorm_to_xn(l, "nffn", N)
            trk.fence("dve", ["pe", "act"])
            for fb in range(22):
                slot, wtok = wload([
                    (lambda s: wv(s, KC, 512)[:, :, 0:256], w_gu[l][:, fb * 256:(fb + 1) * 256].rearrange("(k p) m -> p k m", p=128)),
                    (lambda s: wv(s, KC, 512)[:, :, 256:512], w_gu[l][:, DFF + fb * 256:DFF + (fb + 1) * 256].rearrange("(k p) m -> p k m", p=128)),
                ], "gu")
                sv = wv(slot, KC, 512)
                for cc in range(2):
                    fc = fb * 2 + cc
                    pa, pan = gemm_chunk(lambda k: sv[:, k, cc * 128:(cc + 1) * 128], wtok, KC, xr, xt, N)
                    sg, sgn = next_tmp()
                    op("act", lambda e: e.activation(out=sg[:, 0:N], in_=pa[:, 0:N], func=AF.Silu), R=(pan,), W=(sgn,))
                    pa, pan = gemm_chunk(lambda k: sv[:, k, 256 + cc * 128:256 + (cc + 1) * 128], wtok, KC, xr, xt, N)
                    op("dve", lambda e: e.tensor_tensor(out=H[:, fc, 0:N], in0=pa[:, 0:N], in1=sg[:, 0:N], op=ALU.mult), R=(pan, sgn), W=(f"H{fc}",))
            for cg in range(8):
                accs = None
                for kh in range(2):
                    slot, wtok = wload([(lambda s: wv(s, 22, 256), w_dn[l][kh * 2816:(kh + 1) * 2816, cg * 256:(cg + 1) * 256].rearrange("(k p) m -> p k m", p=128))], "dn")
                    sv = wv(slot, 22, 256)
                    if kh == 0:
                        accs = [next_pa(), next_pa()]
                    for mm in range(2):
                        pa, pan = accs[mm]
                        for k in range(22):
                            kk = kh * 22 + k
                            op("pe", lambda e: e.matmul(pa[:, 0:N], lhsT=sv[:, k, mm * 128:(mm + 1) * 128], rhs=H[:, kk, 0:N], start=(kk == 0), stop=(kk == FC - 1)),
                               R=(wtok, f"H{kk}"), W=(pan,), inc=(k == 21))
                for mm in range(2):
                    m = cg * 2 + mm
                    pa, pan = accs[mm]
                    op("dve", lambda e: e.tensor_tensor(out=X[:, m, 0:N], in0=pa[:, 0:N], in1=X[:, m, 0:N], op=ALU.add), R=(pan, f"X{m}"), W=(f"X{m}",))

        def rope(src, dst, N):
            (a_, an), (b_, bn) = src
            (da, dan), (db, dbn) = dst
            t1, t1n = next_tmp(); t2, t2n = next_tmp()
            cos = CS[:, 0, 0:N]; sin = CS[:, 1, 0:N]
            op("dve", lambda e: e.tensor_tensor(out=t1[:, 0:N], in0=a_[:, 0:N], in1=cos, op=ALU.mult), R=(an, "CS"), W=(t1n,))
            op("dve", lambda e: e.tensor_tensor(out=t2[:, 0:N], in0=b_[:, 0:N], in1=sin, op=ALU.mult), R=(bn, "CS"), W=(t2n,))
            op("dve", lambda e: e.tensor_tensor(out=da[:, 0:N], in0=t1[:, 0:N], in1=t2[:, 0:N], op=ALU.subtract), R=(t1n, t2n), W=(dan,))
            op("dve", lambda e: e.tensor_tensor(out=t1[:, 0:N], in0=b_[:, 0:N], in1=cos, op=ALU.mult), R=(bn, "CS"), W=(t1n,))
            op("dve", lambda e: e.tensor_tensor(out=t2[:, 0:N], in0=a_[:, 0:N], in1=sin, op=ALU.mult), R=(an, "CS"), W=(t2n,))
            op("dve", lambda e: e.tensor_tensor(out=db[:, 0:N], in0=t1[:, 0:N], in1=t2[:, 0:N], op=ALU.add), R=(t1n, t2n), W=(dbn,))

        pt_i = {"i": 0}

        def attention_prompt(l, first_pass):
            QKv = QK[:, 0:8, :].rearrange("p (j pl) t -> p pl j t", pl=2)
            num, numn = PB[2], "PB2"
            den, denn = PB[3], "PB3"
            units = [(qb, kc, half) for qb in range(4) for kc in range(2) for half in range(2)]

            def qk_phase(u):
                qb, kc, half = u
                g = kc * 2 + half
                blocks = []
                if not (first_pass and qb == 0):
                    blocks.append((qb, 1))
                blocks.append((qb + 1, 0))
                pts = []
                for bi, (kb, mk) in enumerate(blocks):
                    sps = PB[bi]; spsn = f"PB{bi}"
                    for pl in range(2):
                        op("pe", lambda e: e.matmul(sps[:, :].rearrange("p (j q) -> p j q", q=128), lhsT=KH[32 * g:32 * g + 32, l, pl, kb * 128:(kb + 1) * 128],
                                                    rhs=QKv[32 * g:32 * g + 32, pl, :, qb * 128:(qb + 1) * 128], start=(pl == 0), stop=(pl == 1), tile_position=(32 * g, 0)),
                           R=("KH",) + tuple(f"QK{j * 2 + pl}" for j in range(4)), W=(spsn,), inc=(pl == 1))
                    i = pt_i["i"] % 4; pt_i["i"] += 1
                    pt, ptn = PT[i], f"PT{i}"
                    op("act", lambda e: e.activation(out=pt[:, :], in_=sps[:, :], func=AF.Exp, scale=0.125), R=(spsn,), W=(ptn,))
                    op("dve", lambda e: e.tensor_tensor(out=pt[:, :], in0=pt[:, :], in1=MSK[:, mk, :, :].rearrange("p j q -> p (j q)"), op=ALU.mult), R=(ptn, "MSK"), W=(ptn,))
                    pts.append((pt, ptn, kb))
                return pts

            def pv_phase(u, pts):
                qb, kc, half = u
                g = kc * 2 + half
                for bi, (pt, ptn, kb) in enumerate(pts):
                    st = (bi == 0); sp_ = (bi == len(pts) - 1)
                    op("pe", lambda e: e.matmul(num[half * 64:(half + 1) * 64, :], lhsT=VH[:, l, kb, g * 64:(g + 1) * 64], rhs=pt[:, :], start=st, stop=sp_),
                       R=(ptn, "VH"), W=(numn,), inc=False)
                    op("pe", lambda e: e.matmul(den[half * 64:(half + 1) * 64, :], lhsT=ONES[:, 0:64], rhs=pt[:, :], start=st, stop=sp_),
                       R=(ptn, "ONES"), W=(denn,), inc=True)
                if half == 1:
                    dn, dnn = next_tmp()
                    op("dve", lambda e: e.tensor_tensor(out=dn[:, 0:512], in0=den[:, :], in1=SNK[:, kc, :, :].rearrange("p j q -> p (j q)"), op=ALU.add), R=(denn, "SNK"), W=(dnn,))
                    op("dve", lambda e: e.reciprocal(out=dn[:, 0:512], in_=dn[:, 0:512]), R=(dnn,), W=(dnn,))
                    op("dve", lambda e: e.tensor_tensor(out=MX[:, kc * 4:kc * 4 + 4, qb * 128:(qb + 1) * 128], in0=num[:, :].rearrange("p (j q) -> p j q", q=128),
                                                        in1=dn[:, 0:512].rearrange("p (j q) -> p j q", q=128), op=ALU.mult),
                       R=(numn, dnn), W=tuple(f"MX{kc * 4 + j}" for j in range(4)))

            prev = None
            for u in units:
                pts = qk_phase(u)
                if prev is not None:
                    pv_phase(*prev)
                prev = (u, pts)
            pv_phase(*prev)

        def attention_sample(l):
            QKv = QK[:, 0:8, :].rearrange("p (j pl) t -> p pl j t", pl=2)
            first = True
            trk.reserve("ks", 8)
            for h_ in range(2):
                for g_ in range(4):
                    trk.dma("pool", KS[:, :, h_ * 128 + g_ * 32:h_ * 128 + g_ * 32 + 32], s_k[l][:, :, g_ * 64 + h_ * 32:g_ * 64 + h_ * 32 + 32].rearrange("n s i -> s n i"),
                            R=(f"skA{l}", f"skB{l}"), W=("KS",), sem="ks", skip_deps=not first)
                    first = False
            trk.dma("pool", VS2[:], s_v[l].rearrange("n s f -> s n f"), R=(f"svA{l}", f"svB{l}"), W=("VS2",), sem="vs")
            for n in range(NS):
                kt, ktn = KTS[n % 2], f"KTS{n % 2}"
                ptb = PB[3][:, 0:128].bitcast(BF16).rearrange("p (pl s) -> p pl s", pl=2)
                for pl in range(2):
                    op("pe", lambda e: e.transpose(ptb[:, pl, :], KS[:, n, pl * 128:(pl + 1) * 128], IDNB[:, :]), R=("KS", "IDNB"), W=("PB3",))
                op("act", lambda e: e.activation(out=kt[:, :, :], in_=ptb, func=AF.Copy), R=("PB3",), W=(ktn,))
                for g in range(4):
                    for pl in range(2):
                        op("pe", lambda e: e.matmul(PA[g][:, n * 4:n * 4 + 4], lhsT=kt[32 * g:32 * g + 32, pl, :], rhs=QKv[32 * g:32 * g + 32, pl, :, n],
                                                    start=(pl == 0), stop=(pl == 1), tile_position=(32 * g, 0)),
                           R=(ktn,) + tuple(f"QK{j * 2 + pl}" for j in range(4)), W=(f"PA{g}",), inc=(pl == 1))
            pssv = PSS[:, :].rearrange("p (n g j) -> p n g j", g=4, j=4)
            for g in range(4):
                op("act", lambda e: e.activation(out=pssv[:, :, g, :], in_=PA[g][:, 0:64].rearrange("p (n j) -> p n j", j=4), func=AF.Exp, scale=0.125), R=(f"PA{g}",), W=("PSS",))
            den, denn = PB[1], "PB1"
            op("pe", lambda e: e.matmul(den[:, 0:256], lhsT=ONES[:, :], rhs=PSS[:, :], start=True, stop=True), R=("PSS", "ONES"), W=(denn,))
            op("dve", lambda e: e.tensor_tensor(out=RD[:, :], in0=den[:, 0:256], in1=SNKS[:, l, :, :].rearrange("p n h -> p (n h)"), op=ALU.add), R=(denn, "SNKS"), W=("RD",))
            op("dve", lambda e: e.reciprocal(out=RD[:, :], in_=RD[:, :]), R=("RD",), W=("RD",))
            num, numn = PB[2], "PB2"
            numv = num[:, 0:128].rearrange("p (kc j n) -> p kc j n", kc=2, j=4)
            for n in range(NS):
                for g in range(4):
                    hf = g % 2
                    op("pe", lambda e: e.matmul(numv[hf * 64:(hf + 1) * 64, g // 2, :, n], lhsT=VS2[:, n, g * 64:(g + 1) * 64], rhs=PSS[:, n * 16 + g * 4:n * 16 + g * 4 + 4], start=True, stop=True),
                       R=("VS2", "PSS"), W=(numn,), inc=(n == NS - 1 and g == 3))
            rdv = RD[:, :].rearrange("p (n g j) -> p g j n", g=4, j=4)
            for hf in range(2):
                op("dve", lambda e: e.tensor_tensor(out=MX[hf * 64:(hf + 1) * 64, 0:8, 0:NS].rearrange("p (kc j) n -> p kc j n", kc=2),
                                                    in0=numv[hf * 64:(hf + 1) * 64, :, :, :],
                                                    in1=rdv[hf * 64:(hf + 1) * 64, hf::2, :, :], op=ALU.mult),
                   R=(numn, "RD"), W=tuple(f"MX{c}" for c in range(8)))

        def run_pass(pi, kind):
            N = T if kind == "p" else NS
            last_pass = (kind == "p" and pi == NPP - 1)
            ci = pi if kind == "p" else NPP
            trk.dma("sp", CS[:], csd[ci], W=("CS",), sem="cs")
            ntb = 4 if kind == "p" else 1
            for tb in range(ntb):
                rows = 128 if kind == "p" else NS
                src = xp[pi * T + tb * 128: pi * T + tb * 128 + 128, :] if kind == "p" else xs
                trk.dma("sp", STG[0:rows, 0:D], src, W=("STG",), sem="xin")
                for c4 in range(4):
                    pb = PB[c4 % 2]; pbn = f"PB{c4 % 2}"
                    for cc in range(4):
                        c = c4 * 4 + cc
                        op("pe", lambda e: e.transpose(pb[:, cc * 128:cc * 128 + rows], STG[0:rows, c * 128:(c + 1) * 128], IDN[0:rows, 0:rows]), R=("STG", "IDN"), W=(pbn,), inc=(cc == 3))
                    op("act", lambda e: e.activation(out=X[:, c4 * 4:c4 * 4 + 4, tb * 128:tb * 128 + rows], in_=pb[:, :].rearrange("p (c t) -> p c t", t=128)[:, :, 0:rows], func=AF.Copy),
                       R=(pbn,), W=tuple(f"X{c4 * 4 + cc}" for cc in range(4)))
            if (_DBG.get("stop") == "load" and kind == _DBG.get("stop_kind", "p")):
                raise _Stop()
            for l in range(2):
                if l == 1 and (_DBG.get("stop") == "l1" and kind == _DBG.get("stop_kind", "p")):
                    raise _Stop()
                layer(l, pi, kind, last_pass)
            if (_DBG.get("stop") == "final" and kind == _DBG.get("stop_kind", "p")):
                raise _Stop()
            pb, pbn = PB[0], "PB0"
            for c in range(KC):
                sq, sqn = next_sq()
                op("act", lambda e: e.activation(out=sq[:, 0:N], in_=X[:, c, 0:N], func=AF.Square), R=(f"X{c}",), W=(sqn,))
                op("pe", lambda e: e.matmul(pb[:, 0:N], lhsT=ONES[:, :], rhs=sq[:, 0:N], start=(c == 0), stop=(c == KC - 1)), R=(sqn, "ONES"), W=(pbn,), inc=True)
            rs = RS[0]
            op("dve", lambda e: e.tensor_scalar(out=rs[:, 0:N], in0=pb[:, 0:N], scalar1=1.0 / D, scalar2=EPS, op0=ALU.mult, op1=ALU.add), R=(pbn,), W=("RS0",))
            op("act", lambda e: e.activation(out=rs[:, 0:N], in_=rs[:, 0:N], func=AF.Sqrt), R=("RS0",), W=("RS0",))
            op("dve", lambda e: e.reciprocal(out=rs[:, 0:N], in_=rs[:, 0:N]), R=("RS0",), W=("RS0",))
            o_f, _ = lay[("nfin", 0)]
            for c in range(KC):
                op("dve", lambda e: e.scalar_tensor_tensor(out=X[:, c, 0:N], in0=X[:, c, 0:N], scalar=PV[:, o_f + c:o_f + c + 1], in1=rs[:, 0:N], op0=ALU.mult, op1=ALU.mult),
                   R=(f"X{c}", "RS0", "PV"), W=(f"X{c}",))
            for tb in range(ntb):
                rows = 128 if kind == "p" else NS
                for c4 in range(4):
                    pb = PB[c4 % 2]; pbn = f"PB{c4 % 2}"
                    for cc in range(4):
                        c = c4 * 4 + cc
                        op("pe", lambda e: e.transpose(pb[0:rows, cc * 128:(cc + 1) * 128], X[:, c, tb * 128:tb * 128 + rows], IDN[:, :]), R=(f"X{c}", "IDN"), W=(pbn,), inc=(cc == 3))
                    op("act", lambda e: e.activation(out=STG[0:rows, c4 * 512:(c4 + 1) * 512], in_=pb[0:rows, :], func=AF.Copy), R=(pbn,), W=("STG",))
                dst = yp[pi * T + tb * 128: pi * T + tb * 128 + 128, :] if kind == "p" else ys
                trk.dma("sp", dst, STG[0:rows, 0:D], R=("STG",), W=("yout",), sem="oSTG")

        try:
            for pi in range(NPP):
                run_pass(pi, "p")
            if not _DBG.get("nosample"):
                run_pass(0, "s")
        except _Stop:
            pass
        trk.wait_all_dma("sp")
    return nc


_CACHE = {}
_DBG = {}


def kernel(x_prompt, x_sample, state_lru_h, state_lru_conv, cache_swa_k, cache_swa_v, state_sconv,
           norm_mix, w_in, norm_grp, w_out, lru_conv_w, lru_conv_b, lru_w_a, lru_b_a, lru_w_i, lru_b_i,
           lru_lambda, sc_conv_w, attn_sinks, norm_ffn, ffn_w_gu, ffn_w_down, norm_final):
    f = lambda a: np.ascontiguousarray(np.asarray(a, dtype=np.float32))
    x_prompt = f(x_prompt); x_sample = f(x_sample)
    NPP = _DBG.get('npp', SEQ // T)
    lay, NV = pv_layout()
    pv = np.zeros((128, NV), np.float32)

    def put(name, l, arr2d):
        o, n = lay[(name, l)]
        assert arr2d.shape == (128, n), (name, arr2d.shape)
        pv[:, o:o + n] = arr2d
    fm = lambda v: f(v).reshape(-1, 128).T
    for l in range(2):
        put("nmix", l, fm(norm_mix[l])); put("nffn", l, fm(norm_ffn[l]))
        g = f(norm_grp[l])
        ga = g[:1024].reshape(2, 2, 4, 64)
        ga = ga.transpose(1, 3, 0, 2).reshape(128, 8)
        put("ngrp", l, np.concatenate([ga, fm(g[1024:1536]), fm(g[1536:2048])], axis=1))
        cw = f(lru_conv_w[l])
        put("lcw", l, np.concatenate([fm(cw[k]) for k in range(4)], axis=1))
        put("lcb", l, fm(lru_conv_b[l])); put("ba", l, fm(lru_b_a[l])); put("bi", l, fm(lru_b_i[l])); put("lam", l, fm(lru_lambda[l]))
        sw = f(sc_conv_w[l])
        put("scw", l, np.concatenate([fm(sw[k]) for k in range(3)], axis=1))
        sk = f(attn_sinks[l]).reshape(2, 2, 4)
        sk2 = np.repeat(sk.transpose(1, 0, 2).reshape(2, 1, 8), 64, axis=1).reshape(128, 8)
        put("sink", l, sk2)
        put("sinkall", l, np.repeat(f(attn_sinks[l])[None, :], 128, axis=0))
    put("nfin", 0, fm(norm_final))
    bdh = np.zeros((2, 128, 8, 128), np.float32)
    for l in range(2):
        for gi, wsrc in enumerate((lru_w_a, lru_w_i)):
            wl = f(wsrc[l])
            for c in range(4):
                bdh[l, 0:64, gi * 4 + c, 0:64] = wl[2 * c]
                bdh[l, 64:128, gi * 4 + c, 64:128] = wl[2 * c + 1]
    half = 32
    inv = (10000.0 ** (-np.arange(half, dtype=np.float32) / half)).astype(np.float32)
    cs = np.zeros((NPP + 1, 128, 2, T), np.float32)
    for pi in range(NPP + 1):
        pos = (np.arange(T) + pi * T).astype(np.float32) if pi < NPP else np.full(T, float(PAST), np.float32)
        ang = pos[None, :] * inv[:, None]
        cs[pi, :, 0, :] = np.tile(np.cos(ang).astype(np.float32), (4, 1))
        cs[pi, :, 1, :] = np.tile(np.sin(ang).astype(np.float32), (4, 1))
    sidx = np.arange(128)[:, None]; qidx = np.arange(128)[None, :]
    msk = np.stack([(sidx <= qidx), (sidx > qidx)], axis=1).astype(np.float32)
    idn = np.eye(128, dtype=np.float32)
    w_in = f(w_in); w_out = f(w_out); ffn_w_gu = f(ffn_w_gu); ffn_w_down = f(ffn_w_down)
    sh = f(state_lru_h); scv = f(state_lru_conv); ssc = f(state_sconv)
    ckk = f(cache_swa_k).reshape(2, 128, 128, 256); cvv = f(cache_swa_v).reshape(2, 128, 128, 256)
    in_maps = []
    for c in range(NCORES):
        b = c % 4
        ns = slice(c * NS, (c + 1) * NS)
        in_maps.append({
            "xp": np.ascontiguousarray(x_prompt[b][:NPP * T]), "xs": np.ascontiguousarray(x_sample[ns, 0, :]),
            "st_h": np.ascontiguousarray(sh[:, ns]), "st_conv": np.ascontiguousarray(scv[:, ns]), "st_sc": np.ascontiguousarray(ssc[:, ns]),
            "ck": np.ascontiguousarray(ckk[:, ns]), "cv": np.ascontiguousarray(cvv[:, ns]),
            "w_in": w_in, "w_out": w_out, "w_gu": ffn_w_gu, "w_dn": ffn_w_down,
            "bd": bdh, "pv": pv, "cs": cs, "msk": msk, "idn": idn,
        })
    if NPP not in _CACHE:
        _CACHE[NPP] = build(NPP)
    nc = _CACHE[NPP]
    if 'ncores' in _DBG:
        n_ = _DBG['ncores']
        res = run_bass_kernel_spmd(nc, in_maps[:n_], core_ids=list(range(n_)), trace=_DBG.get('trace', False))
        _DBG['res'] = res
        return res.results
    res = run_bass_kernel_spmd(nc, in_maps, core_ids=list(range(NCORES)))
    R = res.results
    y_prompt = np.stack([R[b]["yp"] for b in range(4)]).reshape(4, SEQ, D)
    y_sample = np.concatenate([R[c]["ys"] for c in range(NCORES)], axis=0).reshape(128, 1, D)
    stk = lambda name, shp: np.stack([R[b][name] for b in range(4)], axis=1).reshape(shp)
    p_lru_h = stk("p_h", (2, 4, 512))
    p_lru_conv = stk("p_conv", (2, 4, 3, 512))
    p_swa_k = stk("p_k", (2, 4, 128, 4, 64))
    p_swa_v = stk("p_v", (2, 4, 128, 4, 64))
    p_sconv = stk("p_sc", (2, 4, 2, 512))
    cat = lambda name, shp: np.concatenate([R[c][name] for c in range(NCORES)], axis=1).reshape(shp)
    s_lru_h = cat("s_h", (2, 128, 512))
    s_lru_conv = cat("s_conv", (2, 128, 3, 512))
    s_swa_k = cat("s_k", (2, 128, 128, 4, 64))
    s_swa_v = cat("s_v", (2, 128, 128, 4, 64))
    s_sconv = cat("s_sc", (2, 128, 2, 512))
    outs = (y_prompt, y_sample, p_lru_h, p_lru_conv, p_swa_k, p_swa_v, p_sconv, s_lru_h, s_lru_conv, s_swa_k, s_swa_v, s_sconv)
    return tuple(np.ascontiguousarray(o, dtype=np.float32) for o in outs)
```

```python
import numpy as np
from contextlib import ExitStack
import concourse.bass as bass
import concourse.mybir as mybir
from concourse.bass_utils import run_bass_kernel_spmd

F32 = mybir.dt.float32
BF16 = mybir.dt.bfloat16
AF = mybir.ActivationFunctionType
ALU = mybir.AluOpType

D = 2048
KC = 16
T = 512
NS = 16
DFF = 5632
FC = 44
EPS = 1e-6
PAST = 8192
NCORES = 8
SEQ = 2048


def pv_layout():
    lay = {}
    off = 0
    for l in range(3):
        for name, n in (("nmix", 16), ("nffn", 16), ("ngrp", 16), ("lcw", 16), ("lcb", 4), ("ba", 4), ("bi", 4),
                        ("lam", 4), ("scw", 12), ("sink", 8), ("sinkall", 16)):
            lay[(name, l)] = (off, n)
            off += n
    lay[("nfin", 0)] = (off, 16)
    off += 16
    lay[("flags", 0)] = (off, 12)
    off += 12
    return lay, off


class _Stop(Exception):
    pass


class Trk:
    SEM_MAX = 3800

    def __init__(self, nc, es):
        self.nc = nc
        self.es = es
        self.E = {"pe": nc.tensor, "act": nc.scalar, "dve": nc.vector, "pool": nc.gpsimd, "sp": nc.sync}
        self.gen = {k: 0 for k in self.E}
        self.sem = {k: es.enter_context(nc.semaphore("pg_" + k + "_0")) for k in self.E}
        self.semname = {k: "pg_" + k + "_0" for k in self.E}
        self.cnt = {k: 0 for k in self.E}
        self.seen = {k: {} for k in self.E}
        self.lastw = {}
        self.readers = {}
        self.dsems = {}
        self.all_dsems = []
        self.pending_noinc = {k: False for k in self.E}

    def _wait(self, eng, ev):
        name, semh, val = ev
        if eng == "pe" and name.startswith("pg_pe_"):
            return
        if self.seen[eng].get(name, 0) >= val:
            return
        self.E[eng].wait_ge(semh, val)
        self.seen[eng][name] = val

    def _deps(self, eng, R, W):
        best = {}

        def add(ev):
            if ev[0] not in best or best[ev[0]][2] < ev[2]:
                best[ev[0]] = ev
        for r in R:
            if r in self.lastw:
                add(self.lastw[r])
        for w in W:
            if w in self.lastw:
                add(self.lastw[w])
            for ev in self.readers.get(w, {}).values():
                add(ev)
        for ev in best.values():
            self._wait(eng, ev)

    def _commit(self, ev, R, W):
        for r in R:
            d = self.readers.setdefault(r, {})
            if ev[0] not in d or d[ev[0]][2] < ev[2]:
                d[ev[0]] = ev
        for w in W:
            self.lastw[w] = ev
            self.readers[w] = {}

    def _roll(self, eng):
        if self.cnt[eng] >= self.SEM_MAX and not self.pending_noinc[eng]:
            self.gen[eng] += 1
            nm = f"pg_{eng}_{self.gen[eng]}"
            self.sem[eng] = self.es.enter_context(self.nc.semaphore(nm))
            self.semname[eng] = nm
            self.cnt[eng] = 0

    def op(self, eng, fn, R=(), W=(), inc=True):
        self._roll(eng)
        self._deps(eng, R, W)
        ins = fn(self.E[eng])
        if inc:
            self.cnt[eng] += 1
            ins.then_inc(self.sem[eng], 1)
            ev = (self.semname[eng], self.sem[eng], self.cnt[eng])
            self.pending_noinc[eng] = False
        else:
            ev = (self.semname[eng], self.sem[eng], self.cnt[eng] + 1)
            self.pending_noinc[eng] = True
        self._commit(ev, R, W)
        return ins

    def fence(self, eng, others):
        for o in others:
            self._wait(eng, (self.semname[o], self.sem[o], self.cnt[o] + (1 if self.pending_noinc[o] else 0)))

    def reserve(self, sem, n):
        if sem not in self.dsems or self.dsems[sem][1] + 16 * n > self.SEM_MAX:
            g = self.dsems[sem][3] + 1 if sem in self.dsems else 0
            nm = f"d_{sem}_{g}"
            self.dsems[sem] = [self.es.enter_context(self.nc.semaphore(nm)), 0, nm, g]
            self.all_dsems.append(self.dsems[sem])

    def dma(self, q, out, in_, R=(), W=(), sem="g", skip_deps=False, **kw):
        if not skip_deps:
            self._deps(q, R, W)
        self.reserve(sem, 1)
        ds = self.dsems[sem]
        ins = self.E[q].dma_start(out=out, in_=in_, **kw)
        ds[1] += 16
        ins.then_inc(ds[0], 16)
        ev = (ds[2], ds[0], ds[1])
        self._commit(ev, R, W)
        return ins

    def coll(self, fn, R=(), W=()):
        self._deps("pool", R, W)
        if "cc" not in self.dsems:
            self.dsems["cc"] = [self.es.enter_context(self.nc.semaphore("d_cc_0")), 0, "d_cc_0", 0]
            self.all_dsems.append(self.dsems["cc"])
        ds = self.dsems["cc"]
        ins = fn(self.E["pool"])
        ds[1] += 1
        ins.then_inc(ds[0], 1)
        ev = (ds[2], ds[0], ds[1])
        self._commit(ev, R, W)
        return ins

    def wait_all_dma(self, eng):
        for ds in self.all_dsems:
            if ds[1]:
                self.E[eng].wait_ge(ds[0], ds[1])


def build(NPP):
    lay, NV = pv_layout()
    NSLOT = NPP + 1
    TOK = NSLOT * T
    nc = bass.Bass("TRN2", target_bir_lowering=False)

    def din(name, shape, dt=F32):
        return nc.dram_tensor(name, list(shape), dt, kind="ExternalInput").ap()

    def dout(name, shape, dt=F32):
        return nc.dram_tensor(name, list(shape), dt, kind="ExternalOutput").ap()

    xp = din("xp", [TOK, D]); xs = din("xs", [NS, D])
    st_h = din("st_h", [2, NS, 512]); st_conv = din("st_conv", [2, NS, 3, 512]); st_sc = din("st_sc", [2, NS, 2, 512])
    ck = din("ck", [2, NS, 128, 256]); cv = din("cv", [2, NS, 128, 256])
    w_in = din("w_in", [2, D, 4096]); w_out = din("w_out", [2, D, D]); w_gu = din("w_gu", [2, D, 2 * DFF]); w_dn = din("w_dn", [2, DFF, D])
    wo_in = din("wo_in", [1, D, 4096]); wo_out = din("wo_out", [1, D, D]); wo_gu = din("wo_gu", [1, D, 2 * DFF]); wo_dn = din("wo_dn", [1, DFF, D])
    WIN = [w_in[0], w_in[1], wo_in[0]]; WOUT = [w_out[0], w_out[1], wo_out[0]]
    WGU = [w_gu[0], w_gu[1], wo_gu[0]]; WDN = [w_dn[0], w_dn[1], wo_dn[0]]
    bd = din("bd", [3, 128, 8, 128])
    ibs = [nc.dram_tensor(f"ib{j}", [128, 2048], F32) for j in range(4)]
    obs = [nc.dram_tensor(f"ob{j}", [256, 2048], F32) for j in range(4)]
    pvd = din("pv", [128, NV])
    csd = din("cs", [NSLOT + 1, 128, 2, T])
    mskd = din("msk", [128, 2, 128])
    idnd = din("idn", [128, 128])

    yp = dout("yp", [TOK, D]); ys = dout("ys", [NS, D])
    p_h = dout("p_h", [2, 512]); p_conv = dout("p_conv", [2, 3, 512]); p_k = dout("p_k", [2, 128, 256]); p_v = dout("p_v", [2, 128, 256]); p_sc = dout("p_sc", [2, 2, 512])
    s_h = dout("s_h", [2, NS, 512]); s_conv = dout("s_conv", [2, NS, 3, 512]); s_k = dout("s_k", [2, NS, 128, 256]); s_v = dout("s_v", [2, NS, 128, 256]); s_sc = dout("s_sc", [2, NS, 2, 512])

    es = ExitStack()
    with es:
        trk = Trk(nc, es)
        op = trk.op

        def sb(name, shape, dt):
            return es.enter_context(nc.sbuf_tensor(name, list(shape), dt))

        def ps(name, shape, dt):
            return es.enter_context(nc.psum_tensor(name, list(shape), dt))

        X = sb("X", [128, KC, T], F32)
        XN = sb("XN", [128, KC, T], BF16)
        H = sb("H", [128, FC, T], BF16)
        MX = H[:, 0:16, :]
        QK = H[:, 16:26, :]
        KS = H[:, 26:34, :].rearrange("p c (a f) -> p (c a) f", f=256)
        VS2 = H[:, 34:42, :].rearrange("p c (a f) -> p (c a) f", f=256)
        NWS = 3
        WS = [sb(f"WS{i}", [128, 8192], BF16) for i in range(NWS)]
        NTMP = 9
        TMP = [sb(f"TMP{i}", [128, T + 4], F32) for i in range(NTMP)]
        SQ = [sb(f"SQ{i}", [128, T], BF16) for i in range(3)]
        PT = [sb(f"PT{i}", [128, 512], BF16) for i in range(4)]
        STG = XN[:, :, :].rearrange("p c t -> p (c t)").bitcast(F32)
        PV = sb("PV", [128, NV], F32)
        CS = sb("CS", [128, 2, T], F32)
        MSKF = sb("MSKF", [128, 2, 128], F32)
        MSK = sb("MSK", [128, 2, 4, 128], BF16)
        MSK0 = sb("MSK0", [128, 4, 128], BF16)
        IDN = sb("IDN", [128, 128], F32)
        IDNB = sb("IDNB", [128, 128], BF16)
        ONES = sb("ONES", [128, 128], BF16)
        BD = sb("BD", [128, 8, 128], F32)
        RS = [sb(f"RS{i}", [128, T], F32) for i in range(2)]
        SCL = sb("SCL", [128, 3, 2, 4], F32)
        SNK = sb("SNK", [128, 2, 4, 128], F32)
        EXS = sb("EXS", [128, 3, 8], F32)
        SNKS = sb("SNKS", [128, 2, NS, 16], F32)
        HC = sb("HC", [128, 3, 4], F32)
        CUX = sb("CUX", [128, 3, 4, 3], F32)
        CSC = sb("CSC", [128, 3, 4, 2], F32)
        KH = sb("KH", [128, 1, 2, 128 + T], BF16)
        VH = sb("VH", [128, 1, 5, 256], BF16)
        KF = [sb(f"KF{i}", [128, T], F32) for i in range(2)]
        ROW = sb("ROW", [128, 512], F32)
        UXS = sb("UXS", [128, 4, NS, 4], F32)
        GCS = sb("GCS", [128, 4, NS, 3], F32)
        H0S = sb("H0S", [128, 4, NS], F32)
        STS = sb("STS", [48, 512], F32)
        KTS = [sb(f"KTS{i}", [128, 2, 128], BF16) for i in range(2)]
        PSS = sb("PSS", [128, 256], BF16)
        RD = sb("RD", [128, 256], F32)

        PA = [ps(f"PA{i}", [128, 512], F32) for i in range(4)]
        PB = [ps(f"PB{i}", [128, 512], F32) for i in range(4)]

        wstate = {"i": 0}

        def wload(pieces, nm):
            i = wstate["i"]; wstate["i"] += 1
            slot = i % NWS
            trk.reserve(f"w{slot}", len(pieces))
            for pi_, (dst, src) in enumerate(pieces):
                trk.dma("pool", dst(WS[slot]), src, R=(), W=(f"W{slot}",), sem=f"w{slot}", skip_deps=(pi_ > 0))
            return WS[slot], f"W{slot}"

        def wv(slot, kc, n):
            return slot[:, 0:kc * n].rearrange("p (k n) -> p k n", n=n)

        pa_i = {"i": 0}

        def next_pa():
            i = pa_i["i"] % 4; pa_i["i"] += 1
            return PA[i], f"PA{i}"

        tmp_i = {"i": 0}

        def next_tmp():
            i = tmp_i["i"] % NTMP; tmp_i["i"] += 1
            return TMP[i], f"TMP{i}"

        sq_i = {"i": 0}

        def next_sq():
            i = sq_i["i"] % 3; sq_i["i"] += 1
            return SQ[i], f"SQ{i}"

        def colp(l, name, c=0):
            o, n = lay[(name, l)]
            return PV[:, o + c:o + c + 1]

        trk.dma("sp", PV[:], pvd, W=("PV",), sem="c_pv")
        trk.dma("sp", IDN[:], idnd, W=("IDN",), sem="c_idn")
        trk.dma("sp", MSKF[:], mskd, W=("MSKF",), sem="c_msk")
        op("dve", lambda e: e.tensor_copy(out=IDNB[:], in_=IDN[:]), R=("IDN",), W=("IDNB",))
        op("dve", lambda e: e.memset(ONES[:], 1.0), W=("ONES",))
        for m in range(2):
            for j in range(4):
                op("dve", lambda e, m=m, j=j: e.tensor_copy(out=MSK[:, m, j, :], in_=MSKF[:, m, :]), R=("MSKF",), W=("MSK",))
        for t_, nm in ((HC, "HC"), (CUX, "CUX"), (CSC, "CSC"), (KH, "KH"), (VH, "VH")):
            op("dve", lambda e, t_=t_: e.memset(t_[:], 0.0), W=(nm,))
        for l in range(3):
            o, n = lay[("lam", l)]
            tt, tn = next_tmp()
            op("act", lambda e: e.activation(out=tt[:, 0:4], in_=PV[:, o:o + 4], func=AF.Exp, scale=-1.0), R=("PV",), W=(tn,))
            op("act", lambda e: e.activation(out=tt[:, 4:8], in_=tt[:, 0:4], func=AF.Ln, bias=1.0), R=(tn,), W=(tn,))
            op("dve", lambda e: e.tensor_scalar(out=SCL[:, l, 0, :], in0=tt[:, 4:8], scalar1=-8.0, scalar2=None, op0=ALU.mult), R=(tn,), W=("SCL",))
            op("dve", lambda e: e.tensor_scalar(out=SCL[:, l, 1, :], in0=tt[:, 4:8], scalar1=-16.0, scalar2=None, op0=ALU.mult), R=(tn,), W=("SCL",))
            o, n = lay[("sink", l)]
            op("act", lambda e: e.activation(out=EXS[:, l, :], in_=PV[:, o:o + 8], func=AF.Exp), R=("PV",), W=("EXS",))
            o, n = lay[("sinkall", l)]
            op("act", lambda e: e.activation(out=tt[:, 16:32], in_=PV[:, o:o + 16], func=AF.Exp), R=("PV",), W=(tn,))
            for n_ in range(NS):
                if l < 2:
                    op("dve", lambda e, n_=n_: e.tensor_copy(out=SNKS[:, l, n_, :], in_=tt[:, 16:32]), R=(tn,), W=("SNKS",))

        def rmsnorm_to_xn(l, gname, N):
            pb, pbn = PB[0], "PB0"
            for c in range(KC):
                sq, sqn = next_sq()
                op("act", lambda e: e.activation(out=sq[:, 0:N], in_=X[:, c, 0:N], func=AF.Square), R=(f"X{c}",), W=(sqn,))
                op("pe", lambda e: e.matmul(pb[:, 0:N], lhsT=ONES[:, :], rhs=sq[:, 0:N], start=(c == 0), stop=(c == KC - 1)),
                   R=(sqn, "ONES"), W=(pbn,), inc=True)
            rs = RS[0]
            op("dve", lambda e: e.tensor_scalar(out=rs[:, 0:N], in0=pb[:, 0:N], scalar1=1.0 / D, scalar2=EPS, op0=ALU.mult, op1=ALU.add), R=(pbn,), W=("RS0",))
            op("act", lambda e: e.activation(out=rs[:, 0:N], in_=rs[:, 0:N], func=AF.Sqrt), R=("RS0",), W=("RS0",))
            op("dve", lambda e: e.reciprocal(out=rs[:, 0:N], in_=rs[:, 0:N]), R=("RS0",), W=("RS0",))
            for c in range(KC):
                op("dve", lambda e: e.scalar_tensor_tensor(out=XN[:, c, 0:N], in0=X[:, c, 0:N], scalar=colp(l, gname, c), in1=rs[:, 0:N], op0=ALU.mult, op1=ALU.mult),
                   R=(f"X{c}", "RS0", "PV"), W=(f"XN{c}",))

        def gemm_chunk(lhsT_of_k, wtok, nk, rhs_of_k, rtok_of_k, N, last_in_block=False):
            pa, pan = next_pa()
            for k in range(nk):
                op("pe", lambda e: e.matmul(pa[:, 0:N], lhsT=lhsT_of_k(k), rhs=rhs_of_k(k), start=(k == 0), stop=(k == nk - 1)),
                   R=(wtok, rtok_of_k(k)), W=(pan,), inc=(k == nk - 1))
            return pa, pan

        def group_norm_finish(pb, pbn, l, c0, nch, N, rsi):
            rs = RS[rsi]; rsn = f"RS{rsi}"
            op("dve", lambda e: e.tensor_scalar(out=rs[:, 0:N], in0=pb[:, 0:N], scalar1=1.0 / (nch * 128), scalar2=EPS, op0=ALU.mult, op1=ALU.add), R=(pbn,), W=(rsn,))
            op("act", lambda e: e.activation(out=rs[:, 0:N], in_=rs[:, 0:N], func=AF.Sqrt), R=(rsn,), W=(rsn,))
            op("dve", lambda e: e.reciprocal(out=rs[:, 0:N], in_=rs[:, 0:N]), R=(rsn,), W=(rsn,))
            for c in range(c0, c0 + nch):
                op("dve", lambda e: e.scalar_tensor_tensor(out=MX[:, c, 0:N], in0=MX[:, c, 0:N], scalar=colp(l, "ngrp", c), in1=rs[:, 0:N], op0=ALU.mult, op1=ALU.mult),
                   R=(f"MX{c}", rsn, "PV"), W=(f"MX{c}",))

        def sq_accum(pb, pbn, c, first, last, N):
            sq, sqn = next_sq()
            op("act", lambda e: e.activation(out=sq[:, 0:N], in_=MX[:, c, 0:N], func=AF.Square), R=(f"MX{c}",), W=(sqn,))
            op("pe", lambda e: e.matmul(pb[:, 0:N], lhsT=ONES[:, :], rhs=sq[:, 0:N], start=first, stop=last), R=(sqn, "ONES"), W=(pbn,), inc=True)

        def transpose_out(src_ap, rows, cols, dst_dram, rtoks, tag, view=None, multi=None):
            op("pe", lambda e: e.transpose(PB[3][0:cols, 0:rows], src_ap, IDN[0:rows, 0:rows]), R=tuple(rtoks) + ("IDN",), W=("PB3",))
            op("act", lambda e: e.activation(out=ROW[0:cols, 0:rows], in_=PB[3][0:cols, 0:rows], func=AF.Copy), R=("PB3",), W=("ROW",))
            if multi is not None:
                trk.reserve("oROW", len(multi))
                for (d_ap, r0, r1) in multi:
                    trk.dma("sp", d_ap, ROW[r0:r1, 0:rows], R=("ROW",), W=(tag,), sem="oROW")
                return
            srcv = ROW[0:cols, 0:rows]
            if view is not None:
                srcv = view(srcv)
            trk.dma("sp", dst_dram, srcv, R=("ROW",), W=(tag,), sem="oROW")

        def layer(l, pi, kind, pout, ffi, xfinal_cb=None):
            N = T if kind == "p" else NS
            last_pass = pout is not None
            o_fl, _ = lay[("flags", 0)]
            if kind == "p":
                for kc in range(2):
                    for j in range(4):
                        op("dve", lambda e: e.tensor_scalar(out=SNK[:, kc, j, :], in0=IDN[:, :], scalar1=0.0, scalar2=EXS[:, l, kc * 4 + j:kc * 4 + j + 1], op0=ALU.mult, op1=ALU.add), R=("EXS", "IDN"), W=("SNK",))
            rmsnorm_to_xn(l, "nmix", N)
            xr = lambda k: XN[:, k, 0:N]
            xt = lambda k: f"XN{k}"
            if kind == "s":
                trk.dma("sp", STS[0:48, :], st_conv[l].rearrange("n k f -> (n k) f"), W=("STS",), sem="st")
                for c in range(4):
                    op("pe", lambda e: e.transpose(PB[3][:, 0:48], STS[0:48, c * 128:(c + 1) * 128], IDN[0:48, 0:48]), R=("STS", "IDN"), W=("PB3",))
                    op("act", lambda e: e.activation(out=UXS[:, c, :, 0:3], in_=PB[3][:, 0:48].rearrange("p (n k) -> p n k", k=3), func=AF.Copy), R=("PB3",), W=("UXS",))
                trk.dma("sp", STS[0:32, :], st_sc[l].rearrange("n k f -> (n k) f"), R=(), W=("STS",), sem="st")
                for c in range(4):
                    op("pe", lambda e: e.transpose(PB[3][:, 0:32], STS[0:32, c * 128:(c + 1) * 128], IDN[0:32, 0:32]), R=("STS", "IDN"), W=("PB3",))
                    op("act", lambda e: e.activation(out=GCS[:, c, :, 0:2], in_=PB[3][:, 0:32].rearrange("p (n k) -> p n k", k=2), func=AF.Copy), R=("PB3",), W=("GCS",))
                trk.dma("sp", STS[0:16, :], st_h[l], W=("STS",), sem="st")
                for c in range(4):
                    op("pe", lambda e: e.transpose(PB[3][:, 0:16], STS[0:16, c * 128:(c + 1) * 128], IDN[0:16, 0:16]), R=("STS", "IDN"), W=("PB3",))
                    op("act", lambda e: e.activation(out=H0S[:, c, :], in_=PB[3][:, 0:16], func=AF.Copy), R=("PB3",), W=("H0S",))
                trk.dma("sp", s_k[l][:, 0:127, :], ck[l][:, 1:128, :], W=(f"skA{l}",), sem=f"o2a{l}")
                trk.dma("sp", s_v[l][:, 0:127, :], cv[l][:, 1:128, :], W=(f"svA{l}",), sem=f"o2b{l}")
                trk.dma("sp", s_conv[l][:, 0:2, :], st_conv[l][:, 1:3, :], W=(f"scvA{l}",), sem=f"o2c{l}")
                trk.dma("sp", s_sc[l][:, 0:1, :], st_sc[l][:, 1:2, :], W=(f"sscA{l}",), sem=f"o2d{l}")

            if (_DBG.get("stop") == "norm" and kind == _DBG.get("stop_kind", "p") and l == _DBG.get("stop_layer", 0)):
                raise _Stop()
            trk.dma("sp", BD[:], bd[l], W=("BD",), sem="bd")
            pbl, pbln = PB[1], "PB1"
            pbs, pbsn = PB[2], "PB2"
            deferred = []

            def flush_deferred():
                while deferred:
                    deferred.pop(0)()

            def lru_chunk(c):
                cc = 0
                slot, wtok = wload([
                    (lambda s: wv(s, KC, 512)[:, :, 0:128], WIN[l][:, 1536 + c * 128:1536 + (c + 1) * 128].rearrange("(k p) m -> p k m", p=128)),
                    (lambda s: wv(s, KC, 512)[:, :, 256:384], WIN[l][:, 2048 + c * 128:2048 + (c + 1) * 128].rearrange("(k p) m -> p k m", p=128)),
                ], "lru")
                sv = wv(slot, KC, 512)
                pa, pan = gemm_chunk(lambda k: sv[:, k, cc * 128:(cc + 1) * 128], wtok, KC, xr, xt, N)
                ux, uxn = next_tmp()
                xc, xcn = next_tmp()
                if kind == "p":
                    op("dve", lambda e: e.tensor_copy(out=ux[:, 0:3], in_=CUX[:, l, c, :]), R=("CUX",), W=(uxn,))
                    op("act", lambda e: e.activation(out=ux[:, 3:3 + T], in_=pa[:, 0:T], func=AF.Copy), R=(pan,), W=(uxn,))
                    op("dve", lambda e: e.tensor_copy(out=CUX[:, l, c, :], in_=ux[:, T:T + 3]), R=(uxn,), W=("CUX",))
                    tap = lambda k: ux[:, k:k + T]
                else:
                    op("act", lambda e: e.activation(out=UXS[:, c, :, 3], in_=pa[:, 0:NS], func=AF.Copy), R=(pan,), W=("UXS",))
                    uxn = "UXS"
                    tap = lambda k: UXS[:, c, :, k]
                o_w, _ = lay[("lcw", l)]
                op("dve", lambda e: e.tensor_scalar(out=xc[:, 0:N], in0=tap(3), scalar1=PV[:, o_w + 12 + c:o_w + 13 + c], scalar2=colp(l, "lcb", c), op0=ALU.mult, op1=ALU.add), R=(uxn, "PV"), W=(xcn,))
                for k in (2, 1, 0):
                    op("dve", lambda e, k=k: e.scalar_tensor_tensor(out=xc[:, 0:N], in0=tap(k), scalar=PV[:, o_w + k * 4 + c:o_w + k * 4 + c + 1], in1=xc[:, 0:N], op0=ALU.mult, op1=ALU.add), R=(uxn, "PV", xcn), W=(xcn,))
                if kind == "s":
                    transpose_out(UXS[:, c, :, 3], 128, NS, s_conv[l][:, 2, c * 128:(c + 1) * 128], ("UXS",), f"scvB{l}")
                pa2, pa2n = gemm_chunk(lambda k: sv[:, k, 256:384], wtok, KC, xr, xt, N)
                ug, ugn = next_tmp()
                op("act", lambda e: e.activation(out=ug[:, 0:N], in_=pa2[:, 0:N], func=AF.Copy), R=(pa2n,), W=(ugn,))
                gates = []
                for gi, bname in ((0, "ba"), (1, "bi")):
                    pg, pgn = next_pa()
                    op("pe", lambda e: e.matmul(pg[:, 0:N], lhsT=BD[:, gi * 4 + c, :], rhs=xc[:, 0:N], start=True, stop=True), R=("BD", xcn), W=(pgn,))
                    gt, gtn = next_tmp()
                    op("act", lambda e: e.activation(out=gt[:, 0:N], in_=pg[:, 0:N], func=AF.Sigmoid, bias=colp(l, bname, c)), R=(pgn, "PV"), W=(gtn,))
                    gates.append((gt, gtn))
                (rg, rgn), (ig, ign) = gates
                a_, an = next_tmp()
                m_, mn = next_tmp()
                op("act", lambda e: e.activation(out=a_[:, 0:N], in_=rg[:, 0:N], func=AF.Exp, scale=SCL[:, l, 0, c:c + 1]), R=(rgn, "SCL"), W=(an,))
                op("act", lambda e: e.activation(out=m_[:, 0:N], in_=rg[:, 0:N], func=AF.Exp, scale=SCL[:, l, 1, c:c + 1]), R=(rgn, "SCL"), W=(mn,))
                op("dve", lambda e: e.tensor_scalar(out=m_[:, 0:N], in0=m_[:, 0:N], scalar1=-1.0, scalar2=1.0, op0=ALU.mult, op1=ALU.add), R=(mn,), W=(mn,))
                op("act", lambda e: e.activation(out=m_[:, 0:N], in_=m_[:, 0:N], func=AF.Sqrt), R=(mn,), W=(mn,))
                if ffi is not None:
                    op("dve", lambda e: e.tensor_scalar(out=m_[:, 0:1], in0=m_[:, 0:1], scalar1=PV[:, o_fl + 7 + ffi:o_fl + 8 + ffi], scalar2=PV[:, o_fl + 2 + ffi:o_fl + 3 + ffi], op0=ALU.mult, op1=ALU.add), R=(mn, "PV"), W=(mn,))
                op("dve", lambda e: e.tensor_tensor(out=ig[:, 0:N], in0=ig[:, 0:N], in1=xc[:, 0:N], op=ALU.mult), R=(ign, xcn), W=(ign,))
                op("dve", lambda e: e.tensor_tensor(out=ig[:, 0:N], in0=ig[:, 0:N], in1=m_[:, 0:N], op=ALU.mult), R=(ign, mn), W=(ign,))
                hs, hsn = rg, rgn
                if kind == "p":
                    op("dve", lambda e: e.tensor_tensor_scan(out=hs[:, 0:T], data0=a_[:, 0:T], data1=ig[:, 0:T], initial=HC[:, l, c:c + 1], op0=ALU.mult, op1=ALU.add), R=(an, ign, "HC", rgn), W=(hsn,))
                    op("dve", lambda e: e.tensor_copy(out=HC[:, l, c:c + 1], in_=hs[:, T - 1:T]), R=(hsn,), W=("HC",))
                else:
                    op("dve", lambda e: e.tensor_tensor(out=hs[:, 0:N], in0=a_[:, 0:N], in1=H0S[:, c, :], op=ALU.mult), R=(an, "H0S", rgn), W=(hsn,))
                    op("dve", lambda e: e.tensor_tensor(out=hs[:, 0:N], in0=hs[:, 0:N], in1=ig[:, 0:N], op=ALU.add), R=(hsn, ign), W=(hsn,))
                    transpose_out(hs[:, 0:NS], 128, NS, s_h[l][:, c * 128:(c + 1) * 128], (hsn,), f"shB{l}")
                flush_deferred()
                g2, g2n = next_tmp()
                op("dve", lambda e: e.tensor_tensor(out=g2[:, 0:N], in0=ug[:, 0:N], in1=ug[:, 0:N], op=ALU.mult), R=(ugn,), W=(g2n,))
                op("dve", lambda e: e.tensor_scalar(out=g2[:, 0:N], in0=g2[:, 0:N], scalar1=0.044715, scalar2=1.0, op0=ALU.mult, op1=ALU.add), R=(g2n,), W=(g2n,))
                op("dve", lambda e: e.tensor_tensor(out=g2[:, 0:N], in0=g2[:, 0:N], in1=ug[:, 0:N], op=ALU.mult), R=(g2n, ugn), W=(g2n,))
                op("act", lambda e: e.activation(out=g2[:, 0:N], in_=g2[:, 0:N], func=AF.Sigmoid, scale=1.5957691216057308), R=(g2n,), W=(g2n,))
                op("dve", lambda e: e.tensor_tensor(out=g2[:, 0:N], in0=g2[:, 0:N], in1=ug[:, 0:N], op=ALU.mult), R=(g2n, ugn), W=(g2n,))
                op("dve", lambda e: e.tensor_tensor(out=MX[:, 8 + c, 0:N], in0=g2[:, 0:N], in1=hs[:, 0:N], op=ALU.mult), R=(g2n, hsn), W=(f"MX{8 + c}",))
                deferred.append(lambda: sq_accum(pbl, pbln, 8 + c, c == 0, c == 3, N))

            def sc_chunk(c):
                slot, wtok = wload([
                    (lambda s: wv(s, KC, 384)[:, :, 0:128], WIN[l][:, 3072 + c * 128:3072 + (c + 1) * 128].rearrange("(k p) m -> p k m", p=128)),
                    (lambda s: wv(s, KC, 384)[:, :, 128:256], WIN[l][:, 3584 + c * 128:3584 + (c + 1) * 128].rearrange("(k p) m -> p k m", p=128)),
                    (lambda s: wv(s, KC, 384)[:, :, 256:384], WIN[l][:, 2560 + c * 128:2560 + (c + 1) * 128].rearrange("(k p) m -> p k m", p=128)),
                ], "sc")
                sv = wv(slot, KC, 384)
                pa, pan = gemm_chunk(lambda k: sv[:, k, 0:128], wtok, KC, xr, xt, N)
                uc, ucn = next_tmp()
                op("act", lambda e: e.activation(out=uc[:, 0:N], in_=pa[:, 0:N], func=AF.Copy), R=(pan,), W=(ucn,))
                flush_deferred()
                pa, pan = gemm_chunk(lambda k: sv[:, k, 128:256], wtok, KC, xr, xt, N)
                gc, gcn = next_tmp()
                if kind == "p":
                    op("dve", lambda e: e.tensor_copy(out=gc[:, 0:2], in_=CSC[:, l, c, :]), R=("CSC",), W=(gcn,))
                    op("dve", lambda e: e.tensor_tensor(out=gc[:, 2:2 + T], in0=pa[:, 0:T], in1=uc[:, 0:T], op=ALU.mult), R=(pan, ucn), W=(gcn,))
                    op("dve", lambda e: e.tensor_copy(out=CSC[:, l, c, :], in_=gc[:, T:T + 2]), R=(gcn,), W=("CSC",))
                    tap = lambda k: gc[:, k:k + T]
                else:
                    op("dve", lambda e: e.tensor_tensor(out=GCS[:, c, :, 2], in0=pa[:, 0:NS], in1=uc[:, 0:NS], op=ALU.mult), R=(pan, ucn), W=("GCS",))
                    gcn = "GCS"
                    tap = lambda k: GCS[:, c, :, k]
                    transpose_out(GCS[:, c, :, 2], 128, NS, s_sc[l][:, 1, c * 128:(c + 1) * 128], ("GCS",), f"sscB{l}")
                y_, yn = next_tmp()
                o_w, _ = lay[("scw", l)]
                op("dve", lambda e: e.tensor_scalar(out=y_[:, 0:N], in0=tap(2), scalar1=PV[:, o_w + 8 + c:o_w + 9 + c], scalar2=None, op0=ALU.mult), R=(gcn, "PV"), W=(yn,))
                for k in (1, 0):
                    op("dve", lambda e, k=k: e.scalar_tensor_tensor(out=y_[:, 0:N], in0=tap(k), scalar=PV[:, o_w + k * 4 + c:o_w + k * 4 + c + 1], in1=y_[:, 0:N], op0=ALU.mult, op1=ALU.add), R=(gcn, "PV", yn), W=(yn,))
                pa, pan = gemm_chunk(lambda k: sv[:, k, 256:384], wtok, KC, xr, xt, N)
                op("dve", lambda e: e.tensor_tensor(out=MX[:, 12 + c, 0:N], in0=pa[:, 0:N], in1=y_[:, 0:N], op=ALU.mult), R=(pan, yn), W=(f"MX{12 + c}",))
                deferred.append(lambda: sq_accum(pbs, pbsn, 12 + c, c == 0, c == 3, N))

            for c in range(4):
                lru_chunk(c)
                sc_chunk(c)
            flush_deferred()
            group_norm_finish(pbl, pbln, l, 8, 4, N, 1)
            group_norm_finish(pbs, pbsn, l, 12, 4, N, 0)
            if kind == "p" and last_pass:
                transpose_out(HC[:, l, :], 128, 4, pout["h"].rearrange("(c f) -> c f", f=128), ("HC",), "ph")
                transpose_out(CUX[:, l, :, :], 128, 12, None, ("CUX",), "pconv", multi=[(pout["conv"][:, c * 128:(c + 1) * 128], c * 3, c * 3 + 3) for c in range(4)])
                transpose_out(CSC[:, l, :, :], 128, 8, None, ("CSC",), "psc", multi=[(pout["sc"][:, c * 128:(c + 1) * 128], c * 2, c * 2 + 2) for c in range(4)])

            if (_DBG.get("stop") == "sc" and kind == _DBG.get("stop_kind", "p") and l == _DBG.get("stop_layer", 0)):
                raise _Stop()
            pieces = [(lambda s: wv(s, KC, 512)[:, :, 256:512], WIN[l][:, 1280:1536].rearrange("(k p) m -> p k m", p=128))]
            for h_ in range(2):
                for g_ in range(4):
                    c_src = 1024 + g_ * 64 + h_ * 32
                    c_dst = h_ * 128 + g_ * 32
                    pieces.append((lambda s, c_dst=c_dst: wv(s, KC, 512)[:, :, c_dst:c_dst + 32], WIN[l][:, c_src:c_src + 32].rearrange("(k p) m -> p k m", p=128)))
            slot, wtok = wload(pieces, "kv")
            sv = wv(slot, KC, 512)
            if kind == "p":
                for pl in range(2):
                    op("dve", lambda e, pl=pl: e.tensor_copy(out=KH[:, 0, pl, 0:128], in_=KH[:, 0, pl, T:T + 128]), R=("KH",), W=("KH",))
                op("dve", lambda e: e.tensor_copy(out=VH[:, 0, 0, :], in_=VH[:, 0, 4, :]), R=("VH",), W=("VH",))
            kf = []
            for pl in range(2):
                pa, pan = gemm_chunk(lambda k: sv[:, k, pl * 128:(pl + 1) * 128], wtok, KC, xr, xt, N)
                t_, tn = next_tmp()
                op("act", lambda e: e.activation(out=t_[:, 0:N], in_=pa[:, 0:N], func=AF.Copy), R=(pan,), W=(tn,))
                kf.append((t_, tn))
            rope(kf, [(KF[0], "KF0"), (KF[1], "KF1")], N)
            if kind == "p":
                for pl in range(2):
                    op("act", lambda e, pl=pl: e.activation(out=KH[:, 0, pl, 128:128 + T], in_=KF[pl][:, 0:T], func=AF.Copy), R=(f"KF{pl}",), W=("KH",))
                if last_pass:
                    for pl in range(2):
                        transpose_out(KF[pl][:, T - 128:T], 128, 128, pout["k"].rearrange("t (g h i) -> t h g i", h=2, i=32)[:, pl, :, :], (f"KF{pl}",), "pk", view=lambda a: a.rearrange("t (g i) -> t g i", i=32))
            else:
                for pl in range(2):
                    op("act", lambda e, pl=pl: e.activation(out=QK[:, 8 + pl, 0:NS], in_=KF[pl][:, 0:NS], func=AF.Copy), R=(f"KF{pl}",), W=(f"QK{8 + pl}",))
                    transpose_out(KF[pl][:, 0:NS], 128, NS, s_k[l].rearrange("n s (g h i) -> n s h g i", h=2, i=32)[:, 127, pl, :, :], (f"KF{pl}",), f"skB{l}", view=lambda a: a.rearrange("t (g i) -> t g i", i=32))
            ntb = 4 if kind == "p" else 1
            for tb in range(ntb):
                M = 128 if kind == "p" else NS
                pa, pan = next_pa()
                for k in range(KC):
                    op("pe", lambda e: e.matmul(pa[0:M, 0:256], lhsT=XN[:, k, tb * 128:tb * 128 + M], rhs=sv[:, k, 256:512], start=(k == 0), stop=(k == KC - 1)),
                       R=(wtok, f"XN{k}"), W=(pan,), inc=(k == KC - 1))
                if kind == "p":
                    op("act", lambda e: e.activation(out=VH[:, 0, 1 + tb, :], in_=pa[:, 0:256], func=AF.Copy), R=(pan,), W=("VH",))
                    if last_pass and tb == 3:
                        op("act", lambda e: e.activation(out=ROW[:, 0:256], in_=pa[:, 0:256], func=AF.Copy), R=(pan,), W=("ROW",))
                        trk.dma("sp", pout["v"], ROW[:, 0:256], R=("ROW",), W=("pv",), sem="oROW")
                else:
                    op("act", lambda e: e.activation(out=ROW[0:NS, 0:256], in_=pa[0:NS, 0:256], func=AF.Copy), R=(pan,), W=("ROW",))
                    trk.dma("sp", s_v[l][:, 127, :], ROW[0:NS, 0:256], R=("ROW",), W=(f"svB{l}",), sem="oROW")

            if (_DBG.get("stop") == "kv" and kind == _DBG.get("stop_kind", "p") and l == _DBG.get("stop_layer", 0)):
                raise _Stop()
            for jb in range(2):
                pieces = []
                for jj_ in range(2):
                    for h_ in range(2):
                        for g_ in range(4):
                            c_src = (g_ * 4 + jb * 2 + jj_) * 64 + h_ * 32
                            c_dst = (jj_ * 2 + h_) * 128 + g_ * 32
                            pieces.append((lambda s, c_dst=c_dst: wv(s, KC, 512)[:, :, c_dst:c_dst + 32], WIN[l][:, c_src:c_src + 32].rearrange("(k p) m -> p k m", p=128)))
                slot, wtok = wload(pieces, "q")
                sv = wv(slot, KC, 512)
                for jj in range(2):
                    j = jb * 2 + jj
                    qf = []
                    for pl in range(2):
                        pa, pan = gemm_chunk(lambda k: sv[:, k, (jj * 2 + pl) * 128:(jj * 2 + pl + 1) * 128], wtok, KC, xr, xt, N)
                        t_, tn = next_tmp()
                        op("act", lambda e: e.activation(out=t_[:, 0:N], in_=pa[:, 0:N], func=AF.Copy), R=(pan,), W=(tn,))
                        qf.append((t_, tn))
                    rope(qf, [(QK[:, j * 2, :], f"QK{j * 2}"), (QK[:, j * 2 + 1, :], f"QK{j * 2 + 1}")], N)

            if (_DBG.get("stop") == "q" and kind == _DBG.get("stop_kind", "p") and l == _DBG.get("stop_layer", 0)):
                raise _Stop()
            pba, pban = PB[1], "PB1"
            if kind == "p":
                attention_prompt(l, ffi)
            else:
                attention_sample(l)
            for c in range(8):
                sq_accum(pba, pban, c, c == 0, c == 7, N)
            group_norm_finish(pba, pban, l, 0, 8, N, 1)

            if (_DBG.get("stop") == "attn" and kind == _DBG.get("stop_kind", "p") and l == _DBG.get("stop_layer", 0)):
                raise _Stop()
            for cb in range(4):
                cs_ = slice(cb * 512, (cb + 1) * 512)
                pieces = []
                for half in range(2):
                    for kc2 in range(2):
                        r0 = kc2 * 512 + half * 256
                        pieces.append((lambda s, half=half, kc2=kc2: wv(s, KC, 512)[half * 64:(half + 1) * 64, kc2 * 4:(kc2 + 1) * 4, :],
                                       WOUT[l][r0:r0 + 256, cs_].rearrange("(j d) m -> d j m", d=64)))
                pieces.append((lambda s: wv(s, KC, 512)[:, 8:16, :], WOUT[l][1024:2048, cs_].rearrange("(k p) m -> p k m", p=128)))
                slot, wtok = wload(pieces, "out")
                sv = wv(slot, KC, 512)
                for mm in range(4):
                    m = cb * 4 + mm
                    ko = lambda k: (k + 8) % 16
                    pa, pan = gemm_chunk(lambda k: sv[:, ko(k), mm * 128:(mm + 1) * 128], wtok, KC, lambda k: MX[:, ko(k), 0:N], lambda k: f"MX{ko(k)}", N)
                    op("dve", lambda e: e.tensor_tensor(out=X[:, m, 0:N], in0=pa[:, 0:N], in1=X[:, m, 0:N], op=ALU.add), R=(pan, f"X{m}"), W=(f"X{m}",))

            if (_DBG.get("stop") == "out" and kind == _DBG.get("stop_kind", "p") and l == _DBG.get("stop_layer", 0)):
                raise _Stop()
            rmsnorm_to_xn(l, "nffn", N)
            trk.fence("dve", ["pe", "act"])
            for fb in range(22):
                slot, wtok = wload([
                    (lambda s: wv(s, KC, 512)[:, :, 0:256], WGU[l][:, fb * 256:(fb + 1) * 256].rearrange("(k p) m -> p k m", p=128)),
                    (lambda s: wv(s, KC, 512)[:, :, 256:512], WGU[l][:, DFF + fb * 256:DFF + (fb + 1) * 256].rearrange("(k p) m -> p k m", p=128)),
                ], "gu")
                sv = wv(slot, KC, 512)
                for cc in range(2):
                    fc = fb * 2 + cc
                    pa, pan = gemm_chunk(lambda k: sv[:, k, cc * 128:(cc + 1) * 128], wtok, KC, xr, xt, N)
                    sg, sgn = next_tmp()
                    op("act", lambda e: e.activation(out=sg[:, 0:N], in_=pa[:, 0:N], func=AF.Silu), R=(pan,), W=(sgn,))
                    pa, pan = gemm_chunk(lambda k: sv[:, k, 256 + cc * 128:256 + (cc + 1) * 128], wtok, KC, xr, xt, N)
                    op("dve", lambda e: e.tensor_tensor(out=H[:, fc, 0:N], in0=pa[:, 0:N], in1=sg[:, 0:N], op=ALU.mult), R=(pan, sgn), W=(f"H{fc}",))
            for cg in range(8):
                accs = None
                for kh in range(2):
                    slot, wtok = wload([(lambda s: wv(s, 22, 256), WDN[l][kh * 2816:(kh + 1) * 2816, cg * 256:(cg + 1) * 256].rearrange("(k p) m -> p k m", p=128))], "dn")
                    sv = wv(slot, 22, 256)
                    if kh == 0:
                        accs = [next_pa(), next_pa()]
                    for mm in range(2):
                        pa, pan = accs[mm]
                        for k in range(22):
                            kk = kh * 22 + k
                            op("pe", lambda e: e.matmul(pa[:, 0:N], lhsT=sv[:, k, mm * 128:(mm + 1) * 128], rhs=H[:, kk, 0:N], start=(kk == 0), stop=(kk == FC - 1)),
                               R=(wtok, f"H{kk}"), W=(pan,), inc=(k == 21))
                for mm in range(2):
                    m = cg * 2 + mm
                    pa, pan = accs[mm]
                    op("dve", lambda e: e.tensor_tensor(out=X[:, m, 0:N], in0=pa[:, 0:N], in1=X[:, m, 0:N], op=ALU.add), R=(pan, f"X{m}"), W=(f"X{m}",))
                if xfinal_cb is not None and cg % 2 == 1:
                    xfinal_cb(cg // 2)

        def rope(src, dst, N):
            (a_, an), (b_, bn) = src
            (da, dan), (db, dbn) = dst
            t1, t1n = next_tmp(); t2, t2n = next_tmp()
            cos = CS[:, 0, 0:N]; sin = CS[:, 1, 0:N]
            op("dve", lambda e: e.tensor_tensor(out=t1[:, 0:N], in0=a_[:, 0:N], in1=cos, op=ALU.mult), R=(an, "CS"), W=(t1n,))
            op("dve", lambda e: e.tensor_tensor(out=t2[:, 0:N], in0=b_[:, 0:N], in1=sin, op=ALU.mult), R=(bn, "CS"), W=(t2n,))
            op("dve", lambda e: e.tensor_tensor(out=da[:, 0:N], in0=t1[:, 0:N], in1=t2[:, 0:N], op=ALU.subtract), R=(t1n, t2n), W=(dan,))
            op("dve", lambda e: e.tensor_tensor(out=t1[:, 0:N], in0=b_[:, 0:N], in1=cos, op=ALU.mult), R=(bn, "CS"), W=(t1n,))
            op("dve", lambda e: e.tensor_tensor(out=t2[:, 0:N], in0=a_[:, 0:N], in1=sin, op=ALU.mult), R=(an, "CS"), W=(t2n,))
            op("dve", lambda e: e.tensor_tensor(out=db[:, 0:N], in0=t1[:, 0:N], in1=t2[:, 0:N], op=ALU.add), R=(t1n, t2n), W=(dbn,))

        pt_i = {"i": 0}

        def attention_prompt(l, ffi):
            QKv = QK[:, 0:8, :].rearrange("p (j pl) t -> p pl j t", pl=2)
            num, numn = PB[2], "PB2"
            den, denn = PB[3], "PB3"
            units = [(qb, kc, half) for qb in range(4) for kc in range(2) for half in range(2)]

            def qk_phase(u):
                qb, kc, half = u
                g = kc * 2 + half
                blocks = [(qb, 1), (qb + 1, 0)]
                pts = []
                for bi, (kb, mk) in enumerate(blocks):
                    sps = PB[bi]; spsn = f"PB{bi}"
                    for pl in range(2):
                        op("pe", lambda e: e.matmul(sps[:, :].rearrange("p (j q) -> p j q", q=128), lhsT=KH[32 * g:32 * g + 32, 0, pl, kb * 128:(kb + 1) * 128],
                                                    rhs=QKv[32 * g:32 * g + 32, pl, :, qb * 128:(qb + 1) * 128], start=(pl == 0), stop=(pl == 1), tile_position=(32 * g, 0)),
                           R=("KH",) + tuple(f"QK{j * 2 + pl}" for j in range(4)), W=(spsn,), inc=(pl == 1))
                    i = pt_i["i"] % 4; pt_i["i"] += 1
                    pt, ptn = PT[i], f"PT{i}"
                    op("act", lambda e: e.activation(out=pt[:, :], in_=sps[:, :], func=AF.Exp, scale=0.125), R=(spsn,), W=(ptn,))
                    mk_ap = MSK0[:, :, :] if (mk == 1 and qb == 0 and ffi is not None) else MSK[:, mk, :, :]
                    op("dve", lambda e: e.tensor_tensor(out=pt[:, :], in0=pt[:, :], in1=mk_ap.rearrange("p j q -> p (j q)"), op=ALU.mult), R=(ptn, "MSK", "MSK0"), W=(ptn,))
                    pts.append((pt, ptn, kb))
                return pts

            def pv_phase(u, pts):
                qb, kc, half = u
                g = kc * 2 + half
                for bi, (pt, ptn, kb) in enumerate(pts):
                    st = (bi == 0); sp_ = (bi == len(pts) - 1)
                    op("pe", lambda e: e.matmul(num[half * 64:(half + 1) * 64, :], lhsT=VH[:, 0, kb, g * 64:(g + 1) * 64], rhs=pt[:, :], start=st, stop=sp_),
                       R=(ptn, "VH"), W=(numn,), inc=False)
                    op("pe", lambda e: e.matmul(den[half * 64:(half + 1) * 64, :], lhsT=ONES[:, 0:64], rhs=pt[:, :], start=st, stop=sp_),
                       R=(ptn, "ONES"), W=(denn,), inc=True)
                if half == 1:
                    dn, dnn = next_tmp()
                    op("dve", lambda e: e.tensor_tensor(out=dn[:, 0:512], in0=den[:, :], in1=SNK[:, kc, :, :].rearrange("p j q -> p (j q)"), op=ALU.add), R=(denn, "SNK"), W=(dnn,))
                    op("dve", lambda e: e.reciprocal(out=dn[:, 0:512], in_=dn[:, 0:512]), R=(dnn,), W=(dnn,))
                    op("dve", lambda e: e.tensor_tensor(out=MX[:, kc * 4:kc * 4 + 4, qb * 128:(qb + 1) * 128], in0=num[:, :].rearrange("p (j q) -> p j q", q=128),
                                                        in1=dn[:, 0:512].rearrange("p (j q) -> p j q", q=128), op=ALU.mult),
                       R=(numn, dnn), W=tuple(f"MX{kc * 4 + j}" for j in range(4)))

            prev = None
            for u in units:
                pts = qk_phase(u)
                if prev is not None:
                    pv_phase(*prev)
                prev = (u, pts)
            pv_phase(*prev)

        def attention_sample(l):
            QKv = QK[:, 0:8, :].rearrange("p (j pl) t -> p pl j t", pl=2)
            first = True
            trk.reserve("ks", 8)
            for h_ in range(2):
                for g_ in range(4):
                    trk.dma("pool", KS[:, :, h_ * 128 + g_ * 32:h_ * 128 + g_ * 32 + 32], s_k[l][:, :, g_ * 64 + h_ * 32:g_ * 64 + h_ * 32 + 32].rearrange("n s i -> s n i"),
                            R=(f"skA{l}", f"skB{l}"), W=("KS",), sem="ks", skip_deps=not first)
                    first = False
            trk.dma("pool", VS2[:], s_v[l].rearrange("n s f -> s n f"), R=(f"svA{l}", f"svB{l}"), W=("VS2",), sem="vs")
            for n in range(NS):
                kt, ktn = KTS[n % 2], f"KTS{n % 2}"
                ptb = PB[3][:, 0:128].bitcast(BF16).rearrange("p (pl s) -> p pl s", pl=2)
                for pl in range(2):
                    op("pe", lambda e: e.transpose(ptb[:, pl, :], KS[:, n, pl * 128:(pl + 1) * 128], IDNB[:, :]), R=("KS", "IDNB"), W=("PB3",))
                op("act", lambda e: e.activation(out=kt[:, :, :], in_=ptb, func=AF.Copy), R=("PB3",), W=(ktn,))
                for g in range(4):
                    for pl in range(2):
                        op("pe", lambda e: e.matmul(PA[g][:, n * 4:n * 4 + 4], lhsT=kt[32 * g:32 * g + 32, pl, :], rhs=QKv[32 * g:32 * g + 32, pl, :, n],
                                                    start=(pl == 0), stop=(pl == 1), tile_position=(32 * g, 0)),
                           R=(ktn,) + tuple(f"QK{j * 2 + pl}" for j in range(4)), W=(f"PA{g}",), inc=(pl == 1))
            pssv = PSS[:, :].rearrange("p (n g j) -> p n g j", g=4, j=4)
            for g in range(4):
                op("act", lambda e: e.activation(out=pssv[:, :, g, :], in_=PA[g][:, 0:64].rearrange("p (n j) -> p n j", j=4), func=AF.Exp, scale=0.125), R=(f"PA{g}",), W=("PSS",))
            den, denn = PB[1], "PB1"
            op("pe", lambda e: e.matmul(den[:, 0:256], lhsT=ONES[:, :], rhs=PSS[:, :], start=True, stop=True), R=("PSS", "ONES"), W=(denn,))
            op("dve", lambda e: e.tensor_tensor(out=RD[:, :], in0=den[:, 0:256], in1=SNKS[:, l, :, :].rearrange("p n h -> p (n h)"), op=ALU.add), R=(denn, "SNKS"), W=("RD",))
            op("dve", lambda e: e.reciprocal(out=RD[:, :], in_=RD[:, :]), R=("RD",), W=("RD",))
            num, numn = PB[2], "PB2"
            numv = num[:, 0:128].rearrange("p (kc j n) -> p kc j n", kc=2, j=4)
            for n in range(NS):
                for g in range(4):
                    hf = g % 2
                    op("pe", lambda e: e.matmul(numv[hf * 64:(hf + 1) * 64, g // 2, :, n], lhsT=VS2[:, n, g * 64:(g + 1) * 64], rhs=PSS[:, n * 16 + g * 4:n * 16 + g * 4 + 4], start=True, stop=True),
                       R=("VS2", "PSS"), W=(numn,), inc=(n == NS - 1 and g == 3))
            rdv = RD[:, :].rearrange("p (n g j) -> p g j n", g=4, j=4)
            for hf in range(2):
                op("dve", lambda e: e.tensor_tensor(out=MX[hf * 64:(hf + 1) * 64, 0:8, 0:NS].rearrange("p (kc j) n -> p kc j n", kc=2),
                                                    in0=numv[hf * 64:(hf + 1) * 64, :, :, :],
                                                    in1=rdv[hf * 64:(hf + 1) * 64, hf::2, :, :], op=ALU.mult),
                   R=(numn, "RD"), W=tuple(f"MX{c}" for c in range(8)))

        def run_pass(pi, kind):
            N = T if kind == "p" else NS
            ci = pi if kind == "p" else NSLOT
            o_fl, _ = lay[("flags", 0)]
            trk.dma("sp", CS[:], csd[ci], W=("CS",), sem="cs")
            ntb = 4 if kind == "p" else 1
            for tb in range(ntb):
                rows = 128 if kind == "p" else NS
                src = xp[pi * T + tb * 128: pi * T + tb * 128 + 128, :] if kind == "p" else xs
                trk.dma("sp", STG[0:rows, 0:D], src, W=("STG",), sem="xin")
                for c4 in range(4):
                    pb = PB[c4 % 2]; pbn = f"PB{c4 % 2}"
                    for cc in range(4):
                        c = c4 * 4 + cc
                        op("pe", lambda e: e.transpose(pb[:, cc * 128:cc * 128 + rows], STG[0:rows, c * 128:(c + 1) * 128], IDN[0:rows, 0:rows]), R=("STG", "IDN"), W=(pbn,), inc=(cc == 3))
                    op("act", lambda e: e.activation(out=X[:, c4 * 4:c4 * 4 + 4, tb * 128:tb * 128 + rows], in_=pb[:, :].rearrange("p (c t) -> p c t", t=128)[:, :, 0:rows], func=AF.Copy),
                       R=(pbn,), W=tuple(f"X{c4 * 4 + cc}" for cc in range(4)))
            if kind == "p" and pi >= 1:
                for j in range(4):
                    trk.dma("sp", STG[:, 0:2048], obs[j][0:128, :], R=(f"ob{j}",), W=("STG",), sem="xin")
                    for cc in range(4):
                        c = j * 4 + cc
                        op("dve", lambda e: e.scalar_tensor_tensor(out=X[:, c, :], in0=STG[:, cc * 512:(cc + 1) * 512], scalar=PV[:, o_fl + 1:o_fl + 2], in1=X[:, c, :], op0=ALU.mult, op1=ALU.add),
                           R=("STG", "PV", f"X{c}"), W=(f"X{c}",))
            if (_DBG.get("stop") == "load" and kind == _DBG.get("stop_kind", "p")):
                raise _Stop()
            if kind == "p":
                ffi = pi if pi <= 1 else None
                if ffi is not None:
                    for j in range(4):
                        op("dve", lambda e: e.tensor_scalar(out=MSK0[:, j, :], in0=MSK[:, 1, j, :], scalar1=PV[:, o_fl + 7 + ffi:o_fl + 8 + ffi], scalar2=None, op0=ALU.mult), R=("MSK", "PV"), W=("MSK0",))
                pout = None
                if pi >= NPP - 1:
                    w_ = pi - (NPP - 1)
                    pout = {"h": p_h[w_], "conv": p_conv[w_], "k": p_k[w_], "v": p_v[w_], "sc": p_sc[w_]}
                def exchange(j):
                    trk.dma("sp", ibs[j][:, :], X[:, j * 4:(j + 1) * 4, :].rearrange("p c t -> p (c t)"), R=tuple(f"X{j * 4 + cc}" for cc in range(4)), W=(f"ib{j}",), sem=f"xex{j}")
                    trk.coll(lambda e: e.collective_compute("AllGather", ALU.bypass, replica_groups=[[2 * i_, 2 * i_ + 1] for i_ in range(_DBG.get('ncores', NCORES) // 2)],
                                                            ins=[ibs[j].ap().opt()], outs=[obs[j].ap().opt()]), R=(f"ib{j}",), W=(f"ob{j}",))
                layer(2, pi, kind, pout, ffi, xfinal_cb=(exchange if pi < NSLOT - 1 else None))
                if pi == 0:
                    fa = PV[:, o_fl:o_fl + 1]
                    for t_, nm in ((HC[:, 2, :], "HC"), (CUX[:, 2, :, :], "CUX"), (CSC[:, 2, :, :], "CSC"), (KH[:, 0, :, :], "KH"), (VH[:, 0, :, :], "VH")):
                        op("dve", lambda e, t_=t_: e.tensor_scalar(out=t_, in0=t_, scalar1=fa, scalar2=None, op0=ALU.mult), R=(nm, "PV"), W=(nm,))
            else:
                for l in range(2):
                    layer(l, pi, kind, None, None)
            if (_DBG.get("stop") == "final" and kind == _DBG.get("stop_kind", "p")):
                raise _Stop()
            pb, pbn = PB[0], "PB0"
            for c in range(KC):
                sq, sqn = next_sq()
                op("act", lambda e: e.activation(out=sq[:, 0:N], in_=X[:, c, 0:N], func=AF.Square), R=(f"X{c}",), W=(sqn,))
                op("pe", lambda e: e.matmul(pb[:, 0:N], lhsT=ONES[:, :], rhs=sq[:, 0:N], start=(c == 0), stop=(c == KC - 1)), R=(sqn, "ONES"), W=(pbn,), inc=True)
            rs = RS[0]
            op("dve", lambda e: e.tensor_scalar(out=rs[:, 0:N], in0=pb[:, 0:N], scalar1=1.0 / D, scalar2=EPS, op0=ALU.mult, op1=ALU.add), R=(pbn,), W=("RS0",))
            op("act", lambda e: e.activation(out=rs[:, 0:N], in_=rs[:, 0:N], func=AF.Sqrt), R=("RS0",), W=("RS0",))
            op("dve", lambda e: e.reciprocal(out=rs[:, 0:N], in_=rs[:, 0:N]), R=("RS0",), W=("RS0",))
            o_f, _ = lay[("nfin", 0)]
            for c in range(KC):
                op("dve", lambda e: e.scalar_tensor_tensor(out=X[:, c, 0:N], in0=X[:, c, 0:N], scalar=PV[:, o_f + c:o_f + c + 1], in1=rs[:, 0:N], op0=ALU.mult, op1=ALU.mult),
                   R=(f"X{c}", "RS0", "PV"), W=(f"X{c}",))
            for tb in range(ntb):
                rows = 128 if kind == "p" else NS
                for c4 in range(4):
                    pb = PB[c4 % 2]; pbn = f"PB{c4 % 2}"
                    for cc in range(4):
                        c = c4 * 4 + cc
                        op("pe", lambda e: e.transpose(pb[0:rows, cc * 128:(cc + 1) * 128], X[:, c, tb * 128:tb * 128 + rows], IDN[:, :]), R=(f"X{c}", "IDN"), W=(pbn,), inc=(cc == 3))
                    op("act", lambda e: e.activation(out=STG[0:rows, c4 * 512:(c4 + 1) * 512], in_=pb[0:rows, :], func=AF.Copy), R=(pbn,), W=("STG",))
                dst = yp[pi * T + tb * 128: pi * T + tb * 128 + 128, :] if kind == "p" else ys
                trk.dma("sp", dst, STG[0:rows, 0:D], R=("STG",), W=("yout",), sem="oSTG")

        try:
            for pi in range(NSLOT):
                run_pass(pi, "p")
            if not _DBG.get("nosample"):
                run_pass(0, "s")
        except _Stop:
            pass
        trk.wait_all_dma("sp")
    return nc


_CACHE = {}
_DBG = {}


def kernel(x_prompt, x_sample, state_lru_h, state_lru_conv, cache_swa_k, cache_swa_v, state_sconv,
           norm_mix, w_in, norm_grp, w_out, lru_conv_w, lru_conv_b, lru_w_a, lru_b_a, lru_w_i, lru_b_i,
           lru_lambda, sc_conv_w, attn_sinks, norm_ffn, ffn_w_gu, ffn_w_down, norm_final):
    f = lambda a: np.ascontiguousarray(np.asarray(a, dtype=np.float32))
    x_prompt = f(x_prompt); x_sample = f(x_sample)
    NPP = _DBG.get('npp', SEQ // T)
    lay, NV = pv_layout()
    pv = np.zeros((128, NV), np.float32)

    def put(name, l, arr2d):
        o, n = lay[(name, l)]
        assert arr2d.shape == (128, n), (name, arr2d.shape)
        pv[:, o:o + n] = arr2d
    fm = lambda v: f(v).reshape(-1, 128).T
    def put_layer(l, ls):
        put("nmix", l, fm(norm_mix[ls])); put("nffn", l, fm(norm_ffn[ls]))
        g = f(norm_grp[ls])
        ga = g[:1024].reshape(2, 2, 4, 64)
        ga = ga.transpose(1, 3, 0, 2).reshape(128, 8)
        put("ngrp", l, np.concatenate([ga, fm(g[1024:1536]), fm(g[1536:2048])], axis=1))
        cw = f(lru_conv_w[ls])
        put("lcw", l, np.concatenate([fm(cw[k]) for k in range(4)], axis=1))
        put("lcb", l, fm(lru_conv_b[ls])); put("ba", l, fm(lru_b_a[ls])); put("bi", l, fm(lru_b_i[ls])); put("lam", l, fm(lru_lambda[ls]))
        sw = f(sc_conv_w[ls])
        put("scw", l, np.concatenate([fm(sw[k]) for k in range(3)], axis=1))
        sk = f(attn_sinks[ls]).reshape(2, 2, 4)
        sk2 = np.repeat(sk.transpose(1, 0, 2).reshape(2, 1, 8), 64, axis=1).reshape(128, 8)
        put("sink", l, sk2)
        put("sinkall", l, np.repeat(f(attn_sinks[ls])[None, :], 128, axis=0))
    put_layer(0, 0); put_layer(1, 1)
    put("nfin", 0, fm(norm_final))
    bdh = np.zeros((3, 128, 8, 128), np.float32)
    for l in range(2):
        for gi, wsrc in enumerate((lru_w_a, lru_w_i)):
            wl = f(wsrc[l])
            for c in range(4):
                bdh[l, 0:64, gi * 4 + c, 0:64] = wl[2 * c]
                bdh[l, 64:128, gi * 4 + c, 64:128] = wl[2 * c + 1]
    half = 32
    inv = (10000.0 ** (-np.arange(half, dtype=np.float32) / half)).astype(np.float32)
    NSLOT = NPP + 1

    def cs_tables(lc):
        cs = np.zeros((NSLOT + 1, 128, 2, T), np.float32)
        for si in range(NSLOT + 1):
            if si < NSLOT:
                pss = min(max(si - lc, 0), NPP - 1)
                pos = (np.arange(T) + pss * T).astype(np.float32)
            else:
                pos = np.full(T, float(PAST), np.float32)
            ang = pos[None, :] * inv[:, None]
            cs[si, :, 0, :] = np.tile(np.cos(ang).astype(np.float32), (4, 1))
            cs[si, :, 1, :] = np.tile(np.sin(ang).astype(np.float32), (4, 1))
        return cs
    cs_by_lc = [cs_tables(0), cs_tables(1)]
    sidx = np.arange(128)[:, None]; qidx = np.arange(128)[None, :]
    msk = np.stack([(sidx <= qidx), (sidx > qidx)], axis=1).astype(np.float32)
    idn = np.eye(128, dtype=np.float32)
    w_in = f(w_in); w_out = f(w_out); ffn_w_gu = f(ffn_w_gu); ffn_w_down = f(ffn_w_down)
    sh = f(state_lru_h); scv = f(state_lru_conv); ssc = f(state_sconv)
    ckk = f(cache_swa_k).reshape(2, 128, 128, 256); cvv = f(cache_swa_v).reshape(2, 128, 128, 256)
    in_maps = []
    o_fl, _ = lay[("flags", 0)]
    zeros_xp = np.zeros((NSLOT * T, D), np.float32)
    for c in range(NCORES):
        b = c // 2
        lc = c % 2
        ns = slice(c * NS, (c + 1) * NS)
        put_layer(2, lc)
        pvc = pv.copy()
        pvc[:, o_fl + 0] = 1.0 - lc; pvc[:, o_fl + 1] = float(lc)
        for si in range(5):
            ff = 1.0 if si == lc else 0.0
            pvc[:, o_fl + 2 + si] = ff; pvc[:, o_fl + 7 + si] = 1.0 - ff
        bdc = bdh.copy(); bdc[2] = bdh[lc]
        if lc == 0:
            xpc = np.concatenate([x_prompt[b][:NPP * T], np.zeros((T, D), np.float32)], axis=0)
        else:
            xpc = zeros_xp
        in_maps.append({
            "xp": xpc, "xs": np.ascontiguousarray(x_sample[ns, 0, :]),
            "st_h": np.ascontiguousarray(sh[:, ns]), "st_conv": np.ascontiguousarray(scv[:, ns]), "st_sc": np.ascontiguousarray(ssc[:, ns]),
            "ck": np.ascontiguousarray(ckk[:, ns]), "cv": np.ascontiguousarray(cvv[:, ns]),
            "w_in": w_in, "w_out": w_out, "w_gu": ffn_w_gu, "w_dn": ffn_w_down,
            "wo_in": w_in[lc:lc + 1], "wo_out": w_out[lc:lc + 1], "wo_gu": ffn_w_gu[lc:lc + 1], "wo_dn": ffn_w_down[lc:lc + 1],
            "bd": bdc, "pv": pvc, "cs": cs_by_lc[lc], "msk": msk, "idn": idn,
        })
    if NPP not in _CACHE:
        _CACHE[NPP] = build(NPP)
    nc = _CACHE[NPP]
    if 'ncores' in _DBG:
        n_ = _DBG['ncores']
        res = run_bass_kernel_spmd(nc, in_maps[:n_], core_ids=list(range(n_)), trace=_DBG.get('trace', False))
        _DBG['res'] = res
        return res.results
    res = run_bass_kernel_spmd(nc, in_maps, core_ids=list(range(NCORES)))
    R = res.results
    y_prompt = np.stack([R[2 * b + 1]["yp"][T:T + SEQ] for b in range(4)]).reshape(4, SEQ, D)
    y_sample = np.concatenate([R[c]["ys"] for c in range(NCORES)], axis=0).reshape(128, 1, D)
    stk = lambda name, shp: np.stack([np.stack([R[2 * b][name][0], R[2 * b + 1][name][1]]) for b in range(4)], axis=1).reshape(shp)
    p_lru_h = stk("p_h", (2, 4, 512))
    p_lru_conv = stk("p_conv", (2, 4, 3, 512))
    p_swa_k = stk("p_k", (2, 4, 128, 4, 64))
    p_swa_v = stk("p_v", (2, 4, 128, 4, 64))
    p_sconv = stk("p_sc", (2, 4, 2, 512))
    cat = lambda name, shp: np.concatenate([R[c][name] for c in range(NCORES)], axis=1).reshape(shp)
    s_lru_h = cat("s_h", (2, 128, 512))
    s_lru_conv = cat("s_conv", (2, 128, 3, 512))
    s_swa_k = cat("s_k", (2, 128, 128, 4, 64))
    s_swa_v = cat("s_v", (2, 128, 128, 4, 64))
    s_sconv = cat("s_sc", (2, 128, 2, 512))
    outs = (y_prompt, y_sample, p_lru_h, p_lru_conv, p_swa_k, p_swa_v, p_sconv, s_lru_h, s_lru_conv, s_swa_k, s_swa_v, s_sconv)
    return tuple(np.ascontiguousarray(o, dtype=np.float32) for o in outs)
```

```python
import numpy as np
from contextlib import ExitStack
import concourse.bass as bass
import concourse.mybir as mybir
from concourse.bass_utils import run_bass_kernel_spmd

F32 = mybir.dt.float32
BF16 = mybir.dt.bfloat16
AF = mybir.ActivationFunctionType
ALU = mybir.AluOpType

D = 2048
KC = 16
T = 512
NS = 16
DFF = 5632
FC = 44
EPS = 1e-6
PAST = 8192
NCORES = 8
SEQ = 2048


def pv_layout():
    lay = {}
    off = 0
    for l in range(3):
        for name, n in (("nmix", 16), ("nffn", 16), ("ngrp", 16), ("lcw", 16), ("lcb", 4), ("ba", 4), ("bi", 4),
                        ("lam", 4), ("scw", 12), ("sink", 8), ("sinkall", 16)):
            lay[(name, l)] = (off, n)
            off += n
    lay[("nfin", 0)] = (off, 16)
    off += 16
    lay[("flags", 0)] = (off, 12)
    off += 12
    return lay, off


class _Stop(Exception):
    pass


class Trk:
    SEM_MAX = 3800

    def __init__(self, nc, es):
        self.nc = nc
        self.es = es
        self.E = {"pe": nc.tensor, "act": nc.scalar, "dve": nc.vector, "pool": nc.gpsimd, "sp": nc.sync}
        self.gen = {k: 0 for k in self.E}
        self.sem = {k: es.enter_context(nc.semaphore("pg_" + k + "_0")) for k in self.E}
        self.semname = {k: "pg_" + k + "_0" for k in self.E}
        self.cnt = {k: 0 for k in self.E}
        self.seen = {k: {} for k in self.E}
        self.lastw = {}
        self.readers = {}
        self.dsems = {}
        self.all_dsems = []
        self.pending_noinc = {k: False for k in self.E}

    def _wait(self, eng, ev):
        name, semh, val = ev
        if eng == "pe" and name.startswith("pg_pe_"):
            return
        if self.seen[eng].get(name, 0) >= val:
            return
        self.E[eng].wait_ge(semh, val)
        self.seen[eng][name] = val

    def _deps(self, eng, R, W):
        best = {}

        def add(ev):
            if ev[0] not in best or best[ev[0]][2] < ev[2]:
                best[ev[0]] = ev
        for r in R:
            if r in self.lastw:
                add(self.lastw[r])
        for w in W:
            if w in self.lastw:
                add(self.lastw[w])
            for ev in self.readers.get(w, {}).values():
                add(ev)
        for ev in best.values():
            self._wait(eng, ev)

    def _commit(self, ev, R, W):
        for r in R:
            d = self.readers.setdefault(r, {})
            if ev[0] not in d or d[ev[0]][2] < ev[2]:
                d[ev[0]] = ev
        for w in W:
            self.lastw[w] = ev
            self.readers[w] = {}

    def _roll(self, eng):
        if self.cnt[eng] >= self.SEM_MAX and not self.pending_noinc[eng]:
            self.gen[eng] += 1
            nm = f"pg_{eng}_{self.gen[eng]}"
            self.sem[eng] = self.es.enter_context(self.nc.semaphore(nm))
            self.semname[eng] = nm
            self.cnt[eng] = 0

    def op(self, eng, fn, R=(), W=(), inc=True):
        self._roll(eng)
        self._deps(eng, R, W)
        ins = fn(self.E[eng])
        if inc:
            self.cnt[eng] += 1
            ins.then_inc(self.sem[eng], 1)
            ev = (self.semname[eng], self.sem[eng], self.cnt[eng])
            self.pending_noinc[eng] = False
        else:
            ev = (self.semname[eng], self.sem[eng], self.cnt[eng] + 1)
            self.pending_noinc[eng] = True
        self._commit(ev, R, W)
        return ins

    def fence(self, eng, others):
        for o in others:
            self._wait(eng, (self.semname[o], self.sem[o], self.cnt[o] + (1 if self.pending_noinc[o] else 0)))

    def reserve(self, sem, n):
        if sem not in self.dsems or self.dsems[sem][1] + 16 * n > self.SEM_MAX:
            g = self.dsems[sem][3] + 1 if sem in self.dsems else 0
            nm = f"d_{sem}_{g}"
            self.dsems[sem] = [self.es.enter_context(self.nc.semaphore(nm)), 0, nm, g]
            self.all_dsems.append(self.dsems[sem])

    def dma(self, q, out, in_, R=(), W=(), sem="g", skip_deps=False, **kw):
        if not skip_deps:
            self._deps(q, R, W)
        self.reserve(sem, 1)
        ds = self.dsems[sem]
        ins = self.E[q].dma_start(out=out, in_=in_, **kw)
        ds[1] += 16
        ins.then_inc(ds[0], 16)
        ev = (ds[2], ds[0], ds[1])
        self._commit(ev, R, W)
        return ins

    def coll(self, fn, R=(), W=()):
        self._deps("pool", R, W)
        if "cc" not in self.dsems:
            self.dsems["cc"] = [self.es.enter_context(self.nc.semaphore("d_cc_0")), 0, "d_cc_0", 0]
            self.all_dsems.append(self.dsems["cc"])
        ds = self.dsems["cc"]
        ins = fn(self.E["pool"])
        ds[1] += 1
        ins.then_inc(ds[0], 1)
        ev = (ds[2], ds[0], ds[1])
        self._commit(ev, R, W)
        return ins

    def wait_all_dma(self, eng):
        for ds in self.all_dsems:
            if ds[1]:
                self.E[eng].wait_ge(ds[0], ds[1])


def build(NPP):
    lay, NV = pv_layout()
    NSLOT = NPP + 1
    TOK = NSLOT * T
    nc = bass.Bass("TRN2", target_bir_lowering=False)

    def din(name, shape, dt=F32):
        return nc.dram_tensor(name, list(shape), dt, kind="ExternalInput").ap()

    def dout(name, shape, dt=F32):
        return nc.dram_tensor(name, list(shape), dt, kind="ExternalOutput").ap()

    xp = din("xp", [TOK, D]); xs = din("xs", [NS, D])
    st_h = din("st_h", [2, NS, 512]); st_conv = din("st_conv", [2, NS, 3, 512]); st_sc = din("st_sc", [2, NS, 2, 512])
    ck = din("ck", [2, NS, 128, 256]); cv = din("cv", [2, NS, 128, 256])
    w_in = din("w_in", [2, D, 4096]); w_out = din("w_out", [2, D, D]); w_gu = din("w_gu", [2, D, 2 * DFF]); w_dn = din("w_dn", [2, DFF, D])
    wo_in = din("wo_in", [1, D, 4096]); wo_out = din("wo_out", [1, D, D]); wo_gu = din("wo_gu", [1, D, 2 * DFF]); wo_dn = din("wo_dn", [1, DFF, D])
    WIN = [w_in[0], w_in[1], wo_in[0]]; WOUT = [w_out[0], w_out[1], wo_out[0]]
    WGU = [w_gu[0], w_gu[1], wo_gu[0]]; WDN = [w_dn[0], w_dn[1], wo_dn[0]]
    bd = din("bd", [3, 128, 8, 128])
    ibs = [nc.dram_tensor(f"ib{j}", [128, 2048], F32) for j in range(4)]
    obs = [nc.dram_tensor(f"ob{j}", [256, 2048], F32) for j in range(4)]
    pvd = din("pv", [128, NV])
    csd = din("cs", [NSLOT + 1, 128, 2, T])
    mskd = din("msk", [128, 2, 128])
    idnd = din("idn", [128, 128])

    yp = dout("yp", [TOK, D]); ys = dout("ys", [NS, D])
    p_h = dout("p_h", [2, 512]); p_conv = dout("p_conv", [2, 3, 512]); p_k = dout("p_k", [2, 128, 256]); p_v = dout("p_v", [2, 128, 256]); p_sc = dout("p_sc", [2, 2, 512])
    s_h = dout("s_h", [2, NS, 512]); s_conv = dout("s_conv", [2, NS, 3, 512]); s_k = dout("s_k", [2, NS, 128, 256]); s_v = dout("s_v", [2, NS, 128, 256]); s_sc = dout("s_sc", [2, NS, 2, 512])

    es = ExitStack()
    with es:
        trk = Trk(nc, es)
        op = trk.op

        def sb(name, shape, dt):
            return es.enter_context(nc.sbuf_tensor(name, list(shape), dt))

        def ps(name, shape, dt):
            return es.enter_context(nc.psum_tensor(name, list(shape), dt))

        X = sb("X", [128, KC, T], F32)
        XN = sb("XN", [128, KC, T], BF16)
        H = sb("H", [128, FC, T], BF16)
        MX = H[:, 0:16, :]
        QK = H[:, 16:26, :]
        KS = H[:, 26:34, :].rearrange("p c (a f) -> p (c a) f", f=256)
        VS2 = H[:, 34:42, :].rearrange("p c (a f) -> p (c a) f", f=256)
        NWS = 3
        WS = [sb(f"WS{i}", [128, 8192], BF16) for i in range(NWS)]
        NTMP = 9
        TMP = [sb(f"TMP{i}", [128, T + 4], F32) for i in range(NTMP)]
        SQ = [sb(f"SQ{i}", [128, T], BF16) for i in range(3)]
        PT = [sb(f"PT{i}", [128, 512], BF16) for i in range(4)]
        STG = XN[:, :, :].rearrange("p c t -> p (c t)").bitcast(F32)
        PV = sb("PV", [128, NV], F32)
        CS = sb("CS", [128, 2, T], F32)
        MSKF = sb("MSKF", [128, 2, 128], F32)
        MSK = sb("MSK", [128, 2, 4, 128], BF16)
        MSK0 = sb("MSK0", [128, 4, 128], BF16)
        IDN = sb("IDN", [128, 128], F32)
        IDNB = sb("IDNB", [128, 128], BF16)
        ONES = sb("ONES", [128, 128], BF16)
        BD = sb("BD", [128, 8, 128], F32)
        RS = [sb(f"RS{i}", [128, T], F32) for i in range(2)]
        SCL = sb("SCL", [128, 3, 2, 4], F32)
        SNK = sb("SNK", [128, 2, 4, 128], F32)
        EXS = sb("EXS", [128, 3, 8], F32)
        SNKS = sb("SNKS", [128, 2, NS, 16], F32)
        HC = sb("HC", [128, 3, 4], F32)
        CUX = sb("CUX", [128, 3, 4, 3], F32)
        CSC = sb("CSC", [128, 3, 4, 2], F32)
        KH = sb("KH", [128, 1, 2, 128 + T], BF16)
        VH = sb("VH", [128, 1, 5, 256], BF16)
        KF = [sb(f"KF{i}", [128, T], F32) for i in range(2)]
        ROW = sb("ROW", [128, 512], F32)
        UXS = sb("UXS", [128, 4, NS, 4], F32)
        GCS = sb("GCS", [128, 4, NS, 3], F32)
        H0S = sb("H0S", [128, 4, NS], F32)
        STS = sb("STS", [48, 512], F32)
        KTS = [sb(f"KTS{i}", [128, 2, 128], BF16) for i in range(2)]
        PSS = sb("PSS", [128, 256], BF16)
        RD = sb("RD", [128, 256], F32)

        PA = [ps(f"PA{i}", [128, 512], F32) for i in range(4)]
        PB = [ps(f"PB{i}", [128, 512], F32) for i in range(4)]

        wstate = {"i": 0}

        def wload(pieces, nm):
            i = wstate["i"]; wstate["i"] += 1
            slot = i % NWS
            trk.reserve(f"w{slot}", len(pieces))
            for pi_, (dst, src) in enumerate(pieces):
                trk.dma("pool", dst(WS[slot]), src, R=(), W=(f"W{slot}",), sem=f"w{slot}", skip_deps=(pi_ > 0))
            return WS[slot], f"W{slot}"

        def wv(slot, kc, n):
            return slot[:, 0:kc * n].rearrange("p (k n) -> p k n", n=n)

        pa_i = {"i": 0}

        def next_pa():
            i = pa_i["i"] % 4; pa_i["i"] += 1
            return PA[i], f"PA{i}"

        tmp_i = {"i": 0}

        def next_tmp():
            i = tmp_i["i"] % NTMP; tmp_i["i"] += 1
            return TMP[i], f"TMP{i}"

        sq_i = {"i": 0}

        def next_sq():
            i = sq_i["i"] % 3; sq_i["i"] += 1
            return SQ[i], f"SQ{i}"

        def colp(l, name, c=0):
            o, n = lay[(name, l)]
            return PV[:, o + c:o + c + 1]

        trk.dma("sp", PV[:], pvd, W=("PV",), sem="c_pv")
        trk.dma("sp", IDN[:], idnd, W=("IDN",), sem="c_idn")
        trk.dma("sp", MSKF[:], mskd, W=("MSKF",), sem="c_msk")
        op("dve", lambda e: e.tensor_copy(out=IDNB[:], in_=IDN[:]), R=("IDN",), W=("IDNB",))
        op("dve", lambda e: e.memset(ONES[:], 1.0), W=("ONES",))
        for m in range(2):
            for j in range(4):
                op("dve", lambda e, m=m, j=j: e.tensor_copy(out=MSK[:, m, j, :], in_=MSKF[:, m, :]), R=("MSKF",), W=("MSK",))
        for t_, nm in ((HC, "HC"), (CUX, "CUX"), (CSC, "CSC"), (KH, "KH"), (VH, "VH")):
            op("dve", lambda e, t_=t_: e.memset(t_[:], 0.0), W=(nm,))
        for l in range(3):
            o, n = lay[("lam", l)]
            tt, tn = next_tmp()
            op("act", lambda e: e.activation(out=tt[:, 0:4], in_=PV[:, o:o + 4], func=AF.Exp, scale=-1.0), R=("PV",), W=(tn,))
            op("act", lambda e: e.activation(out=tt[:, 4:8], in_=tt[:, 0:4], func=AF.Ln, bias=1.0), R=(tn,), W=(tn,))
            op("dve", lambda e: e.tensor_scalar(out=SCL[:, l, 0, :], in0=tt[:, 4:8], scalar1=-8.0, scalar2=None, op0=ALU.mult), R=(tn,), W=("SCL",))
            op("dve", lambda e: e.tensor_scalar(out=SCL[:, l, 1, :], in0=tt[:, 4:8], scalar1=-16.0, scalar2=None, op0=ALU.mult), R=(tn,), W=("SCL",))
            o, n = lay[("sink", l)]
            op("act", lambda e: e.activation(out=EXS[:, l, :], in_=PV[:, o:o + 8], func=AF.Exp), R=("PV",), W=("EXS",))
            o, n = lay[("sinkall", l)]
            op("act", lambda e: e.activation(out=tt[:, 16:32], in_=PV[:, o:o + 16], func=AF.Exp), R=("PV",), W=(tn,))
            for n_ in range(NS):
                if l < 2:
                    op("dve", lambda e, n_=n_: e.tensor_copy(out=SNKS[:, l, n_, :], in_=tt[:, 16:32]), R=(tn,), W=("SNKS",))

        def rmsnorm_to_xn(l, gname, N):
            pb, pbn = PB[0], "PB0"
            for c in range(KC):
                sq, sqn = next_sq()
                op("act", lambda e: e.activation(out=sq[:, 0:N], in_=X[:, c, 0:N], func=AF.Square), R=(f"X{c}",), W=(sqn,))
                op("pe", lambda e: e.matmul(pb[:, 0:N], lhsT=ONES[:, :], rhs=sq[:, 0:N], start=(c == 0), stop=(c == KC - 1)),
                   R=(sqn, "ONES"), W=(pbn,), inc=True)
            rs = RS[0]
            op("dve", lambda e: e.tensor_scalar(out=rs[:, 0:N], in0=pb[:, 0:N], scalar1=1.0 / D, scalar2=EPS, op0=ALU.mult, op1=ALU.add), R=(pbn,), W=("RS0",))
            op("act", lambda e: e.activation(out=rs[:, 0:N], in_=rs[:, 0:N], func=AF.Sqrt), R=("RS0",), W=("RS0",))
            op("dve", lambda e: e.reciprocal(out=rs[:, 0:N], in_=rs[:, 0:N]), R=("RS0",), W=("RS0",))
            for c in range(KC):
                op("dve", lambda e: e.scalar_tensor_tensor(out=XN[:, c, 0:N], in0=X[:, c, 0:N], scalar=colp(l, gname, c), in1=rs[:, 0:N], op0=ALU.mult, op1=ALU.mult),
                   R=(f"X{c}", "RS0", "PV"), W=(f"XN{c}",))

        def gemm_chunk(lhsT_of_k, wtok, nk, rhs_of_k, rtok_of_k, N, last_in_block=False):
            pa, pan = next_pa()
            for k in range(nk):
                op("pe", lambda e: e.matmul(pa[:, 0:N], lhsT=lhsT_of_k(k), rhs=rhs_of_k(k), start=(k == 0), stop=(k == nk - 1)),
                   R=(wtok, rtok_of_k(k)), W=(pan,), inc=(k == nk - 1))
            return pa, pan

        def group_norm_finish(pb, pbn, l, c0, nch, N, rsi):
            rs = RS[rsi]; rsn = f"RS{rsi}"
            op("dve", lambda e: e.tensor_scalar(out=rs[:, 0:N], in0=pb[:, 0:N], scalar1=1.0 / (nch * 128), scalar2=EPS, op0=ALU.mult, op1=ALU.add), R=(pbn,), W=(rsn,))
            op("act", lambda e: e.activation(out=rs[:, 0:N], in_=rs[:, 0:N], func=AF.Sqrt), R=(rsn,), W=(rsn,))
            op("dve", lambda e: e.reciprocal(out=rs[:, 0:N], in_=rs[:, 0:N]), R=(rsn,), W=(rsn,))
            for c in range(c0, c0 + nch):
                op("dve", lambda e: e.scalar_tensor_tensor(out=MX[:, c, 0:N], in0=MX[:, c, 0:N], scalar=colp(l, "ngrp", c), in1=rs[:, 0:N], op0=ALU.mult, op1=ALU.mult),
                   R=(f"MX{c}", rsn, "PV"), W=(f"MX{c}",))

        def sq_accum(pb, pbn, c, first, last, N):
            sq, sqn = next_sq()
            op("act", lambda e: e.activation(out=sq[:, 0:N], in_=MX[:, c, 0:N], func=AF.Square), R=(f"MX{c}",), W=(sqn,))
            op("pe", lambda e: e.matmul(pb[:, 0:N], lhsT=ONES[:, :], rhs=sq[:, 0:N], start=first, stop=last), R=(sqn, "ONES"), W=(pbn,), inc=True)

        def transpose_out(src_ap, rows, cols, dst_dram, rtoks, tag, view=None, multi=None):
            op("pe", lambda e: e.transpose(PB[3][0:cols, 0:rows], src_ap, IDN[0:rows, 0:rows]), R=tuple(rtoks) + ("IDN",), W=("PB3",))
            op("act", lambda e: e.activation(out=ROW[0:cols, 0:rows], in_=PB[3][0:cols, 0:rows], func=AF.Copy), R=("PB3",), W=("ROW",))
            if multi is not None:
                trk.reserve("oROW", len(multi))
                for (d_ap, r0, r1) in multi:
                    trk.dma("sp", d_ap, ROW[r0:r1, 0:rows], R=("ROW",), W=(tag,), sem="oROW")
                return
            srcv = ROW[0:cols, 0:rows]
            if view is not None:
                srcv = view(srcv)
            trk.dma("sp", dst_dram, srcv, R=("ROW",), W=(tag,), sem="oROW")

        def layer(l, pi, kind, pout, ffi, xfinal_cb=None):
            N = T if kind == "p" else NS
            last_pass = pout is not None
            o_fl, _ = lay[("flags", 0)]
            if kind == "p":
                for kc in range(2):
                    for j in range(4):
                        op("dve", lambda e: e.tensor_scalar(out=SNK[:, kc, j, :], in0=IDN[:, :], scalar1=0.0, scalar2=EXS[:, l, kc * 4 + j:kc * 4 + j + 1], op0=ALU.mult, op1=ALU.add), R=("EXS", "IDN"), W=("SNK",))
            rmsnorm_to_xn(l, "nmix", N)
            xr = lambda k: XN[:, k, 0:N]
            xt = lambda k: f"XN{k}"
            if kind == "s":
                trk.dma("sp", STS[0:48, :], st_conv[l].rearrange("n k f -> (n k) f"), W=("STS",), sem="st")
                for c in range(4):
                    op("pe", lambda e: e.transpose(PB[3][:, 0:48], STS[0:48, c * 128:(c + 1) * 128], IDN[0:48, 0:48]), R=("STS", "IDN"), W=("PB3",))
                    op("act", lambda e: e.activation(out=UXS[:, c, :, 0:3], in_=PB[3][:, 0:48].rearrange("p (n k) -> p n k", k=3), func=AF.Copy), R=("PB3",), W=("UXS",))
                trk.dma("sp", STS[0:32, :], st_sc[l].rearrange("n k f -> (n k) f"), R=(), W=("STS",), sem="st")
                for c in range(4):
                    op("pe", lambda e: e.transpose(PB[3][:, 0:32], STS[0:32, c * 128:(c + 1) * 128], IDN[0:32, 0:32]), R=("STS", "IDN"), W=("PB3",))
                    op("act", lambda e: e.activation(out=GCS[:, c, :, 0:2], in_=PB[3][:, 0:32].rearrange("p (n k) -> p n k", k=2), func=AF.Copy), R=("PB3",), W=("GCS",))
                trk.dma("sp", STS[0:16, :], st_h[l], W=("STS",), sem="st")
                for c in range(4):
                    op("pe", lambda e: e.transpose(PB[3][:, 0:16], STS[0:16, c * 128:(c + 1) * 128], IDN[0:16, 0:16]), R=("STS", "IDN"), W=("PB3",))
                    op("act", lambda e: e.activation(out=H0S[:, c, :], in_=PB[3][:, 0:16], func=AF.Copy), R=("PB3",), W=("H0S",))
                trk.dma("sp", s_k[l][:, 0:127, :], ck[l][:, 1:128, :], W=(f"skA{l}",), sem=f"o2a{l}")
                trk.dma("sp", s_v[l][:, 0:127, :], cv[l][:, 1:128, :], W=(f"svA{l}",), sem=f"o2b{l}")
                trk.dma("sp", s_conv[l][:, 0:2, :], st_conv[l][:, 1:3, :], W=(f"scvA{l}",), sem=f"o2c{l}")
                trk.dma("sp", s_sc[l][:, 0:1, :], st_sc[l][:, 1:2, :], W=(f"sscA{l}",), sem=f"o2d{l}")

            if (_DBG.get("stop") == "norm" and kind == _DBG.get("stop_kind", "p") and l == _DBG.get("stop_layer", 0)):
                raise _Stop()
            trk.dma("sp", BD[:], bd[l], W=("BD",), sem="bd")
            pbl, pbln = PB[1], "PB1"
            pbs, pbsn = PB[2], "PB2"
            deferred = []

            def flush_deferred():
                while deferred:
                    deferred.pop(0)()

            def lru_chunk(c):
                cc = 0
                slot, wtok = wload([
                    (lambda s: wv(s, KC, 512)[:, :, 0:128], WIN[l][:, 1536 + c * 128:1536 + (c + 1) * 128].rearrange("(k p) m -> p k m", p=128)),
                    (lambda s: wv(s, KC, 512)[:, :, 256:384], WIN[l][:, 2048 + c * 128:2048 + (c + 1) * 128].rearrange("(k p) m -> p k m", p=128)),
                ], "lru")
                sv = wv(slot, KC, 512)
                pa, pan = gemm_chunk(lambda k: sv[:, k, cc * 128:(cc + 1) * 128], wtok, KC, xr, xt, N)
                ux, uxn = next_tmp()
                xc, xcn = next_tmp()
                if kind == "p":
                    op("dve", lambda e: e.tensor_copy(out=ux[:, 0:3], in_=CUX[:, l, c, :]), R=("CUX",), W=(uxn,))
                    op("act", lambda e: e.activation(out=ux[:, 3:3 + T], in_=pa[:, 0:T], func=AF.Copy), R=(pan,), W=(uxn,))
                    op("dve", lambda e: e.tensor_copy(out=CUX[:, l, c, :], in_=ux[:, T:T + 3]), R=(uxn,), W=("CUX",))
                    tap = lambda k: ux[:, k:k + T]
                else:
                    op("act", lambda e: e.activation(out=UXS[:, c, :, 3], in_=pa[:, 0:NS], func=AF.Copy), R=(pan,), W=("UXS",))
                    uxn = "UXS"
                    tap = lambda k: UXS[:, c, :, k]
                o_w, _ = lay[("lcw", l)]
                op("dve", lambda e: e.tensor_scalar(out=xc[:, 0:N], in0=tap(3), scalar1=PV[:, o_w + 12 + c:o_w + 13 + c], scalar2=colp(l, "lcb", c), op0=ALU.mult, op1=ALU.add), R=(uxn, "PV"), W=(xcn,))
                for k in (2, 1, 0):
                    op("dve", lambda e, k=k: e.scalar_tensor_tensor(out=xc[:, 0:N], in0=tap(k), scalar=PV[:, o_w + k * 4 + c:o_w + k * 4 + c + 1], in1=xc[:, 0:N], op0=ALU.mult, op1=ALU.add), R=(uxn, "PV", xcn), W=(xcn,))
                if kind == "s":
                    transpose_out(UXS[:, c, :, 3], 128, NS, s_conv[l][:, 2, c * 128:(c + 1) * 128], ("UXS",), f"scvB{l}")
                pa2, pa2n = gemm_chunk(lambda k: sv[:, k, 256:384], wtok, KC, xr, xt, N)
                ug, ugn = next_tmp()
                op("act", lambda e: e.activation(out=ug[:, 0:N], in_=pa2[:, 0:N], func=AF.Copy), R=(pa2n,), W=(ugn,))
                gates = []
                for gi, bname in ((0, "ba"), (1, "bi")):
                    pg, pgn = next_pa()
                    op("pe", lambda e: e.matmul(pg[:, 0:N], lhsT=BD[:, gi * 4 + c, :], rhs=xc[:, 0:N], start=True, stop=True), R=("BD", xcn), W=(pgn,))
                    gt, gtn = next_tmp()
                    op("act", lambda e: e.activation(out=gt[:, 0:N], in_=pg[:, 0:N], func=AF.Sigmoid, bias=colp(l, bname, c)), R=(pgn, "PV"), W=(gtn,))
                    gates.append((gt, gtn))
                (rg, rgn), (ig, ign) = gates
                a_, an = next_tmp()
                m_, mn = next_tmp()
                op("act", lambda e: e.activation(out=a_[:, 0:N], in_=rg[:, 0:N], func=AF.Exp, scale=SCL[:, l, 0, c:c + 1]), R=(rgn, "SCL"), W=(an,))
                op("act", lambda e: e.activation(out=m_[:, 0:N], in_=rg[:, 0:N], func=AF.Exp, scale=SCL[:, l, 1, c:c + 1]), R=(rgn, "SCL"), W=(mn,))
                op("dve", lambda e: e.tensor_scalar(out=m_[:, 0:N], in0=m_[:, 0:N], scalar1=-1.0, scalar2=1.0, op0=ALU.mult, op1=ALU.add), R=(mn,), W=(mn,))
                op("act", lambda e: e.activation(out=m_[:, 0:N], in_=m_[:, 0:N], func=AF.Sqrt), R=(mn,), W=(mn,))
                if ffi is not None:
                    op("dve", lambda e: e.tensor_scalar(out=m_[:, 0:1], in0=m_[:, 0:1], scalar1=PV[:, o_fl + 7 + ffi:o_fl + 8 + ffi], scalar2=PV[:, o_fl + 2 + ffi:o_fl + 3 + ffi], op0=ALU.mult, op1=ALU.add), R=(mn, "PV"), W=(mn,))
                op("dve", lambda e: e.tensor_tensor(out=ig[:, 0:N], in0=ig[:, 0:N], in1=xc[:, 0:N], op=ALU.mult), R=(ign, xcn), W=(ign,))
                op("dve", lambda e: e.tensor_tensor(out=ig[:, 0:N], in0=ig[:, 0:N], in1=m_[:, 0:N], op=ALU.mult), R=(ign, mn), W=(ign,))
                hs, hsn = rg, rgn
                if kind == "p":
                    op("dve", lambda e: e.tensor_tensor_scan(out=hs[:, 0:T], data0=a_[:, 0:T], data1=ig[:, 0:T], initial=HC[:, l, c:c + 1], op0=ALU.mult, op1=ALU.add), R=(an, ign, "HC", rgn), W=(hsn,))
                    op("dve", lambda e: e.tensor_copy(out=HC[:, l, c:c + 1], in_=hs[:, T - 1:T]), R=(hsn,), W=("HC",))
                else:
                    op("dve", lambda e: e.tensor_tensor(out=hs[:, 0:N], in0=a_[:, 0:N], in1=H0S[:, c, :], op=ALU.mult), R=(an, "H0S", rgn), W=(hsn,))
                    op("dve", lambda e: e.tensor_tensor(out=hs[:, 0:N], in0=hs[:, 0:N], in1=ig[:, 0:N], op=ALU.add), R=(hsn, ign), W=(hsn,))
                    transpose_out(hs[:, 0:NS], 128, NS, s_h[l][:, c * 128:(c + 1) * 128], (hsn,), f"shB{l}")
                flush_deferred()
                g2, g2n = next_tmp()
                op("dve", lambda e: e.tensor_tensor(out=g2[:, 0:N], in0=ug[:, 0:N], in1=ug[:, 0:N], op=ALU.mult), R=(ugn,), W=(g2n,))
                op("dve", lambda e: e.tensor_scalar(out=g2[:, 0:N], in0=g2[:, 0:N], scalar1=0.044715, scalar2=1.0, op0=ALU.mult, op1=ALU.add), R=(g2n,), W=(g2n,))
                op("dve", lambda e: e.tensor_tensor(out=g2[:, 0:N], in0=g2[:, 0:N], in1=ug[:, 0:N], op=ALU.mult), R=(g2n, ugn), W=(g2n,))
                op("act", lambda e: e.activation(out=g2[:, 0:N], in_=g2[:, 0:N], func=AF.Sigmoid, scale=1.5957691216057308), R=(g2n,), W=(g2n,))
                op("dve", lambda e: e.tensor_tensor(out=g2[:, 0:N], in0=g2[:, 0:N], in1=ug[:, 0:N], op=ALU.mult), R=(g2n, ugn), W=(g2n,))
                op("dve", lambda e: e.tensor_tensor(out=MX[:, 8 + c, 0:N], in0=g2[:, 0:N], in1=hs[:, 0:N], op=ALU.mult), R=(g2n, hsn), W=(f"MX{8 + c}",))
                deferred.append(lambda: sq_accum(pbl, pbln, 8 + c, c == 0, c == 3, N))

            def sc_chunk(c):
                slot, wtok = wload([
                    (lambda s: wv(s, KC, 384)[:, :, 0:128], WIN[l][:, 3072 + c * 128:3072 + (c + 1) * 128].rearrange("(k p) m -> p k m", p=128)),
                    (lambda s: wv(s, KC, 384)[:, :, 128:256], WIN[l][:, 3584 + c * 128:3584 + (c + 1) * 128].rearrange("(k p) m -> p k m", p=128)),
                    (lambda s: wv(s, KC, 384)[:, :, 256:384], WIN[l][:, 2560 + c * 128:2560 + (c + 1) * 128].rearrange("(k p) m -> p k m", p=128)),
                ], "sc")
                sv = wv(slot, KC, 384)
                pa, pan = gemm_chunk(lambda k: sv[:, k, 0:128], wtok, KC, xr, xt, N)
                uc, ucn = next_tmp()
                op("act", lambda e: e.activation(out=uc[:, 0:N], in_=pa[:, 0:N], func=AF.Copy), R=(pan,), W=(ucn,))
                flush_deferred()
                pa, pan = gemm_chunk(lambda k: sv[:, k, 128:256], wtok, KC, xr, xt, N)
                gc, gcn = next_tmp()
                if kind == "p":
                    op("dve", lambda e: e.tensor_copy(out=gc[:, 0:2], in_=CSC[:, l, c, :]), R=("CSC",), W=(gcn,))
                    op("dve", lambda e: e.tensor_tensor(out=gc[:, 2:2 + T], in0=pa[:, 0:T], in1=uc[:, 0:T], op=ALU.mult), R=(pan, ucn), W=(gcn,))
                    op("dve", lambda e: e.tensor_copy(out=CSC[:, l, c, :], in_=gc[:, T:T + 2]), R=(gcn,), W=("CSC",))
                    tap = lambda k: gc[:, k:k + T]
                else:
                    op("dve", lambda e: e.tensor_tensor(out=GCS[:, c, :, 2], in0=pa[:, 0:NS], in1=uc[:, 0:NS], op=ALU.mult), R=(pan, ucn), W=("GCS",))
                    gcn = "GCS"
                    tap = lambda k: GCS[:, c, :, k]
                    transpose_out(GCS[:, c, :, 2], 128, NS, s_sc[l][:, 1, c * 128:(c + 1) * 128], ("GCS",), f"sscB{l}")
                y_, yn = next_tmp()
                o_w, _ = lay[("scw", l)]
                op("dve", lambda e: e.tensor_scalar(out=y_[:, 0:N], in0=tap(2), scalar1=PV[:, o_w + 8 + c:o_w + 9 + c], scalar2=None, op0=ALU.mult), R=(gcn, "PV"), W=(yn,))
                for k in (1, 0):
                    op("dve", lambda e, k=k: e.scalar_tensor_tensor(out=y_[:, 0:N], in0=tap(k), scalar=PV[:, o_w + k * 4 + c:o_w + k * 4 + c + 1], in1=y_[:, 0:N], op0=ALU.mult, op1=ALU.add), R=(gcn, "PV", yn), W=(yn,))
                pa, pan = gemm_chunk(lambda k: sv[:, k, 256:384], wtok, KC, xr, xt, N)
                op("dve", lambda e: e.tensor_tensor(out=MX[:, 12 + c, 0:N], in0=pa[:, 0:N], in1=y_[:, 0:N], op=ALU.mult), R=(pan, yn), W=(f"MX{12 + c}",))
                deferred.append(lambda: sq_accum(pbs, pbsn, 12 + c, c == 0, c == 3, N))

            def kv_blk():
                pieces = [(lambda s: wv(s, KC, 512)[:, :, 256:512], WIN[l][:, 1280:1536].rearrange("(k p) m -> p k m", p=128))]
                for h_ in range(2):
                    for g_ in range(4):
                        c_src = 1024 + g_ * 64 + h_ * 32
                        c_dst = h_ * 128 + g_ * 32
                        pieces.append((lambda s, c_dst=c_dst: wv(s, KC, 512)[:, :, c_dst:c_dst + 32], WIN[l][:, c_src:c_src + 32].rearrange("(k p) m -> p k m", p=128)))
                slot, wtok = wload(pieces, "kv")
                sv = wv(slot, KC, 512)
                if kind == "p":
                    for pl in range(2):
                        op("dve", lambda e, pl=pl: e.tensor_copy(out=KH[:, 0, pl, 0:128], in_=KH[:, 0, pl, T:T + 128]), R=("KH",), W=("KH",))
                    op("dve", lambda e: e.tensor_copy(out=VH[:, 0, 0, :], in_=VH[:, 0, 4, :]), R=("VH",), W=("VH",))
                kf = []
                for pl in range(2):
                    pa, pan = gemm_chunk(lambda k: sv[:, k, pl * 128:(pl + 1) * 128], wtok, KC, xr, xt, N)
                    t_, tn = next_tmp()
                    op("act", lambda e: e.activation(out=t_[:, 0:N], in_=pa[:, 0:N], func=AF.Copy), R=(pan,), W=(tn,))
                    kf.append((t_, tn))
                rope(kf, [(KF[0], "KF0"), (KF[1], "KF1")], N)
                if kind == "p":
                    for pl in range(2):
                        op("act", lambda e, pl=pl: e.activation(out=KH[:, 0, pl, 128:128 + T], in_=KF[pl][:, 0:T], func=AF.Copy), R=(f"KF{pl}",), W=("KH",))
                    if last_pass:
                        for pl in range(2):
                            transpose_out(KF[pl][:, T - 128:T], 128, 128, pout["k"].rearrange("t (g h i) -> t h g i", h=2, i=32)[:, pl, :, :], (f"KF{pl}",), "pk", view=lambda a: a.rearrange("t (g i) -> t g i", i=32))
                else:
                    for pl in range(2):
                        op("act", lambda e, pl=pl: e.activation(out=QK[:, 8 + pl, 0:NS], in_=KF[pl][:, 0:NS], func=AF.Copy), R=(f"KF{pl}",), W=(f"QK{8 + pl}",))
                        transpose_out(KF[pl][:, 0:NS], 128, NS, s_k[l].rearrange("n s (g h i) -> n s h g i", h=2, i=32)[:, 127, pl, :, :], (f"KF{pl}",), f"skB{l}", view=lambda a: a.rearrange("t (g i) -> t g i", i=32))
                ntb = 4 if kind == "p" else 1
                for tb in range(ntb):
                    M = 128 if kind == "p" else NS
                    pa, pan = next_pa()
                    for k in range(KC):
                        op("pe", lambda e: e.matmul(pa[0:M, 0:256], lhsT=XN[:, k, tb * 128:tb * 128 + M], rhs=sv[:, k, 256:512], start=(k == 0), stop=(k == KC - 1)),
                           R=(wtok, f"XN{k}"), W=(pan,), inc=(k == KC - 1))
                    if kind == "p":
                        op("act", lambda e: e.activation(out=VH[:, 0, 1 + tb, :], in_=pa[:, 0:256], func=AF.Copy), R=(pan,), W=("VH",))
                        if last_pass and tb == 3:
                            op("act", lambda e: e.activation(out=ROW[:, 0:256], in_=pa[:, 0:256], func=AF.Copy), R=(pan,), W=("ROW",))
                            trk.dma("sp", pout["v"], ROW[:, 0:256], R=("ROW",), W=("pv",), sem="oROW")
                    else:
                        op("act", lambda e: e.activation(out=ROW[0:NS, 0:256], in_=pa[0:NS, 0:256], func=AF.Copy), R=(pan,), W=("ROW",))
                        trk.dma("sp", s_v[l][:, 127, :], ROW[0:NS, 0:256], R=("ROW",), W=(f"svB{l}",), sem="oROW")


            def q_blk(jb):
                pieces = []
                for jj_ in range(2):
                    for h_ in range(2):
                        for g_ in range(4):
                            c_src = (g_ * 4 + jb * 2 + jj_) * 64 + h_ * 32
                            c_dst = (jj_ * 2 + h_) * 128 + g_ * 32
                            pieces.append((lambda s, c_dst=c_dst: wv(s, KC, 512)[:, :, c_dst:c_dst + 32], WIN[l][:, c_src:c_src + 32].rearrange("(k p) m -> p k m", p=128)))
                slot, wtok = wload(pieces, "q")
                sv = wv(slot, KC, 512)
                for jj in range(2):
                    j = jb * 2 + jj
                    qf = []
                    for pl in range(2):
                        pa, pan = gemm_chunk(lambda k: sv[:, k, (jj * 2 + pl) * 128:(jj * 2 + pl + 1) * 128], wtok, KC, xr, xt, N)
                        t_, tn = next_tmp()
                        op("act", lambda e: e.activation(out=t_[:, 0:N], in_=pa[:, 0:N], func=AF.Copy), R=(pan,), W=(tn,))
                        qf.append((t_, tn))
                    rope(qf, [(QK[:, j * 2, :], f"QK{j * 2}"), (QK[:, j * 2 + 1, :], f"QK{j * 2 + 1}")], N)


            for c in range(4):
                lru_chunk(c)
                sc_chunk(c)
                if c == 0:
                    q_blk(0)
                elif c == 1:
                    q_blk(1)
                elif c == 2:
                    kv_blk()
            flush_deferred()
            group_norm_finish(pbl, pbln, l, 8, 4, N, 1)
            group_norm_finish(pbs, pbsn, l, 12, 4, N, 0)
            if kind == "p" and last_pass:
                transpose_out(HC[:, l, :], 128, 4, pout["h"].rearrange("(c f) -> c f", f=128), ("HC",), "ph")
                transpose_out(CUX[:, l, :, :], 128, 12, None, ("CUX",), "pconv", multi=[(pout["conv"][:, c * 128:(c + 1) * 128], c * 3, c * 3 + 3) for c in range(4)])
                transpose_out(CSC[:, l, :, :], 128, 8, None, ("CSC",), "psc", multi=[(pout["sc"][:, c * 128:(c + 1) * 128], c * 2, c * 2 + 2) for c in range(4)])

            if (_DBG.get("stop") == "sc" and kind == _DBG.get("stop_kind", "p") and l == _DBG.get("stop_layer", 0)):
                raise _Stop()
            pba, pban = PB[1], "PB1"
            if kind == "p":
                attention_prompt(l, ffi)
            else:
                attention_sample(l)
            for c in range(8):
                sq_accum(pba, pban, c, c == 0, c == 7, N)
            group_norm_finish(pba, pban, l, 0, 8, N, 1)

            if (_DBG.get("stop") == "attn" and kind == _DBG.get("stop_kind", "p") and l == _DBG.get("stop_layer", 0)):
                raise _Stop()
            for cb in range(4):
                cs_ = slice(cb * 512, (cb + 1) * 512)
                pieces = []
                for half in range(2):
                    for kc2 in range(2):
                        r0 = kc2 * 512 + half * 256
                        pieces.append((lambda s, half=half, kc2=kc2: wv(s, KC, 512)[half * 64:(half + 1) * 64, kc2 * 4:(kc2 + 1) * 4, :],
                                       WOUT[l][r0:r0 + 256, cs_].rearrange("(j d) m -> d j m", d=64)))
                pieces.append((lambda s: wv(s, KC, 512)[:, 8:16, :], WOUT[l][1024:2048, cs_].rearrange("(k p) m -> p k m", p=128)))
                slot, wtok = wload(pieces, "out")
                sv = wv(slot, KC, 512)
                for mm in range(4):
                    m = cb * 4 + mm
                    ko = lambda k: (k + 8) % 16
                    pa, pan = gemm_chunk(lambda k: sv[:, ko(k), mm * 128:(mm + 1) * 128], wtok, KC, lambda k: MX[:, ko(k), 0:N], lambda k: f"MX{ko(k)}", N)
                    op("dve", lambda e: e.tensor_tensor(out=X[:, m, 0:N], in0=pa[:, 0:N], in1=X[:, m, 0:N], op=ALU.add), R=(pan, f"X{m}"), W=(f"X{m}",))

            if (_DBG.get("stop") == "out" and kind == _DBG.get("stop_kind", "p") and l == _DBG.get("stop_layer", 0)):
                raise _Stop()
            rmsnorm_to_xn(l, "nffn", N)
            trk.fence("dve", ["pe", "act"])
            for fb in range(22):
                slot, wtok = wload([
                    (lambda s: wv(s, KC, 512)[:, :, 0:256], WGU[l][:, fb * 256:(fb + 1) * 256].rearrange("(k p) m -> p k m", p=128)),
                    (lambda s: wv(s, KC, 512)[:, :, 256:512], WGU[l][:, DFF + fb * 256:DFF + (fb + 1) * 256].rearrange("(k p) m -> p k m", p=128)),
                ], "gu")
                sv = wv(slot, KC, 512)
                for cc in range(2):
                    fc = fb * 2 + cc
                    pa, pan = gemm_chunk(lambda k: sv[:, k, cc * 128:(cc + 1) * 128], wtok, KC, xr, xt, N)
                    sg, sgn = next_tmp()
                    op("act", lambda e: e.activation(out=sg[:, 0:N], in_=pa[:, 0:N], func=AF.Silu), R=(pan,), W=(sgn,))
                    pa, pan = gemm_chunk(lambda k: sv[:, k, 256 + cc * 128:256 + (cc + 1) * 128], wtok, KC, xr, xt, N)
                    op("dve", lambda e: e.tensor_tensor(out=H[:, fc, 0:N], in0=pa[:, 0:N], in1=sg[:, 0:N], op=ALU.mult), R=(pan, sgn), W=(f"H{fc}",))
            dn_issued = {}

            def dn_issue(cg_, kh_):
                if cg_ < 8 and (cg_, kh_) not in dn_issued:
                    dn_issued[(cg_, kh_)] = wload([(lambda s: wv(s, 22, 256), WDN[l][kh_ * 2816:(kh_ + 1) * 2816, cg_ * 256:(cg_ + 1) * 256].rearrange("(k p) m -> p k m", p=128))], "dn")
            for cg in range(8):
                accs = None
                for kh in range(2):
                    dn_issue(cg, kh)
                    slot, wtok = dn_issued[(cg, kh)]
                    sv = wv(slot, 22, 256)
                    if kh == 0:
                        accs = [next_pa(), next_pa()]
                    for mm in range(2):
                        pa, pan = accs[mm]
                        for k in range(22):
                            kk = kh * 22 + k
                            op("pe", lambda e: e.matmul(pa[:, 0:N], lhsT=sv[:, k, mm * 128:(mm + 1) * 128], rhs=H[:, kk, 0:N], start=(kk == 0), stop=(kk == FC - 1)),
                               R=(wtok, f"H{kk}"), W=(pan,), inc=(k == 21))
                for mm in range(2):
                    m = cg * 2 + mm
                    pa, pan = accs[mm]
                    op("dve", lambda e: e.tensor_tensor(out=X[:, m, 0:N], in0=pa[:, 0:N], in1=X[:, m, 0:N], op=ALU.add), R=(pan, f"X{m}"), W=(f"X{m}",))
                if xfinal_cb is not None and cg % 2 == 1:
                    dn_issue(cg + 1, 0); dn_issue(cg + 1, 1)
                    xfinal_cb(cg // 2)

        def rope(src, dst, N):
            (a_, an), (b_, bn) = src
            (da, dan), (db, dbn) = dst
            t1, t1n = next_tmp(); t2, t2n = next_tmp()
            cos = CS[:, 0, 0:N]; sin = CS[:, 1, 0:N]
            op("dve", lambda e: e.tensor_tensor(out=t1[:, 0:N], in0=a_[:, 0:N], in1=cos, op=ALU.mult), R=(an, "CS"), W=(t1n,))
            op("dve", lambda e: e.tensor_tensor(out=t2[:, 0:N], in0=b_[:, 0:N], in1=sin, op=ALU.mult), R=(bn, "CS"), W=(t2n,))
            op("dve", lambda e: e.tensor_tensor(out=da[:, 0:N], in0=t1[:, 0:N], in1=t2[:, 0:N], op=ALU.subtract), R=(t1n, t2n), W=(dan,))
            op("dve", lambda e: e.tensor_tensor(out=t1[:, 0:N], in0=b_[:, 0:N], in1=cos, op=ALU.mult), R=(bn, "CS"), W=(t1n,))
            op("dve", lambda e: e.tensor_tensor(out=t2[:, 0:N], in0=a_[:, 0:N], in1=sin, op=ALU.mult), R=(an, "CS"), W=(t2n,))
            op("dve", lambda e: e.tensor_tensor(out=db[:, 0:N], in0=t1[:, 0:N], in1=t2[:, 0:N], op=ALU.add), R=(t1n, t2n), W=(dbn,))

        pt_i = {"i": 0}

        def attention_prompt(l, ffi):
            QKv = QK[:, 0:8, :].rearrange("p (j pl) t -> p pl j t", pl=2)
            num, numn = PB[2], "PB2"
            den, denn = PB[3], "PB3"
            units = [(qb, kc, half) for qb in range(4) for kc in range(2) for half in range(2)]

            def qk_phase(u):
                qb, kc, half = u
                g = kc * 2 + half
                blocks = [(qb, 1), (qb + 1, 0)]
                pts = []
                for bi, (kb, mk) in enumerate(blocks):
                    sps = PB[bi]; spsn = f"PB{bi}"
                    for pl in range(2):
                        op("pe", lambda e: e.matmul(sps[:, :].rearrange("p (j q) -> p j q", q=128), lhsT=KH[32 * g:32 * g + 32, 0, pl, kb * 128:(kb + 1) * 128],
                                                    rhs=QKv[32 * g:32 * g + 32, pl, :, qb * 128:(qb + 1) * 128], start=(pl == 0), stop=(pl == 1), tile_position=(32 * g, 0)),
                           R=("KH",) + tuple(f"QK{j * 2 + pl}" for j in range(4)), W=(spsn,), inc=(pl == 1))
                    i = pt_i["i"] % 4; pt_i["i"] += 1
                    pt, ptn = PT[i], f"PT{i}"
                    op("act", lambda e: e.activation(out=pt[:, :], in_=sps[:, :], func=AF.Exp, scale=0.125), R=(spsn,), W=(ptn,))
                    mk_ap = MSK0[:, :, :] if (mk == 1 and qb == 0 and ffi is not None) else MSK[:, mk, :, :]
                    op("dve", lambda e: e.tensor_tensor(out=pt[:, :], in0=pt[:, :], in1=mk_ap.rearrange("p j q -> p (j q)"), op=ALU.mult), R=(ptn, "MSK", "MSK0"), W=(ptn,))
                    pts.append((pt, ptn, kb))
                return pts

            def pv_phase(u, pts):
                qb, kc, half = u
                g = kc * 2 + half
                for bi, (pt, ptn, kb) in enumerate(pts):
                    st = (bi == 0); sp_ = (bi == len(pts) - 1)
                    op("pe", lambda e: e.matmul(num[half * 64:(half + 1) * 64, :], lhsT=VH[:, 0, kb, g * 64:(g + 1) * 64], rhs=pt[:, :], start=st, stop=sp_),
                       R=(ptn, "VH"), W=(numn,), inc=False)
                    op("pe", lambda e: e.matmul(den[half * 64:(half + 1) * 64, :], lhsT=ONES[:, 0:64], rhs=pt[:, :], start=st, stop=sp_),
                       R=(ptn, "ONES"), W=(denn,), inc=True)
                if half == 1:
                    dn, dnn = next_tmp()
                    op("dve", lambda e: e.tensor_tensor(out=dn[:, 0:512], in0=den[:, :], in1=SNK[:, kc, :, :].rearrange("p j q -> p (j q)"), op=ALU.add), R=(denn, "SNK"), W=(dnn,))
                    op("dve", lambda e: e.reciprocal(out=dn[:, 0:512], in_=dn[:, 0:512]), R=(dnn,), W=(dnn,))
                    op("dve", lambda e: e.tensor_tensor(out=MX[:, kc * 4:kc * 4 + 4, qb * 128:(qb + 1) * 128], in0=num[:, :].rearrange("p (j q) -> p j q", q=128),
                                                        in1=dn[:, 0:512].rearrange("p (j q) -> p j q", q=128), op=ALU.mult),
                       R=(numn, dnn), W=tuple(f"MX{kc * 4 + j}" for j in range(4)))

            prev = None
            for u in units:
                pts = qk_phase(u)
                if prev is not None:
                    pv_phase(*prev)
                prev = (u, pts)
            pv_phase(*prev)

        def attention_sample(l):
            QKv = QK[:, 0:8, :].rearrange("p (j pl) t -> p pl j t", pl=2)
            first = True
            trk.reserve("ks", 8)
            for h_ in range(2):
                for g_ in range(4):
                    trk.dma("pool", KS[:, :, h_ * 128 + g_ * 32:h_ * 128 + g_ * 32 + 32], s_k[l][:, :, g_ * 64 + h_ * 32:g_ * 64 + h_ * 32 + 32].rearrange("n s i -> s n i"),
                            R=(f"skA{l}", f"skB{l}"), W=("KS",), sem="ks", skip_deps=not first)
                    first = False
            trk.dma("pool", VS2[:], s_v[l].rearrange("n s f -> s n f"), R=(f"svA{l}", f"svB{l}"), W=("VS2",), sem="vs")
            for n in range(NS):
                kt, ktn = KTS[n % 2], f"KTS{n % 2}"
                ptb = PB[3][:, 0:128].bitcast(BF16).rearrange("p (pl s) -> p pl s", pl=2)
                for pl in range(2):
                    op("pe", lambda e: e.transpose(ptb[:, pl, :], KS[:, n, pl * 128:(pl + 1) * 128], IDNB[:, :]), R=("KS", "IDNB"), W=("PB3",))
                op("act", lambda e: e.activation(out=kt[:, :, :], in_=ptb, func=AF.Copy), R=("PB3",), W=(ktn,))
                for g in range(4):
                    for pl in range(2):
                        op("pe", lambda e: e.matmul(PA[g][:, n * 4:n * 4 + 4], lhsT=kt[32 * g:32 * g + 32, pl, :], rhs=QKv[32 * g:32 * g + 32, pl, :, n],
                                                    start=(pl == 0), stop=(pl == 1), tile_position=(32 * g, 0)),
                           R=(ktn,) + tuple(f"QK{j * 2 + pl}" for j in range(4)), W=(f"PA{g}",), inc=(pl == 1))
            pssv = PSS[:, :].rearrange("p (n g j) -> p n g j", g=4, j=4)
            for g in range(4):
                op("act", lambda e: e.activation(out=pssv[:, :, g, :], in_=PA[g][:, 0:64].rearrange("p (n j) -> p n j", j=4), func=AF.Exp, scale=0.125), R=(f"PA{g}",), W=("PSS",))
            den, denn = PB[1], "PB1"
            op("pe", lambda e: e.matmul(den[:, 0:256], lhsT=ONES[:, :], rhs=PSS[:, :], start=True, stop=True), R=("PSS", "ONES"), W=(denn,))
            op("dve", lambda e: e.tensor_tensor(out=RD[:, :], in0=den[:, 0:256], in1=SNKS[:, l, :, :].rearrange("p n h -> p (n h)"), op=ALU.add), R=(denn, "SNKS"), W=("RD",))
            op("dve", lambda e: e.reciprocal(out=RD[:, :], in_=RD[:, :]), R=("RD",), W=("RD",))
            num, numn = PB[2], "PB2"
            numv = num[:, 0:128].rearrange("p (kc j n) -> p kc j n", kc=2, j=4)
            for n in range(NS):
                for g in range(4):
                    hf = g % 2
                    op("pe", lambda e: e.matmul(numv[hf * 64:(hf + 1) * 64, g // 2, :, n], lhsT=VS2[:, n, g * 64:(g + 1) * 64], rhs=PSS[:, n * 16 + g * 4:n * 16 + g * 4 + 4], start=True, stop=True),
                       R=("VS2", "PSS"), W=(numn,), inc=(n == NS - 1 and g == 3))
            rdv = RD[:, :].rearrange("p (n g j) -> p g j n", g=4, j=4)
            for hf in range(2):
                op("dve", lambda e: e.tensor_tensor(out=MX[hf * 64:(hf + 1) * 64, 0:8, 0:NS].rearrange("p (kc j) n -> p kc j n", kc=2),
                                                    in0=numv[hf * 64:(hf + 1) * 64, :, :, :],
                                                    in1=rdv[hf * 64:(hf + 1) * 64, hf::2, :, :], op=ALU.mult),
                   R=(numn, "RD"), W=tuple(f"MX{c}" for c in range(8)))

        def run_pass(pi, kind):
            N = T if kind == "p" else NS
            ci = pi if kind == "p" else NSLOT
            o_fl, _ = lay[("flags", 0)]
            trk.dma("sp", CS[:], csd[ci], W=("CS",), sem="cs")
            ntb = 4 if kind == "p" else 1
            for tb in range(ntb):
                rows = 128 if kind == "p" else NS
                src = xp[pi * T + tb * 128: pi * T + tb * 128 + 128, :] if kind == "p" else xs
                trk.dma("sp", STG[0:rows, 0:D], src, W=("STG",), sem="xin")
                for c4 in range(4):
                    pb = PB[c4 % 2]; pbn = f"PB{c4 % 2}"
                    for cc in range(4):
                        c = c4 * 4 + cc
                        op("pe", lambda e: e.transpose(pb[:, cc * 128:cc * 128 + rows], STG[0:rows, c * 128:(c + 1) * 128], IDN[0:rows, 0:rows]), R=("STG", "IDN"), W=(pbn,), inc=(cc == 3))
                    op("act", lambda e: e.activation(out=X[:, c4 * 4:c4 * 4 + 4, tb * 128:tb * 128 + rows], in_=pb[:, :].rearrange("p (c t) -> p c t", t=128)[:, :, 0:rows], func=AF.Copy),
                       R=(pbn,), W=tuple(f"X{c4 * 4 + cc}" for cc in range(4)))
            if kind == "p" and pi >= 1:
                for j in range(4):
                    trk.dma("sp", STG[:, 0:2048], obs[j][0:128, :], R=(f"ob{j}",), W=("STG",), sem="xin")
                    for cc in range(4):
                        c = j * 4 + cc
                        op("dve", lambda e: e.scalar_tensor_tensor(out=X[:, c, :], in0=STG[:, cc * 512:(cc + 1) * 512], scalar=PV[:, o_fl + 1:o_fl + 2], in1=X[:, c, :], op0=ALU.mult, op1=ALU.add),
                           R=("STG", "PV", f"X{c}"), W=(f"X{c}",))
            if (_DBG.get("stop") == "load" and kind == _DBG.get("stop_kind", "p")):
                raise _Stop()
            if kind == "p":
                ffi = pi if pi <= 1 else None
                if ffi is not None:
                    for j in range(4):
                        op("dve", lambda e: e.tensor_scalar(out=MSK0[:, j, :], in0=MSK[:, 1, j, :], scalar1=PV[:, o_fl + 7 + ffi:o_fl + 8 + ffi], scalar2=None, op0=ALU.mult), R=("MSK", "PV"), W=("MSK0",))
                pout = None
                if pi >= NPP - 1:
                    w_ = pi - (NPP - 1)
                    pout = {"h": p_h[w_], "conv": p_conv[w_], "k": p_k[w_], "v": p_v[w_], "sc": p_sc[w_]}
                def exchange(j):
                    trk.dma("sp", ibs[j][:, :], X[:, j * 4:(j + 1) * 4, :].rearrange("p c t -> p (c t)"), R=tuple(f"X{j * 4 + cc}" for cc in range(4)), W=(f"ib{j}",), sem=f"xex{j}")
                    trk.coll(lambda e: e.collective_compute("AllGather", ALU.bypass, replica_groups=[[2 * i_, 2 * i_ + 1] for i_ in range(_DBG.get('ncores', NCORES) // 2)],
                                                            ins=[ibs[j].ap().opt()], outs=[obs[j].ap().opt()]), R=(f"ib{j}",), W=(f"ob{j}",))
                layer(2, pi, kind, pout, ffi, xfinal_cb=(exchange if pi < NSLOT - 1 else None))
                if pi == 0:
                    fa = PV[:, o_fl:o_fl + 1]
                    for t_, nm in ((HC[:, 2, :], "HC"), (CUX[:, 2, :, :], "CUX"), (CSC[:, 2, :, :], "CSC"), (KH[:, 0, :, :], "KH"), (VH[:, 0, :, :], "VH")):
                        op("dve", lambda e, t_=t_: e.tensor_scalar(out=t_, in0=t_, scalar1=fa, scalar2=None, op0=ALU.mult), R=(nm, "PV"), W=(nm,))
            else:
                for l in range(2):
                    layer(l, pi, kind, None, None)
            if (_DBG.get("stop") == "final" and kind == _DBG.get("stop_kind", "p")):
                raise _Stop()
            pb, pbn = PB[0], "PB0"
            for c in range(KC):
                sq, sqn = next_sq()
                op("act", lambda e: e.activation(out=sq[:, 0:N], in_=X[:, c, 0:N], func=AF.Square), R=(f"X{c}",), W=(sqn,))
                op("pe", lambda e: e.matmul(pb[:, 0:N], lhsT=ONES[:, :], rhs=sq[:, 0:N], start=(c == 0), stop=(c == KC - 1)), R=(sqn, "ONES"), W=(pbn,), inc=True)
            rs = RS[0]
            op("dve", lambda e: e.tensor_scalar(out=rs[:, 0:N], in0=pb[:, 0:N], scalar1=1.0 / D, scalar2=EPS, op0=ALU.mult, op1=ALU.add), R=(pbn,), W=("RS0",))
            op("act", lambda e: e.activation(out=rs[:, 0:N], in_=rs[:, 0:N], func=AF.Sqrt), R=("RS0",), W=("RS0",))
            op("dve", lambda e: e.reciprocal(out=rs[:, 0:N], in_=rs[:, 0:N]), R=("RS0",), W=("RS0",))
            o_f, _ = lay[("nfin", 0)]
            for c in range(KC):
                op("dve", lambda e: e.scalar_tensor_tensor(out=X[:, c, 0:N], in0=X[:, c, 0:N], scalar=PV[:, o_f + c:o_f + c + 1], in1=rs[:, 0:N], op0=ALU.mult, op1=ALU.mult),
                   R=(f"X{c}", "RS0", "PV"), W=(f"X{c}",))
            for tb in range(ntb):
                rows = 128 if kind == "p" else NS
                for c4 in range(4):
                    pb = PB[c4 % 2]; pbn = f"PB{c4 % 2}"
                    for cc in range(4):
                        c = c4 * 4 + cc
                        op("pe", lambda e: e.transpose(pb[0:rows, cc * 128:(cc + 1) * 128], X[:, c, tb * 128:tb * 128 + rows], IDN[:, :]), R=(f"X{c}", "IDN"), W=(pbn,), inc=(cc == 3))
                    op("act", lambda e: e.activation(out=STG[0:rows, c4 * 512:(c4 + 1) * 512], in_=pb[0:rows, :], func=AF.Copy), R=(pbn,), W=("STG",))
                dst = yp[pi * T + tb * 128: pi * T + tb * 128 + 128, :] if kind == "p" else ys
                trk.dma("sp", dst, STG[0:rows, 0:D], R=("STG",), W=("yout",), sem="oSTG")

        try:
            for pi in range(NSLOT):
                run_pass(pi, "p")
            if not _DBG.get("nosample"):
                run_pass(0, "s")
        except _Stop:
            pass
        trk.wait_all_dma("sp")
    return nc


_CACHE = {}
_DBG = {}


def kernel(x_prompt, x_sample, state_lru_h, state_lru_conv, cache_swa_k, cache_swa_v, state_sconv,
           norm_mix, w_in, norm_grp, w_out, lru_conv_w, lru_conv_b, lru_w_a, lru_b_a, lru_w_i, lru_b_i,
           lru_lambda, sc_conv_w, attn_sinks, norm_ffn, ffn_w_gu, ffn_w_down, norm_final):
    f = lambda a: np.ascontiguousarray(np.asarray(a, dtype=np.float32))
    x_prompt = f(x_prompt); x_sample = f(x_sample)
    NPP = _DBG.get('npp', SEQ // T)
    lay, NV = pv_layout()
    pv = np.zeros((128, NV), np.float32)

    def put(name, l, arr2d):
        o, n = lay[(name, l)]
        assert arr2d.shape == (128, n), (name, arr2d.shape)
        pv[:, o:o + n] = arr2d
    fm = lambda v: f(v).reshape(-1, 128).T
    def put_layer(l, ls):
        put("nmix", l, fm(norm_mix[ls])); put("nffn", l, fm(norm_ffn[ls]))
        g = f(norm_grp[ls])
        ga = g[:1024].reshape(2, 2, 4, 64)
        ga = ga.transpose(1, 3, 0, 2).reshape(128, 8)
        put("ngrp", l, np.concatenate([ga, fm(g[1024:1536]), fm(g[1536:2048])], axis=1))
        cw = f(lru_conv_w[ls])
        put("lcw", l, np.concatenate([fm(cw[k]) for k in range(4)], axis=1))
        put("lcb", l, fm(lru_conv_b[ls])); put("ba", l, fm(lru_b_a[ls])); put("bi", l, fm(lru_b_i[ls])); put("lam", l, fm(lru_lambda[ls]))
        sw = f(sc_conv_w[ls])
        put("scw", l, np.concatenate([fm(sw[k]) for k in range(3)], axis=1))
        sk = f(attn_sinks[ls]).reshape(2, 2, 4)
        sk2 = np.repeat(sk.transpose(1, 0, 2).reshape(2, 1, 8), 64, axis=1).reshape(128, 8)
        put("sink", l, sk2)
        put("sinkall", l, np.repeat(f(attn_sinks[ls])[None, :], 128, axis=0))
    put_layer(0, 0); put_layer(1, 1)
    put("nfin", 0, fm(norm_final))
    bdh = np.zeros((3, 128, 8, 128), np.float32)
    for l in range(2):
        for gi, wsrc in enumerate((lru_w_a, lru_w_i)):
            wl = f(wsrc[l])
            for c in range(4):
                bdh[l, 0:64, gi * 4 + c, 0:64] = wl[2 * c]
                bdh[l, 64:128, gi * 4 + c, 64:128] = wl[2 * c + 1]
    half = 32
    inv = (10000.0 ** (-np.arange(half, dtype=np.float32) / half)).astype(np.float32)
    NSLOT = NPP + 1

    def cs_tables(lc):
        cs = np.zeros((NSLOT + 1, 128, 2, T), np.float32)
        for si in range(NSLOT + 1):
            if si < NSLOT:
                pss = min(max(si - lc, 0), NPP - 1)
                pos = (np.arange(T) + pss * T).astype(np.float32)
            else:
                pos = np.full(T, float(PAST), np.float32)
            ang = pos[None, :] * inv[:, None]
            cs[si, :, 0, :] = np.tile(np.cos(ang).astype(np.float32), (4, 1))
            cs[si, :, 1, :] = np.tile(np.sin(ang).astype(np.float32), (4, 1))
        return cs
    cs_by_lc = [cs_tables(0), cs_tables(1)]
    sidx = np.arange(128)[:, None]; qidx = np.arange(128)[None, :]
    msk = np.stack([(sidx <= qidx), (sidx > qidx)], axis=1).astype(np.float32)
    idn = np.eye(128, dtype=np.float32)
    w_in = f(w_in); w_out = f(w_out); ffn_w_gu = f(ffn_w_gu); ffn_w_down = f(ffn_w_down)
    sh = f(state_lru_h); scv = f(state_lru_conv); ssc = f(state_sconv)
    ckk = f(cache_swa_k).reshape(2, 128, 128, 256); cvv = f(cache_swa_v).reshape(2, 128, 128, 256)
    in_maps = []
    o_fl, _ = lay[("flags", 0)]
    zeros_xp = np.zeros((NSLOT * T, D), np.float32)
    for c in range(NCORES):
        b = c // 2
        lc = c % 2
        ns = slice(c * NS, (c + 1) * NS)
        put_layer(2, lc)
        pvc = pv.copy()
        pvc[:, o_fl + 0] = 1.0 - lc; pvc[:, o_fl + 1] = float(lc)
        for si in range(5):
            ff = 1.0 if si == lc else 0.0
            pvc[:, o_fl + 2 + si] = ff; pvc[:, o_fl + 7 + si] = 1.0 - ff
        bdc = bdh.copy(); bdc[2] = bdh[lc]
        if lc == 0:
            xpc = np.concatenate([x_prompt[b][:NPP * T], np.zeros((T, D), np.float32)], axis=0)
        else:
            xpc = zeros_xp
        in_maps.append({
            "xp": xpc, "xs": np.ascontiguousarray(x_sample[ns, 0, :]),
            "st_h": np.ascontiguousarray(sh[:, ns]), "st_conv": np.ascontiguousarray(scv[:, ns]), "st_sc": np.ascontiguousarray(ssc[:, ns]),
            "ck": np.ascontiguousarray(ckk[:, ns]), "cv": np.ascontiguousarray(cvv[:, ns]),
            "w_in": w_in, "w_out": w_out, "w_gu": ffn_w_gu, "w_dn": ffn_w_down,
            "wo_in": w_in[lc:lc + 1], "wo_out": w_out[lc:lc + 1], "wo_gu": ffn_w_gu[lc:lc + 1], "wo_dn": ffn_w_down[lc:lc + 1],
            "bd": bdc, "pv": pvc, "cs": cs_by_lc[lc], "msk": msk, "idn": idn,
        })
    if NPP not in _CACHE:
        _CACHE[NPP] = build(NPP)
    nc = _CACHE[NPP]
    if 'ncores' in _DBG:
        n_ = _DBG['ncores']
        res = run_bass_kernel_spmd(nc, in_maps[:n_], core_ids=list(range(n_)), trace=_DBG.get('trace', False))
        _DBG['res'] = res
        return res.results
    res = run_bass_kernel_spmd(nc, in_maps, core_ids=list(range(NCORES)))
    R = res.results
    y_prompt = np.stack([R[2 * b + 1]["yp"][T:T + SEQ] for b in range(4)]).reshape(4, SEQ, D)
    y_sample = np.concatenate([R[c]["ys"] for c in range(NCORES)], axis=0).reshape(128, 1, D)
    stk = lambda name, shp: np.stack([np.stack([R[2 * b][name][0], R[2 * b + 1][name][1]]) for b in range(4)], axis=1).reshape(shp)
    p_lru_h = stk("p_h", (2, 4, 512))
    p_lru_conv = stk("p_conv", (2, 4, 3, 512))
    p_swa_k = stk("p_k", (2, 4, 128, 4, 64))
    p_swa_v = stk("p_v", (2, 4, 128, 4, 64))
    p_sconv = stk("p_sc", (2, 4, 2, 512))
    cat = lambda name, shp: np.concatenate([R[c][name] for c in range(NCORES)], axis=1).reshape(shp)
    s_lru_h = cat("s_h", (2, 128, 512))
    s_lru_conv = cat("s_conv", (2, 128, 3, 512))
    s_swa_k = cat("s_k", (2, 128, 128, 4, 64))
    s_swa_v = cat("s_v", (2, 128, 128, 4, 64))
    s_sconv = cat("s_sc", (2, 128, 2, 512))
    outs = (y_prompt, y_sample, p_lru_h, p_lru_conv, p_swa_k, p_swa_v, p_sconv, s_lru_h, s_lru_conv, s_swa_k, s_swa_v, s_sconv)
    return tuple(np.ascontiguousarray(o, dtype=np.float32) for o in outs)
```

```python
import numpy as np
from contextlib import ExitStack
import concourse.bass as bass
import concourse.mybir as mybir
from concourse.bass_utils import run_bass_kernel_spmd

F32 = mybir.dt.float32
BF16 = mybir.dt.bfloat16
AF = mybir.ActivationFunctionType
ALU = mybir.AluOpType

D = 2048
KC = 16
T = 512
NS = 16
DFF = 5632
FC = 44
EPS = 1e-6
PAST = 8192
NCORES = 8
SEQ = 2048


def pv_layout():
    lay = {}
    off = 0
    for l in range(3):
        for name, n in (("nmix", 16), ("nffn", 16), ("ngrp", 16), ("lcw", 16), ("lcb", 4), ("ba", 4), ("bi", 4),
                        ("lam", 4), ("scw", 12), ("sink", 8), ("sinkall", 16)):
            lay[(name, l)] = (off, n)
            off += n
    lay[("nfin", 0)] = (off, 16)
    off += 16
    lay[("flags", 0)] = (off, 12)
    off += 12
    return lay, off


class _Stop(Exception):
    pass


class Trk:
    SEM_MAX = 3800

    def __init__(self, nc, es):
        self.nc = nc
        self.es = es
        self.E = {"pe": nc.tensor, "act": nc.scalar, "dve": nc.vector, "pool": nc.gpsimd, "sp": nc.sync}
        self.gen = {k: 0 for k in self.E}
        self.sem = {k: es.enter_context(nc.semaphore("pg_" + k + "_0")) for k in self.E}
        self.semname = {k: "pg_" + k + "_0" for k in self.E}
        self.cnt = {k: 0 for k in self.E}
        self.seen = {k: {} for k in self.E}
        self.lastw = {}
        self.readers = {}
        self.dsems = {}
        self.all_dsems = []
        self.pending_noinc = {k: False for k in self.E}

    def _wait(self, eng, ev):
        name, semh, val = ev
        if eng == "pe" and name.startswith("pg_pe_"):
            return
        if self.seen[eng].get(name, 0) >= val:
            return
        self.E[eng].wait_ge(semh, val)
        self.seen[eng][name] = val

    def _deps(self, eng, R, W):
        best = {}

        def add(ev):
            if ev[0] not in best or best[ev[0]][2] < ev[2]:
                best[ev[0]] = ev
        for r in R:
            if r in self.lastw:
                add(self.lastw[r])
        for w in W:
            if w in self.lastw:
                add(self.lastw[w])
            for ev in self.readers.get(w, {}).values():
                add(ev)
        for ev in best.values():
            self._wait(eng, ev)

    def _commit(self, ev, R, W):
        for r in R:
            d = self.readers.setdefault(r, {})
            if ev[0] not in d or d[ev[0]][2] < ev[2]:
                d[ev[0]] = ev
        for w in W:
            self.lastw[w] = ev
            self.readers[w] = {}

    def _roll(self, eng):
        if self.cnt[eng] >= self.SEM_MAX and not self.pending_noinc[eng]:
            self.gen[eng] += 1
            nm = f"pg_{eng}_{self.gen[eng]}"
            self.sem[eng] = self.es.enter_context(self.nc.semaphore(nm))
            self.semname[eng] = nm
            self.cnt[eng] = 0

    def op(self, eng, fn, R=(), W=(), inc=True):
        self._roll(eng)
        self._deps(eng, R, W)
        ins = fn(self.E[eng])
        if inc:
            self.cnt[eng] += 1
            ins.then_inc(self.sem[eng], 1)
            ev = (self.semname[eng], self.sem[eng], self.cnt[eng])
            self.pending_noinc[eng] = False
        else:
            ev = (self.semname[eng], self.sem[eng], self.cnt[eng] + 1)
            self.pending_noinc[eng] = True
        self._commit(ev, R, W)
        return ins

    def fence(self, eng, others):
        for o in others:
            self._wait(eng, (self.semname[o], self.sem[o], self.cnt[o] + (1 if self.pending_noinc[o] else 0)))

    def reserve(self, sem, n):
        if sem not in self.dsems or self.dsems[sem][1] + 16 * n > self.SEM_MAX:
            g = self.dsems[sem][3] + 1 if sem in self.dsems else 0
            nm = f"d_{sem}_{g}"
            self.dsems[sem] = [self.es.enter_context(self.nc.semaphore(nm)), 0, nm, g]
            self.all_dsems.append(self.dsems[sem])

    def dma(self, q, out, in_, R=(), W=(), sem="g", skip_deps=False, **kw):
        if not skip_deps:
            self._deps(q, R, W)
        self.reserve(sem, 1)
        ds = self.dsems[sem]
        ins = self.E[q].dma_start(out=out, in_=in_, **kw)
        ds[1] += 16
        ins.then_inc(ds[0], 16)
        ev = (ds[2], ds[0], ds[1])
        self._commit(ev, R, W)
        return ins

    def coll(self, fn, R=(), W=()):
        self._deps("pool", R, W)
        if "cc" not in self.dsems:
            self.dsems["cc"] = [self.es.enter_context(self.nc.semaphore("d_cc_0")), 0, "d_cc_0", 0]
            self.all_dsems.append(self.dsems["cc"])
        ds = self.dsems["cc"]
        ins = fn(self.E["pool"])
        ds[1] += 1
        ins.then_inc(ds[0], 1)
        ev = (ds[2], ds[0], ds[1])
        self._commit(ev, R, W)
        return ins

    def wait_all_dma(self, eng):
        for ds in self.all_dsems:
            if ds[1]:
                self.E[eng].wait_ge(ds[0], ds[1])


def build(NPP):
    lay, NV = pv_layout()
    NSLOT = NPP + 1
    TOK = NSLOT * T
    nc = bass.Bass("TRN2", target_bir_lowering=False)

    def din(name, shape, dt=F32):
        return nc.dram_tensor(name, list(shape), dt, kind="ExternalInput").ap()

    def dout(name, shape, dt=F32):
        return nc.dram_tensor(name, list(shape), dt, kind="ExternalOutput").ap()

    xp = din("xp", [TOK, D]); xs = din("xs", [NS, D])
    st_h = din("st_h", [2, NS, 512]); st_conv = din("st_conv", [2, NS, 3, 512]); st_sc = din("st_sc", [2, NS, 2, 512])
    ck = din("ck", [2, NS, 128, 256]); cv = din("cv", [2, NS, 128, 256])
    w_in = din("w_in", [2, D, 4096]); w_out = din("w_out", [2, D, D]); w_gu = din("w_gu", [2, D, 2 * DFF]); w_dn = din("w_dn", [2, DFF, D])
    wo_in = din("wo_in", [1, D, 4096]); wo_out = din("wo_out", [1, D, D]); wo_gu = din("wo_gu", [1, D, 2 * DFF]); wo_dn = din("wo_dn", [1, DFF, D])
    WIN = [w_in[0], w_in[1], wo_in[0]]; WOUT = [w_out[0], w_out[1], wo_out[0]]
    WGU = [w_gu[0], w_gu[1], wo_gu[0]]; WDN = [w_dn[0], w_dn[1], wo_dn[0]]
    bd = din("bd", [3, 128, 8, 128])
    ibs = [nc.dram_tensor(f"ib{j}", [128, 2048], F32) for j in range(4)]
    obs = [nc.dram_tensor(f"ob{j}", [256, 2048], F32) for j in range(4)]
    pvd = din("pv", [128, NV])
    csd = din("cs", [NSLOT + 1, 128, 2, T])
    mskd = din("msk", [128, 2, 128])
    idnd = din("idn", [128, 128])

    yp = dout("yp", [TOK, D]); ys = dout("ys", [NS, D])
    p_h = dout("p_h", [2, 512]); p_conv = dout("p_conv", [2, 3, 512]); p_k = dout("p_k", [2, 128, 256]); p_v = dout("p_v", [2, 128, 256]); p_sc = dout("p_sc", [2, 2, 512])
    s_h = dout("s_h", [2, NS, 512]); s_conv = dout("s_conv", [2, NS, 3, 512]); s_k = dout("s_k", [2, NS, 128, 256]); s_v = dout("s_v", [2, NS, 128, 256]); s_sc = dout("s_sc", [2, NS, 2, 512])

    es = ExitStack()
    with es:
        trk = Trk(nc, es)
        op = trk.op

        def sb(name, shape, dt):
            return es.enter_context(nc.sbuf_tensor(name, list(shape), dt))

        def ps(name, shape, dt):
            return es.enter_context(nc.psum_tensor(name, list(shape), dt))

        X = sb("X", [128, KC, T], F32)
        XN = sb("XN", [128, KC, T], BF16)
        H = sb("H", [128, FC, T], BF16)
        MX = H[:, 0:16, :]
        QK = H[:, 16:26, :]
        KS = H[:, 26:34, :].rearrange("p c (a f) -> p (c a) f", f=256)
        VS2 = H[:, 34:42, :].rearrange("p c (a f) -> p (c a) f", f=256)
        NWS = 3
        WS = [sb(f"WS{i}", [128, 8192], BF16) for i in range(NWS)]
        NTMP = 9
        TMP = [sb(f"TMP{i}", [128, T + 4], F32) for i in range(NTMP)]
        SQ = [sb(f"SQ{i}", [128, T], BF16) for i in range(3)]
        PT = [sb(f"PT{i}", [128, 512], BF16) for i in range(4)]
        STG = XN[:, :, :].rearrange("p c t -> p (c t)").bitcast(F32)
        PV = sb("PV", [128, NV], F32)
        CS = sb("CS", [128, 2, T], F32)
        MSKF = sb("MSKF", [128, 2, 128], F32)
        MSK = sb("MSK", [128, 2, 4, 128], BF16)
        MSK0 = sb("MSK0", [128, 4, 128], BF16)
        IDN = sb("IDN", [128, 128], F32)
        IDNB = sb("IDNB", [128, 128], BF16)
        ONES = sb("ONES", [128, 128], BF16)
        BD = sb("BD", [128, 8, 128], F32)
        RS = [sb(f"RS{i}", [128, T], F32) for i in range(2)]
        SCL = sb("SCL", [128, 3, 2, 4], F32)
        SNK = sb("SNK", [128, 2, 4, 128], F32)
        EXS = sb("EXS", [128, 3, 8], F32)
        SNKS = sb("SNKS", [128, 2, NS, 16], F32)
        HC = sb("HC", [128, 3, 4], F32)
        CUX = sb("CUX", [128, 3, 4, 3], F32)
        CSC = sb("CSC", [128, 3, 4, 2], F32)
        KH = sb("KH", [128, 1, 2, 128 + T], BF16)
        VH = sb("VH", [128, 1, 5, 256], BF16)
        KF = [sb(f"KF{i}", [128, T], F32) for i in range(2)]
        ROW = sb("ROW", [128, 512], F32)
        UXS = sb("UXS", [128, 4, NS, 4], F32)
        GCS = sb("GCS", [128, 4, NS, 3], F32)
        H0S = sb("H0S", [128, 4, NS], F32)
        STS = sb("STS", [48, 512], F32)
        KTS = [sb(f"KTS{i}", [128, 2, 128], BF16) for i in range(2)]
        PSS = sb("PSS", [128, 256], BF16)
        RD = sb("RD", [128, 256], F32)

        PA = [ps(f"PA{i}", [128, 512], F32) for i in range(4)]
        PB = [ps(f"PB{i}", [128, 512], F32) for i in range(4)]

        wstate = {"i": 0}

        def wload(pieces, nm):
            i = wstate["i"]; wstate["i"] += 1
            slot = i % NWS
            trk.reserve(f"w{slot}", len(pieces))
            for pi_, (dst, src) in enumerate(pieces):
                trk.dma("pool", dst(WS[slot]), src, R=(), W=(f"W{slot}",), sem=f"w{slot}", skip_deps=(pi_ > 0))
            return WS[slot], f"W{slot}"

        def wv(slot, kc, n):
            return slot[:, 0:kc * n].rearrange("p (k n) -> p k n", n=n)

        pa_i = {"i": 0}

        def next_pa():
            i = pa_i["i"] % 4; pa_i["i"] += 1
            return PA[i], f"PA{i}"

        tmp_i = {"i": 0}

        def next_tmp():
            i = tmp_i["i"] % NTMP; tmp_i["i"] += 1
            return TMP[i], f"TMP{i}"

        sq_i = {"i": 0}

        def next_sq():
            i = sq_i["i"] % 3; sq_i["i"] += 1
            return SQ[i], f"SQ{i}"

        def colp(l, name, c=0):
            o, n = lay[(name, l)]
            return PV[:, o + c:o + c + 1]

        trk.dma("sp", PV[:], pvd, W=("PV",), sem="c_pv")
        trk.dma("sp", IDN[:], idnd, W=("IDN",), sem="c_idn")
        trk.dma("sp", MSKF[:], mskd, W=("MSKF",), sem="c_msk")
        op("dve", lambda e: e.tensor_copy(out=IDNB[:], in_=IDN[:]), R=("IDN",), W=("IDNB",))
        op("dve", lambda e: e.memset(ONES[:], 1.0), W=("ONES",))
        for m in range(2):
            for j in range(4):
                op("dve", lambda e, m=m, j=j: e.tensor_copy(out=MSK[:, m, j, :], in_=MSKF[:, m, :]), R=("MSKF",), W=("MSK",))
        for t_, nm in ((HC, "HC"), (CUX, "CUX"), (CSC, "CSC"), (KH, "KH"), (VH, "VH")):
            op("dve", lambda e, t_=t_: e.memset(t_[:], 0.0), W=(nm,))
        for l in range(3):
            o, n = lay[("lam", l)]
            tt, tn = next_tmp()
            op("act", lambda e: e.activation(out=tt[:, 0:4], in_=PV[:, o:o + 4], func=AF.Exp, scale=-1.0), R=("PV",), W=(tn,))
            op("act", lambda e: e.activation(out=tt[:, 4:8], in_=tt[:, 0:4], func=AF.Ln, bias=1.0), R=(tn,), W=(tn,))
            op("dve", lambda e: e.tensor_scalar(out=SCL[:, l, 0, :], in0=tt[:, 4:8], scalar1=-8.0, scalar2=None, op0=ALU.mult), R=(tn,), W=("SCL",))
            op("dve", lambda e: e.tensor_scalar(out=SCL[:, l, 1, :], in0=tt[:, 4:8], scalar1=-16.0, scalar2=None, op0=ALU.mult), R=(tn,), W=("SCL",))
            o, n = lay[("sink", l)]
            op("act", lambda e: e.activation(out=EXS[:, l, :], in_=PV[:, o:o + 8], func=AF.Exp), R=("PV",), W=("EXS",))
            o, n = lay[("sinkall", l)]
            op("act", lambda e: e.activation(out=tt[:, 16:32], in_=PV[:, o:o + 16], func=AF.Exp), R=("PV",), W=(tn,))
            for n_ in range(NS):
                if l < 2:
                    op("dve", lambda e, n_=n_: e.tensor_copy(out=SNKS[:, l, n_, :], in_=tt[:, 16:32]), R=(tn,), W=("SNKS",))

        def rmsnorm_to_xn(l, gname, N):
            pb, pbn = PB[0], "PB0"
            for c in range(KC):
                sq, sqn = next_sq()
                op("act", lambda e: e.activation(out=sq[:, 0:N], in_=X[:, c, 0:N], func=AF.Square), R=(f"X{c}",), W=(sqn,))
                op("pe", lambda e: e.matmul(pb[:, 0:N], lhsT=ONES[:, :], rhs=sq[:, 0:N], start=(c == 0), stop=(c == KC - 1)),
                   R=(sqn, "ONES"), W=(pbn,), inc=True)
            rs = RS[0]
            op("dve", lambda e: e.tensor_scalar(out=rs[:, 0:N], in0=pb[:, 0:N], scalar1=1.0 / D, scalar2=EPS, op0=ALU.mult, op1=ALU.add), R=(pbn,), W=("RS0",))
            op("act", lambda e: e.activation(out=rs[:, 0:N], in_=rs[:, 0:N], func=AF.Sqrt), R=("RS0",), W=("RS0",))
            op("dve", lambda e: e.reciprocal(out=rs[:, 0:N], in_=rs[:, 0:N]), R=("RS0",), W=("RS0",))
            for c in range(KC):
                op("dve", lambda e: e.scalar_tensor_tensor(out=XN[:, c, 0:N], in0=X[:, c, 0:N], scalar=colp(l, gname, c), in1=rs[:, 0:N], op0=ALU.mult, op1=ALU.mult),
                   R=(f"X{c}", "RS0", "PV"), W=(f"XN{c}", f"STG{c // 8}"))

        def gemm_chunk(lhsT_of_k, wtok, nk, rhs_of_k, rtok_of_k, N, last_in_block=False):
            pa, pan = next_pa()
            for k in range(nk):
                op("pe", lambda e: e.matmul(pa[:, 0:N], lhsT=lhsT_of_k(k), rhs=rhs_of_k(k), start=(k == 0), stop=(k == nk - 1)),
                   R=(wtok, rtok_of_k(k)), W=(pan,), inc=(k == nk - 1))
            return pa, pan

        def group_norm_finish(pb, pbn, l, c0, nch, N, rsi):
            rs = RS[rsi]; rsn = f"RS{rsi}"
            op("dve", lambda e: e.tensor_scalar(out=rs[:, 0:N], in0=pb[:, 0:N], scalar1=1.0 / (nch * 128), scalar2=EPS, op0=ALU.mult, op1=ALU.add), R=(pbn,), W=(rsn,))
            op("act", lambda e: e.activation(out=rs[:, 0:N], in_=rs[:, 0:N], func=AF.Sqrt), R=(rsn,), W=(rsn,))
            op("dve", lambda e: e.reciprocal(out=rs[:, 0:N], in_=rs[:, 0:N]), R=(rsn,), W=(rsn,))
            for c in range(c0, c0 + nch):
                op("dve", lambda e: e.scalar_tensor_tensor(out=MX[:, c, 0:N], in0=MX[:, c, 0:N], scalar=colp(l, "ngrp", c), in1=rs[:, 0:N], op0=ALU.mult, op1=ALU.mult),
                   R=(f"MX{c}", rsn, "PV"), W=(f"MX{c}",))

        def sq_accum(pb, pbn, c, first, last, N):
            sq, sqn = next_sq()
            op("act", lambda e: e.activation(out=sq[:, 0:N], in_=MX[:, c, 0:N], func=AF.Square), R=(f"MX{c}",), W=(sqn,))
            op("pe", lambda e: e.matmul(pb[:, 0:N], lhsT=ONES[:, :], rhs=sq[:, 0:N], start=first, stop=last), R=(sqn, "ONES"), W=(pbn,), inc=True)

        def transpose_out(src_ap, rows, cols, dst_dram, rtoks, tag, view=None, multi=None):
            op("pe", lambda e: e.transpose(PB[3][0:cols, 0:rows], src_ap, IDN[0:rows, 0:rows]), R=tuple(rtoks) + ("IDN",), W=("PB3",))
            op("act", lambda e: e.activation(out=ROW[0:cols, 0:rows], in_=PB[3][0:cols, 0:rows], func=AF.Copy), R=("PB3",), W=("ROW",))
            if multi is not None:
                trk.reserve("oROW", len(multi))
                for (d_ap, r0, r1) in multi:
                    trk.dma("sp", d_ap, ROW[r0:r1, 0:rows], R=("ROW",), W=(tag,), sem="oROW")
                return
            srcv = ROW[0:cols, 0:rows]
            if view is not None:
                srcv = view(srcv)
            trk.dma("sp", dst_dram, srcv, R=("ROW",), W=(tag,), sem="oROW")

        def layer(l, pi, kind, pout, ffi, xfinal_cb=None):
            N = T if kind == "p" else NS
            last_pass = pout is not None
            o_fl, _ = lay[("flags", 0)]
            if kind == "p":
                for kc in range(2):
                    for j in range(4):
                        op("dve", lambda e: e.tensor_scalar(out=SNK[:, kc, j, :], in0=IDN[:, :], scalar1=0.0, scalar2=EXS[:, l, kc * 4 + j:kc * 4 + j + 1], op0=ALU.mult, op1=ALU.add), R=("EXS", "IDN"), W=("SNK",))
            rmsnorm_to_xn(l, "nmix", N)
            xr = lambda k: XN[:, k, 0:N]
            xt = lambda k: f"XN{k}"
            if kind == "s":
                trk.dma("sp", STS[0:48, :], st_conv[l].rearrange("n k f -> (n k) f"), W=("STS",), sem="st")
                for c in range(4):
                    op("pe", lambda e: e.transpose(PB[3][:, 0:48], STS[0:48, c * 128:(c + 1) * 128], IDN[0:48, 0:48]), R=("STS", "IDN"), W=("PB3",))
                    op("act", lambda e: e.activation(out=UXS[:, c, :, 0:3], in_=PB[3][:, 0:48].rearrange("p (n k) -> p n k", k=3), func=AF.Copy), R=("PB3",), W=("UXS",))
                trk.dma("sp", STS[0:32, :], st_sc[l].rearrange("n k f -> (n k) f"), R=(), W=("STS",), sem="st")
                for c in range(4):
                    op("pe", lambda e: e.transpose(PB[3][:, 0:32], STS[0:32, c * 128:(c + 1) * 128], IDN[0:32, 0:32]), R=("STS", "IDN"), W=("PB3",))
                    op("act", lambda e: e.activation(out=GCS[:, c, :, 0:2], in_=PB[3][:, 0:32].rearrange("p (n k) -> p n k", k=2), func=AF.Copy), R=("PB3",), W=("GCS",))
                trk.dma("sp", STS[0:16, :], st_h[l], W=("STS",), sem="st")
                for c in range(4):
                    op("pe", lambda e: e.transpose(PB[3][:, 0:16], STS[0:16, c * 128:(c + 1) * 128], IDN[0:16, 0:16]), R=("STS", "IDN"), W=("PB3",))
                    op("act", lambda e: e.activation(out=H0S[:, c, :], in_=PB[3][:, 0:16], func=AF.Copy), R=("PB3",), W=("H0S",))
                trk.dma("sp", s_k[l][:, 0:127, :], ck[l][:, 1:128, :], W=(f"skA{l}",), sem=f"o2a{l}")
                trk.dma("sp", s_v[l][:, 0:127, :], cv[l][:, 1:128, :], W=(f"svA{l}",), sem=f"o2b{l}")
                trk.dma("sp", s_conv[l][:, 0:2, :], st_conv[l][:, 1:3, :], W=(f"scvA{l}",), sem=f"o2c{l}")
                trk.dma("sp", s_sc[l][:, 0:1, :], st_sc[l][:, 1:2, :], W=(f"sscA{l}",), sem=f"o2d{l}")

            if (_DBG.get("stop") == "norm" and kind == _DBG.get("stop_kind", "p") and l == _DBG.get("stop_layer", 0)):
                raise _Stop()
            trk.dma("sp", BD[:], bd[l], W=("BD",), sem="bd")
            pbl, pbln = PB[1], "PB1"
            pbs, pbsn = PB[2], "PB2"
            deferred = []

            def flush_deferred():
                while deferred:
                    deferred.pop(0)()

            def lru_chunk(c):
                cc = 0
                slot, wtok = wload([
                    (lambda s: wv(s, KC, 512)[:, :, 0:128], WIN[l][:, 1536 + c * 128:1536 + (c + 1) * 128].rearrange("(k p) m -> p k m", p=128)),
                    (lambda s: wv(s, KC, 512)[:, :, 256:384], WIN[l][:, 2048 + c * 128:2048 + (c + 1) * 128].rearrange("(k p) m -> p k m", p=128)),
                ], "lru")
                sv = wv(slot, KC, 512)
                pa, pan = gemm_chunk(lambda k: sv[:, k, cc * 128:(cc + 1) * 128], wtok, KC, xr, xt, N)
                ux, uxn = next_tmp()
                xc, xcn = next_tmp()
                if kind == "p":
                    op("dve", lambda e: e.tensor_copy(out=ux[:, 0:3], in_=CUX[:, l, c, :]), R=("CUX",), W=(uxn,))
                    op("act", lambda e: e.activation(out=ux[:, 3:3 + T], in_=pa[:, 0:T], func=AF.Copy), R=(pan,), W=(uxn,))
                    op("dve", lambda e: e.tensor_copy(out=CUX[:, l, c, :], in_=ux[:, T:T + 3]), R=(uxn,), W=("CUX",))
                    tap = lambda k: ux[:, k:k + T]
                else:
                    op("act", lambda e: e.activation(out=UXS[:, c, :, 3], in_=pa[:, 0:NS], func=AF.Copy), R=(pan,), W=("UXS",))
                    uxn = "UXS"
                    tap = lambda k: UXS[:, c, :, k]
                o_w, _ = lay[("lcw", l)]
                op("dve", lambda e: e.tensor_scalar(out=xc[:, 0:N], in0=tap(3), scalar1=PV[:, o_w + 12 + c:o_w + 13 + c], scalar2=colp(l, "lcb", c), op0=ALU.mult, op1=ALU.add), R=(uxn, "PV"), W=(xcn,))
                for k in (2, 1, 0):
                    op("dve", lambda e, k=k: e.scalar_tensor_tensor(out=xc[:, 0:N], in0=tap(k), scalar=PV[:, o_w + k * 4 + c:o_w + k * 4 + c + 1], in1=xc[:, 0:N], op0=ALU.mult, op1=ALU.add), R=(uxn, "PV", xcn), W=(xcn,))
                if kind == "s":
                    transpose_out(UXS[:, c, :, 3], 128, NS, s_conv[l][:, 2, c * 128:(c + 1) * 128], ("UXS",), f"scvB{l}")
                pa2, pa2n = gemm_chunk(lambda k: sv[:, k, 256:384], wtok, KC, xr, xt, N)
                ug, ugn = next_tmp()
                op("act", lambda e: e.activation(out=ug[:, 0:N], in_=pa2[:, 0:N], func=AF.Copy), R=(pa2n,), W=(ugn,))
                gates = []
                for gi, bname in ((0, "ba"), (1, "bi")):
                    pg, pgn = next_pa()
                    op("pe", lambda e: e.matmul(pg[:, 0:N], lhsT=BD[:, gi * 4 + c, :], rhs=xc[:, 0:N], start=True, stop=True), R=("BD", xcn), W=(pgn,))
                    gt, gtn = next_tmp()
                    op("act", lambda e: e.activation(out=gt[:, 0:N], in_=pg[:, 0:N], func=AF.Sigmoid, bias=colp(l, bname, c)), R=(pgn, "PV"), W=(gtn,))
                    gates.append((gt, gtn))
                (rg, rgn), (ig, ign) = gates
                a_, an = next_tmp()
                m_, mn = next_tmp()
                op("act", lambda e: e.activation(out=a_[:, 0:N], in_=rg[:, 0:N], func=AF.Exp, scale=SCL[:, l, 0, c:c + 1]), R=(rgn, "SCL"), W=(an,))
                op("act", lambda e: e.activation(out=m_[:, 0:N], in_=rg[:, 0:N], func=AF.Exp, scale=SCL[:, l, 1, c:c + 1]), R=(rgn, "SCL"), W=(mn,))
                op("dve", lambda e: e.tensor_scalar(out=m_[:, 0:N], in0=m_[:, 0:N], scalar1=-1.0, scalar2=1.0, op0=ALU.mult, op1=ALU.add), R=(mn,), W=(mn,))
                op("act", lambda e: e.activation(out=m_[:, 0:N], in_=m_[:, 0:N], func=AF.Sqrt), R=(mn,), W=(mn,))
                if ffi is not None:
                    op("dve", lambda e: e.tensor_scalar(out=m_[:, 0:1], in0=m_[:, 0:1], scalar1=PV[:, o_fl + 7 + ffi:o_fl + 8 + ffi], scalar2=PV[:, o_fl + 2 + ffi:o_fl + 3 + ffi], op0=ALU.mult, op1=ALU.add), R=(mn, "PV"), W=(mn,))
                op("dve", lambda e: e.tensor_tensor(out=ig[:, 0:N], in0=ig[:, 0:N], in1=xc[:, 0:N], op=ALU.mult), R=(ign, xcn), W=(ign,))
                op("dve", lambda e: e.tensor_tensor(out=ig[:, 0:N], in0=ig[:, 0:N], in1=m_[:, 0:N], op=ALU.mult), R=(ign, mn), W=(ign,))
                hs, hsn = rg, rgn
                if kind == "p":
                    op("dve", lambda e: e.tensor_tensor_scan(out=hs[:, 0:T], data0=a_[:, 0:T], data1=ig[:, 0:T], initial=HC[:, l, c:c + 1], op0=ALU.mult, op1=ALU.add), R=(an, ign, "HC", rgn), W=(hsn,))
                    op("dve", lambda e: e.tensor_copy(out=HC[:, l, c:c + 1], in_=hs[:, T - 1:T]), R=(hsn,), W=("HC",))
                else:
                    op("dve", lambda e: e.tensor_tensor(out=hs[:, 0:N], in0=a_[:, 0:N], in1=H0S[:, c, :], op=ALU.mult), R=(an, "H0S", rgn), W=(hsn,))
                    op("dve", lambda e: e.tensor_tensor(out=hs[:, 0:N], in0=hs[:, 0:N], in1=ig[:, 0:N], op=ALU.add), R=(hsn, ign), W=(hsn,))
                    transpose_out(hs[:, 0:NS], 128, NS, s_h[l][:, c * 128:(c + 1) * 128], (hsn,), f"shB{l}")
                flush_deferred()
                g2, g2n = next_tmp()
                op("dve", lambda e: e.tensor_tensor(out=g2[:, 0:N], in0=ug[:, 0:N], in1=ug[:, 0:N], op=ALU.mult), R=(ugn,), W=(g2n,))
                op("dve", lambda e: e.tensor_scalar(out=g2[:, 0:N], in0=g2[:, 0:N], scalar1=0.044715, scalar2=1.0, op0=ALU.mult, op1=ALU.add), R=(g2n,), W=(g2n,))
                op("dve", lambda e: e.tensor_tensor(out=g2[:, 0:N], in0=g2[:, 0:N], in1=ug[:, 0:N], op=ALU.mult), R=(g2n, ugn), W=(g2n,))
                op("act", lambda e: e.activation(out=g2[:, 0:N], in_=g2[:, 0:N], func=AF.Sigmoid, scale=1.5957691216057308), R=(g2n,), W=(g2n,))
                op("dve", lambda e: e.tensor_tensor(out=g2[:, 0:N], in0=g2[:, 0:N], in1=ug[:, 0:N], op=ALU.mult), R=(g2n, ugn), W=(g2n,))
                op("dve", lambda e: e.tensor_tensor(out=MX[:, 8 + c, 0:N], in0=g2[:, 0:N], in1=hs[:, 0:N], op=ALU.mult), R=(g2n, hsn), W=(f"MX{8 + c}",))
                deferred.append(lambda: sq_accum(pbl, pbln, 8 + c, c == 0, c == 3, N))

            def sc_chunk(c):
                slot, wtok = wload([
                    (lambda s: wv(s, KC, 384)[:, :, 0:128], WIN[l][:, 3072 + c * 128:3072 + (c + 1) * 128].rearrange("(k p) m -> p k m", p=128)),
                    (lambda s: wv(s, KC, 384)[:, :, 128:256], WIN[l][:, 3584 + c * 128:3584 + (c + 1) * 128].rearrange("(k p) m -> p k m", p=128)),
                    (lambda s: wv(s, KC, 384)[:, :, 256:384], WIN[l][:, 2560 + c * 128:2560 + (c + 1) * 128].rearrange("(k p) m -> p k m", p=128)),
                ], "sc")
                sv = wv(slot, KC, 384)
                pa, pan = gemm_chunk(lambda k: sv[:, k, 0:128], wtok, KC, xr, xt, N)
                uc, ucn = next_tmp()
                op("act", lambda e: e.activation(out=uc[:, 0:N], in_=pa[:, 0:N], func=AF.Copy), R=(pan,), W=(ucn,))
                flush_deferred()
                pa, pan = gemm_chunk(lambda k: sv[:, k, 128:256], wtok, KC, xr, xt, N)
                gc, gcn = next_tmp()
                if kind == "p":
                    op("dve", lambda e: e.tensor_copy(out=gc[:, 0:2], in_=CSC[:, l, c, :]), R=("CSC",), W=(gcn,))
                    op("dve", lambda e: e.tensor_tensor(out=gc[:, 2:2 + T], in0=pa[:, 0:T], in1=uc[:, 0:T], op=ALU.mult), R=(pan, ucn), W=(gcn,))
                    op("dve", lambda e: e.tensor_copy(out=CSC[:, l, c, :], in_=gc[:, T:T + 2]), R=(gcn,), W=("CSC",))
                    tap = lambda k: gc[:, k:k + T]
                else:
                    op("dve", lambda e: e.tensor_tensor(out=GCS[:, c, :, 2], in0=pa[:, 0:NS], in1=uc[:, 0:NS], op=ALU.mult), R=(pan, ucn), W=("GCS",))
                    gcn = "GCS"
                    tap = lambda k: GCS[:, c, :, k]
                    transpose_out(GCS[:, c, :, 2], 128, NS, s_sc[l][:, 1, c * 128:(c + 1) * 128], ("GCS",), f"sscB{l}")
                y_, yn = next_tmp()
                o_w, _ = lay[("scw", l)]
                op("dve", lambda e: e.tensor_scalar(out=y_[:, 0:N], in0=tap(2), scalar1=PV[:, o_w + 8 + c:o_w + 9 + c], scalar2=None, op0=ALU.mult), R=(gcn, "PV"), W=(yn,))
                for k in (1, 0):
                    op("dve", lambda e, k=k: e.scalar_tensor_tensor(out=y_[:, 0:N], in0=tap(k), scalar=PV[:, o_w + k * 4 + c:o_w + k * 4 + c + 1], in1=y_[:, 0:N], op0=ALU.mult, op1=ALU.add), R=(gcn, "PV", yn), W=(yn,))
                pa, pan = gemm_chunk(lambda k: sv[:, k, 256:384], wtok, KC, xr, xt, N)
                op("dve", lambda e: e.tensor_tensor(out=MX[:, 12 + c, 0:N], in0=pa[:, 0:N], in1=y_[:, 0:N], op=ALU.mult), R=(pan, yn), W=(f"MX{12 + c}",))
                deferred.append(lambda: sq_accum(pbs, pbsn, 12 + c, c == 0, c == 3, N))

            def kv_blk():
                pieces = [(lambda s: wv(s, KC, 512)[:, :, 256:512], WIN[l][:, 1280:1536].rearrange("(k p) m -> p k m", p=128))]
                for h_ in range(2):
                    for g_ in range(4):
                        c_src = 1024 + g_ * 64 + h_ * 32
                        c_dst = h_ * 128 + g_ * 32
                        pieces.append((lambda s, c_dst=c_dst: wv(s, KC, 512)[:, :, c_dst:c_dst + 32], WIN[l][:, c_src:c_src + 32].rearrange("(k p) m -> p k m", p=128)))
                slot, wtok = wload(pieces, "kv")
                sv = wv(slot, KC, 512)
                if kind == "p":
                    for pl in range(2):
                        op("dve", lambda e, pl=pl: e.tensor_copy(out=KH[:, 0, pl, 0:128], in_=KH[:, 0, pl, T:T + 128]), R=("KH",), W=("KH",))
                    op("dve", lambda e: e.tensor_copy(out=VH[:, 0, 0, :], in_=VH[:, 0, 4, :]), R=("VH",), W=("VH",))
                kf = []
                for pl in range(2):
                    pa, pan = gemm_chunk(lambda k: sv[:, k, pl * 128:(pl + 1) * 128], wtok, KC, xr, xt, N)
                    t_, tn = next_tmp()
                    op("act", lambda e: e.activation(out=t_[:, 0:N], in_=pa[:, 0:N], func=AF.Copy), R=(pan,), W=(tn,))
                    kf.append((t_, tn))
                rope(kf, [(KF[0], "KF0"), (KF[1], "KF1")], N)
                if kind == "p":
                    for pl in range(2):
                        op("act", lambda e, pl=pl: e.activation(out=KH[:, 0, pl, 128:128 + T], in_=KF[pl][:, 0:T], func=AF.Copy), R=(f"KF{pl}",), W=("KH",))
                    if last_pass:
                        for pl in range(2):
                            transpose_out(KF[pl][:, T - 128:T], 128, 128, pout["k"].rearrange("t (g h i) -> t h g i", h=2, i=32)[:, pl, :, :], (f"KF{pl}",), "pk", view=lambda a: a.rearrange("t (g i) -> t g i", i=32))
                else:
                    for pl in range(2):
                        op("act", lambda e, pl=pl: e.activation(out=QK[:, 8 + pl, 0:NS], in_=KF[pl][:, 0:NS], func=AF.Copy), R=(f"KF{pl}",), W=(f"QK{8 + pl}",))
                        transpose_out(KF[pl][:, 0:NS], 128, NS, s_k[l].rearrange("n s (g h i) -> n s h g i", h=2, i=32)[:, 127, pl, :, :], (f"KF{pl}",), f"skB{l}", view=lambda a: a.rearrange("t (g i) -> t g i", i=32))
                ntb = 4 if kind == "p" else 1
                for tb in range(ntb):
                    M = 128 if kind == "p" else NS
                    pa, pan = next_pa()
                    for k in range(KC):
                        op("pe", lambda e: e.matmul(pa[0:M, 0:256], lhsT=XN[:, k, tb * 128:tb * 128 + M], rhs=sv[:, k, 256:512], start=(k == 0), stop=(k == KC - 1)),
                           R=(wtok, f"XN{k}"), W=(pan,), inc=(k == KC - 1))
                    if kind == "p":
                        op("act", lambda e: e.activation(out=VH[:, 0, 1 + tb, :], in_=pa[:, 0:256], func=AF.Copy), R=(pan,), W=("VH",))
                        if last_pass and tb == 3:
                            op("act", lambda e: e.activation(out=ROW[:, 0:256], in_=pa[:, 0:256], func=AF.Copy), R=(pan,), W=("ROW",))
                            trk.dma("sp", pout["v"], ROW[:, 0:256], R=("ROW",), W=("pv",), sem="oROW")
                    else:
                        op("act", lambda e: e.activation(out=ROW[0:NS, 0:256], in_=pa[0:NS, 0:256], func=AF.Copy), R=(pan,), W=("ROW",))
                        trk.dma("sp", s_v[l][:, 127, :], ROW[0:NS, 0:256], R=("ROW",), W=(f"svB{l}",), sem="oROW")


            def q_blk(jb):
                pieces = []
                for jj_ in range(2):
                    for h_ in range(2):
                        for g_ in range(4):
                            c_src = (g_ * 4 + jb * 2 + jj_) * 64 + h_ * 32
                            c_dst = (jj_ * 2 + h_) * 128 + g_ * 32
                            pieces.append((lambda s, c_dst=c_dst: wv(s, KC, 512)[:, :, c_dst:c_dst + 32], WIN[l][:, c_src:c_src + 32].rearrange("(k p) m -> p k m", p=128)))
                slot, wtok = wload(pieces, "q")
                sv = wv(slot, KC, 512)
                for jj in range(2):
                    j = jb * 2 + jj
                    qf = []
                    for pl in range(2):
                        pa, pan = gemm_chunk(lambda k: sv[:, k, (jj * 2 + pl) * 128:(jj * 2 + pl + 1) * 128], wtok, KC, xr, xt, N)
                        t_, tn = next_tmp()
                        op("act", lambda e: e.activation(out=t_[:, 0:N], in_=pa[:, 0:N], func=AF.Copy), R=(pan,), W=(tn,))
                        qf.append((t_, tn))
                    rope(qf, [(QK[:, j * 2, :], f"QK{j * 2}"), (QK[:, j * 2 + 1, :], f"QK{j * 2 + 1}")], N)


            for c in range(4):
                lru_chunk(c)
                sc_chunk(c)
                if c == 0:
                    q_blk(0)
                elif c == 1:
                    q_blk(1)
                elif c == 2:
                    kv_blk()
            flush_deferred()
            group_norm_finish(pbl, pbln, l, 8, 4, N, 1)
            group_norm_finish(pbs, pbsn, l, 12, 4, N, 0)
            if kind == "p" and last_pass:
                transpose_out(HC[:, l, :], 128, 4, pout["h"].rearrange("(c f) -> c f", f=128), ("HC",), "ph")
                transpose_out(CUX[:, l, :, :], 128, 12, None, ("CUX",), "pconv", multi=[(pout["conv"][:, c * 128:(c + 1) * 128], c * 3, c * 3 + 3) for c in range(4)])
                transpose_out(CSC[:, l, :, :], 128, 8, None, ("CSC",), "psc", multi=[(pout["sc"][:, c * 128:(c + 1) * 128], c * 2, c * 2 + 2) for c in range(4)])

            if (_DBG.get("stop") == "sc" and kind == _DBG.get("stop_kind", "p") and l == _DBG.get("stop_layer", 0)):
                raise _Stop()
            pba, pban = PB[1], "PB1"
            if kind == "p":
                attention_prompt(l, ffi)
            else:
                attention_sample(l)
            for c in range(8):
                sq_accum(pba, pban, c, c == 0, c == 7, N)
            group_norm_finish(pba, pban, l, 0, 8, N, 1)

            if (_DBG.get("stop") == "attn" and kind == _DBG.get("stop_kind", "p") and l == _DBG.get("stop_layer", 0)):
                raise _Stop()
            for cb in range(4):
                cs_ = slice(cb * 512, (cb + 1) * 512)
                pieces = []
                for half in range(2):
                    for kc2 in range(2):
                        r0 = kc2 * 512 + half * 256
                        pieces.append((lambda s, half=half, kc2=kc2: wv(s, KC, 512)[half * 64:(half + 1) * 64, kc2 * 4:(kc2 + 1) * 4, :],
                                       WOUT[l][r0:r0 + 256, cs_].rearrange("(j d) m -> d j m", d=64)))
                pieces.append((lambda s: wv(s, KC, 512)[:, 8:16, :], WOUT[l][1024:2048, cs_].rearrange("(k p) m -> p k m", p=128)))
                slot, wtok = wload(pieces, "out")
                sv = wv(slot, KC, 512)
                for mm in range(4):
                    m = cb * 4 + mm
                    ko = lambda k: (k + 8) % 16
                    pa, pan = gemm_chunk(lambda k: sv[:, ko(k), mm * 128:(mm + 1) * 128], wtok, KC, lambda k: MX[:, ko(k), 0:N], lambda k: f"MX{ko(k)}", N)
                    op("dve", lambda e: e.tensor_tensor(out=X[:, m, 0:N], in0=pa[:, 0:N], in1=X[:, m, 0:N], op=ALU.add), R=(pan, f"X{m}"), W=(f"X{m}",))

            if (_DBG.get("stop") == "out" and kind == _DBG.get("stop_kind", "p") and l == _DBG.get("stop_layer", 0)):
                raise _Stop()
            rmsnorm_to_xn(l, "nffn", N)
            trk.fence("dve", ["pe", "act"])
            for fb in range(22):
                slot, wtok = wload([
                    (lambda s: wv(s, KC, 512)[:, :, 0:256], WGU[l][:, fb * 256:(fb + 1) * 256].rearrange("(k p) m -> p k m", p=128)),
                    (lambda s: wv(s, KC, 512)[:, :, 256:512], WGU[l][:, DFF + fb * 256:DFF + (fb + 1) * 256].rearrange("(k p) m -> p k m", p=128)),
                ], "gu")
                sv = wv(slot, KC, 512)
                for cc in range(2):
                    fc = fb * 2 + cc
                    pa, pan = gemm_chunk(lambda k: sv[:, k, cc * 128:(cc + 1) * 128], wtok, KC, xr, xt, N)
                    sg, sgn = next_tmp()
                    op("act", lambda e: e.activation(out=sg[:, 0:N], in_=pa[:, 0:N], func=AF.Silu), R=(pan,), W=(sgn,))
                    pa, pan = gemm_chunk(lambda k: sv[:, k, 256 + cc * 128:256 + (cc + 1) * 128], wtok, KC, xr, xt, N)
                    op("dve", lambda e: e.tensor_tensor(out=H[:, fc, 0:N], in0=pa[:, 0:N], in1=sg[:, 0:N], op=ALU.mult), R=(pan, sgn), W=(f"H{fc}",))
            dn_issued = {}

            def dn_issue(cg_, kh_):
                if cg_ < 8 and (cg_, kh_) not in dn_issued:
                    dn_issued[(cg_, kh_)] = wload([(lambda s: wv(s, 22, 256), WDN[l][kh_ * 2816:(kh_ + 1) * 2816, cg_ * 256:(cg_ + 1) * 256].rearrange("(k p) m -> p k m", p=128))], "dn")
            for cg in range(8):
                accs = None
                for kh in range(2):
                    dn_issue(cg, kh)
                    slot, wtok = dn_issued[(cg, kh)]
                    sv = wv(slot, 22, 256)
                    if kh == 0:
                        accs = [next_pa(), next_pa()]
                    for mm in range(2):
                        pa, pan = accs[mm]
                        for k in range(22):
                            kk = kh * 22 + k
                            op("pe", lambda e: e.matmul(pa[:, 0:N], lhsT=sv[:, k, mm * 128:(mm + 1) * 128], rhs=H[:, kk, 0:N], start=(kk == 0), stop=(kk == FC - 1)),
                               R=(wtok, f"H{kk}"), W=(pan,), inc=(k == 21))
                for mm in range(2):
                    m = cg * 2 + mm
                    pa, pan = accs[mm]
                    op("dve", lambda e: e.tensor_tensor(out=X[:, m, 0:N], in0=pa[:, 0:N], in1=X[:, m, 0:N], op=ALU.add), R=(pan, f"X{m}"), W=(f"X{m}",))
                if xfinal_cb is not None and cg % 2 == 1:
                    dn_issue(cg + 1, 0); dn_issue(cg + 1, 1)
                    xfinal_cb(cg // 2)

        def rope(src, dst, N):
            (a_, an), (b_, bn) = src
            (da, dan), (db, dbn) = dst
            t1, t1n = next_tmp(); t2, t2n = next_tmp()
            cos = CS[:, 0, 0:N]; sin = CS[:, 1, 0:N]
            op("dve", lambda e: e.tensor_tensor(out=t1[:, 0:N], in0=a_[:, 0:N], in1=cos, op=ALU.mult), R=(an, "CS"), W=(t1n,))
            op("dve", lambda e: e.tensor_tensor(out=t2[:, 0:N], in0=b_[:, 0:N], in1=sin, op=ALU.mult), R=(bn, "CS"), W=(t2n,))
            op("dve", lambda e: e.tensor_tensor(out=da[:, 0:N], in0=t1[:, 0:N], in1=t2[:, 0:N], op=ALU.subtract), R=(t1n, t2n), W=(dan,))
            op("dve", lambda e: e.tensor_tensor(out=t1[:, 0:N], in0=b_[:, 0:N], in1=cos, op=ALU.mult), R=(bn, "CS"), W=(t1n,))
            op("dve", lambda e: e.tensor_tensor(out=t2[:, 0:N], in0=a_[:, 0:N], in1=sin, op=ALU.mult), R=(an, "CS"), W=(t2n,))
            op("dve", lambda e: e.tensor_tensor(out=db[:, 0:N], in0=t1[:, 0:N], in1=t2[:, 0:N], op=ALU.add), R=(t1n, t2n), W=(dbn,))

        pt_i = {"i": 0}

        def attention_prompt(l, ffi):
            QKv = QK[:, 0:8, :].rearrange("p (j pl) t -> p pl j t", pl=2)
            num, numn = PB[2], "PB2"
            den, denn = PB[3], "PB3"
            units = [(qb, kc, half) for qb in range(4) for kc in range(2) for half in range(2)]

            def qk_phase(u):
                qb, kc, half = u
                g = kc * 2 + half
                blocks = [(qb, 1), (qb + 1, 0)]
                pts = []
                for bi, (kb, mk) in enumerate(blocks):
                    sps = PB[bi]; spsn = f"PB{bi}"
                    for pl in range(2):
                        op("pe", lambda e: e.matmul(sps[:, :].rearrange("p (j q) -> p j q", q=128), lhsT=KH[32 * g:32 * g + 32, 0, pl, kb * 128:(kb + 1) * 128],
                                                    rhs=QKv[32 * g:32 * g + 32, pl, :, qb * 128:(qb + 1) * 128], start=(pl == 0), stop=(pl == 1), tile_position=(32 * g, 0)),
                           R=("KH",) + tuple(f"QK{j * 2 + pl}" for j in range(4)), W=(spsn,), inc=(pl == 1))
                    i = pt_i["i"] % 4; pt_i["i"] += 1
                    pt, ptn = PT[i], f"PT{i}"
                    op("act", lambda e: e.activation(out=pt[:, :], in_=sps[:, :], func=AF.Exp, scale=0.125), R=(spsn,), W=(ptn,))
                    mk_ap = MSK0[:, :, :] if (mk == 1 and qb == 0 and ffi is not None) else MSK[:, mk, :, :]
                    op("dve", lambda e: e.tensor_tensor(out=pt[:, :], in0=pt[:, :], in1=mk_ap.rearrange("p j q -> p (j q)"), op=ALU.mult), R=(ptn, "MSK", "MSK0"), W=(ptn,))
                    pts.append((pt, ptn, kb))
                return pts

            def pv_phase(u, pts):
                qb, kc, half = u
                g = kc * 2 + half
                for bi, (pt, ptn, kb) in enumerate(pts):
                    st = (bi == 0); sp_ = (bi == len(pts) - 1)
                    op("pe", lambda e: e.matmul(num[half * 64:(half + 1) * 64, :], lhsT=VH[:, 0, kb, g * 64:(g + 1) * 64], rhs=pt[:, :], start=st, stop=sp_),
                       R=(ptn, "VH"), W=(numn,), inc=False)
                    op("pe", lambda e: e.matmul(den[half * 64:(half + 1) * 64, :], lhsT=ONES[:, 0:64], rhs=pt[:, :], start=st, stop=sp_),
                       R=(ptn, "ONES"), W=(denn,), inc=True)
                if half == 1:
                    dn, dnn = next_tmp()
                    op("dve", lambda e: e.tensor_tensor(out=dn[:, 0:512], in0=den[:, :], in1=SNK[:, kc, :, :].rearrange("p j q -> p (j q)"), op=ALU.add), R=(denn, "SNK"), W=(dnn,))
                    op("dve", lambda e: e.reciprocal(out=dn[:, 0:512], in_=dn[:, 0:512]), R=(dnn,), W=(dnn,))
                    op("dve", lambda e: e.tensor_tensor(out=MX[:, kc * 4:kc * 4 + 4, qb * 128:(qb + 1) * 128], in0=num[:, :].rearrange("p (j q) -> p j q", q=128),
                                                        in1=dn[:, 0:512].rearrange("p (j q) -> p j q", q=128), op=ALU.mult),
                       R=(numn, dnn), W=tuple(f"MX{kc * 4 + j}" for j in range(4)))

            prev = None
            for u in units:
                pts = qk_phase(u)
                if prev is not None:
                    pv_phase(*prev)
                prev = (u, pts)
            pv_phase(*prev)

        def attention_sample(l):
            QKv = QK[:, 0:8, :].rearrange("p (j pl) t -> p pl j t", pl=2)
            first = True
            trk.reserve("ks", 8)
            for h_ in range(2):
                for g_ in range(4):
                    trk.dma("pool", KS[:, :, h_ * 128 + g_ * 32:h_ * 128 + g_ * 32 + 32], s_k[l][:, :, g_ * 64 + h_ * 32:g_ * 64 + h_ * 32 + 32].rearrange("n s i -> s n i"),
                            R=(f"skA{l}", f"skB{l}"), W=("KS",), sem="ks", skip_deps=not first)
                    first = False
            trk.dma("pool", VS2[:], s_v[l].rearrange("n s f -> s n f"), R=(f"svA{l}", f"svB{l}"), W=("VS2",), sem="vs")
            for n in range(NS):
                kt, ktn = KTS[n % 2], f"KTS{n % 2}"
                ptb = PB[3][:, 0:128].bitcast(BF16).rearrange("p (pl s) -> p pl s", pl=2)
                for pl in range(2):
                    op("pe", lambda e: e.transpose(ptb[:, pl, :], KS[:, n, pl * 128:(pl + 1) * 128], IDNB[:, :]), R=("KS", "IDNB"), W=("PB3",))
                op("act", lambda e: e.activation(out=kt[:, :, :], in_=ptb, func=AF.Copy), R=("PB3",), W=(ktn,))
                for g in range(4):
                    for pl in range(2):
                        op("pe", lambda e: e.matmul(PA[g][:, n * 4:n * 4 + 4], lhsT=kt[32 * g:32 * g + 32, pl, :], rhs=QKv[32 * g:32 * g + 32, pl, :, n],
                                                    start=(pl == 0), stop=(pl == 1), tile_position=(32 * g, 0)),
                           R=(ktn,) + tuple(f"QK{j * 2 + pl}" for j in range(4)), W=(f"PA{g}",), inc=(pl == 1))
            pssv = PSS[:, :].rearrange("p (n g j) -> p n g j", g=4, j=4)
            for g in range(4):
                op("act", lambda e: e.activation(out=pssv[:, :, g, :], in_=PA[g][:, 0:64].rearrange("p (n j) -> p n j", j=4), func=AF.Exp, scale=0.125), R=(f"PA{g}",), W=("PSS",))
            den, denn = PB[1], "PB1"
            op("pe", lambda e: e.matmul(den[:, 0:256], lhsT=ONES[:, :], rhs=PSS[:, :], start=True, stop=True), R=("PSS", "ONES"), W=(denn,))
            op("dve", lambda e: e.tensor_tensor(out=RD[:, :], in0=den[:, 0:256], in1=SNKS[:, l, :, :].rearrange("p n h -> p (n h)"), op=ALU.add), R=(denn, "SNKS"), W=("RD",))
            op("dve", lambda e: e.reciprocal(out=RD[:, :], in_=RD[:, :]), R=("RD",), W=("RD",))
            num, numn = PB[2], "PB2"
            numv = num[:, 0:128].rearrange("p (kc j n) -> p kc j n", kc=2, j=4)
            for n in range(NS):
                for g in range(4):
                    hf = g % 2
                    op("pe", lambda e: e.matmul(numv[hf * 64:(hf + 1) * 64, g // 2, :, n], lhsT=VS2[:, n, g * 64:(g + 1) * 64], rhs=PSS[:, n * 16 + g * 4:n * 16 + g * 4 + 4], start=True, stop=True),
                       R=("VS2", "PSS"), W=(numn,), inc=(n == NS - 1 and g == 3))
            rdv = RD[:, :].rearrange("p (n g j) -> p g j n", g=4, j=4)
            for hf in range(2):
                op("dve", lambda e: e.tensor_tensor(out=MX[hf * 64:(hf + 1) * 64, 0:8, 0:NS].rearrange("p (kc j) n -> p kc j n", kc=2),
                                                    in0=numv[hf * 64:(hf + 1) * 64, :, :, :],
                                                    in1=rdv[hf * 64:(hf + 1) * 64, hf::2, :, :], op=ALU.mult),
                   R=(numn, "RD"), W=tuple(f"MX{c}" for c in range(8)))

        def run_pass(pi, kind):
            N = T if kind == "p" else NS
            ci = pi if kind == "p" else NSLOT
            o_fl, _ = lay[("flags", 0)]
            trk.dma("sp", CS[:], csd[ci], W=("CS",), sem="cs")
            ntb = 4 if kind == "p" else 1
            for tb in range(ntb):
                rows = 128 if kind == "p" else NS
                src = xp[pi * T + tb * 128: pi * T + tb * 128 + 128, :] if kind == "p" else xs
                so = (tb % 2) * D; stn = f"STG{tb % 2}"
                trk.dma("sp", STG[0:rows, so:so + D], src, W=(stn,), sem=f"xin{tb % 2}")
                for c4 in range(4):
                    pb = PB[c4 % 2]; pbn = f"PB{c4 % 2}"
                    for cc in range(4):
                        c = c4 * 4 + cc
                        op("pe", lambda e: e.transpose(pb[:, cc * 128:cc * 128 + rows], STG[0:rows, so + c * 128:so + (c + 1) * 128], IDN[0:rows, 0:rows]), R=(stn, "IDN"), W=(pbn,), inc=(cc == 3))
                    op("act", lambda e: e.activation(out=X[:, c4 * 4:c4 * 4 + 4, tb * 128:tb * 128 + rows], in_=pb[:, :].rearrange("p (c t) -> p c t", t=128)[:, :, 0:rows], func=AF.Copy),
                       R=(pbn,), W=tuple(f"X{c4 * 4 + cc}" for cc in range(4)))
            if kind == "p" and pi >= 1:
                for j in range(4):
                    so = (j % 2) * D; stn = f"STG{j % 2}"
                    trk.dma("sp", STG[:, so:so + 2048], obs[j][0:128, :], R=(f"ob{j}",), W=(stn,), sem=f"xin{j % 2}")
                    for cc in range(4):
                        c = j * 4 + cc
                        op("dve", lambda e: e.scalar_tensor_tensor(out=X[:, c, :], in0=STG[:, so + cc * 512:so + (cc + 1) * 512], scalar=PV[:, o_fl + 1:o_fl + 2], in1=X[:, c, :], op0=ALU.mult, op1=ALU.add),
                           R=(stn, "PV", f"X{c}"), W=(f"X{c}",))
            if (_DBG.get("stop") == "load" and kind == _DBG.get("stop_kind", "p")):
                raise _Stop()
            if kind == "p":
                ffi = pi if pi <= 1 else None
                if ffi is not None:
                    for j in range(4):
                        op("dve", lambda e: e.tensor_scalar(out=MSK0[:, j, :], in0=MSK[:, 1, j, :], scalar1=PV[:, o_fl + 7 + ffi:o_fl + 8 + ffi], scalar2=None, op0=ALU.mult), R=("MSK", "PV"), W=("MSK0",))
                pout = None
                if pi >= NPP - 1:
                    w_ = pi - (NPP - 1)
                    pout = {"h": p_h[w_], "conv": p_conv[w_], "k": p_k[w_], "v": p_v[w_], "sc": p_sc[w_]}
                def exchange(j):
                    trk.dma("sp", ibs[j][:, :], X[:, j * 4:(j + 1) * 4, :].rearrange("p c t -> p (c t)"), R=tuple(f"X{j * 4 + cc}" for cc in range(4)), W=(f"ib{j}",), sem=f"xex{j}")
                    trk.coll(lambda e: e.collective_compute("AllGather", ALU.bypass, replica_groups=[[2 * i_, 2 * i_ + 1] for i_ in range(_DBG.get('ncores', NCORES) // 2)],
                                                            ins=[ibs[j].ap().opt()], outs=[obs[j].ap().opt()]), R=(f"ib{j}",), W=(f"ob{j}",))
                layer(2, pi, kind, pout, ffi, xfinal_cb=(exchange if pi < NSLOT - 1 else None))
                if pi == 0:
                    fa = PV[:, o_fl:o_fl + 1]
                    for t_, nm in ((HC[:, 2, :], "HC"), (CUX[:, 2, :, :], "CUX"), (CSC[:, 2, :, :], "CSC"), (KH[:, 0, :, :], "KH"), (VH[:, 0, :, :], "VH")):
                        op("dve", lambda e, t_=t_: e.tensor_scalar(out=t_, in0=t_, scalar1=fa, scalar2=None, op0=ALU.mult), R=(nm, "PV"), W=(nm,))
            else:
                for l in range(2):
                    layer(l, pi, kind, None, None)
            if (_DBG.get("stop") == "final" and kind == _DBG.get("stop_kind", "p")):
                raise _Stop()
            pb, pbn = PB[0], "PB0"
            for c in range(KC):
                sq, sqn = next_sq()
                op("act", lambda e: e.activation(out=sq[:, 0:N], in_=X[:, c, 0:N], func=AF.Square), R=(f"X{c}",), W=(sqn,))
                op("pe", lambda e: e.matmul(pb[:, 0:N], lhsT=ONES[:, :], rhs=sq[:, 0:N], start=(c == 0), stop=(c == KC - 1)), R=(sqn, "ONES"), W=(pbn,), inc=True)
            rs = RS[0]
            op("dve", lambda e: e.tensor_scalar(out=rs[:, 0:N], in0=pb[:, 0:N], scalar1=1.0 / D, scalar2=EPS, op0=ALU.mult, op1=ALU.add), R=(pbn,), W=("RS0",))
            op("act", lambda e: e.activation(out=rs[:, 0:N], in_=rs[:, 0:N], func=AF.Sqrt), R=("RS0",), W=("RS0",))
            op("dve", lambda e: e.reciprocal(out=rs[:, 0:N], in_=rs[:, 0:N]), R=("RS0",), W=("RS0",))
            o_f, _ = lay[("nfin", 0)]
            for c in range(KC):
                op("dve", lambda e: e.scalar_tensor_tensor(out=X[:, c, 0:N], in0=X[:, c, 0:N], scalar=PV[:, o_f + c:o_f + c + 1], in1=rs[:, 0:N], op0=ALU.mult, op1=ALU.mult),
                   R=(f"X{c}", "RS0", "PV"), W=(f"X{c}",))
            for tb in range(ntb):
                rows = 128 if kind == "p" else NS
                so = (tb % 2) * D; stn = f"STG{tb % 2}"
                for c4 in range(4):
                    pb = PB[c4 % 2]; pbn = f"PB{c4 % 2}"
                    for cc in range(4):
                        c = c4 * 4 + cc
                        op("pe", lambda e: e.transpose(pb[0:rows, cc * 128:(cc + 1) * 128], X[:, c, tb * 128:tb * 128 + rows], IDN[:, :]), R=(f"X{c}", "IDN"), W=(pbn,), inc=(cc == 3))
                    op("act", lambda e: e.activation(out=STG[0:rows, so + c4 * 512:so + (c4 + 1) * 512], in_=pb[0:rows, :], func=AF.Copy), R=(pbn,), W=(stn,))
                dst = yp[pi * T + tb * 128: pi * T + tb * 128 + 128, :] if kind == "p" else ys
                trk.dma("sp", dst, STG[0:rows, so:so + D], R=(stn,), W=("yout",), sem=f"oSTG{tb % 2}")

        try:
            for pi in range(NSLOT):
                run_pass(pi, "p")
            if not _DBG.get("nosample"):
                run_pass(0, "s")
        except _Stop:
            pass
        trk.wait_all_dma("sp")
    return nc


_CACHE = {}
_DBG = {}


def kernel(x_prompt, x_sample, state_lru_h, state_lru_conv, cache_swa_k, cache_swa_v, state_sconv,
           norm_mix, w_in, norm_grp, w_out, lru_conv_w, lru_conv_b, lru_w_a, lru_b_a, lru_w_i, lru_b_i,
           lru_lambda, sc_conv_w, attn_sinks, norm_ffn, ffn_w_gu, ffn_w_down, norm_final):
    f = lambda a: np.ascontiguousarray(np.asarray(a, dtype=np.float32))
    x_prompt = f(x_prompt); x_sample = f(x_sample)
    NPP = _DBG.get('npp', SEQ // T)
    lay, NV = pv_layout()
    pv = np.zeros((128, NV), np.float32)

    def put(name, l, arr2d):
        o, n = lay[(name, l)]
        assert arr2d.shape == (128, n), (name, arr2d.shape)
        pv[:, o:o + n] = arr2d
    fm = lambda v: f(v).reshape(-1, 128).T
    def put_layer(l, ls):
        put("nmix", l, fm(norm_mix[ls])); put("nffn", l, fm(norm_ffn[ls]))
        g = f(norm_grp[ls])
        ga = g[:1024].reshape(2, 2, 4, 64)
        ga = ga.transpose(1, 3, 0, 2).reshape(128, 8)
        put("ngrp", l, np.concatenate([ga, fm(g[1024:1536]), fm(g[1536:2048])], axis=1))
        cw = f(lru_conv_w[ls])
        put("lcw", l, np.concatenate([fm(cw[k]) for k in range(4)], axis=1))
        put("lcb", l, fm(lru_conv_b[ls])); put("ba", l, fm(lru_b_a[ls])); put("bi", l, fm(lru_b_i[ls])); put("lam", l, fm(lru_lambda[ls]))
        sw = f(sc_conv_w[ls])
        put("scw", l, np.concatenate([fm(sw[k]) for k in range(3)], axis=1))
        sk = f(attn_sinks[ls]).reshape(2, 2, 4)
        sk2 = np.repeat(sk.transpose(1, 0, 2).reshape(2, 1, 8), 64, axis=1).reshape(128, 8)
        put("sink", l, sk2)
        put("sinkall", l, np.repeat(f(attn_sinks[ls])[None, :], 128, axis=0))
    put_layer(0, 0); put_layer(1, 1)
    put("nfin", 0, fm(norm_final))
    bdh = np.zeros((3, 128, 8, 128), np.float32)
    for l in range(2):
        for gi, wsrc in enumerate((lru_w_a, lru_w_i)):
            wl = f(wsrc[l])
            for c in range(4):
                bdh[l, 0:64, gi * 4 + c, 0:64] = wl[2 * c]
                bdh[l, 64:128, gi * 4 + c, 64:128] = wl[2 * c + 1]
    half = 32
    inv = (10000.0 ** (-np.arange(half, dtype=np.float32) / half)).astype(np.float32)
    NSLOT = NPP + 1

    def cs_tables(lc):
        cs = np.zeros((NSLOT + 1, 128, 2, T), np.float32)
        for si in range(NSLOT + 1):
            if si < NSLOT:
                pss = min(max(si - lc, 0), NPP - 1)
                pos = (np.arange(T) + pss * T).astype(np.float32)
            else:
                pos = np.full(T, float(PAST), np.float32)
            ang = pos[None, :] * inv[:, None]
            cs[si, :, 0, :] = np.tile(np.cos(ang).astype(np.float32), (4, 1))
            cs[si, :, 1, :] = np.tile(np.sin(ang).astype(np.float32), (4, 1))
        return cs
    cs_by_lc = [cs_tables(0), cs_tables(1)]
    sidx = np.arange(128)[:, None]; qidx = np.arange(128)[None, :]
    msk = np.stack([(sidx <= qidx), (sidx > qidx)], axis=1).astype(np.float32)
    idn = np.eye(128, dtype=np.float32)
    w_in = f(w_in); w_out = f(w_out); ffn_w_gu = f(ffn_w_gu); ffn_w_down = f(ffn_w_down)
    sh = f(state_lru_h); scv = f(state_lru_conv); ssc = f(state_sconv)
    ckk = f(cache_swa_k).reshape(2, 128, 128, 256); cvv = f(cache_swa_v).reshape(2, 128, 128, 256)
    in_maps = []
    o_fl, _ = lay[("flags", 0)]
    zeros_xp = np.zeros((NSLOT * T, D), np.float32)
    for c in range(NCORES):
        b = c // 2
        lc = c % 2
        ns = slice(c * NS, (c + 1) * NS)
        put_layer(2, lc)
        pvc = pv.copy()
        pvc[:, o_fl + 0] = 1.0 - lc; pvc[:, o_fl + 1] = float(lc)
        for si in range(5):
            ff = 1.0 if si == lc else 0.0
            pvc[:, o_fl + 2 + si] = ff; pvc[:, o_fl + 7 + si] = 1.0 - ff
        bdc = bdh.copy(); bdc[2] = bdh[lc]
        if lc == 0:
            xpc = np.concatenate([x_prompt[b][:NPP * T], np.zeros((T, D), np.float32)], axis=0)
        else:
            xpc = zeros_xp
        in_maps.append({
            "xp": xpc, "xs": np.ascontiguousarray(x_sample[ns, 0, :]),
            "st_h": np.ascontiguousarray(sh[:, ns]), "st_conv": np.ascontiguousarray(scv[:, ns]), "st_sc": np.ascontiguousarray(ssc[:, ns]),
            "ck": np.ascontiguousarray(ckk[:, ns]), "cv": np.ascontiguousarray(cvv[:, ns]),
            "w_in": w_in, "w_out": w_out, "w_gu": ffn_w_gu, "w_dn": ffn_w_down,
            "wo_in": w_in[lc:lc + 1], "wo_out": w_out[lc:lc + 1], "wo_gu": ffn_w_gu[lc:lc + 1], "wo_dn": ffn_w_down[lc:lc + 1],
            "bd": bdc, "pv": pvc, "cs": cs_by_lc[lc], "msk": msk, "idn": idn,
        })
    if NPP not in _CACHE:
        _CACHE[NPP] = build(NPP)
    nc = _CACHE[NPP]
    if 'ncores' in _DBG:
        n_ = _DBG['ncores']
        res = run_bass_kernel_spmd(nc, in_maps[:n_], core_ids=list(range(n_)), trace=_DBG.get('trace', False))
        _DBG['res'] = res
        return res.results
    res = run_bass_kernel_spmd(nc, in_maps, core_ids=list(range(NCORES)))
    R = res.results
    y_prompt = np.stack([R[2 * b + 1]["yp"][T:T + SEQ] for b in range(4)]).reshape(4, SEQ, D)
    y_sample = np.concatenate([R[c]["ys"] for c in range(NCORES)], axis=0).reshape(128, 1, D)
    stk = lambda name, shp: np.stack([np.stack([R[2 * b][name][0], R[2 * b + 1][name][1]]) for b in range(4)], axis=1).reshape(shp)
    p_lru_h = stk("p_h", (2, 4, 512))
    p_lru_conv = stk("p_conv", (2, 4, 3, 512))
    p_swa_k = stk("p_k", (2, 4, 128, 4, 64))
    p_swa_v = stk("p_v", (2, 4, 128, 4, 64))
    p_sconv = stk("p_sc", (2, 4, 2, 512))
    cat = lambda name, shp: np.concatenate([R[c][name] for c in range(NCORES)], axis=1).reshape(shp)
    s_lru_h = cat("s_h", (2, 128, 512))
    s_lru_conv = cat("s_conv", (2, 128, 3, 512))
    s_swa_k = cat("s_k", (2, 128, 128, 4, 64))
    s_swa_v = cat("s_v", (2, 128, 128, 4, 64))
    s_sconv = cat("s_sc", (2, 128, 2, 512))
    outs = (y_prompt, y_sample, p_lru_h, p_lru_conv, p_swa_k, p_swa_v, p_sconv, s_lru_h, s_lru_conv, s_swa_k, s_swa_v, s_sconv)
    return tuple(np.ascontiguousarray(o, dtype=np.float32) for o in outs)
```

```python
import numpy as np
from contextlib import ExitStack
import concourse.bass as bass
import concourse.mybir as mybir
from concourse.bass_utils import run_bass_kernel_spmd

F32 = mybir.dt.float32
BF16 = mybir.dt.bfloat16
AF = mybir.ActivationFunctionType
ALU = mybir.AluOpType

D = 2048
KC = 16
T = 512
NS = 16
DFF = 5632
FC = 44
EPS = 1e-6
PAST = 8192
NCORES = 8
SEQ = 2048


def pv_layout():
    lay = {}
    off = 0
    for l in range(3):
        for name, n in (("nmix", 16), ("nffn", 16), ("ngrp", 16), ("lcw", 16), ("lcb", 4), ("ba", 4), ("bi", 4),
                        ("lam", 4), ("scw", 12), ("sink", 8), ("sinkall", 16)):
            lay[(name, l)] = (off, n)
            off += n
    lay[("nfin", 0)] = (off, 16)
    off += 16
    lay[("flags", 0)] = (off, 12)
    off += 12
    return lay, off


class _Stop(Exception):
    pass


class Trk:
    SEM_MAX = 3800

    def __init__(self, nc, es):
        self.nc = nc
        self.es = es
        self.E = {"pe": nc.tensor, "act": nc.scalar, "dve": nc.vector, "pool": nc.gpsimd, "sp": nc.sync}
        self.gen = {k: 0 for k in self.E}
        self.sem = {k: es.enter_context(nc.semaphore("pg_" + k + "_0")) for k in self.E}
        self.semname = {k: "pg_" + k + "_0" for k in self.E}
        self.cnt = {k: 0 for k in self.E}
        self.seen = {k: {} for k in self.E}
        self.lastw = {}
        self.readers = {}
        self.dsems = {}
        self.all_dsems = []
        self.pending_noinc = {k: False for k in self.E}

    def _wait(self, eng, ev):
        name, semh, val = ev
        if eng == "pe" and name.startswith("pg_pe_"):
            return
        if self.seen[eng].get(name, 0) >= val:
            return
        self.E[eng].wait_ge(semh, val)
        self.seen[eng][name] = val

    def _deps(self, eng, R, W):
        best = {}

        def add(ev):
            if ev[0] not in best or best[ev[0]][2] < ev[2]:
                best[ev[0]] = ev
        for r in R:
            if r in self.lastw:
                add(self.lastw[r])
        for w in W:
            if w in self.lastw:
                add(self.lastw[w])
            for ev in self.readers.get(w, {}).values():
                add(ev)
        for ev in best.values():
            self._wait(eng, ev)

    def _commit(self, ev, R, W):
        for r in R:
            d = self.readers.setdefault(r, {})
            if ev[0] not in d or d[ev[0]][2] < ev[2]:
                d[ev[0]] = ev
        for w in W:
            self.lastw[w] = ev
            self.readers[w] = {}

    def _roll(self, eng):
        if self.cnt[eng] >= self.SEM_MAX and not self.pending_noinc[eng]:
            self.gen[eng] += 1
            nm = f"pg_{eng}_{self.gen[eng]}"
            self.sem[eng] = self.es.enter_context(self.nc.semaphore(nm))
            self.semname[eng] = nm
            self.cnt[eng] = 0

    def op(self, eng, fn, R=(), W=(), inc=True):
        self._roll(eng)
        self._deps(eng, R, W)
        ins = fn(self.E[eng])
        if inc:
            self.cnt[eng] += 1
            ins.then_inc(self.sem[eng], 1)
            ev = (self.semname[eng], self.sem[eng], self.cnt[eng])
            self.pending_noinc[eng] = False
        else:
            ev = (self.semname[eng], self.sem[eng], self.cnt[eng] + 1)
            self.pending_noinc[eng] = True
        self._commit(ev, R, W)
        return ins

    def fence(self, eng, others):
        for o in others:
            self._wait(eng, (self.semname[o], self.sem[o], self.cnt[o] + (1 if self.pending_noinc[o] else 0)))

    def reserve(self, sem, n):
        if sem not in self.dsems or self.dsems[sem][1] + 16 * n > self.SEM_MAX:
            g = self.dsems[sem][3] + 1 if sem in self.dsems else 0
            nm = f"d_{sem}_{g}"
            self.dsems[sem] = [self.es.enter_context(self.nc.semaphore(nm)), 0, nm, g]
            self.all_dsems.append(self.dsems[sem])

    def dma(self, q, out, in_, R=(), W=(), sem="g", skip_deps=False, **kw):
        if not skip_deps:
            self._deps(q, R, W)
        self.reserve(sem, 1)
        ds = self.dsems[sem]
        ins = self.E[q].dma_start(out=out, in_=in_, **kw)
        ds[1] += 16
        ins.then_inc(ds[0], 16)
        ev = (ds[2], ds[0], ds[1])
        self._commit(ev, R, W)
        return ins

    def coll(self, fn, R=(), W=()):
        self._deps("pool", R, W)
        if "cc" not in self.dsems:
            self.dsems["cc"] = [self.es.enter_context(self.nc.semaphore("d_cc_0")), 0, "d_cc_0", 0]
            self.all_dsems.append(self.dsems["cc"])
        ds = self.dsems["cc"]
        ins = fn(self.E["pool"])
        ds[1] += 1
        ins.then_inc(ds[0], 1)
        ev = (ds[2], ds[0], ds[1])
        self._commit(ev, R, W)
        return ins

    def wait_all_dma(self, eng):
        for ds in self.all_dsems:
            if ds[1]:
                self.E[eng].wait_ge(ds[0], ds[1])


def build(NPP):
    lay, NV = pv_layout()
    NSLOT = NPP + 1
    TOK = NSLOT * T
    nc = bass.Bass("TRN2", target_bir_lowering=False)

    def din(name, shape, dt=F32):
        return nc.dram_tensor(name, list(shape), dt, kind="ExternalInput").ap()

    def dout(name, shape, dt=F32):
        return nc.dram_tensor(name, list(shape), dt, kind="ExternalOutput").ap()

    xp = din("xp", [TOK, D]); xs = din("xs", [NS, D])
    st_h = din("st_h", [2, NS, 512]); st_conv = din("st_conv", [2, NS, 3, 512]); st_sc = din("st_sc", [2, NS, 2, 512])
    ck = din("ck", [2, NS, 128, 256]); cv = din("cv", [2, NS, 128, 256])
    w_in = din("w_in", [2, D, 4096]); w_out = din("w_out", [2, D, D]); w_gu = din("w_gu", [2, D, 2 * DFF]); w_dn = din("w_dn", [2, DFF, D])
    wo_in = din("wo_in", [1, D, 4096]); wo_out = din("wo_out", [1, D, D]); wo_gu = din("wo_gu", [1, D, 2 * DFF]); wo_dn = din("wo_dn", [1, DFF, D])
    WIN = [w_in[0], w_in[1], wo_in[0]]; WOUT = [w_out[0], w_out[1], wo_out[0]]
    WGU = [w_gu[0], w_gu[1], wo_gu[0]]; WDN = [w_dn[0], w_dn[1], wo_dn[0]]
    bd = din("bd", [3, 128, 8, 128])
    ibs = [nc.dram_tensor(f"ib{j}", [128, 2048], F32) for j in range(4)]
    obs = [nc.dram_tensor(f"ob{j}", [256, 2048], F32) for j in range(4)]
    pvd = din("pv", [128, NV])
    csd = din("cs", [NSLOT + 1, 128, 2, T])
    mskd = din("msk", [128, 2, 128])
    idnd = din("idn", [128, 128])

    yp = dout("yp", [TOK, D]); ys = dout("ys", [NS, D])
    p_h = dout("p_h", [2, 512]); p_conv = dout("p_conv", [2, 3, 512]); p_k = dout("p_k", [2, 128, 256]); p_v = dout("p_v", [2, 128, 256]); p_sc = dout("p_sc", [2, 2, 512])
    s_h = dout("s_h", [2, NS, 512]); s_conv = dout("s_conv", [2, NS, 3, 512]); s_k = dout("s_k", [2, NS, 128, 256]); s_v = dout("s_v", [2, NS, 128, 256]); s_sc = dout("s_sc", [2, NS, 2, 512])

    es = ExitStack()
    with es:
        trk = Trk(nc, es)
        op = trk.op

        def sb(name, shape, dt):
            return es.enter_context(nc.sbuf_tensor(name, list(shape), dt))

        def ps(name, shape, dt):
            return es.enter_context(nc.psum_tensor(name, list(shape), dt))

        X = sb("X", [128, KC, T], F32)
        XN = sb("XN", [128, KC, T], BF16)
        H = sb("H", [128, FC, T], BF16)
        MX = H[:, 0:16, :]
        QK = H[:, 16:26, :]
        KS = H[:, 26:34, :].rearrange("p c (a f) -> p (c a) f", f=256)
        VS2 = H[:, 34:42, :].rearrange("p c (a f) -> p (c a) f", f=256)
        NWS = 3
        WS = [sb(f"WS{i}", [128, 8192], BF16) for i in range(NWS)]
        NTMP = 9
        TMP = [sb(f"TMP{i}", [128, T + 4], F32) for i in range(NTMP)]
        SQ = [sb(f"SQ{i}", [128, T], BF16) for i in range(3)]
        PT = [sb(f"PT{i}", [128, 512], BF16) for i in range(4)]
        STG = XN[:, :, :].rearrange("p c t -> p (c t)").bitcast(F32)
        PV = sb("PV", [128, NV], F32)
        CS = sb("CS", [128, 2, T], F32)
        MSKF = sb("MSKF", [128, 2, 128], F32)
        MSK = sb("MSK", [128, 2, 4, 128], BF16)
        MSK0 = sb("MSK0", [128, 4, 128], BF16)
        IDN = sb("IDN", [128, 128], F32)
        IDNB = sb("IDNB", [128, 128], BF16)
        ONES = sb("ONES", [128, 128], BF16)
        BD = sb("BD", [128, 8, 128], F32)
        RS = [sb(f"RS{i}", [128, T], F32) for i in range(2)]
        SCL = sb("SCL", [128, 3, 2, 4], F32)
        SNK = sb("SNK", [128, 2, 4, 128], F32)
        EXS = sb("EXS", [128, 3, 8], F32)
        SNKS = sb("SNKS", [128, 2, NS, 16], F32)
        HC = sb("HC", [128, 3, 4], F32)
        CUX = sb("CUX", [128, 3, 4, 3], F32)
        CSC = sb("CSC", [128, 3, 4, 2], F32)
        KH = sb("KH", [128, 1, 2, 128 + T], BF16)
        VH = sb("VH", [128, 1, 5, 256], BF16)
        KF = [sb(f"KF{i}", [128, T], F32) for i in range(2)]
        ROW = sb("ROW", [128, 512], F32)
        UXS = sb("UXS", [128, 4, NS, 4], F32)
        GCS = sb("GCS", [128, 4, NS, 3], F32)
        H0S = sb("H0S", [128, 4, NS], F32)
        STS = sb("STS", [48, 512], F32)
        KTS = [sb(f"KTS{i}", [128, 2, 128], BF16) for i in range(2)]
        PSS = sb("PSS", [128, 256], BF16)
        RD = sb("RD", [128, 256], F32)

        PA = [ps(f"PA{i}", [128, 512], F32) for i in range(4)]
        PB = [ps(f"PB{i}", [128, 512], F32) for i in range(4)]

        wstate = {"i": 0}

        def wload(pieces, nm):
            i = wstate["i"]; wstate["i"] += 1
            slot = i % NWS
            trk.reserve(f"w{slot}", len(pieces))
            for pi_, (dst, src) in enumerate(pieces):
                trk.dma("pool", dst(WS[slot]), src, R=(), W=(f"W{slot}",), sem=f"w{slot}", skip_deps=(pi_ > 0))
            return WS[slot], f"W{slot}"

        def wv(slot, kc, n):
            return slot[:, 0:kc * n].rearrange("p (k n) -> p k n", n=n)

        pa_i = {"i": 0}

        def next_pa():
            i = pa_i["i"] % 4; pa_i["i"] += 1
            return PA[i], f"PA{i}"

        tmp_i = {"i": 0}

        def next_tmp():
            i = tmp_i["i"] % NTMP; tmp_i["i"] += 1
            return TMP[i], f"TMP{i}"

        sq_i = {"i": 0}

        def next_sq():
            i = sq_i["i"] % 3; sq_i["i"] += 1
            return SQ[i], f"SQ{i}"

        def colp(l, name, c=0):
            o, n = lay[(name, l)]
            return PV[:, o + c:o + c + 1]

        trk.dma("sp", PV[:], pvd, W=("PV",), sem="c_pv")
        trk.dma("sp", IDN[:], idnd, W=("IDN",), sem="c_idn")
        trk.dma("sp", MSKF[:], mskd, W=("MSKF",), sem="c_msk")
        op("dve", lambda e: e.tensor_copy(out=IDNB[:], in_=IDN[:]), R=("IDN",), W=("IDNB",))
        op("dve", lambda e: e.memset(ONES[:], 1.0), W=("ONES",))
        for m in range(2):
            for j in range(4):
                op("dve", lambda e, m=m, j=j: e.tensor_copy(out=MSK[:, m, j, :], in_=MSKF[:, m, :]), R=("MSKF",), W=("MSK",))
        for t_, nm in ((HC, "HC"), (CUX, "CUX"), (CSC, "CSC"), (KH, "KH"), (VH, "VH")):
            op("dve", lambda e, t_=t_: e.memset(t_[:], 0.0), W=(nm,))
        for l in range(3):
            o, n = lay[("lam", l)]
            tt, tn = next_tmp()
            op("act", lambda e: e.activation(out=tt[:, 0:4], in_=PV[:, o:o + 4], func=AF.Exp, scale=-1.0), R=("PV",), W=(tn,))
            op("act", lambda e: e.activation(out=tt[:, 4:8], in_=tt[:, 0:4], func=AF.Ln, bias=1.0), R=(tn,), W=(tn,))
            op("dve", lambda e: e.tensor_scalar(out=SCL[:, l, 0, :], in0=tt[:, 4:8], scalar1=-8.0, scalar2=None, op0=ALU.mult), R=(tn,), W=("SCL",))
            op("dve", lambda e: e.tensor_scalar(out=SCL[:, l, 1, :], in0=tt[:, 4:8], scalar1=-16.0, scalar2=None, op0=ALU.mult), R=(tn,), W=("SCL",))
            o, n = lay[("sink", l)]
            op("act", lambda e: e.activation(out=EXS[:, l, :], in_=PV[:, o:o + 8], func=AF.Exp), R=("PV",), W=("EXS",))
            o, n = lay[("sinkall", l)]
            op("act", lambda e: e.activation(out=tt[:, 16:32], in_=PV[:, o:o + 16], func=AF.Exp), R=("PV",), W=(tn,))
            for n_ in range(NS):
                if l < 2:
                    op("dve", lambda e, n_=n_: e.tensor_copy(out=SNKS[:, l, n_, :], in_=tt[:, 16:32]), R=(tn,), W=("SNKS",))

        def rmsnorm_to_xn(l, gname, N):
            pb, pbn = PB[0], "PB0"
            for c in range(KC):
                sq, sqn = next_sq()
                op("act", lambda e: e.activation(out=sq[:, 0:N], in_=X[:, c, 0:N], func=AF.Square), R=(f"X{c}",), W=(sqn,))
                op("pe", lambda e: e.matmul(pb[:, 0:N], lhsT=ONES[:, :], rhs=sq[:, 0:N], start=(c == 0), stop=(c == KC - 1)),
                   R=(sqn, "ONES"), W=(pbn,), inc=True)
            rs = RS[0]
            op("dve", lambda e: e.tensor_scalar(out=rs[:, 0:N], in0=pb[:, 0:N], scalar1=1.0 / D, scalar2=EPS, op0=ALU.mult, op1=ALU.add), R=(pbn,), W=("RS0",))
            op("act", lambda e: e.activation(out=rs[:, 0:N], in_=rs[:, 0:N], func=AF.Sqrt), R=("RS0",), W=("RS0",))
            op("dve", lambda e: e.reciprocal(out=rs[:, 0:N], in_=rs[:, 0:N]), R=("RS0",), W=("RS0",))
            for c in range(KC):
                op("dve", lambda e: e.scalar_tensor_tensor(out=XN[:, c, 0:N], in0=X[:, c, 0:N], scalar=colp(l, gname, c), in1=rs[:, 0:N], op0=ALU.mult, op1=ALU.mult),
                   R=(f"X{c}", "RS0", "PV"), W=(f"XN{c}", f"STG{c // 8}"))

        def gemm_chunk(lhsT_of_k, wtok, nk, rhs_of_k, rtok_of_k, N, last_in_block=False):
            pa, pan = next_pa()
            for k in range(nk):
                op("pe", lambda e: e.matmul(pa[:, 0:N], lhsT=lhsT_of_k(k), rhs=rhs_of_k(k), start=(k == 0), stop=(k == nk - 1)),
                   R=(wtok, rtok_of_k(k)), W=(pan,), inc=(k == nk - 1))
            return pa, pan

        def group_norm_finish(pb, pbn, l, c0, nch, N, rsi):
            rs = RS[rsi]; rsn = f"RS{rsi}"
            op("dve", lambda e: e.tensor_scalar(out=rs[:, 0:N], in0=pb[:, 0:N], scalar1=1.0 / (nch * 128), scalar2=EPS, op0=ALU.mult, op1=ALU.add), R=(pbn,), W=(rsn,))
            op("act", lambda e: e.activation(out=rs[:, 0:N], in_=rs[:, 0:N], func=AF.Sqrt), R=(rsn,), W=(rsn,))
            op("dve", lambda e: e.reciprocal(out=rs[:, 0:N], in_=rs[:, 0:N]), R=(rsn,), W=(rsn,))
            for c in range(c0, c0 + nch):
                op("dve", lambda e: e.scalar_tensor_tensor(out=MX[:, c, 0:N], in0=MX[:, c, 0:N], scalar=colp(l, "ngrp", c), in1=rs[:, 0:N], op0=ALU.mult, op1=ALU.mult),
                   R=(f"MX{c}", rsn, "PV"), W=(f"MX{c}",))

        def sq_accum(pb, pbn, c, first, last, N):
            sq, sqn = next_sq()
            op("act", lambda e: e.activation(out=sq[:, 0:N], in_=MX[:, c, 0:N], func=AF.Square), R=(f"MX{c}",), W=(sqn,))
            op("pe", lambda e: e.matmul(pb[:, 0:N], lhsT=ONES[:, :], rhs=sq[:, 0:N], start=first, stop=last), R=(sqn, "ONES"), W=(pbn,), inc=True)

        def transpose_out(src_ap, rows, cols, dst_dram, rtoks, tag, view=None, multi=None):
            op("pe", lambda e: e.transpose(PB[3][0:cols, 0:rows], src_ap, IDN[0:rows, 0:rows]), R=tuple(rtoks) + ("IDN",), W=("PB3",))
            op("act", lambda e: e.activation(out=ROW[0:cols, 0:rows], in_=PB[3][0:cols, 0:rows], func=AF.Copy), R=("PB3",), W=("ROW",))
            if multi is not None:
                trk.reserve("oROW", len(multi))
                for (d_ap, r0, r1) in multi:
                    trk.dma("sp", d_ap, ROW[r0:r1, 0:rows], R=("ROW",), W=(tag,), sem="oROW")
                return
            srcv = ROW[0:cols, 0:rows]
            if view is not None:
                srcv = view(srcv)
            trk.dma("sp", dst_dram, srcv, R=("ROW",), W=(tag,), sem="oROW")

        def layer(l, pi, kind, pout, ffi, xfinal_cb=None):
            N = T if kind == "p" else NS
            last_pass = pout is not None
            o_fl, _ = lay[("flags", 0)]
            if kind == "p":
                for kc in range(2):
                    for j in range(4):
                        op("dve", lambda e: e.tensor_scalar(out=SNK[:, kc, j, :], in0=IDN[:, :], scalar1=0.0, scalar2=EXS[:, l, kc * 4 + j:kc * 4 + j + 1], op0=ALU.mult, op1=ALU.add), R=("EXS", "IDN"), W=("SNK",))
            rmsnorm_to_xn(l, "nmix", N)
            xr = lambda k: XN[:, k, 0:N]
            xt = lambda k: f"XN{k}"
            if kind == "s":
                trk.dma("sp", STS[0:48, :], st_conv[l].rearrange("n k f -> (n k) f"), W=("STS",), sem="st")
                for c in range(4):
                    op("pe", lambda e: e.transpose(PB[3][:, 0:48], STS[0:48, c * 128:(c + 1) * 128], IDN[0:48, 0:48]), R=("STS", "IDN"), W=("PB3",))
                    op("act", lambda e: e.activation(out=UXS[:, c, :, 0:3], in_=PB[3][:, 0:48].rearrange("p (n k) -> p n k", k=3), func=AF.Copy), R=("PB3",), W=("UXS",))
                trk.dma("sp", STS[0:32, :], st_sc[l].rearrange("n k f -> (n k) f"), R=(), W=("STS",), sem="st")
                for c in range(4):
                    op("pe", lambda e: e.transpose(PB[3][:, 0:32], STS[0:32, c * 128:(c + 1) * 128], IDN[0:32, 0:32]), R=("STS", "IDN"), W=("PB3",))
                    op("act", lambda e: e.activation(out=GCS[:, c, :, 0:2], in_=PB[3][:, 0:32].rearrange("p (n k) -> p n k", k=2), func=AF.Copy), R=("PB3",), W=("GCS",))
                trk.dma("sp", STS[0:16, :], st_h[l], W=("STS",), sem="st")
                for c in range(4):
                    op("pe", lambda e: e.transpose(PB[3][:, 0:16], STS[0:16, c * 128:(c + 1) * 128], IDN[0:16, 0:16]), R=("STS", "IDN"), W=("PB3",))
                    op("act", lambda e: e.activation(out=H0S[:, c, :], in_=PB[3][:, 0:16], func=AF.Copy), R=("PB3",), W=("H0S",))
                trk.dma("sp", s_k[l][:, 0:127, :], ck[l][:, 1:128, :], W=(f"skA{l}",), sem=f"o2a{l}")
                trk.dma("sp", s_v[l][:, 0:127, :], cv[l][:, 1:128, :], W=(f"svA{l}",), sem=f"o2b{l}")
                trk.dma("sp", s_conv[l][:, 0:2, :], st_conv[l][:, 1:3, :], W=(f"scvA{l}",), sem=f"o2c{l}")
                trk.dma("sp", s_sc[l][:, 0:1, :], st_sc[l][:, 1:2, :], W=(f"sscA{l}",), sem=f"o2d{l}")

            if (_DBG.get("stop") == "norm" and kind == _DBG.get("stop_kind", "p") and l == _DBG.get("stop_layer", 0)):
                raise _Stop()
            trk.dma("sp", BD[:], bd[l], W=("BD",), sem="bd")
            pbl, pbln = PB[1], "PB1"
            pbs, pbsn = PB[2], "PB2"
            deferred = []

            def flush_deferred():
                while deferred:
                    deferred.pop(0)()

            def lru_chunk(c):
                cc = 0
                slot, wtok = wload([
                    (lambda s: wv(s, KC, 512)[:, :, 0:128], WIN[l][:, 1536 + c * 128:1536 + (c + 1) * 128].rearrange("(k p) m -> p k m", p=128)),
                    (lambda s: wv(s, KC, 512)[:, :, 256:384], WIN[l][:, 2048 + c * 128:2048 + (c + 1) * 128].rearrange("(k p) m -> p k m", p=128)),
                ], "lru")
                sv = wv(slot, KC, 512)
                pa, pan = gemm_chunk(lambda k: sv[:, k, cc * 128:(cc + 1) * 128], wtok, KC, xr, xt, N)
                ux, uxn = next_tmp()
                xc, xcn = next_tmp()
                if kind == "p":
                    op("dve", lambda e: e.tensor_copy(out=ux[:, 0:3], in_=CUX[:, l, c, :]), R=("CUX",), W=(uxn,))
                    op("act", lambda e: e.activation(out=ux[:, 3:3 + T], in_=pa[:, 0:T], func=AF.Copy), R=(pan,), W=(uxn,))
                    op("dve", lambda e: e.tensor_copy(out=CUX[:, l, c, :], in_=ux[:, T:T + 3]), R=(uxn,), W=("CUX",))
                    tap = lambda k: ux[:, k:k + T]
                else:
                    op("act", lambda e: e.activation(out=UXS[:, c, :, 3], in_=pa[:, 0:NS], func=AF.Copy), R=(pan,), W=("UXS",))
                    uxn = "UXS"
                    tap = lambda k: UXS[:, c, :, k]
                o_w, _ = lay[("lcw", l)]
                op("dve", lambda e: e.tensor_scalar(out=xc[:, 0:N], in0=tap(3), scalar1=PV[:, o_w + 12 + c:o_w + 13 + c], scalar2=colp(l, "lcb", c), op0=ALU.mult, op1=ALU.add), R=(uxn, "PV"), W=(xcn,))
                for k in (2, 1, 0):
                    op("dve", lambda e, k=k: e.scalar_tensor_tensor(out=xc[:, 0:N], in0=tap(k), scalar=PV[:, o_w + k * 4 + c:o_w + k * 4 + c + 1], in1=xc[:, 0:N], op0=ALU.mult, op1=ALU.add), R=(uxn, "PV", xcn), W=(xcn,))
                if kind == "s":
                    transpose_out(UXS[:, c, :, 3], 128, NS, s_conv[l][:, 2, c * 128:(c + 1) * 128], ("UXS",), f"scvB{l}")
                pa2, pa2n = gemm_chunk(lambda k: sv[:, k, 256:384], wtok, KC, xr, xt, N)
                ug, ugn = next_tmp()
                op("act", lambda e: e.activation(out=ug[:, 0:N], in_=pa2[:, 0:N], func=AF.Copy), R=(pa2n,), W=(ugn,))
                gates = []
                for gi, bname in ((0, "ba"), (1, "bi")):
                    pg, pgn = next_pa()
                    op("pe", lambda e: e.matmul(pg[:, 0:N], lhsT=BD[:, gi * 4 + c, :], rhs=xc[:, 0:N], start=True, stop=True), R=("BD", xcn), W=(pgn,))
                    gt, gtn = next_tmp()
                    op("act", lambda e: e.activation(out=gt[:, 0:N], in_=pg[:, 0:N], func=AF.Sigmoid, bias=colp(l, bname, c)), R=(pgn, "PV"), W=(gtn,))
                    gates.append((gt, gtn))
                (rg, rgn), (ig, ign) = gates
                a_, an = next_tmp()
                m_, mn = next_tmp()
                op("act", lambda e: e.activation(out=a_[:, 0:N], in_=rg[:, 0:N], func=AF.Exp, scale=SCL[:, l, 0, c:c + 1]), R=(rgn, "SCL"), W=(an,))
                op("act", lambda e: e.activation(out=m_[:, 0:N], in_=rg[:, 0:N], func=AF.Exp, scale=SCL[:, l, 1, c:c + 1]), R=(rgn, "SCL"), W=(mn,))
                op("dve", lambda e: e.tensor_scalar(out=m_[:, 0:N], in0=m_[:, 0:N], scalar1=-1.0, scalar2=1.0, op0=ALU.mult, op1=ALU.add), R=(mn,), W=(mn,))
                op("act", lambda e: e.activation(out=m_[:, 0:N], in_=m_[:, 0:N], func=AF.Sqrt), R=(mn,), W=(mn,))
                if ffi is not None:
                    op("dve", lambda e: e.tensor_scalar(out=m_[:, 0:1], in0=m_[:, 0:1], scalar1=PV[:, o_fl + 7 + ffi:o_fl + 8 + ffi], scalar2=PV[:, o_fl + 2 + ffi:o_fl + 3 + ffi], op0=ALU.mult, op1=ALU.add), R=(mn, "PV"), W=(mn,))
                op("dve", lambda e: e.tensor_tensor(out=ig[:, 0:N], in0=ig[:, 0:N], in1=xc[:, 0:N], op=ALU.mult), R=(ign, xcn), W=(ign,))
                op("dve", lambda e: e.tensor_tensor(out=ig[:, 0:N], in0=ig[:, 0:N], in1=m_[:, 0:N], op=ALU.mult), R=(ign, mn), W=(ign,))
                hs, hsn = rg, rgn
                if kind == "p":
                    op("dve", lambda e: e.tensor_tensor_scan(out=hs[:, 0:T], data0=a_[:, 0:T], data1=ig[:, 0:T], initial=HC[:, l, c:c + 1], op0=ALU.mult, op1=ALU.add), R=(an, ign, "HC", rgn), W=(hsn,))
                    op("dve", lambda e: e.tensor_copy(out=HC[:, l, c:c + 1], in_=hs[:, T - 1:T]), R=(hsn,), W=("HC",))
                else:
                    op("dve", lambda e: e.tensor_tensor(out=hs[:, 0:N], in0=a_[:, 0:N], in1=H0S[:, c, :], op=ALU.mult), R=(an, "H0S", rgn), W=(hsn,))
                    op("dve", lambda e: e.tensor_tensor(out=hs[:, 0:N], in0=hs[:, 0:N], in1=ig[:, 0:N], op=ALU.add), R=(hsn, ign), W=(hsn,))
                    transpose_out(hs[:, 0:NS], 128, NS, s_h[l][:, c * 128:(c + 1) * 128], (hsn,), f"shB{l}")
                flush_deferred()
                g2, g2n = next_tmp()
                op("dve", lambda e: e.tensor_tensor(out=g2[:, 0:N], in0=ug[:, 0:N], in1=ug[:, 0:N], op=ALU.mult), R=(ugn,), W=(g2n,))
                op("dve", lambda e: e.tensor_scalar(out=g2[:, 0:N], in0=g2[:, 0:N], scalar1=0.044715, scalar2=1.0, op0=ALU.mult, op1=ALU.add), R=(g2n,), W=(g2n,))
                op("dve", lambda e: e.tensor_tensor(out=g2[:, 0:N], in0=g2[:, 0:N], in1=ug[:, 0:N], op=ALU.mult), R=(g2n, ugn), W=(g2n,))
                op("act", lambda e: e.activation(out=g2[:, 0:N], in_=g2[:, 0:N], func=AF.Sigmoid, scale=1.5957691216057308), R=(g2n,), W=(g2n,))
                op("dve", lambda e: e.tensor_tensor(out=g2[:, 0:N], in0=g2[:, 0:N], in1=ug[:, 0:N], op=ALU.mult), R=(g2n, ugn), W=(g2n,))
                op("dve", lambda e: e.tensor_tensor(out=MX[:, 8 + c, 0:N], in0=g2[:, 0:N], in1=hs[:, 0:N], op=ALU.mult), R=(g2n, hsn), W=(f"MX{8 + c}",))
                deferred.append(lambda: sq_accum(pbl, pbln, 8 + c, c == 0, c == 3, N))

            def sc_chunk(c):
                slot, wtok = wload([
                    (lambda s: wv(s, KC, 384)[:, :, 0:128], WIN[l][:, 3072 + c * 128:3072 + (c + 1) * 128].rearrange("(k p) m -> p k m", p=128)),
                    (lambda s: wv(s, KC, 384)[:, :, 128:256], WIN[l][:, 3584 + c * 128:3584 + (c + 1) * 128].rearrange("(k p) m -> p k m", p=128)),
                    (lambda s: wv(s, KC, 384)[:, :, 256:384], WIN[l][:, 2560 + c * 128:2560 + (c + 1) * 128].rearrange("(k p) m -> p k m", p=128)),
                ], "sc")
                sv = wv(slot, KC, 384)
                pa, pan = gemm_chunk(lambda k: sv[:, k, 0:128], wtok, KC, xr, xt, N)
                uc, ucn = next_tmp()
                op("act", lambda e: e.activation(out=uc[:, 0:N], in_=pa[:, 0:N], func=AF.Copy), R=(pan,), W=(ucn,))
                flush_deferred()
                pa, pan = gemm_chunk(lambda k: sv[:, k, 128:256], wtok, KC, xr, xt, N)
                gc, gcn = next_tmp()
                if kind == "p":
                    op("dve", lambda e: e.tensor_copy(out=gc[:, 0:2], in_=CSC[:, l, c, :]), R=("CSC",), W=(gcn,))
                    op("dve", lambda e: e.tensor_tensor(out=gc[:, 2:2 + T], in0=pa[:, 0:T], in1=uc[:, 0:T], op=ALU.mult), R=(pan, ucn), W=(gcn,))
                    op("dve", lambda e: e.tensor_copy(out=CSC[:, l, c, :], in_=gc[:, T:T + 2]), R=(gcn,), W=("CSC",))
                    tap = lambda k: gc[:, k:k + T]
                else:
                    op("dve", lambda e: e.tensor_tensor(out=GCS[:, c, :, 2], in0=pa[:, 0:NS], in1=uc[:, 0:NS], op=ALU.mult), R=(pan, ucn), W=("GCS",))
                    gcn = "GCS"
                    tap = lambda k: GCS[:, c, :, k]
                    transpose_out(GCS[:, c, :, 2], 128, NS, s_sc[l][:, 1, c * 128:(c + 1) * 128], ("GCS",), f"sscB{l}")
                y_, yn = next_tmp()
                o_w, _ = lay[("scw", l)]
                op("dve", lambda e: e.tensor_scalar(out=y_[:, 0:N], in0=tap(2), scalar1=PV[:, o_w + 8 + c:o_w + 9 + c], scalar2=None, op0=ALU.mult), R=(gcn, "PV"), W=(yn,))
                for k in (1, 0):
                    op("dve", lambda e, k=k: e.scalar_tensor_tensor(out=y_[:, 0:N], in0=tap(k), scalar=PV[:, o_w + k * 4 + c:o_w + k * 4 + c + 1], in1=y_[:, 0:N], op0=ALU.mult, op1=ALU.add), R=(gcn, "PV", yn), W=(yn,))
                pa, pan = gemm_chunk(lambda k: sv[:, k, 256:384], wtok, KC, xr, xt, N)
                op("dve", lambda e: e.tensor_tensor(out=MX[:, 12 + c, 0:N], in0=pa[:, 0:N], in1=y_[:, 0:N], op=ALU.mult), R=(pan, yn), W=(f"MX{12 + c}",))
                deferred.append(lambda: sq_accum(pbs, pbsn, 12 + c, c == 0, c == 3, N))

            def kv_blk():
                pieces = [(lambda s: wv(s, KC, 512)[:, :, 256:512], WIN[l][:, 1280:1536].rearrange("(k p) m -> p k m", p=128))]
                for h_ in range(2):
                    for g_ in range(4):
                        c_src = 1024 + g_ * 64 + h_ * 32
                        c_dst = h_ * 128 + g_ * 32
                        pieces.append((lambda s, c_dst=c_dst: wv(s, KC, 512)[:, :, c_dst:c_dst + 32], WIN[l][:, c_src:c_src + 32].rearrange("(k p) m -> p k m", p=128)))
                slot, wtok = wload(pieces, "kv")
                sv = wv(slot, KC, 512)
                if kind == "p":
                    for pl in range(2):
                        op("dve", lambda e, pl=pl: e.tensor_copy(out=KH[:, 0, pl, 0:128], in_=KH[:, 0, pl, T:T + 128]), R=("KH",), W=("KH",))
                    op("dve", lambda e: e.tensor_copy(out=VH[:, 0, 0, :], in_=VH[:, 0, 4, :]), R=("VH",), W=("VH",))
                kf = []
                for pl in range(2):
                    pa, pan = gemm_chunk(lambda k: sv[:, k, pl * 128:(pl + 1) * 128], wtok, KC, xr, xt, N)
                    t_, tn = next_tmp()
                    op("act", lambda e: e.activation(out=t_[:, 0:N], in_=pa[:, 0:N], func=AF.Copy), R=(pan,), W=(tn,))
                    kf.append((t_, tn))
                rope(kf, [(KF[0], "KF0"), (KF[1], "KF1")], N)
                if kind == "p":
                    for pl in range(2):
                        op("act", lambda e, pl=pl: e.activation(out=KH[:, 0, pl, 128:128 + T], in_=KF[pl][:, 0:T], func=AF.Copy), R=(f"KF{pl}",), W=("KH",))
                    if last_pass:
                        for pl in range(2):
                            transpose_out(KF[pl][:, T - 128:T], 128, 128, pout["k"].rearrange("t (g h i) -> t h g i", h=2, i=32)[:, pl, :, :], (f"KF{pl}",), "pk", view=lambda a: a.rearrange("t (g i) -> t g i", i=32))
                else:
                    for pl in range(2):
                        op("act", lambda e, pl=pl: e.activation(out=QK[:, 8 + pl, 0:NS], in_=KF[pl][:, 0:NS], func=AF.Copy), R=(f"KF{pl}",), W=(f"QK{8 + pl}",))
                        transpose_out(KF[pl][:, 0:NS], 128, NS, s_k[l].rearrange("n s (g h i) -> n s h g i", h=2, i=32)[:, 127, pl, :, :], (f"KF{pl}",), f"skB{l}", view=lambda a: a.rearrange("t (g i) -> t g i", i=32))
                ntb = 4 if kind == "p" else 1
                for tb in range(ntb):
                    M = 128 if kind == "p" else NS
                    pa, pan = next_pa()
                    for k in range(KC):
                        op("pe", lambda e: e.matmul(pa[0:M, 0:256], lhsT=XN[:, k, tb * 128:tb * 128 + M], rhs=sv[:, k, 256:512], start=(k == 0), stop=(k == KC - 1)),
                           R=(wtok, f"XN{k}"), W=(pan,), inc=(k == KC - 1))
                    if kind == "p":
                        op("act", lambda e: e.activation(out=VH[:, 0, 1 + tb, :], in_=pa[:, 0:256], func=AF.Copy), R=(pan,), W=("VH",))
                        if last_pass and tb == 3:
                            op("act", lambda e: e.activation(out=ROW[:, 0:256], in_=pa[:, 0:256], func=AF.Copy), R=(pan,), W=("ROW",))
                            trk.dma("sp", pout["v"], ROW[:, 0:256], R=("ROW",), W=("pv",), sem="oROW")
                    else:
                        op("act", lambda e: e.activation(out=ROW[0:NS, 0:256], in_=pa[0:NS, 0:256], func=AF.Copy), R=(pan,), W=("ROW",))
                        trk.dma("sp", s_v[l][:, 127, :], ROW[0:NS, 0:256], R=("ROW",), W=(f"svB{l}",), sem="oROW")


            def q_blk(jb):
                pieces = []
                for jj_ in range(2):
                    for h_ in range(2):
                        for g_ in range(4):
                            c_src = (g_ * 4 + jb * 2 + jj_) * 64 + h_ * 32
                            c_dst = (jj_ * 2 + h_) * 128 + g_ * 32
                            pieces.append((lambda s, c_dst=c_dst: wv(s, KC, 512)[:, :, c_dst:c_dst + 32], WIN[l][:, c_src:c_src + 32].rearrange("(k p) m -> p k m", p=128)))
                slot, wtok = wload(pieces, "q")
                sv = wv(slot, KC, 512)
                for jj in range(2):
                    j = jb * 2 + jj
                    qf = []
                    for pl in range(2):
                        pa, pan = gemm_chunk(lambda k: sv[:, k, (jj * 2 + pl) * 128:(jj * 2 + pl + 1) * 128], wtok, KC, xr, xt, N)
                        t_, tn = next_tmp()
                        op("act", lambda e: e.activation(out=t_[:, 0:N], in_=pa[:, 0:N], func=AF.Copy), R=(pan,), W=(tn,))
                        qf.append((t_, tn))
                    rope(qf, [(QK[:, j * 2, :], f"QK{j * 2}"), (QK[:, j * 2 + 1, :], f"QK{j * 2 + 1}")], N)


            for c in range(4):
                lru_chunk(c)
                sc_chunk(c)
                if c == 0:
                    q_blk(0)
                elif c == 1:
                    q_blk(1)
                elif c == 2:
                    kv_blk()
            flush_deferred()
            group_norm_finish(pbl, pbln, l, 8, 4, N, 1)
            group_norm_finish(pbs, pbsn, l, 12, 4, N, 0)
            if kind == "p" and last_pass:
                transpose_out(HC[:, l, :], 128, 4, pout["h"].rearrange("(c f) -> c f", f=128), ("HC",), "ph")
                transpose_out(CUX[:, l, :, :], 128, 12, None, ("CUX",), "pconv", multi=[(pout["conv"][:, c * 128:(c + 1) * 128], c * 3, c * 3 + 3) for c in range(4)])
                transpose_out(CSC[:, l, :, :], 128, 8, None, ("CSC",), "psc", multi=[(pout["sc"][:, c * 128:(c + 1) * 128], c * 2, c * 2 + 2) for c in range(4)])

            if (_DBG.get("stop") == "sc" and kind == _DBG.get("stop_kind", "p") and l == _DBG.get("stop_layer", 0)):
                raise _Stop()
            pba, pban = PB[1], "PB1"
            if kind == "p":
                attention_prompt(l, ffi)
            else:
                attention_sample(l)
            for c in range(8):
                sq_accum(pba, pban, c, c == 0, c == 7, N)
            group_norm_finish(pba, pban, l, 0, 8, N, 1)

            if (_DBG.get("stop") == "attn" and kind == _DBG.get("stop_kind", "p") and l == _DBG.get("stop_layer", 0)):
                raise _Stop()
            for cb in range(4):
                cs_ = slice(cb * 512, (cb + 1) * 512)
                pieces = []
                for half in range(2):
                    for kc2 in range(2):
                        r0 = kc2 * 512 + half * 256
                        pieces.append((lambda s, half=half, kc2=kc2: wv(s, KC, 512)[half * 64:(half + 1) * 64, kc2 * 4:(kc2 + 1) * 4, :],
                                       WOUT[l][r0:r0 + 256, cs_].rearrange("(j d) m -> d j m", d=64)))
                pieces.append((lambda s: wv(s, KC, 512)[:, 8:16, :], WOUT[l][1024:2048, cs_].rearrange("(k p) m -> p k m", p=128)))
                slot, wtok = wload(pieces, "out")
                sv = wv(slot, KC, 512)
                for mm in range(4):
                    m = cb * 4 + mm
                    ko = lambda k: (k + 8) % 16
                    pa, pan = gemm_chunk(lambda k: sv[:, ko(k), mm * 128:(mm + 1) * 128], wtok, KC, lambda k: MX[:, ko(k), 0:N], lambda k: f"MX{ko(k)}", N)
                    op("dve", lambda e: e.tensor_tensor(out=X[:, m, 0:N], in0=pa[:, 0:N], in1=X[:, m, 0:N], op=ALU.add), R=(pan, f"X{m}"), W=(f"X{m}",))

            if (_DBG.get("stop") == "out" and kind == _DBG.get("stop_kind", "p") and l == _DBG.get("stop_layer", 0)):
                raise _Stop()
            rmsnorm_to_xn(l, "nffn", N)
            trk.fence("dve", ["pe", "act"])
            trk.fence("act", ["pe", "dve"])
            for fb in range(11):
                for part in range(2):
                    c0_ = part * DFF + fb * 512
                    slot, wtok = wload([(lambda s: wv(s, KC, 512), WGU[l][:, c0_:c0_ + 512].rearrange("(k p) m -> p k m", p=128))], "gu")
                    sv = wv(slot, KC, 512)
                    for cc in range(4):
                        fc = fb * 4 + cc
                        pa, pan = gemm_chunk(lambda k: sv[:, k, cc * 128:(cc + 1) * 128], wtok, KC, xr, xt, N)
                        if part == 0:
                            op("act", lambda e: e.activation(out=H[:, fc, 0:N], in_=pa[:, 0:N], func=AF.Silu), R=(pan,), W=(f"H{fc}",))
                        else:
                            op("dve", lambda e: e.tensor_tensor(out=H[:, fc, 0:N], in0=pa[:, 0:N], in1=H[:, fc, 0:N], op=ALU.mult), R=(pan, f"H{fc}"), W=(f"H{fc}",))
            dn_issued = {}

            def dn_issue(cg_, kh_):
                if cg_ < 8 and (cg_, kh_) not in dn_issued:
                    dn_issued[(cg_, kh_)] = wload([(lambda s: wv(s, 22, 256), WDN[l][kh_ * 2816:(kh_ + 1) * 2816, cg_ * 256:(cg_ + 1) * 256].rearrange("(k p) m -> p k m", p=128))], "dn")
            for cg in range(8):
                accs = None
                for kh in range(2):
                    dn_issue(cg, kh)
                    slot, wtok = dn_issued[(cg, kh)]
                    sv = wv(slot, 22, 256)
                    if kh == 0:
                        accs = [next_pa(), next_pa()]
                    for mm in range(2):
                        pa, pan = accs[mm]
                        for k in range(22):
                            kk = kh * 22 + k
                            op("pe", lambda e: e.matmul(pa[:, 0:N], lhsT=sv[:, k, mm * 128:(mm + 1) * 128], rhs=H[:, kk, 0:N], start=(kk == 0), stop=(kk == FC - 1)),
                               R=(wtok, f"H{kk}"), W=(pan,), inc=(k == 21))
                for mm in range(2):
                    m = cg * 2 + mm
                    pa, pan = accs[mm]
                    op("dve", lambda e: e.tensor_tensor(out=X[:, m, 0:N], in0=pa[:, 0:N], in1=X[:, m, 0:N], op=ALU.add), R=(pan, f"X{m}"), W=(f"X{m}",))
                if xfinal_cb is not None and cg % 2 == 1:
                    dn_issue(cg + 1, 0); dn_issue(cg + 1, 1)
                    xfinal_cb(cg // 2)

        def rope(src, dst, N):
            (a_, an), (b_, bn) = src
            (da, dan), (db, dbn) = dst
            t1, t1n = next_tmp(); t2, t2n = next_tmp()
            cos = CS[:, 0, 0:N]; sin = CS[:, 1, 0:N]
            op("dve", lambda e: e.tensor_tensor(out=t1[:, 0:N], in0=a_[:, 0:N], in1=cos, op=ALU.mult), R=(an, "CS"), W=(t1n,))
            op("dve", lambda e: e.tensor_tensor(out=t2[:, 0:N], in0=b_[:, 0:N], in1=sin, op=ALU.mult), R=(bn, "CS"), W=(t2n,))
            op("dve", lambda e: e.tensor_tensor(out=da[:, 0:N], in0=t1[:, 0:N], in1=t2[:, 0:N], op=ALU.subtract), R=(t1n, t2n), W=(dan,))
            op("dve", lambda e: e.tensor_tensor(out=t1[:, 0:N], in0=b_[:, 0:N], in1=cos, op=ALU.mult), R=(bn, "CS"), W=(t1n,))
            op("dve", lambda e: e.tensor_tensor(out=t2[:, 0:N], in0=a_[:, 0:N], in1=sin, op=ALU.mult), R=(an, "CS"), W=(t2n,))
            op("dve", lambda e: e.tensor_tensor(out=db[:, 0:N], in0=t1[:, 0:N], in1=t2[:, 0:N], op=ALU.add), R=(t1n, t2n), W=(dbn,))

        pt_i = {"i": 0}

        def attention_prompt(l, ffi):
            QKv = QK[:, 0:8, :].rearrange("p (j pl) t -> p pl j t", pl=2)
            num, numn = PB[2], "PB2"
            den, denn = PB[3], "PB3"
            units = [(qb, kc, half) for qb in range(4) for kc in range(2) for half in range(2)]

            def qk_phase(u):
                qb, kc, half = u
                g = kc * 2 + half
                blocks = [(qb, 1), (qb + 1, 0)]
                pts = []
                for bi, (kb, mk) in enumerate(blocks):
                    sps = PB[bi]; spsn = f"PB{bi}"
                    for pl in range(2):
                        op("pe", lambda e: e.matmul(sps[:, :].rearrange("p (j q) -> p j q", q=128), lhsT=KH[32 * g:32 * g + 32, 0, pl, kb * 128:(kb + 1) * 128],
                                                    rhs=QKv[32 * g:32 * g + 32, pl, :, qb * 128:(qb + 1) * 128], start=(pl == 0), stop=(pl == 1), tile_position=(32 * g, 0)),
                           R=("KH",) + tuple(f"QK{j * 2 + pl}" for j in range(4)), W=(spsn,), inc=(pl == 1))
                    i = pt_i["i"] % 4; pt_i["i"] += 1
                    pt, ptn = PT[i], f"PT{i}"
                    op("act", lambda e: e.activation(out=pt[:, :], in_=sps[:, :], func=AF.Exp, scale=0.125), R=(spsn,), W=(ptn,))
                    mk_ap = MSK0[:, :, :] if (mk == 1 and qb == 0 and ffi is not None) else MSK[:, mk, :, :]
                    op("dve", lambda e: e.tensor_tensor(out=pt[:, :], in0=pt[:, :], in1=mk_ap.rearrange("p j q -> p (j q)"), op=ALU.mult), R=(ptn, "MSK", "MSK0"), W=(ptn,))
                    pts.append((pt, ptn, kb))
                return pts

            def pv_phase(u, pts):
                qb, kc, half = u
                g = kc * 2 + half
                for bi, (pt, ptn, kb) in enumerate(pts):
                    st = (bi == 0); sp_ = (bi == len(pts) - 1)
                    op("pe", lambda e: e.matmul(num[half * 64:(half + 1) * 64, :], lhsT=VH[:, 0, kb, g * 64:(g + 1) * 64], rhs=pt[:, :], start=st, stop=sp_),
                       R=(ptn, "VH"), W=(numn,), inc=False)
                    op("pe", lambda e: e.matmul(den[half * 64:(half + 1) * 64, :], lhsT=ONES[:, 0:64], rhs=pt[:, :], start=st, stop=sp_),
                       R=(ptn, "ONES"), W=(denn,), inc=True)
                if half == 1:
                    dn, dnn = next_tmp()
                    op("dve", lambda e: e.tensor_tensor(out=dn[:, 0:512], in0=den[:, :], in1=SNK[:, kc, :, :].rearrange("p j q -> p (j q)"), op=ALU.add), R=(denn, "SNK"), W=(dnn,))
                    op("dve", lambda e: e.reciprocal(out=dn[:, 0:512], in_=dn[:, 0:512]), R=(dnn,), W=(dnn,))
                    op("dve", lambda e: e.tensor_tensor(out=MX[:, kc * 4:kc * 4 + 4, qb * 128:(qb + 1) * 128], in0=num[:, :].rearrange("p (j q) -> p j q", q=128),
                                                        in1=dn[:, 0:512].rearrange("p (j q) -> p j q", q=128), op=ALU.mult),
                       R=(numn, dnn), W=tuple(f"MX{kc * 4 + j}" for j in range(4)))

            prev = None
            for u in units:
                pts = qk_phase(u)
                if prev is not None:
                    pv_phase(*prev)
                prev = (u, pts)
            pv_phase(*prev)

        def attention_sample(l):
            QKv = QK[:, 0:8, :].rearrange("p (j pl) t -> p pl j t", pl=2)
            first = True
            trk.reserve("ks", 8)
            for h_ in range(2):
                for g_ in range(4):
                    trk.dma("pool", KS[:, :, h_ * 128 + g_ * 32:h_ * 128 + g_ * 32 + 32], s_k[l][:, :, g_ * 64 + h_ * 32:g_ * 64 + h_ * 32 + 32].rearrange("n s i -> s n i"),
                            R=(f"skA{l}", f"skB{l}"), W=("KS",), sem="ks", skip_deps=not first)
                    first = False
            trk.dma("pool", VS2[:], s_v[l].rearrange("n s f -> s n f"), R=(f"svA{l}", f"svB{l}"), W=("VS2",), sem="vs")
            for n in range(NS):
                kt, ktn = KTS[n % 2], f"KTS{n % 2}"
                ptb = PB[3][:, 0:128].bitcast(BF16).rearrange("p (pl s) -> p pl s", pl=2)
                for pl in range(2):
                    op("pe", lambda e: e.transpose(ptb[:, pl, :], KS[:, n, pl * 128:(pl + 1) * 128], IDNB[:, :]), R=("KS", "IDNB"), W=("PB3",))
                op("act", lambda e: e.activation(out=kt[:, :, :], in_=ptb, func=AF.Copy), R=("PB3",), W=(ktn,))
                for g in range(4):
                    for pl in range(2):
                        op("pe", lambda e: e.matmul(PA[g][:, n * 4:n * 4 + 4], lhsT=kt[32 * g:32 * g + 32, pl, :], rhs=QKv[32 * g:32 * g + 32, pl, :, n],
                                                    start=(pl == 0), stop=(pl == 1), tile_position=(32 * g, 0)),
                           R=(ktn,) + tuple(f"QK{j * 2 + pl}" for j in range(4)), W=(f"PA{g}",), inc=(pl == 1))
            pssv = PSS[:, :].rearrange("p (n g j) -> p n g j", g=4, j=4)
            for g in range(4):
                op("act", lambda e: e.activation(out=pssv[:, :, g, :], in_=PA[g][:, 0:64].rearrange("p (n j) -> p n j", j=4), func=AF.Exp, scale=0.125), R=(f"PA{g}",), W=("PSS",))
            den, denn = PB[1], "PB1"
            op("pe", lambda e: e.matmul(den[:, 0:256], lhsT=ONES[:, :], rhs=PSS[:, :], start=True, stop=True), R=("PSS", "ONES"), W=(denn,))
            op("dve", lambda e: e.tensor_tensor(out=RD[:, :], in0=den[:, 0:256], in1=SNKS[:, l, :, :].rearrange("p n h -> p (n h)"), op=ALU.add), R=(denn, "SNKS"), W=("RD",))
            op("dve", lambda e: e.reciprocal(out=RD[:, :], in_=RD[:, :]), R=("RD",), W=("RD",))
            num, numn = PB[2], "PB2"
            numv = num[:, 0:128].rearrange("p (kc j n) -> p kc j n", kc=2, j=4)
            for n in range(NS):
                for g in range(4):
                    hf = g % 2
                    op("pe", lambda e: e.matmul(numv[hf * 64:(hf + 1) * 64, g // 2, :, n], lhsT=VS2[:, n, g * 64:(g + 1) * 64], rhs=PSS[:, n * 16 + g * 4:n * 16 + g * 4 + 4], start=True, stop=True),
                       R=("VS2", "PSS"), W=(numn,), inc=(n == NS - 1 and g == 3))
            rdv = RD[:, :].rearrange("p (n g j) -> p g j n", g=4, j=4)
            for hf in range(2):
                op("dve", lambda e: e.tensor_tensor(out=MX[hf * 64:(hf + 1) * 64, 0:8, 0:NS].rearrange("p (kc j) n -> p kc j n", kc=2),
                                                    in0=numv[hf * 64:(hf + 1) * 64, :, :, :],
                                                    in1=rdv[hf * 64:(hf + 1) * 64, hf::2, :, :], op=ALU.mult),
                   R=(numn, "RD"), W=tuple(f"MX{c}" for c in range(8)))

        def run_pass(pi, kind):
            N = T if kind == "p" else NS
            ci = pi if kind == "p" else NSLOT
            o_fl, _ = lay[("flags", 0)]
            trk.dma("sp", CS[:], csd[ci], W=("CS",), sem="cs")
            ntb = 4 if kind == "p" else 1
            for tb in range(ntb):
                rows = 128 if kind == "p" else NS
                src = xp[pi * T + tb * 128: pi * T + tb * 128 + 128, :] if kind == "p" else xs
                so = (tb % 2) * D; stn = f"STG{tb % 2}"
                trk.dma("sp", STG[0:rows, so:so + D], src, W=(stn,), sem=f"xin{tb % 2}")
                for c4 in range(4):
                    pb = PB[c4 % 2]; pbn = f"PB{c4 % 2}"
                    for cc in range(4):
                        c = c4 * 4 + cc
                        op("pe", lambda e: e.transpose(pb[:, cc * 128:cc * 128 + rows], STG[0:rows, so + c * 128:so + (c + 1) * 128], IDN[0:rows, 0:rows]), R=(stn, "IDN"), W=(pbn,), inc=(cc == 3))
                    op("act", lambda e: e.activation(out=X[:, c4 * 4:c4 * 4 + 4, tb * 128:tb * 128 + rows], in_=pb[:, :].rearrange("p (c t) -> p c t", t=128)[:, :, 0:rows], func=AF.Copy),
                       R=(pbn,), W=tuple(f"X{c4 * 4 + cc}" for cc in range(4)))
            if kind == "p" and pi >= 1:
                for j in range(4):
                    so = (j % 2) * D; stn = f"STG{j % 2}"
                    trk.dma("sp", STG[:, so:so + 2048], obs[j][0:128, :], R=(f"ob{j}",), W=(stn,), sem=f"xin{j % 2}")
                    for cc in range(4):
                        c = j * 4 + cc
                        op("dve", lambda e: e.scalar_tensor_tensor(out=X[:, c, :], in0=STG[:, so + cc * 512:so + (cc + 1) * 512], scalar=PV[:, o_fl + 1:o_fl + 2], in1=X[:, c, :], op0=ALU.mult, op1=ALU.add),
                           R=(stn, "PV", f"X{c}"), W=(f"X{c}",))
            if (_DBG.get("stop") == "load" and kind == _DBG.get("stop_kind", "p")):
                raise _Stop()
            if kind == "p":
                ffi = pi if pi <= 1 else None
                if ffi is not None:
                    for j in range(4):
                        op("dve", lambda e: e.tensor_scalar(out=MSK0[:, j, :], in0=MSK[:, 1, j, :], scalar1=PV[:, o_fl + 7 + ffi:o_fl + 8 + ffi], scalar2=None, op0=ALU.mult), R=("MSK", "PV"), W=("MSK0",))
                pout = None
                if pi >= NPP - 1:
                    w_ = pi - (NPP - 1)
                    pout = {"h": p_h[w_], "conv": p_conv[w_], "k": p_k[w_], "v": p_v[w_], "sc": p_sc[w_]}
                def exchange(j):
                    trk.dma("sp", ibs[j][:, :], X[:, j * 4:(j + 1) * 4, :].rearrange("p c t -> p (c t)"), R=tuple(f"X{j * 4 + cc}" for cc in range(4)), W=(f"ib{j}",), sem=f"xex{j}")
                    trk.coll(lambda e: e.collective_compute("AllGather", ALU.bypass, replica_groups=[[2 * i_, 2 * i_ + 1] for i_ in range(_DBG.get('ncores', NCORES) // 2)],
                                                            ins=[ibs[j].ap().opt()], outs=[obs[j].ap().opt()]), R=(f"ib{j}",), W=(f"ob{j}",))
                layer(2, pi, kind, pout, ffi, xfinal_cb=(exchange if pi < NSLOT - 1 else None))
                if pi == 0:
                    fa = PV[:, o_fl:o_fl + 1]
                    for t_, nm in ((HC[:, 2, :], "HC"), (CUX[:, 2, :, :], "CUX"), (CSC[:, 2, :, :], "CSC"), (KH[:, 0, :, :], "KH"), (VH[:, 0, :, :], "VH")):
                        op("dve", lambda e, t_=t_: e.tensor_scalar(out=t_, in0=t_, scalar1=fa, scalar2=None, op0=ALU.mult), R=(nm, "PV"), W=(nm,))
            else:
                for l in range(2):
                    layer(l, pi, kind, None, None)
            if (_DBG.get("stop") == "final" and kind == _DBG.get("stop_kind", "p")):
                raise _Stop()
            pb, pbn = PB[0], "PB0"
            for c in range(KC):
                sq, sqn = next_sq()
                op("act", lambda e: e.activation(out=sq[:, 0:N], in_=X[:, c, 0:N], func=AF.Square), R=(f"X{c}",), W=(sqn,))
                op("pe", lambda e: e.matmul(pb[:, 0:N], lhsT=ONES[:, :], rhs=sq[:, 0:N], start=(c == 0), stop=(c == KC - 1)), R=(sqn, "ONES"), W=(pbn,), inc=True)
            rs = RS[0]
            op("dve", lambda e: e.tensor_scalar(out=rs[:, 0:N], in0=pb[:, 0:N], scalar1=1.0 / D, scalar2=EPS, op0=ALU.mult, op1=ALU.add), R=(pbn,), W=("RS0",))
            op("act", lambda e: e.activation(out=rs[:, 0:N], in_=rs[:, 0:N], func=AF.Sqrt), R=("RS0",), W=("RS0",))
            op("dve", lambda e: e.reciprocal(out=rs[:, 0:N], in_=rs[:, 0:N]), R=("RS0",), W=("RS0",))
            o_f, _ = lay[("nfin", 0)]
            for c in range(KC):
                op("dve", lambda e: e.scalar_tensor_tensor(out=X[:, c, 0:N], in0=X[:, c, 0:N], scalar=PV[:, o_f + c:o_f + c + 1], in1=rs[:, 0:N], op0=ALU.mult, op1=ALU.mult),
                   R=(f"X{c}", "RS0", "PV"), W=(f"X{c}",))
            for tb in range(ntb):
                rows = 128 if kind == "p" else NS
                so = (tb % 2) * D; stn = f"STG{tb % 2}"
                for c4 in range(4):
                    pb = PB[c4 % 2]; pbn = f"PB{c4 % 2}"
                    for cc in range(4):
                        c = c4 * 4 + cc
                        op("pe", lambda e: e.transpose(pb[0:rows, cc * 128:(cc + 1) * 128], X[:, c, tb * 128:tb * 128 + rows], IDN[:, :]), R=(f"X{c}", "IDN"), W=(pbn,), inc=(cc == 3))
                    op("act", lambda e: e.activation(out=STG[0:rows, so + c4 * 512:so + (c4 + 1) * 512], in_=pb[0:rows, :], func=AF.Copy), R=(pbn,), W=(stn,))
                dst = yp[pi * T + tb * 128: pi * T + tb * 128 + 128, :] if kind == "p" else ys
                trk.dma("sp", dst, STG[0:rows, so:so + D], R=(stn,), W=("yout",), sem=f"oSTG{tb % 2}")

        try:
            for pi in range(NSLOT):
                run_pass(pi, "p")
            if not _DBG.get("nosample"):
                run_pass(0, "s")
        except _Stop:
            pass
        trk.wait_all_dma("sp")
    return nc


_CACHE = {}
_DBG = {}


def kernel(x_prompt, x_sample, state_lru_h, state_lru_conv, cache_swa_k, cache_swa_v, state_sconv,
           norm_mix, w_in, norm_grp, w_out, lru_conv_w, lru_conv_b, lru_w_a, lru_b_a, lru_w_i, lru_b_i,
           lru_lambda, sc_conv_w, attn_sinks, norm_ffn, ffn_w_gu, ffn_w_down, norm_final):
    f = lambda a: np.ascontiguousarray(np.asarray(a, dtype=np.float32))
    x_prompt = f(x_prompt); x_sample = f(x_sample)
    NPP = _DBG.get('npp', SEQ // T)
    lay, NV = pv_layout()
    pv = np.zeros((128, NV), np.float32)

    def put(name, l, arr2d):
        o, n = lay[(name, l)]
        assert arr2d.shape == (128, n), (name, arr2d.shape)
        pv[:, o:o + n] = arr2d
    fm = lambda v: f(v).reshape(-1, 128).T
    def put_layer(l, ls):
        put("nmix", l, fm(norm_mix[ls])); put("nffn", l, fm(norm_ffn[ls]))
        g = f(norm_grp[ls])
        ga = g[:1024].reshape(2, 2, 4, 64)
        ga = ga.transpose(1, 3, 0, 2).reshape(128, 8)
        put("ngrp", l, np.concatenate([ga, fm(g[1024:1536]), fm(g[1536:2048])], axis=1))
        cw = f(lru_conv_w[ls])
        put("lcw", l, np.concatenate([fm(cw[k]) for k in range(4)], axis=1))
        put("lcb", l, fm(lru_conv_b[ls])); put("ba", l, fm(lru_b_a[ls])); put("bi", l, fm(lru_b_i[ls])); put("lam", l, fm(lru_lambda[ls]))
        sw = f(sc_conv_w[ls])
        put("scw", l, np.concatenate([fm(sw[k]) for k in range(3)], axis=1))
        sk = f(attn_sinks[ls]).reshape(2, 2, 4)
        sk2 = np.repeat(sk.transpose(1, 0, 2).reshape(2, 1, 8), 64, axis=1).reshape(128, 8)
        put("sink", l, sk2)
        put("sinkall", l, np.repeat(f(attn_sinks[ls])[None, :], 128, axis=0))
    put_layer(0, 0); put_layer(1, 1)
    put("nfin", 0, fm(norm_final))
    bdh = np.zeros((3, 128, 8, 128), np.float32)
    for l in range(2):
        for gi, wsrc in enumerate((lru_w_a, lru_w_i)):
            wl = f(wsrc[l])
            for c in range(4):
                bdh[l, 0:64, gi * 4 + c, 0:64] = wl[2 * c]
                bdh[l, 64:128, gi * 4 + c, 64:128] = wl[2 * c + 1]
    half = 32
    inv = (10000.0 ** (-np.arange(half, dtype=np.float32) / half)).astype(np.float32)
    NSLOT = NPP + 1

    def cs_tables(lc):
        cs = np.zeros((NSLOT + 1, 128, 2, T), np.float32)
        for si in range(NSLOT + 1):
            if si < NSLOT:
                pss = min(max(si - lc, 0), NPP - 1)
                pos = (np.arange(T) + pss * T).astype(np.float32)
            else:
                pos = np.full(T, float(PAST), np.float32)
            ang = pos[None, :] * inv[:, None]
            cs[si, :, 0, :] = np.tile(np.cos(ang).astype(np.float32), (4, 1))
            cs[si, :, 1, :] = np.tile(np.sin(ang).astype(np.float32), (4, 1))
        return cs
    cs_by_lc = [cs_tables(0), cs_tables(1)]
    sidx = np.arange(128)[:, None]; qidx = np.arange(128)[None, :]
    msk = np.stack([(sidx <= qidx), (sidx > qidx)], axis=1).astype(np.float32)
    idn = np.eye(128, dtype=np.float32)
    w_in = f(w_in); w_out = f(w_out); ffn_w_gu = f(ffn_w_gu); ffn_w_down = f(ffn_w_down)
    sh = f(state_lru_h); scv = f(state_lru_conv); ssc = f(state_sconv)
    ckk = f(cache_swa_k).reshape(2, 128, 128, 256); cvv = f(cache_swa_v).reshape(2, 128, 128, 256)
    in_maps = []
    o_fl, _ = lay[("flags", 0)]
    zeros_xp = np.zeros((NSLOT * T, D), np.float32)
    for c in range(NCORES):
        b = c // 2
        lc = c % 2
        ns = slice(c * NS, (c + 1) * NS)
        put_layer(2, lc)
        pvc = pv.copy()
        pvc[:, o_fl + 0] = 1.0 - lc; pvc[:, o_fl + 1] = float(lc)
        for si in range(5):
            ff = 1.0 if si == lc else 0.0
            pvc[:, o_fl + 2 + si] = ff; pvc[:, o_fl + 7 + si] = 1.0 - ff
        bdc = bdh.copy(); bdc[2] = bdh[lc]
        if lc == 0:
            xpc = np.concatenate([x_prompt[b][:NPP * T], np.zeros((T, D), np.float32)], axis=0)
        else:
            xpc = zeros_xp
        in_maps.append({
            "xp": xpc, "xs": np.ascontiguousarray(x_sample[ns, 0, :]),
            "st_h": np.ascontiguousarray(sh[:, ns]), "st_conv": np.ascontiguousarray(scv[:, ns]), "st_sc": np.ascontiguousarray(ssc[:, ns]),
            "ck": np.ascontiguousarray(ckk[:, ns]), "cv": np.ascontiguousarray(cvv[:, ns]),
            "w_in": w_in, "w_out": w_out, "w_gu": ffn_w_gu, "w_dn": ffn_w_down,
            "wo_in": w_in[lc:lc + 1], "wo_out": w_out[lc:lc + 1], "wo_gu": ffn_w_gu[lc:lc + 1], "wo_dn": ffn_w_down[lc:lc + 1],
            "bd": bdc, "pv": pvc, "cs": cs_by_lc[lc], "msk": msk, "idn": idn,
        })
    if NPP not in _CACHE:
        _CACHE[NPP] = build(NPP)
    nc = _CACHE[NPP]
    if 'ncores' in _DBG:
        n_ = _DBG['ncores']
        res = run_bass_kernel_spmd(nc, in_maps[:n_], core_ids=list(range(n_)), trace=_DBG.get('trace', False))
        _DBG['res'] = res
        return res.results
    res = run_bass_kernel_spmd(nc, in_maps, core_ids=list(range(NCORES)))
    R = res.results
    y_prompt = np.stack([R[2 * b + 1]["yp"][T:T + SEQ] for b in range(4)]).reshape(4, SEQ, D)
    y_sample = np.concatenate([R[c]["ys"] for c in range(NCORES)], axis=0).reshape(128, 1, D)
    stk = lambda name, shp: np.stack([np.stack([R[2 * b][name][0], R[2 * b + 1][name][1]]) for b in range(4)], axis=1).reshape(shp)
    p_lru_h = stk("p_h", (2, 4, 512))
    p_lru_conv = stk("p_conv", (2, 4, 3, 512))
    p_swa_k = stk("p_k", (2, 4, 128, 4, 64))
    p_swa_v = stk("p_v", (2, 4, 128, 4, 64))
    p_sconv = stk("p_sc", (2, 4, 2, 512))
    cat = lambda name, shp: np.concatenate([R[c][name] for c in range(NCORES)], axis=1).reshape(shp)
    s_lru_h = cat("s_h", (2, 128, 512))
    s_lru_conv = cat("s_conv", (2, 128, 3, 512))
    s_swa_k = cat("s_k", (2, 128, 128, 4, 64))
    s_swa_v = cat("s_v", (2, 128, 128, 4, 64))
    s_sconv = cat("s_sc", (2, 128, 2, 512))
    outs = (y_prompt, y_sample, p_lru_h, p_lru_conv, p_swa_k, p_swa_v, p_sconv, s_lru_h, s_lru_conv, s_swa_k, s_swa_v, s_sconv)
    return tuple(np.ascontiguousarray(o, dtype=np.float32) for o in outs)
```

```python
import numpy as np
from contextlib import ExitStack
import concourse.bass as bass
import concourse.mybir as mybir
from concourse.bass_utils import run_bass_kernel_spmd

F32 = mybir.dt.float32
BF16 = mybir.dt.bfloat16
AF = mybir.ActivationFunctionType
ALU = mybir.AluOpType

D = 2048
KC = 16
T = 512
NS = 16
DFF = 5632
FC = 44
EPS = 1e-6
PAST = 8192
NCORES = 8
SEQ = 2048


def pv_layout():
    lay = {}
    off = 0
    for l in range(3):
        for name, n in (("nmix", 16), ("nffn", 16), ("ngrp", 16), ("lcw", 16), ("lcb", 4), ("ba", 4), ("bi", 4),
                        ("lam", 4), ("scw", 12), ("sink", 8), ("sinkall", 16)):
            lay[(name, l)] = (off, n)
            off += n
    lay[("nfin", 0)] = (off, 16)
    off += 16
    lay[("flags", 0)] = (off, 12)
    off += 12
    return lay, off


class _Stop(Exception):
    pass


class Trk:
    SEM_MAX = 3800

    def __init__(self, nc, es):
        self.nc = nc
        self.es = es
        self.E = {"pe": nc.tensor, "act": nc.scalar, "dve": nc.vector, "pool": nc.gpsimd, "sp": nc.sync}
        self.gen = {k: 0 for k in self.E}
        self.sem = {k: es.enter_context(nc.semaphore("pg_" + k + "_0")) for k in self.E}
        self.semname = {k: "pg_" + k + "_0" for k in self.E}
        self.cnt = {k: 0 for k in self.E}
        self.seen = {k: {} for k in self.E}
        self.lastw = {}
        self.readers = {}
        self.dsems = {}
        self.all_dsems = []
        self.pending_noinc = {k: False for k in self.E}

    def _wait(self, eng, ev):
        name, semh, val = ev
        if eng == "pe" and name.startswith("pg_pe_"):
            return
        if self.seen[eng].get(name, 0) >= val:
            return
        self.E[eng].wait_ge(semh, val)
        self.seen[eng][name] = val

    def _deps(self, eng, R, W):
        best = {}

        def add(ev):
            if ev[0] not in best or best[ev[0]][2] < ev[2]:
                best[ev[0]] = ev
        for r in R:
            if r in self.lastw:
                add(self.lastw[r])
        for w in W:
            if w in self.lastw:
                add(self.lastw[w])
            for ev in self.readers.get(w, {}).values():
                add(ev)
        for ev in best.values():
            self._wait(eng, ev)

    def _commit(self, ev, R, W):
        for r in R:
            d = self.readers.setdefault(r, {})
            if ev[0] not in d or d[ev[0]][2] < ev[2]:
                d[ev[0]] = ev
        for w in W:
            self.lastw[w] = ev
            self.readers[w] = {}

    def _roll(self, eng):
        if self.cnt[eng] >= self.SEM_MAX and not self.pending_noinc[eng]:
            self.gen[eng] += 1
            nm = f"pg_{eng}_{self.gen[eng]}"
            self.sem[eng] = self.es.enter_context(self.nc.semaphore(nm))
            self.semname[eng] = nm
            self.cnt[eng] = 0

    def op(self, eng, fn, R=(), W=(), inc=True):
        self._roll(eng)
        self._deps(eng, R, W)
        ins = fn(self.E[eng])
        if inc:
            self.cnt[eng] += 1
            ins.then_inc(self.sem[eng], 1)
            ev = (self.semname[eng], self.sem[eng], self.cnt[eng])
            self.pending_noinc[eng] = False
        else:
            ev = (self.semname[eng], self.sem[eng], self.cnt[eng] + 1)
            self.pending_noinc[eng] = True
        self._commit(ev, R, W)
        return ins

    def fence(self, eng, others):
        for o in others:
            self._wait(eng, (self.semname[o], self.sem[o], self.cnt[o] + (1 if self.pending_noinc[o] else 0)))

    def reserve(self, sem, n):
        if sem not in self.dsems or self.dsems[sem][1] + 16 * n > self.SEM_MAX:
            g = self.dsems[sem][3] + 1 if sem in self.dsems else 0
            nm = f"d_{sem}_{g}"
            self.dsems[sem] = [self.es.enter_context(self.nc.semaphore(nm)), 0, nm, g]
            self.all_dsems.append(self.dsems[sem])

    def dma(self, q, out, in_, R=(), W=(), sem="g", skip_deps=False, **kw):
        if not skip_deps:
            self._deps(q, R, W)
        self.reserve(sem, 1)
        ds = self.dsems[sem]
        ins = self.E[q].dma_start(out=out, in_=in_, **kw)
        ds[1] += 16
        ins.then_inc(ds[0], 16)
        ev = (ds[2], ds[0], ds[1])
        self._commit(ev, R, W)
        return ins

    def coll(self, fn, R=(), W=()):
        self._deps("pool", R, W)
        if "cc" not in self.dsems:
            self.dsems["cc"] = [self.es.enter_context(self.nc.semaphore("d_cc_0")), 0, "d_cc_0", 0]
            self.all_dsems.append(self.dsems["cc"])
        ds = self.dsems["cc"]
        ins = fn(self.E["pool"])
        ds[1] += 1
        ins.then_inc(ds[0], 1)
        ev = (ds[2], ds[0], ds[1])
        self._commit(ev, R, W)
        return ins

    def wait_all_dma(self, eng):
        for ds in self.all_dsems:
            if ds[1]:
                self.E[eng].wait_ge(ds[0], ds[1])


def build(NPP):
    lay, NV = pv_layout()
    NSLOT = NPP + 1
    TOK = NSLOT * T
    nc = bass.Bass("TRN2", target_bir_lowering=False)

    def din(name, shape, dt=F32):
        return nc.dram_tensor(name, list(shape), dt, kind="ExternalInput").ap()

    def dout(name, shape, dt=F32):
        return nc.dram_tensor(name, list(shape), dt, kind="ExternalOutput").ap()

    xp = din("xp", [TOK, D]); xs = din("xs", [NS, D])
    st_h = din("st_h", [2, NS, 512]); st_conv = din("st_conv", [2, NS, 3, 512]); st_sc = din("st_sc", [2, NS, 2, 512])
    ck = din("ck", [2, NS, 128, 256]); cv = din("cv", [2, NS, 128, 256])
    w_in = din("w_in", [2, D, 4096]); w_out = din("w_out", [2, D, D]); w_gu = din("w_gu", [2, D, 2 * DFF]); w_dn = din("w_dn", [2, DFF, D])
    wo_in = din("wo_in", [1, D, 4096]); wo_out = din("wo_out", [1, D, D]); wo_gu = din("wo_gu", [1, D, 2 * DFF]); wo_dn = din("wo_dn", [1, DFF, D])
    WIN = [w_in[0], w_in[1], wo_in[0]]; WOUT = [w_out[0], w_out[1], wo_out[0]]
    WGU = [w_gu[0], w_gu[1], wo_gu[0]]; WDN = [w_dn[0], w_dn[1], wo_dn[0]]
    bd = din("bd", [3, 128, 8, 128])
    ibs = [nc.dram_tensor(f"ib{j}", [128, 2048], F32) for j in range(4)]
    obs = [nc.dram_tensor(f"ob{j}", [256, 2048], F32) for j in range(4)]
    pvd = din("pv", [128, NV])
    csd = din("cs", [NSLOT + 1, 128, 2, T])
    mskd = din("msk", [128, 2, 128])
    idnd = din("idn", [128, 128])

    yp = dout("yp", [TOK, D]); ys = dout("ys", [NS, D])
    p_h = dout("p_h", [2, 512]); p_conv = dout("p_conv", [2, 3, 512]); p_k = dout("p_k", [2, 128, 256]); p_v = dout("p_v", [2, 128, 256]); p_sc = dout("p_sc", [2, 2, 512])
    s_h = dout("s_h", [2, NS, 512]); s_conv = dout("s_conv", [2, NS, 3, 512]); s_k = dout("s_k", [2, NS, 128, 256]); s_v = dout("s_v", [2, NS, 128, 256]); s_sc = dout("s_sc", [2, NS, 2, 512])

    es = ExitStack()
    with es:
        trk = Trk(nc, es)
        op = trk.op

        def sb(name, shape, dt):
            return es.enter_context(nc.sbuf_tensor(name, list(shape), dt))

        def ps(name, shape, dt):
            return es.enter_context(nc.psum_tensor(name, list(shape), dt))

        X = sb("X", [128, KC, T], F32)
        XN = sb("XN", [128, KC, T], BF16)
        H = sb("H", [128, FC, T], BF16)
        MX = H[:, 0:16, :]
        QK = H[:, 16:26, :]
        KS = H[:, 26:34, :].rearrange("p c (a f) -> p (c a) f", f=256)
        VS2 = H[:, 34:42, :].rearrange("p c (a f) -> p (c a) f", f=256)
        NWS = 3
        WS = [sb(f"WS{i}", [128, 8192], BF16) for i in range(NWS)]
        NTMP = 9
        TMP = [sb(f"TMP{i}", [128, T + 4], F32) for i in range(NTMP)]
        SQ = [sb(f"SQ{i}", [128, T], BF16) for i in range(3)]
        PT = [sb(f"PT{i}", [128, 512], BF16) for i in range(4)]
        STG = XN[:, :, :].rearrange("p c t -> p (c t)").bitcast(F32)
        PV = sb("PV", [128, NV], F32)
        CS = sb("CS", [128, 2, T], F32)
        MSKF = sb("MSKF", [128, 2, 128], F32)
        MSK = sb("MSK", [128, 2, 4, 128], BF16)
        MSK0 = sb("MSK0", [128, 4, 128], BF16)
        IDN = sb("IDN", [128, 128], F32)
        IDNB = sb("IDNB", [128, 128], BF16)
        ONES = sb("ONES", [128, 128], BF16)
        BD = sb("BD", [128, 8, 128], F32)
        RS = [sb(f"RS{i}", [128, T], F32) for i in range(2)]
        SCL = sb("SCL", [128, 3, 2, 4], F32)
        SNK = sb("SNK", [128, 2, 4, 128], F32)
        EXS = sb("EXS", [128, 3, 8], F32)
        SNKS = sb("SNKS", [128, 2, NS, 16], F32)
        HC = sb("HC", [128, 3, 4], F32)
        CUX = sb("CUX", [128, 3, 4, 3], F32)
        CSC = sb("CSC", [128, 3, 4, 2], F32)
        KH = sb("KH", [128, 1, 2, 128 + T], BF16)
        VH = sb("VH", [128, 1, 5, 256], BF16)
        KF = [sb(f"KF{i}", [128, T], F32) for i in range(2)]
        ROW = sb("ROW", [128, 512], F32)
        UXS = sb("UXS", [128, 4, NS, 4], F32)
        GCS = sb("GCS", [128, 4, NS, 3], F32)
        H0S = sb("H0S", [128, 4, NS], F32)
        STS = sb("STS", [48, 512], F32)
        KTS = [sb(f"KTS{i}", [128, 2, 128], BF16) for i in range(2)]
        PSS = sb("PSS", [128, 256], BF16)
        RD = sb("RD", [128, 256], F32)

        PA = [ps(f"PA{i}", [128, 512], F32) for i in range(4)]
        PB = [ps(f"PB{i}", [128, 512], F32) for i in range(4)]

        wstate = {"i": 0}

        def wload(pieces, nm):
            i = wstate["i"]; wstate["i"] += 1
            slot = i % NWS
            trk.reserve(f"w{slot}", len(pieces))
            for pi_, (dst, src) in enumerate(pieces):
                trk.dma("pool", dst(WS[slot]), src, R=(), W=(f"W{slot}",), sem=f"w{slot}", skip_deps=(pi_ > 0))
            return WS[slot], f"W{slot}"

        def wv(slot, kc, n):
            return slot[:, 0:kc * n].rearrange("p (k n) -> p k n", n=n)

        pa_i = {"i": 0}

        def next_pa():
            i = pa_i["i"] % 4; pa_i["i"] += 1
            return PA[i], f"PA{i}"

        tmp_i = {"i": 0}

        def next_tmp():
            i = tmp_i["i"] % NTMP; tmp_i["i"] += 1
            return TMP[i], f"TMP{i}"

        sq_i = {"i": 0}

        def next_sq():
            i = sq_i["i"] % 3; sq_i["i"] += 1
            return SQ[i], f"SQ{i}"

        def colp(l, name, c=0):
            o, n = lay[(name, l)]
            return PV[:, o + c:o + c + 1]

        trk.dma("sp", PV[:], pvd, W=("PV",), sem="c_pv")
        trk.dma("sp", IDN[:], idnd, W=("IDN",), sem="c_idn")
        trk.dma("sp", MSKF[:], mskd, W=("MSKF",), sem="c_msk")
        op("dve", lambda e: e.tensor_copy(out=IDNB[:], in_=IDN[:]), R=("IDN",), W=("IDNB",))
        op("dve", lambda e: e.memset(ONES[:], 1.0), W=("ONES",))
        for m in range(2):
            for j in range(4):
                op("dve", lambda e, m=m, j=j: e.tensor_copy(out=MSK[:, m, j, :], in_=MSKF[:, m, :]), R=("MSKF",), W=("MSK",))
        for t_, nm in ((HC, "HC"), (CUX, "CUX"), (CSC, "CSC"), (KH, "KH"), (VH, "VH")):
            op("dve", lambda e, t_=t_: e.memset(t_[:], 0.0), W=(nm,))
        for l in range(3):
            o, n = lay[("lam", l)]
            tt, tn = next_tmp()
            op("act", lambda e: e.activation(out=tt[:, 0:4], in_=PV[:, o:o + 4], func=AF.Exp, scale=-1.0), R=("PV",), W=(tn,))
            op("act", lambda e: e.activation(out=tt[:, 4:8], in_=tt[:, 0:4], func=AF.Ln, bias=1.0), R=(tn,), W=(tn,))
            op("dve", lambda e: e.tensor_scalar(out=SCL[:, l, 0, :], in0=tt[:, 4:8], scalar1=-8.0, scalar2=None, op0=ALU.mult), R=(tn,), W=("SCL",))
            op("dve", lambda e: e.tensor_scalar(out=SCL[:, l, 1, :], in0=tt[:, 4:8], scalar1=-16.0, scalar2=None, op0=ALU.mult), R=(tn,), W=("SCL",))
            o, n = lay[("sink", l)]
            op("act", lambda e: e.activation(out=EXS[:, l, :], in_=PV[:, o:o + 8], func=AF.Exp), R=("PV",), W=("EXS",))
            o, n = lay[("sinkall", l)]
            op("act", lambda e: e.activation(out=tt[:, 16:32], in_=PV[:, o:o + 16], func=AF.Exp), R=("PV",), W=(tn,))
            for n_ in range(NS):
                if l < 2:
                    op("dve", lambda e, n_=n_: e.tensor_copy(out=SNKS[:, l, n_, :], in_=tt[:, 16:32]), R=(tn,), W=("SNKS",))

        def rmsnorm_to_xn(l, gname, N):
            pb, pbn = PB[0], "PB0"
            for c in range(KC):
                sq, sqn = next_sq()
                op("act", lambda e: e.activation(out=sq[:, 0:N], in_=X[:, c, 0:N], func=AF.Square), R=(f"X{c}",), W=(sqn,))
                op("pe", lambda e: e.matmul(pb[:, 0:N], lhsT=ONES[:, :], rhs=sq[:, 0:N], start=(c == 0), stop=(c == KC - 1)),
                   R=(sqn, "ONES"), W=(pbn,), inc=True)
            rs = RS[0]
            op("dve", lambda e: e.tensor_scalar(out=rs[:, 0:N], in0=pb[:, 0:N], scalar1=1.0 / D, scalar2=EPS, op0=ALU.mult, op1=ALU.add), R=(pbn,), W=("RS0",))
            op("act", lambda e: e.activation(out=rs[:, 0:N], in_=rs[:, 0:N], func=AF.Sqrt), R=("RS0",), W=("RS0",))
            op("dve", lambda e: e.reciprocal(out=rs[:, 0:N], in_=rs[:, 0:N]), R=("RS0",), W=("RS0",))
            for c in range(KC):
                op("dve", lambda e: e.scalar_tensor_tensor(out=XN[:, c, 0:N], in0=X[:, c, 0:N], scalar=colp(l, gname, c), in1=rs[:, 0:N], op0=ALU.mult, op1=ALU.mult),
                   R=(f"X{c}", "RS0", "PV"), W=(f"XN{c}", f"STG{c // 8}"))

        def gemm_chunk(lhsT_of_k, wtok, nk, rhs_of_k, rtok_of_k, N, last_in_block=False):
            pa, pan = next_pa()
            for k in range(nk):
                op("pe", lambda e: e.matmul(pa[:, 0:N], lhsT=lhsT_of_k(k), rhs=rhs_of_k(k), start=(k == 0), stop=(k == nk - 1)),
                   R=(wtok, rtok_of_k(k)), W=(pan,), inc=(k == nk - 1))
            return pa, pan

        def group_norm_finish(pb, pbn, l, c0, nch, N, rsi):
            rs = RS[rsi]; rsn = f"RS{rsi}"
            op("dve", lambda e: e.tensor_scalar(out=rs[:, 0:N], in0=pb[:, 0:N], scalar1=1.0 / (nch * 128), scalar2=EPS, op0=ALU.mult, op1=ALU.add), R=(pbn,), W=(rsn,))
            op("act", lambda e: e.activation(out=rs[:, 0:N], in_=rs[:, 0:N], func=AF.Sqrt), R=(rsn,), W=(rsn,))
            op("dve", lambda e: e.reciprocal(out=rs[:, 0:N], in_=rs[:, 0:N]), R=(rsn,), W=(rsn,))
            for c in range(c0, c0 + nch):
                op("dve", lambda e: e.scalar_tensor_tensor(out=MX[:, c, 0:N], in0=MX[:, c, 0:N], scalar=colp(l, "ngrp", c), in1=rs[:, 0:N], op0=ALU.mult, op1=ALU.mult),
                   R=(f"MX{c}", rsn, "PV"), W=(f"MX{c}",))

        def sq_accum(pb, pbn, c, first, last, N):
            sq, sqn = next_sq()
            op("act", lambda e: e.activation(out=sq[:, 0:N], in_=MX[:, c, 0:N], func=AF.Square), R=(f"MX{c}",), W=(sqn,))
            op("pe", lambda e: e.matmul(pb[:, 0:N], lhsT=ONES[:, :], rhs=sq[:, 0:N], start=first, stop=last), R=(sqn, "ONES"), W=(pbn,), inc=True)

        def transpose_out(src_ap, rows, cols, dst_dram, rtoks, tag, view=None, multi=None):
            op("pe", lambda e: e.transpose(PB[3][0:cols, 0:rows], src_ap, IDN[0:rows, 0:rows]), R=tuple(rtoks) + ("IDN",), W=("PB3",))
            op("act", lambda e: e.activation(out=ROW[0:cols, 0:rows], in_=PB[3][0:cols, 0:rows], func=AF.Copy), R=("PB3",), W=("ROW",))
            if multi is not None:
                trk.reserve("oROW", len(multi))
                for (d_ap, r0, r1) in multi:
                    trk.dma("sp", d_ap, ROW[r0:r1, 0:rows], R=("ROW",), W=(tag,), sem="oROW")
                return
            srcv = ROW[0:cols, 0:rows]
            if view is not None:
                srcv = view(srcv)
            trk.dma("sp", dst_dram, srcv, R=("ROW",), W=(tag,), sem="oROW")

        def layer(l, pi, kind, pout, ffi, xfinal_cb=None):
            N = T if kind == "p" else NS
            last_pass = pout is not None
            o_fl, _ = lay[("flags", 0)]
            if kind == "p":
                for kc in range(2):
                    for j in range(4):
                        op("dve", lambda e: e.tensor_scalar(out=SNK[:, kc, j, :], in0=IDN[:, :], scalar1=0.0, scalar2=EXS[:, l, kc * 4 + j:kc * 4 + j + 1], op0=ALU.mult, op1=ALU.add), R=("EXS", "IDN"), W=("SNK",))
            rmsnorm_to_xn(l, "nmix", N)
            xr = lambda k: XN[:, k, 0:N]
            xt = lambda k: f"XN{k}"
            if kind == "s":
                trk.dma("sp", STS[0:48, :], st_conv[l].rearrange("n k f -> (n k) f"), W=("STS",), sem="st")
                for c in range(4):
                    op("pe", lambda e: e.transpose(PB[3][:, 0:48], STS[0:48, c * 128:(c + 1) * 128], IDN[0:48, 0:48]), R=("STS", "IDN"), W=("PB3",))
                    op("act", lambda e: e.activation(out=UXS[:, c, :, 0:3], in_=PB[3][:, 0:48].rearrange("p (n k) -> p n k", k=3), func=AF.Copy), R=("PB3",), W=("UXS",))
                trk.dma("sp", STS[0:32, :], st_sc[l].rearrange("n k f -> (n k) f"), R=(), W=("STS",), sem="st")
                for c in range(4):
                    op("pe", lambda e: e.transpose(PB[3][:, 0:32], STS[0:32, c * 128:(c + 1) * 128], IDN[0:32, 0:32]), R=("STS", "IDN"), W=("PB3",))
                    op("act", lambda e: e.activation(out=GCS[:, c, :, 0:2], in_=PB[3][:, 0:32].rearrange("p (n k) -> p n k", k=2), func=AF.Copy), R=("PB3",), W=("GCS",))
                trk.dma("sp", STS[0:16, :], st_h[l], W=("STS",), sem="st")
                for c in range(4):
                    op("pe", lambda e: e.transpose(PB[3][:, 0:16], STS[0:16, c * 128:(c + 1) * 128], IDN[0:16, 0:16]), R=("STS", "IDN"), W=("PB3",))
                    op("act", lambda e: e.activation(out=H0S[:, c, :], in_=PB[3][:, 0:16], func=AF.Copy), R=("PB3",), W=("H0S",))
                trk.dma("sp", s_k[l][:, 0:127, :], ck[l][:, 1:128, :], W=(f"skA{l}",), sem=f"o2a{l}")
                trk.dma("sp", s_v[l][:, 0:127, :], cv[l][:, 1:128, :], W=(f"svA{l}",), sem=f"o2b{l}")
                trk.dma("sp", s_conv[l][:, 0:2, :], st_conv[l][:, 1:3, :], W=(f"scvA{l}",), sem=f"o2c{l}")
                trk.dma("sp", s_sc[l][:, 0:1, :], st_sc[l][:, 1:2, :], W=(f"sscA{l}",), sem=f"o2d{l}")

            if (_DBG.get("stop") == "norm" and kind == _DBG.get("stop_kind", "p") and l == _DBG.get("stop_layer", 0)):
                raise _Stop()
            trk.dma("sp", BD[:], bd[l], W=("BD",), sem="bd")
            pbl, pbln = PB[1], "PB1"
            pbs, pbsn = PB[2], "PB2"
            deferred = []

            def flush_deferred():
                while deferred:
                    deferred.pop(0)()

            def lru_chunk(c):
                cc = 0
                slot, wtok = wload([
                    (lambda s: wv(s, KC, 512)[:, :, 0:128], WIN[l][:, 1536 + c * 128:1536 + (c + 1) * 128].rearrange("(k p) m -> p k m", p=128)),
                    (lambda s: wv(s, KC, 512)[:, :, 256:384], WIN[l][:, 2048 + c * 128:2048 + (c + 1) * 128].rearrange("(k p) m -> p k m", p=128)),
                ], "lru")
                sv = wv(slot, KC, 512)
                pa, pan = gemm_chunk(lambda k: sv[:, k, cc * 128:(cc + 1) * 128], wtok, KC, xr, xt, N)
                ux, uxn = next_tmp()
                xc, xcn = next_tmp()
                if kind == "p":
                    op("dve", lambda e: e.tensor_copy(out=ux[:, 0:3], in_=CUX[:, l, c, :]), R=("CUX",), W=(uxn,))
                    op("act", lambda e: e.activation(out=ux[:, 3:3 + T], in_=pa[:, 0:T], func=AF.Copy), R=(pan,), W=(uxn,))
                    op("dve", lambda e: e.tensor_copy(out=CUX[:, l, c, :], in_=ux[:, T:T + 3]), R=(uxn,), W=("CUX",))
                    tap = lambda k: ux[:, k:k + T]
                else:
                    op("act", lambda e: e.activation(out=UXS[:, c, :, 3], in_=pa[:, 0:NS], func=AF.Copy), R=(pan,), W=("UXS",))
                    uxn = "UXS"
                    tap = lambda k: UXS[:, c, :, k]
                o_w, _ = lay[("lcw", l)]
                op("dve", lambda e: e.tensor_scalar(out=xc[:, 0:N], in0=tap(3), scalar1=PV[:, o_w + 12 + c:o_w + 13 + c], scalar2=colp(l, "lcb", c), op0=ALU.mult, op1=ALU.add), R=(uxn, "PV"), W=(xcn,))
                for k in (2, 1, 0):
                    op("dve", lambda e, k=k: e.scalar_tensor_tensor(out=xc[:, 0:N], in0=tap(k), scalar=PV[:, o_w + k * 4 + c:o_w + k * 4 + c + 1], in1=xc[:, 0:N], op0=ALU.mult, op1=ALU.add), R=(uxn, "PV", xcn), W=(xcn,))
                if kind == "s":
                    transpose_out(UXS[:, c, :, 3], 128, NS, s_conv[l][:, 2, c * 128:(c + 1) * 128], ("UXS",), f"scvB{l}")
                pa2, pa2n = gemm_chunk(lambda k: sv[:, k, 256:384], wtok, KC, xr, xt, N)
                ug, ugn = next_tmp()
                op("act", lambda e: e.activation(out=ug[:, 0:N], in_=pa2[:, 0:N], func=AF.Copy), R=(pa2n,), W=(ugn,))
                gates = []
                for gi, bname in ((0, "ba"), (1, "bi")):
                    pg, pgn = next_pa()
                    op("pe", lambda e: e.matmul(pg[:, 0:N], lhsT=BD[:, gi * 4 + c, :], rhs=xc[:, 0:N], start=True, stop=True), R=("BD", xcn), W=(pgn,))
                    gt, gtn = next_tmp()
                    op("act", lambda e: e.activation(out=gt[:, 0:N], in_=pg[:, 0:N], func=AF.Sigmoid, bias=colp(l, bname, c)), R=(pgn, "PV"), W=(gtn,))
                    gates.append((gt, gtn))
                (rg, rgn), (ig, ign) = gates
                a_, an = next_tmp()
                m_, mn = next_tmp()
                op("act", lambda e: e.activation(out=a_[:, 0:N], in_=rg[:, 0:N], func=AF.Exp, scale=SCL[:, l, 0, c:c + 1]), R=(rgn, "SCL"), W=(an,))
                op("act", lambda e: e.activation(out=m_[:, 0:N], in_=rg[:, 0:N], func=AF.Exp, scale=SCL[:, l, 1, c:c + 1]), R=(rgn, "SCL"), W=(mn,))
                op("dve", lambda e: e.tensor_scalar(out=m_[:, 0:N], in0=m_[:, 0:N], scalar1=-1.0, scalar2=1.0, op0=ALU.mult, op1=ALU.add), R=(mn,), W=(mn,))
                op("act", lambda e: e.activation(out=m_[:, 0:N], in_=m_[:, 0:N], func=AF.Sqrt), R=(mn,), W=(mn,))
                if ffi is not None:
                    op("dve", lambda e: e.tensor_scalar(out=m_[:, 0:1], in0=m_[:, 0:1], scalar1=PV[:, o_fl + 7 + ffi:o_fl + 8 + ffi], scalar2=PV[:, o_fl + 2 + ffi:o_fl + 3 + ffi], op0=ALU.mult, op1=ALU.add), R=(mn, "PV"), W=(mn,))
                op("dve", lambda e: e.tensor_tensor(out=ig[:, 0:N], in0=ig[:, 0:N], in1=xc[:, 0:N], op=ALU.mult), R=(ign, xcn), W=(ign,))
                op("dve", lambda e: e.tensor_tensor(out=ig[:, 0:N], in0=ig[:, 0:N], in1=m_[:, 0:N], op=ALU.mult), R=(ign, mn), W=(ign,))
                hs, hsn = rg, rgn
                if kind == "p":
                    op("dve", lambda e: e.tensor_tensor_scan(out=hs[:, 0:T], data0=a_[:, 0:T], data1=ig[:, 0:T], initial=HC[:, l, c:c + 1], op0=ALU.mult, op1=ALU.add), R=(an, ign, "HC", rgn), W=(hsn,))
                    op("dve", lambda e: e.tensor_copy(out=HC[:, l, c:c + 1], in_=hs[:, T - 1:T]), R=(hsn,), W=("HC",))
                else:
                    op("dve", lambda e: e.tensor_tensor(out=hs[:, 0:N], in0=a_[:, 0:N], in1=H0S[:, c, :], op=ALU.mult), R=(an, "H0S", rgn), W=(hsn,))
                    op("dve", lambda e: e.tensor_tensor(out=hs[:, 0:N], in0=hs[:, 0:N], in1=ig[:, 0:N], op=ALU.add), R=(hsn, ign), W=(hsn,))
                    transpose_out(hs[:, 0:NS], 128, NS, s_h[l][:, c * 128:(c + 1) * 128], (hsn,), f"shB{l}")
                flush_deferred()
                g2, g2n = next_tmp()
                op("dve", lambda e: e.tensor_tensor(out=g2[:, 0:N], in0=ug[:, 0:N], in1=ug[:, 0:N], op=ALU.mult), R=(ugn,), W=(g2n,))
                op("dve", lambda e: e.tensor_scalar(out=g2[:, 0:N], in0=g2[:, 0:N], scalar1=0.044715, scalar2=1.0, op0=ALU.mult, op1=ALU.add), R=(g2n,), W=(g2n,))
                op("dve", lambda e: e.tensor_tensor(out=g2[:, 0:N], in0=g2[:, 0:N], in1=ug[:, 0:N], op=ALU.mult), R=(g2n, ugn), W=(g2n,))
                op("act", lambda e: e.activation(out=g2[:, 0:N], in_=g2[:, 0:N], func=AF.Sigmoid, scale=1.5957691216057308), R=(g2n,), W=(g2n,))
                op("dve", lambda e: e.tensor_tensor(out=g2[:, 0:N], in0=g2[:, 0:N], in1=ug[:, 0:N], op=ALU.mult), R=(g2n, ugn), W=(g2n,))
                op("dve", lambda e: e.tensor_tensor(out=MX[:, 8 + c, 0:N], in0=g2[:, 0:N], in1=hs[:, 0:N], op=ALU.mult), R=(g2n, hsn), W=(f"MX{8 + c}",))
                deferred.append(lambda: sq_accum(pbl, pbln, 8 + c, c == 0, c == 3, N))

            def sc_chunk(c):
                slot, wtok = wload([
                    (lambda s: wv(s, KC, 384)[:, :, 0:128], WIN[l][:, 3072 + c * 128:3072 + (c + 1) * 128].rearrange("(k p) m -> p k m", p=128)),
                    (lambda s: wv(s, KC, 384)[:, :, 128:256], WIN[l][:, 3584 + c * 128:3584 + (c + 1) * 128].rearrange("(k p) m -> p k m", p=128)),
                    (lambda s: wv(s, KC, 384)[:, :, 256:384], WIN[l][:, 2560 + c * 128:2560 + (c + 1) * 128].rearrange("(k p) m -> p k m", p=128)),
                ], "sc")
                sv = wv(slot, KC, 384)
                pa, pan = gemm_chunk(lambda k: sv[:, k, 0:128], wtok, KC, xr, xt, N)
                uc, ucn = next_tmp()
                op("act", lambda e: e.activation(out=uc[:, 0:N], in_=pa[:, 0:N], func=AF.Copy), R=(pan,), W=(ucn,))
                flush_deferred()
                pa, pan = gemm_chunk(lambda k: sv[:, k, 128:256], wtok, KC, xr, xt, N)
                gc, gcn = next_tmp()
                if kind == "p":
                    op("dve", lambda e: e.tensor_copy(out=gc[:, 0:2], in_=CSC[:, l, c, :]), R=("CSC",), W=(gcn,))
                    op("dve", lambda e: e.tensor_tensor(out=gc[:, 2:2 + T], in0=pa[:, 0:T], in1=uc[:, 0:T], op=ALU.mult), R=(pan, ucn), W=(gcn,))
                    op("dve", lambda e: e.tensor_copy(out=CSC[:, l, c, :], in_=gc[:, T:T + 2]), R=(gcn,), W=("CSC",))
                    tap = lambda k: gc[:, k:k + T]
                else:
                    op("dve", lambda e: e.tensor_tensor(out=GCS[:, c, :, 2], in0=pa[:, 0:NS], in1=uc[:, 0:NS], op=ALU.mult), R=(pan, ucn), W=("GCS",))
                    gcn = "GCS"
                    tap = lambda k: GCS[:, c, :, k]
                    transpose_out(GCS[:, c, :, 2], 128, NS, s_sc[l][:, 1, c * 128:(c + 1) * 128], ("GCS",), f"sscB{l}")
                y_, yn = next_tmp()
                o_w, _ = lay[("scw", l)]
                op("dve", lambda e: e.tensor_scalar(out=y_[:, 0:N], in0=tap(2), scalar1=PV[:, o_w + 8 + c:o_w + 9 + c], scalar2=None, op0=ALU.mult), R=(gcn, "PV"), W=(yn,))
                for k in (1, 0):
                    op("dve", lambda e, k=k: e.scalar_tensor_tensor(out=y_[:, 0:N], in0=tap(k), scalar=PV[:, o_w + k * 4 + c:o_w + k * 4 + c + 1], in1=y_[:, 0:N], op0=ALU.mult, op1=ALU.add), R=(gcn, "PV", yn), W=(yn,))
                pa, pan = gemm_chunk(lambda k: sv[:, k, 256:384], wtok, KC, xr, xt, N)
                op("dve", lambda e: e.tensor_tensor(out=MX[:, 12 + c, 0:N], in0=pa[:, 0:N], in1=y_[:, 0:N], op=ALU.mult), R=(pan, yn), W=(f"MX{12 + c}",))
                deferred.append(lambda: sq_accum(pbs, pbsn, 12 + c, c == 0, c == 3, N))

            def kv_blk():
                pieces = [(lambda s: wv(s, KC, 512)[:, :, 256:512], WIN[l][:, 1280:1536].rearrange("(k p) m -> p k m", p=128))]
                for h_ in range(2):
                    for g_ in range(4):
                        c_src = 1024 + g_ * 64 + h_ * 32
                        c_dst = h_ * 128 + g_ * 32
                        pieces.append((lambda s, c_dst=c_dst: wv(s, KC, 512)[:, :, c_dst:c_dst + 32], WIN[l][:, c_src:c_src + 32].rearrange("(k p) m -> p k m", p=128)))
                slot, wtok = wload(pieces, "kv")
                sv = wv(slot, KC, 512)
                if kind == "p":
                    for pl in range(2):
                        op("dve", lambda e, pl=pl: e.tensor_copy(out=KH[:, 0, pl, 0:128], in_=KH[:, 0, pl, T:T + 128]), R=("KH",), W=("KH",))
                    op("dve", lambda e: e.tensor_copy(out=VH[:, 0, 0, :], in_=VH[:, 0, 4, :]), R=("VH",), W=("VH",))
                kf = []
                for pl in range(2):
                    pa, pan = gemm_chunk(lambda k: sv[:, k, pl * 128:(pl + 1) * 128], wtok, KC, xr, xt, N)
                    t_, tn = next_tmp()
                    op("act", lambda e: e.activation(out=t_[:, 0:N], in_=pa[:, 0:N], func=AF.Copy), R=(pan,), W=(tn,))
                    kf.append((t_, tn))
                rope(kf, [(KF[0], "KF0"), (KF[1], "KF1")], N)
                if kind == "p":
                    for pl in range(2):
                        op("act", lambda e, pl=pl: e.activation(out=KH[:, 0, pl, 128:128 + T], in_=KF[pl][:, 0:T], func=AF.Copy), R=(f"KF{pl}",), W=("KH",))
                    if last_pass:
                        for pl in range(2):
                            transpose_out(KF[pl][:, T - 128:T], 128, 128, pout["k"].rearrange("t (g h i) -> t h g i", h=2, i=32)[:, pl, :, :], (f"KF{pl}",), "pk", view=lambda a: a.rearrange("t (g i) -> t g i", i=32))
                else:
                    for pl in range(2):
                        op("act", lambda e, pl=pl: e.activation(out=QK[:, 8 + pl, 0:NS], in_=KF[pl][:, 0:NS], func=AF.Copy), R=(f"KF{pl}",), W=(f"QK{8 + pl}",))
                        transpose_out(KF[pl][:, 0:NS], 128, NS, s_k[l].rearrange("n s (g h i) -> n s h g i", h=2, i=32)[:, 127, pl, :, :], (f"KF{pl}",), f"skB{l}", view=lambda a: a.rearrange("t (g i) -> t g i", i=32))
                ntb = 4 if kind == "p" else 1
                for tb in range(ntb):
                    M = 128 if kind == "p" else NS
                    pa, pan = next_pa()
                    for k in range(KC):
                        op("pe", lambda e: e.matmul(pa[0:M, 0:256], lhsT=XN[:, k, tb * 128:tb * 128 + M], rhs=sv[:, k, 256:512], start=(k == 0), stop=(k == KC - 1)),
                           R=(wtok, f"XN{k}"), W=(pan,), inc=(k == KC - 1))
                    if kind == "p":
                        op("act", lambda e: e.activation(out=VH[:, 0, 1 + tb, :], in_=pa[:, 0:256], func=AF.Copy), R=(pan,), W=("VH",))
                        if last_pass and tb == 3:
                            op("act", lambda e: e.activation(out=ROW[:, 0:256], in_=pa[:, 0:256], func=AF.Copy), R=(pan,), W=("ROW",))
                            trk.dma("sp", pout["v"], ROW[:, 0:256], R=("ROW",), W=("pv",), sem="oROW")
                    else:
                        op("act", lambda e: e.activation(out=ROW[0:NS, 0:256], in_=pa[0:NS, 0:256], func=AF.Copy), R=(pan,), W=("ROW",))
                        trk.dma("sp", s_v[l][:, 127, :], ROW[0:NS, 0:256], R=("ROW",), W=(f"svB{l}",), sem="oROW")


            def q_blk(jb):
                pieces = []
                for jj_ in range(2):
                    for h_ in range(2):
                        for g_ in range(4):
                            c_src = (g_ * 4 + jb * 2 + jj_) * 64 + h_ * 32
                            c_dst = (jj_ * 2 + h_) * 128 + g_ * 32
                            pieces.append((lambda s, c_dst=c_dst: wv(s, KC, 512)[:, :, c_dst:c_dst + 32], WIN[l][:, c_src:c_src + 32].rearrange("(k p) m -> p k m", p=128)))
                slot, wtok = wload(pieces, "q")
                sv = wv(slot, KC, 512)
                for jj in range(2):
                    j = jb * 2 + jj
                    qf = []
                    for pl in range(2):
                        pa, pan = gemm_chunk(lambda k: sv[:, k, (jj * 2 + pl) * 128:(jj * 2 + pl + 1) * 128], wtok, KC, xr, xt, N)
                        t_, tn = next_tmp()
                        op("act", lambda e: e.activation(out=t_[:, 0:N], in_=pa[:, 0:N], func=AF.Copy), R=(pan,), W=(tn,))
                        qf.append((t_, tn))
                    rope(qf, [(QK[:, j * 2, :], f"QK{j * 2}"), (QK[:, j * 2 + 1, :], f"QK{j * 2 + 1}")], N)


            for c in range(4):
                lru_chunk(c)
                sc_chunk(c)
                if c == 0:
                    q_blk(0)
                elif c == 1:
                    q_blk(1)
                elif c == 2:
                    kv_blk()
            flush_deferred()
            group_norm_finish(pbl, pbln, l, 8, 4, N, 1)
            group_norm_finish(pbs, pbsn, l, 12, 4, N, 0)
            if kind == "p" and last_pass:
                transpose_out(HC[:, l, :], 128, 4, pout["h"].rearrange("(c f) -> c f", f=128), ("HC",), "ph")
                transpose_out(CUX[:, l, :, :], 128, 12, None, ("CUX",), "pconv", multi=[(pout["conv"][:, c * 128:(c + 1) * 128], c * 3, c * 3 + 3) for c in range(4)])
                transpose_out(CSC[:, l, :, :], 128, 8, None, ("CSC",), "psc", multi=[(pout["sc"][:, c * 128:(c + 1) * 128], c * 2, c * 2 + 2) for c in range(4)])

            if (_DBG.get("stop") == "sc" and kind == _DBG.get("stop_kind", "p") and l == _DBG.get("stop_layer", 0)):
                raise _Stop()
            pba, pban = PB[1], "PB1"
            if kind == "p":
                attention_prompt(l, ffi)
            else:
                attention_sample(l)
            for c in range(8):
                sq_accum(pba, pban, c, c == 0, c == 7, N)
            group_norm_finish(pba, pban, l, 0, 8, N, 1)

            if (_DBG.get("stop") == "attn" and kind == _DBG.get("stop_kind", "p") and l == _DBG.get("stop_layer", 0)):
                raise _Stop()
            for cb in range(4):
                cs_ = slice(cb * 512, (cb + 1) * 512)
                pieces = []
                for half in range(2):
                    for kc2 in range(2):
                        r0 = kc2 * 512 + half * 256
                        pieces.append((lambda s, half=half, kc2=kc2: wv(s, KC, 512)[half * 64:(half + 1) * 64, kc2 * 4:(kc2 + 1) * 4, :],
                                       WOUT[l][r0:r0 + 256, cs_].rearrange("(j d) m -> d j m", d=64)))
                pieces.append((lambda s: wv(s, KC, 512)[:, 8:16, :], WOUT[l][1024:2048, cs_].rearrange("(k p) m -> p k m", p=128)))
                slot, wtok = wload(pieces, "out")
                sv = wv(slot, KC, 512)
                for mm in range(4):
                    m = cb * 4 + mm
                    ko = lambda k: (k + 8) % 16
                    pa, pan = gemm_chunk(lambda k: sv[:, ko(k), mm * 128:(mm + 1) * 128], wtok, KC, lambda k: MX[:, ko(k), 0:N], lambda k: f"MX{ko(k)}", N)
                    op("dve", lambda e: e.tensor_tensor(out=X[:, m, 0:N], in0=pa[:, 0:N], in1=X[:, m, 0:N], op=ALU.add), R=(pan, f"X{m}"), W=(f"X{m}",))

            if (_DBG.get("stop") == "out" and kind == _DBG.get("stop_kind", "p") and l == _DBG.get("stop_layer", 0)):
                raise _Stop()
            rmsnorm_to_xn(l, "nffn", N)
            trk.fence("dve", ["pe", "act"])
            trk.fence("act", ["pe", "dve"])
            for fb in range(11):
                for part in range(2):
                    c0_ = part * DFF + fb * 512
                    slot, wtok = wload([(lambda s: wv(s, KC, 512), WGU[l][:, c0_:c0_ + 512].rearrange("(k p) m -> p k m", p=128))], "gu")
                    sv = wv(slot, KC, 512)
                    for cc in range(4):
                        fc = fb * 4 + cc
                        pa, pan = gemm_chunk(lambda k: sv[:, k, cc * 128:(cc + 1) * 128], wtok, KC, xr, xt, N)
                        if part == 0:
                            op("act", lambda e: e.activation(out=H[:, fc, 0:N], in_=pa[:, 0:N], func=AF.Silu), R=(pan,), W=(f"H{fc}",))
                        else:
                            op("dve", lambda e: e.tensor_tensor(out=H[:, fc, 0:N], in0=pa[:, 0:N], in1=H[:, fc, 0:N], op=ALU.mult), R=(pan, f"H{fc}"), W=(f"H{fc}",))
            dn_issued = {}

            def dn_issue(cg_, kq_):
                if cg_ < 4 and (cg_, kq_) not in dn_issued:
                    dn_issued[(cg_, kq_)] = wload([(lambda s: wv(s, 11, 512), WDN[l][kq_ * 1408:(kq_ + 1) * 1408, cg_ * 512:(cg_ + 1) * 512].rearrange("(k p) m -> p k m", p=128))], "dn")
            for cg in range(4):
                accs = [next_pa() for _ in range(4)]
                for kq in range(4):
                    dn_issue(cg, kq)
                    slot, wtok = dn_issued[(cg, kq)]
                    sv = wv(slot, 11, 512)
                    for mm in range(4):
                        pa, pan = accs[mm]
                        for k in range(11):
                            kk = kq * 11 + k
                            op("pe", lambda e: e.matmul(pa[:, 0:N], lhsT=sv[:, k, mm * 128:(mm + 1) * 128], rhs=H[:, kk, 0:N], start=(kk == 0), stop=(kk == FC - 1)),
                               R=(wtok, f"H{kk}"), W=(pan,), inc=(k == 10))
                for mm in range(4):
                    m = cg * 4 + mm
                    pa, pan = accs[mm]
                    op("dve", lambda e: e.tensor_tensor(out=X[:, m, 0:N], in0=pa[:, 0:N], in1=X[:, m, 0:N], op=ALU.add), R=(pan, f"X{m}"), W=(f"X{m}",))
                if xfinal_cb is not None:
                    dn_issue(cg + 1, 0); dn_issue(cg + 1, 1)
                    xfinal_cb(cg)

        def rope(src, dst, N):
            (a_, an), (b_, bn) = src
            (da, dan), (db, dbn) = dst
            t1, t1n = next_tmp(); t2, t2n = next_tmp()
            cos = CS[:, 0, 0:N]; sin = CS[:, 1, 0:N]
            op("dve", lambda e: e.tensor_tensor(out=t1[:, 0:N], in0=a_[:, 0:N], in1=cos, op=ALU.mult), R=(an, "CS"), W=(t1n,))
            op("dve", lambda e: e.tensor_tensor(out=t2[:, 0:N], in0=b_[:, 0:N], in1=sin, op=ALU.mult), R=(bn, "CS"), W=(t2n,))
            op("dve", lambda e: e.tensor_tensor(out=da[:, 0:N], in0=t1[:, 0:N], in1=t2[:, 0:N], op=ALU.subtract), R=(t1n, t2n), W=(dan,))
            op("dve", lambda e: e.tensor_tensor(out=t1[:, 0:N], in0=b_[:, 0:N], in1=cos, op=ALU.mult), R=(bn, "CS"), W=(t1n,))
            op("dve", lambda e: e.tensor_tensor(out=t2[:, 0:N], in0=a_[:, 0:N], in1=sin, op=ALU.mult), R=(an, "CS"), W=(t2n,))
            op("dve", lambda e: e.tensor_tensor(out=db[:, 0:N], in0=t1[:, 0:N], in1=t2[:, 0:N], op=ALU.add), R=(t1n, t2n), W=(dbn,))

        pt_i = {"i": 0}

        def attention_prompt(l, ffi):
            QKv = QK[:, 0:8, :].rearrange("p (j pl) t -> p pl j t", pl=2)
            num, numn = PB[2], "PB2"
            den, denn = PB[3], "PB3"
            units = [(qb, kc, half) for qb in range(4) for kc in range(2) for half in range(2)]

            def qk_phase(u):
                qb, kc, half = u
                g = kc * 2 + half
                blocks = [(qb, 1), (qb + 1, 0)]
                pts = []
                for bi, (kb, mk) in enumerate(blocks):
                    sps = PB[bi]; spsn = f"PB{bi}"
                    for pl in range(2):
                        op("pe", lambda e: e.matmul(sps[:, :].rearrange("p (j q) -> p j q", q=128), lhsT=KH[32 * g:32 * g + 32, 0, pl, kb * 128:(kb + 1) * 128],
                                                    rhs=QKv[32 * g:32 * g + 32, pl, :, qb * 128:(qb + 1) * 128], start=(pl == 0), stop=(pl == 1), tile_position=(32 * g, 0)),
                           R=("KH",) + tuple(f"QK{j * 2 + pl}" for j in range(4)), W=(spsn,), inc=(pl == 1))
                    i = pt_i["i"] % 4; pt_i["i"] += 1
                    pt, ptn = PT[i], f"PT{i}"
                    op("act", lambda e: e.activation(out=pt[:, :], in_=sps[:, :], func=AF.Exp, scale=0.125), R=(spsn,), W=(ptn,))
                    mk_ap = MSK0[:, :, :] if (mk == 1 and qb == 0 and ffi is not None) else MSK[:, mk, :, :]
                    op("dve", lambda e: e.tensor_tensor(out=pt[:, :], in0=pt[:, :], in1=mk_ap.rearrange("p j q -> p (j q)"), op=ALU.mult), R=(ptn, "MSK", "MSK0"), W=(ptn,))
                    pts.append((pt, ptn, kb))
                return pts

            def pv_phase(u, pts):
                qb, kc, half = u
                g = kc * 2 + half
                for bi, (pt, ptn, kb) in enumerate(pts):
                    st = (bi == 0); sp_ = (bi == len(pts) - 1)
                    op("pe", lambda e: e.matmul(num[half * 64:(half + 1) * 64, :], lhsT=VH[:, 0, kb, g * 64:(g + 1) * 64], rhs=pt[:, :], start=st, stop=sp_),
                       R=(ptn, "VH"), W=(numn,), inc=False)
                    op("pe", lambda e: e.matmul(den[half * 64:(half + 1) * 64, :], lhsT=ONES[:, 0:64], rhs=pt[:, :], start=st, stop=sp_),
                       R=(ptn, "ONES"), W=(denn,), inc=True)
                if half == 1:
                    dn, dnn = next_tmp()
                    op("dve", lambda e: e.tensor_tensor(out=dn[:, 0:512], in0=den[:, :], in1=SNK[:, kc, :, :].rearrange("p j q -> p (j q)"), op=ALU.add), R=(denn, "SNK"), W=(dnn,))
                    op("dve", lambda e: e.reciprocal(out=dn[:, 0:512], in_=dn[:, 0:512]), R=(dnn,), W=(dnn,))
                    op("dve", lambda e: e.tensor_tensor(out=MX[:, kc * 4:kc * 4 + 4, qb * 128:(qb + 1) * 128], in0=num[:, :].rearrange("p (j q) -> p j q", q=128),
                                                        in1=dn[:, 0:512].rearrange("p (j q) -> p j q", q=128), op=ALU.mult),
                       R=(numn, dnn), W=tuple(f"MX{kc * 4 + j}" for j in range(4)))

            prev = None
            for u in units:
                pts = qk_phase(u)
                if prev is not None:
                    pv_phase(*prev)
                prev = (u, pts)
            pv_phase(*prev)

        def attention_sample(l):
            QKv = QK[:, 0:8, :].rearrange("p (j pl) t -> p pl j t", pl=2)
            first = True
            trk.reserve("ks", 8)
            for h_ in range(2):
                for g_ in range(4):
                    trk.dma("pool", KS[:, :, h_ * 128 + g_ * 32:h_ * 128 + g_ * 32 + 32], s_k[l][:, :, g_ * 64 + h_ * 32:g_ * 64 + h_ * 32 + 32].rearrange("n s i -> s n i"),
                            R=(f"skA{l}", f"skB{l}"), W=("KS",), sem="ks", skip_deps=not first)
                    first = False
            trk.dma("pool", VS2[:], s_v[l].rearrange("n s f -> s n f"), R=(f"svA{l}", f"svB{l}"), W=("VS2",), sem="vs")
            for n in range(NS):
                kt, ktn = KTS[n % 2], f"KTS{n % 2}"
                ptb = PB[3][:, 0:128].bitcast(BF16).rearrange("p (pl s) -> p pl s", pl=2)
                for pl in range(2):
                    op("pe", lambda e: e.transpose(ptb[:, pl, :], KS[:, n, pl * 128:(pl + 1) * 128], IDNB[:, :]), R=("KS", "IDNB"), W=("PB3",))
                op("act", lambda e: e.activation(out=kt[:, :, :], in_=ptb, func=AF.Copy), R=("PB3",), W=(ktn,))
                for g in range(4):
                    for pl in range(2):
                        op("pe", lambda e: e.matmul(PA[g][:, n * 4:n * 4 + 4], lhsT=kt[32 * g:32 * g + 32, pl, :], rhs=QKv[32 * g:32 * g + 32, pl, :, n],
                                                    start=(pl == 0), stop=(pl == 1), tile_position=(32 * g, 0)),
                           R=(ktn,) + tuple(f"QK{j * 2 + pl}" for j in range(4)), W=(f"PA{g}",), inc=(pl == 1))
            pssv = PSS[:, :].rearrange("p (n g j) -> p n g j", g=4, j=4)
            for g in range(4):
                op("act", lambda e: e.activation(out=pssv[:, :, g, :], in_=PA[g][:, 0:64].rearrange("p (n j) -> p n j", j=4), func=AF.Exp, scale=0.125), R=(f"PA{g}",), W=("PSS",))
            den, denn = PB[1], "PB1"
            op("pe", lambda e: e.matmul(den[:, 0:256], lhsT=ONES[:, :], rhs=PSS[:, :], start=True, stop=True), R=("PSS", "ONES"), W=(denn,))
            op("dve", lambda e: e.tensor_tensor(out=RD[:, :], in0=den[:, 0:256], in1=SNKS[:, l, :, :].rearrange("p n h -> p (n h)"), op=ALU.add), R=(denn, "SNKS"), W=("RD",))
            op("dve", lambda e: e.reciprocal(out=RD[:, :], in_=RD[:, :]), R=("RD",), W=("RD",))
            num, numn = PB[2], "PB2"
            numv = num[:, 0:128].rearrange("p (kc j n) -> p kc j n", kc=2, j=4)
            for n in range(NS):
                for g in range(4):
                    hf = g % 2
                    op("pe", lambda e: e.matmul(numv[hf * 64:(hf + 1) * 64, g // 2, :, n], lhsT=VS2[:, n, g * 64:(g + 1) * 64], rhs=PSS[:, n * 16 + g * 4:n * 16 + g * 4 + 4], start=True, stop=True),
                       R=("VS2", "PSS"), W=(numn,), inc=(n == NS - 1 and g == 3))
            rdv = RD[:, :].rearrange("p (n g j) -> p g j n", g=4, j=4)
            for hf in range(2):
                op("dve", lambda e: e.tensor_tensor(out=MX[hf * 64:(hf + 1) * 64, 0:8, 0:NS].rearrange("p (kc j) n -> p kc j n", kc=2),
                                                    in0=numv[hf * 64:(hf + 1) * 64, :, :, :],
                                                    in1=rdv[hf * 64:(hf + 1) * 64, hf::2, :, :], op=ALU.mult),
                   R=(numn, "RD"), W=tuple(f"MX{c}" for c in range(8)))

        def run_pass(pi, kind):
            N = T if kind == "p" else NS
            ci = pi if kind == "p" else NSLOT
            o_fl, _ = lay[("flags", 0)]
            trk.dma("sp", CS[:], csd[ci], W=("CS",), sem="cs")
            ntb = 4 if kind == "p" else 1
            for tb in range(ntb):
                rows = 128 if kind == "p" else NS
                src = xp[pi * T + tb * 128: pi * T + tb * 128 + 128, :] if kind == "p" else xs
                so = (tb % 2) * D; stn = f"STG{tb % 2}"
                trk.dma("sp", STG[0:rows, so:so + D], src, W=(stn,), sem=f"xin{tb % 2}")
                for c4 in range(4):
                    pb = PB[c4 % 2]; pbn = f"PB{c4 % 2}"
                    for cc in range(4):
                        c = c4 * 4 + cc
                        op("pe", lambda e: e.transpose(pb[:, cc * 128:cc * 128 + rows], STG[0:rows, so + c * 128:so + (c + 1) * 128], IDN[0:rows, 0:rows]), R=(stn, "IDN"), W=(pbn,), inc=(cc == 3))
                    op("act", lambda e: e.activation(out=X[:, c4 * 4:c4 * 4 + 4, tb * 128:tb * 128 + rows], in_=pb[:, :].rearrange("p (c t) -> p c t", t=128)[:, :, 0:rows], func=AF.Copy),
                       R=(pbn,), W=tuple(f"X{c4 * 4 + cc}" for cc in range(4)))
            if kind == "p" and pi >= 1:
                for j in range(4):
                    so = (j % 2) * D; stn = f"STG{j % 2}"
                    trk.dma("sp", STG[:, so:so + 2048], obs[j][0:128, :], R=(f"ob{j}",), W=(stn,), sem=f"xin{j % 2}")
                    for cc in range(4):
                        c = j * 4 + cc
                        op("dve", lambda e: e.scalar_tensor_tensor(out=X[:, c, :], in0=STG[:, so + cc * 512:so + (cc + 1) * 512], scalar=PV[:, o_fl + 1:o_fl + 2], in1=X[:, c, :], op0=ALU.mult, op1=ALU.add),
                           R=(stn, "PV", f"X{c}"), W=(f"X{c}",))
            if (_DBG.get("stop") == "load" and kind == _DBG.get("stop_kind", "p")):
                raise _Stop()
            if kind == "p":
                ffi = pi if pi <= 1 else None
                if ffi is not None:
                    for j in range(4):
                        op("dve", lambda e: e.tensor_scalar(out=MSK0[:, j, :], in0=MSK[:, 1, j, :], scalar1=PV[:, o_fl + 7 + ffi:o_fl + 8 + ffi], scalar2=None, op0=ALU.mult), R=("MSK", "PV"), W=("MSK0",))
                pout = None
                if pi >= NPP - 1:
                    w_ = pi - (NPP - 1)
                    pout = {"h": p_h[w_], "conv": p_conv[w_], "k": p_k[w_], "v": p_v[w_], "sc": p_sc[w_]}
                def exchange(j):
                    trk.dma("sp", ibs[j][:, :], X[:, j * 4:(j + 1) * 4, :].rearrange("p c t -> p (c t)"), R=tuple(f"X{j * 4 + cc}" for cc in range(4)), W=(f"ib{j}",), sem=f"xex{j}")
                    trk.coll(lambda e: e.collective_compute("AllGather", ALU.bypass, replica_groups=[[2 * i_, 2 * i_ + 1] for i_ in range(_DBG.get('ncores', NCORES) // 2)],
                                                            ins=[ibs[j].ap().opt()], outs=[obs[j].ap().opt()]), R=(f"ib{j}",), W=(f"ob{j}",))
                layer(2, pi, kind, pout, ffi, xfinal_cb=(exchange if pi < NSLOT - 1 else None))
                if pi == 0:
                    fa = PV[:, o_fl:o_fl + 1]
                    for t_, nm in ((HC[:, 2, :], "HC"), (CUX[:, 2, :, :], "CUX"), (CSC[:, 2, :, :], "CSC"), (KH[:, 0, :, :], "KH"), (VH[:, 0, :, :], "VH")):
                        op("dve", lambda e, t_=t_: e.tensor_scalar(out=t_, in0=t_, scalar1=fa, scalar2=None, op0=ALU.mult), R=(nm, "PV"), W=(nm,))
            else:
                for l in range(2):
                    layer(l, pi, kind, None, None)
            if (_DBG.get("stop") == "final" and kind == _DBG.get("stop_kind", "p")):
                raise _Stop()
            pb, pbn = PB[0], "PB0"
            for c in range(KC):
                sq, sqn = next_sq()
                op("act", lambda e: e.activation(out=sq[:, 0:N], in_=X[:, c, 0:N], func=AF.Square), R=(f"X{c}",), W=(sqn,))
                op("pe", lambda e: e.matmul(pb[:, 0:N], lhsT=ONES[:, :], rhs=sq[:, 0:N], start=(c == 0), stop=(c == KC - 1)), R=(sqn, "ONES"), W=(pbn,), inc=True)
            rs = RS[0]
            op("dve", lambda e: e.tensor_scalar(out=rs[:, 0:N], in0=pb[:, 0:N], scalar1=1.0 / D, scalar2=EPS, op0=ALU.mult, op1=ALU.add), R=(pbn,), W=("RS0",))
            op("act", lambda e: e.activation(out=rs[:, 0:N], in_=rs[:, 0:N], func=AF.Sqrt), R=("RS0",), W=("RS0",))
            op("dve", lambda e: e.reciprocal(out=rs[:, 0:N], in_=rs[:, 0:N]), R=("RS0",), W=("RS0",))
            o_f, _ = lay[("nfin", 0)]
            for c in range(KC):
                op("dve", lambda e: e.scalar_tensor_tensor(out=X[:, c, 0:N], in0=X[:, c, 0:N], scalar=PV[:, o_f + c:o_f + c + 1], in1=rs[:, 0:N], op0=ALU.mult, op1=ALU.mult),
                   R=(f"X{c}", "RS0", "PV"), W=(f"X{c}",))
            for tb in range(ntb):
                rows = 128 if kind == "p" else NS
                so = (tb % 2) * D; stn = f"STG{tb % 2}"
                for c4 in range(4):
                    pb = PB[c4 % 2]; pbn = f"PB{c4 % 2}"
                    for cc in range(4):
                        c = c4 * 4 + cc
                        op("pe", lambda e: e.transpose(pb[0:rows, cc * 128:(cc + 1) * 128], X[:, c, tb * 128:tb * 128 + rows], IDN[:, :]), R=(f"X{c}", "IDN"), W=(pbn,), inc=(cc == 3))
                    op("act", lambda e: e.activation(out=STG[0:rows, so + c4 * 512:so + (c4 + 1) * 512], in_=pb[0:rows, :], func=AF.Copy), R=(pbn,), W=(stn,))
                dst = yp[pi * T + tb * 128: pi * T + tb * 128 + 128, :] if kind == "p" else ys
                trk.dma("sp", dst, STG[0:rows, so:so + D], R=(stn,), W=("yout",), sem=f"oSTG{tb % 2}")

        try:
            for pi in range(NSLOT):
                run_pass(pi, "p")
            if not _DBG.get("nosample"):
                run_pass(0, "s")
        except _Stop:
            pass
        trk.wait_all_dma("sp")
    return nc


_CACHE = {}
_DBG = {}


def kernel(x_prompt, x_sample, state_lru_h, state_lru_conv, cache_swa_k, cache_swa_v, state_sconv,
           norm_mix, w_in, norm_grp, w_out, lru_conv_w, lru_conv_b, lru_w_a, lru_b_a, lru_w_i, lru_b_i,
           lru_lambda, sc_conv_w, attn_sinks, norm_ffn, ffn_w_gu, ffn_w_down, norm_final):
    f = lambda a: np.ascontiguousarray(np.asarray(a, dtype=np.float32))
    x_prompt = f(x_prompt); x_sample = f(x_sample)
    NPP = _DBG.get('npp', SEQ // T)
    lay, NV = pv_layout()
    pv = np.zeros((128, NV), np.float32)

    def put(name, l, arr2d):
        o, n = lay[(name, l)]
        assert arr2d.shape == (128, n), (name, arr2d.shape)
        pv[:, o:o + n] = arr2d
    fm = lambda v: f(v).reshape(-1, 128).T
    def put_layer(l, ls):
        put("nmix", l, fm(norm_mix[ls])); put("nffn", l, fm(norm_ffn[ls]))
        g = f(norm_grp[ls])
        ga = g[:1024].reshape(2, 2, 4, 64)
        ga = ga.transpose(1, 3, 0, 2).reshape(128, 8)
        put("ngrp", l, np.concatenate([ga, fm(g[1024:1536]), fm(g[1536:2048])], axis=1))
        cw = f(lru_conv_w[ls])
        put("lcw", l, np.concatenate([fm(cw[k]) for k in range(4)], axis=1))
        put("lcb", l, fm(lru_conv_b[ls])); put("ba", l, fm(lru_b_a[ls])); put("bi", l, fm(lru_b_i[ls])); put("lam", l, fm(lru_lambda[ls]))
        sw = f(sc_conv_w[ls])
        put("scw", l, np.concatenate([fm(sw[k]) for k in range(3)], axis=1))
        sk = f(attn_sinks[ls]).reshape(2, 2, 4)
        sk2 = np.repeat(sk.transpose(1, 0, 2).reshape(2, 1, 8), 64, axis=1).reshape(128, 8)
        put("sink", l, sk2)
        put("sinkall", l, np.repeat(f(attn_sinks[ls])[None, :], 128, axis=0))
    put_layer(0, 0); put_layer(1, 1)
    put("nfin", 0, fm(norm_final))
    bdh = np.zeros((3, 128, 8, 128), np.float32)
    for l in range(2):
        for gi, wsrc in enumerate((lru_w_a, lru_w_i)):
            wl = f(wsrc[l])
            for c in range(4):
                bdh[l, 0:64, gi * 4 + c, 0:64] = wl[2 * c]
                bdh[l, 64:128, gi * 4 + c, 64:128] = wl[2 * c + 1]
    half = 32
    inv = (10000.0 ** (-np.arange(half, dtype=np.float32) / half)).astype(np.float32)
    NSLOT = NPP + 1

    def cs_tables(lc):
        cs = np.zeros((NSLOT + 1, 128, 2, T), np.float32)
        for si in range(NSLOT + 1):
            if si < NSLOT:
                pss = min(max(si - lc, 0), NPP - 1)
                pos = (np.arange(T) + pss * T).astype(np.float32)
            else:
                pos = np.full(T, float(PAST), np.float32)
            ang = pos[None, :] * inv[:, None]
            cs[si, :, 0, :] = np.tile(np.cos(ang).astype(np.float32), (4, 1))
            cs[si, :, 1, :] = np.tile(np.sin(ang).astype(np.float32), (4, 1))
        return cs
    cs_by_lc = [cs_tables(0), cs_tables(1)]
    sidx = np.arange(128)[:, None]; qidx = np.arange(128)[None, :]
    msk = np.stack([(sidx <= qidx), (sidx > qidx)], axis=1).astype(np.float32)
    idn = np.eye(128, dtype=np.float32)
    w_in = f(w_in); w_out = f(w_out); ffn_w_gu = f(ffn_w_gu); ffn_w_down = f(ffn_w_down)
    sh = f(state_lru_h); scv = f(state_lru_conv); ssc = f(state_sconv)
    ckk = f(cache_swa_k).reshape(2, 128, 128, 256); cvv = f(cache_swa_v).reshape(2, 128, 128, 256)
    in_maps = []
    o_fl, _ = lay[("flags", 0)]
    zeros_xp = np.zeros((NSLOT * T, D), np.float32)
    for c in range(NCORES):
        b = c // 2
        lc = c % 2
        ns = slice(c * NS, (c + 1) * NS)
        put_layer(2, lc)
        pvc = pv.copy()
        pvc[:, o_fl + 0] = 1.0 - lc; pvc[:, o_fl + 1] = float(lc)
        for si in range(5):
            ff = 1.0 if si == lc else 0.0
            pvc[:, o_fl + 2 + si] = ff; pvc[:, o_fl + 7 + si] = 1.0 - ff
        bdc = bdh.copy(); bdc[2] = bdh[lc]
        if lc == 0:
            xpc = np.concatenate([x_prompt[b][:NPP * T], np.zeros((T, D), np.float32)], axis=0)
        else:
            xpc = zeros_xp
        in_maps.append({
            "xp": xpc, "xs": np.ascontiguousarray(x_sample[ns, 0, :]),
            "st_h": np.ascontiguousarray(sh[:, ns]), "st_conv": np.ascontiguousarray(scv[:, ns]), "st_sc": np.ascontiguousarray(ssc[:, ns]),
            "ck": np.ascontiguousarray(ckk[:, ns]), "cv": np.ascontiguousarray(cvv[:, ns]),
            "w_in": w_in, "w_out": w_out, "w_gu": ffn_w_gu, "w_dn": ffn_w_down,
            "wo_in": w_in[lc:lc + 1], "wo_out": w_out[lc:lc + 1], "wo_gu": ffn_w_gu[lc:lc + 1], "wo_dn": ffn_w_down[lc:lc + 1],
            "bd": bdc, "pv": pvc, "cs": cs_by_lc[lc], "msk": msk, "idn": idn,
        })
    if NPP not in _CACHE:
        _CACHE[NPP] = build(NPP)
    nc = _CACHE[NPP]
    if 'ncores' in _DBG:
        n_ = _DBG['ncores']
        res = run_bass_kernel_spmd(nc, in_maps[:n_], core_ids=list(range(n_)), trace=_DBG.get('trace', False))
        _DBG['res'] = res
        return res.results
    res = run_bass_kernel_spmd(nc, in_maps, core_ids=list(range(NCORES)))
    R = res.results
    y_prompt = np.stack([R[2 * b + 1]["yp"][T:T + SEQ] for b in range(4)]).reshape(4, SEQ, D)
    y_sample = np.concatenate([R[c]["ys"] for c in range(NCORES)], axis=0).reshape(128, 1, D)
    stk = lambda name, shp: np.stack([np.stack([R[2 * b][name][0], R[2 * b + 1][name][1]]) for b in range(4)], axis=1).reshape(shp)
    p_lru_h = stk("p_h", (2, 4, 512))
    p_lru_conv = stk("p_conv", (2, 4, 3, 512))
    p_swa_k = stk("p_k", (2, 4, 128, 4, 64))
    p_swa_v = stk("p_v", (2, 4, 128, 4, 64))
    p_sconv = stk("p_sc", (2, 4, 2, 512))
    cat = lambda name, shp: np.concatenate([R[c][name] for c in range(NCORES)], axis=1).reshape(shp)
    s_lru_h = cat("s_h", (2, 128, 512))
    s_lru_conv = cat("s_conv", (2, 128, 3, 512))
    s_swa_k = cat("s_k", (2, 128, 128, 4, 64))
    s_swa_v = cat("s_v", (2, 128, 128, 4, 64))
    s_sconv = cat("s_sc", (2, 128, 2, 512))
    outs = (y_prompt, y_sample, p_lru_h, p_lru_conv, p_swa_k, p_swa_v, p_sconv, s_lru_h, s_lru_conv, s_swa_k, s_swa_v, s_sconv)
    return tuple(np.ascontiguousarray(o, dtype=np.float32) for o in outs)
```

```python
import numpy as np
from contextlib import ExitStack
import concourse.bass as bass
import concourse.mybir as mybir
from concourse.bass_utils import run_bass_kernel_spmd

F32 = mybir.dt.float32
BF16 = mybir.dt.bfloat16
AF = mybir.ActivationFunctionType
ALU = mybir.AluOpType

D = 2048
KC = 16
T = 512
NS = 16
DFF = 5632
FC = 44
EPS = 1e-6
PAST = 8192
NCORES = 8
SEQ = 2048


def pv_layout():
    lay = {}
    off = 0
    for l in range(3):
        for name, n in (("nmix", 16), ("nffn", 16), ("ngrp", 16), ("lcw", 16), ("lcb", 4), ("ba", 4), ("bi", 4),
                        ("lam", 4), ("scw", 12), ("sink", 8), ("sinkall", 16)):
            lay[(name, l)] = (off, n)
            off += n
    lay[("nfin", 0)] = (off, 16)
    off += 16
    lay[("flags", 0)] = (off, 12)
    off += 12
    return lay, off


class _Stop(Exception):
    pass


class Trk:
    SEM_MAX = 3800

    def __init__(self, nc, es):
        self.nc = nc
        self.es = es
        self.E = {"pe": nc.tensor, "act": nc.scalar, "dve": nc.vector, "pool": nc.gpsimd, "sp": nc.sync}
        self.gen = {k: 0 for k in self.E}
        self.sem = {k: es.enter_context(nc.semaphore("pg_" + k + "_0")) for k in self.E}
        self.semname = {k: "pg_" + k + "_0" for k in self.E}
        self.cnt = {k: 0 for k in self.E}
        self.seen = {k: {} for k in self.E}
        self.lastw = {}
        self.readers = {}
        self.dsems = {}
        self.all_dsems = []
        self.pending_noinc = {k: False for k in self.E}

    def _wait(self, eng, ev):
        name, semh, val = ev
        if eng == "pe" and name.startswith("pg_pe_"):
            return
        if self.seen[eng].get(name, 0) >= val:
            return
        self.E[eng].wait_ge(semh, val)
        self.seen[eng][name] = val

    def _deps(self, eng, R, W):
        best = {}

        def add(ev):
            if ev[0] not in best or best[ev[0]][2] < ev[2]:
                best[ev[0]] = ev
        for r in R:
            if r in self.lastw:
                add(self.lastw[r])
        for w in W:
            if w in self.lastw:
                add(self.lastw[w])
            for ev in self.readers.get(w, {}).values():
                add(ev)
        for ev in best.values():
            self._wait(eng, ev)

    def _commit(self, ev, R, W):
        for r in R:
            d = self.readers.setdefault(r, {})
            if ev[0] not in d or d[ev[0]][2] < ev[2]:
                d[ev[0]] = ev
        for w in W:
            self.lastw[w] = ev
            self.readers[w] = {}

    def _roll(self, eng):
        if self.cnt[eng] >= self.SEM_MAX and not self.pending_noinc[eng]:
            self.gen[eng] += 1
            nm = f"pg_{eng}_{self.gen[eng]}"
            self.sem[eng] = self.es.enter_context(self.nc.semaphore(nm))
            self.semname[eng] = nm
            self.cnt[eng] = 0

    def op(self, eng, fn, R=(), W=(), inc=True):
        self._roll(eng)
        self._deps(eng, R, W)
        ins = fn(self.E[eng])
        if inc:
            self.cnt[eng] += 1
            ins.then_inc(self.sem[eng], 1)
            ev = (self.semname[eng], self.sem[eng], self.cnt[eng])
            self.pending_noinc[eng] = False
        else:
            ev = (self.semname[eng], self.sem[eng], self.cnt[eng] + 1)
            self.pending_noinc[eng] = True
        self._commit(ev, R, W)
        return ins

    def fence(self, eng, others):
        for o in others:
            self._wait(eng, (self.semname[o], self.sem[o], self.cnt[o] + (1 if self.pending_noinc[o] else 0)))

    def reserve(self, sem, n):
        if sem not in self.dsems or self.dsems[sem][1] + 16 * n > self.SEM_MAX:
            g = self.dsems[sem][3] + 1 if sem in self.dsems else 0
            nm = f"d_{sem}_{g}"
            self.dsems[sem] = [self.es.enter_context(self.nc.semaphore(nm)), 0, nm, g]
            self.all_dsems.append(self.dsems[sem])

    def dma(self, q, out, in_, R=(), W=(), sem="g", skip_deps=False, **kw):
        if not skip_deps:
            self._deps(q, R, W)
        self.reserve(sem, 1)
        ds = self.dsems[sem]
        ins = self.E[q].dma_start(out=out, in_=in_, **kw)
        ds[1] += 16
        ins.then_inc(ds[0], 16)
        ev = (ds[2], ds[0], ds[1])
        self._commit(ev, R, W)
        return ins

    def coll(self, fn, R=(), W=()):
        self._deps("pool", R, W)
        if "cc" not in self.dsems:
            self.dsems["cc"] = [self.es.enter_context(self.nc.semaphore("d_cc_0")), 0, "d_cc_0", 0]
            self.all_dsems.append(self.dsems["cc"])
        ds = self.dsems["cc"]
        ins = fn(self.E["pool"])
        ds[1] += 1
        ins.then_inc(ds[0], 1)
        ev = (ds[2], ds[0], ds[1])
        self._commit(ev, R, W)
        return ins

    def wait_all_dma(self, eng):
        for ds in self.all_dsems:
            if ds[1]:
                self.E[eng].wait_ge(ds[0], ds[1])


def build(NPP):
    lay, NV = pv_layout()
    NSLOT = NPP + 1
    TOK = NSLOT * T
    nc = bass.Bass("TRN2", target_bir_lowering=False)

    def din(name, shape, dt=F32):
        return nc.dram_tensor(name, list(shape), dt, kind="ExternalInput").ap()

    def dout(name, shape, dt=F32):
        return nc.dram_tensor(name, list(shape), dt, kind="ExternalOutput").ap()

    xp = din("xp", [TOK, D]); xs = din("xs", [NS, D])
    st_h = din("st_h", [2, NS, 512]); st_conv = din("st_conv", [2, NS, 3, 512]); st_sc = din("st_sc", [2, NS, 2, 512])
    ck = din("ck", [2, NS, 128, 256]); cv = din("cv", [2, NS, 128, 256])
    w_in = din("w_in", [2, D, 4096]); w_out = din("w_out", [2, D, D]); w_gu = din("w_gu", [2, D, 2 * DFF]); w_dn = din("w_dn", [2, DFF, D])
    wo_in = din("wo_in", [1, D, 4096]); wo_out = din("wo_out", [1, D, D]); wo_gu = din("wo_gu", [1, D, 2 * DFF]); wo_dn = din("wo_dn", [1, DFF, D])
    WIN = [w_in[0], w_in[1], wo_in[0]]; WOUT = [w_out[0], w_out[1], wo_out[0]]
    WGU = [w_gu[0], w_gu[1], wo_gu[0]]; WDN = [w_dn[0], w_dn[1], wo_dn[0]]
    bd = din("bd", [3, 128, 8, 128])
    ibs = [nc.dram_tensor(f"ib{j}", [128, 2048], F32) for j in range(4)]
    obs = [nc.dram_tensor(f"ob{j}", [256, 2048], F32) for j in range(4)]
    pvd = din("pv", [128, NV])
    csd = din("cs", [NSLOT + 1, 128, 2, T])
    mskd = din("msk", [128, 2, 128])
    idnd = din("idn", [128, 128])

    yp = dout("yp", [TOK, D]); ys = dout("ys", [NS, D])
    p_h = dout("p_h", [2, 512]); p_conv = dout("p_conv", [2, 3, 512]); p_k = dout("p_k", [2, 128, 256]); p_v = dout("p_v", [2, 128, 256]); p_sc = dout("p_sc", [2, 2, 512])
    s_h = dout("s_h", [2, NS, 512]); s_conv = dout("s_conv", [2, NS, 3, 512]); s_k = dout("s_k", [2, NS, 128, 256]); s_v = dout("s_v", [2, NS, 128, 256]); s_sc = dout("s_sc", [2, NS, 2, 512])

    es = ExitStack()
    with es:
        trk = Trk(nc, es)
        op = trk.op

        def sb(name, shape, dt):
            return es.enter_context(nc.sbuf_tensor(name, list(shape), dt))

        def ps(name, shape, dt):
            return es.enter_context(nc.psum_tensor(name, list(shape), dt))

        X = sb("X", [128, KC, T], F32)
        XN = sb("XN", [128, KC, T], BF16)
        H = sb("H", [128, FC, T], BF16)
        MX = H[:, 0:16, :]
        QK = H[:, 16:26, :]
        KS = H[:, 26:34, :].rearrange("p c (a f) -> p (c a) f", f=256)
        VS2 = H[:, 34:42, :].rearrange("p c (a f) -> p (c a) f", f=256)
        NWS = 3
        WS = [sb(f"WS{i}", [128, 8192], BF16) for i in range(NWS)]
        NTMP = 9
        TMP = [sb(f"TMP{i}", [128, T + 4], F32) for i in range(NTMP)]
        SQ = [sb(f"SQ{i}", [128, T], BF16) for i in range(3)]
        PT = [sb(f"PT{i}", [128, 512], BF16) for i in range(4)]
        STG = XN[:, :, :].rearrange("p c t -> p (c t)").bitcast(F32)
        PV = sb("PV", [128, NV], F32)
        CS = sb("CS", [128, 2, T], F32)
        MSKF = sb("MSKF", [128, 2, 128], F32)
        MSK = sb("MSK", [128, 2, 4, 128], BF16)
        MSK0 = sb("MSK0", [128, 4, 128], BF16)
        IDN = sb("IDN", [128, 128], F32)
        IDNB = sb("IDNB", [128, 128], BF16)
        ONES = sb("ONES", [128, 128], BF16)
        BD = sb("BD", [128, 8, 128], F32)
        RS = [sb(f"RS{i}", [128, T], F32) for i in range(2)]
        SCL = sb("SCL", [128, 3, 2, 4], F32)
        SNK = sb("SNK", [128, 2, 4, 128], F32)
        EXS = sb("EXS", [128, 3, 8], F32)
        SNKS = sb("SNKS", [128, 2, NS, 16], F32)
        HC = sb("HC", [128, 3, 4], F32)
        CUX = sb("CUX", [128, 3, 4, 3], F32)
        CSC = sb("CSC", [128, 3, 4, 2], F32)
        KH = sb("KH", [128, 1, 2, 128 + T], BF16)
        VH = sb("VH", [128, 1, 5, 256], BF16)
        KF = [sb(f"KF{i}", [128, T], F32) for i in range(2)]
        ROW = sb("ROW", [128, 512], F32)
        UXS = sb("UXS", [128, 4, NS, 4], F32)
        GCS = sb("GCS", [128, 4, NS, 3], F32)
        H0S = sb("H0S", [128, 4, NS], F32)
        STS = sb("STS", [48, 512], F32)
        KTS = [sb(f"KTS{i}", [128, 2, 128], BF16) for i in range(2)]
        PSS = sb("PSS", [128, 256], BF16)
        RD = sb("RD", [128, 256], F32)

        PA = [ps(f"PA{i}", [128, 512], F32) for i in range(4)]
        PB = [ps(f"PB{i}", [128, 512], F32) for i in range(4)]

        wstate = {"i": 0}

        def wload(pieces, nm):
            i = wstate["i"]; wstate["i"] += 1
            slot = i % NWS
            trk.reserve(f"w{slot}", len(pieces))
            for pi_, (dst, src) in enumerate(pieces):
                trk.dma("pool", dst(WS[slot]), src, R=(), W=(f"W{slot}",), sem=f"w{slot}", skip_deps=(pi_ > 0))
            return WS[slot], f"W{slot}"

        def wv(slot, kc, n):
            return slot[:, 0:kc * n].rearrange("p (k n) -> p k n", n=n)

        pa_i = {"i": 0}

        def next_pa():
            i = pa_i["i"] % 4; pa_i["i"] += 1
            return PA[i], f"PA{i}"

        tmp_i = {"i": 0}

        def next_tmp():
            i = tmp_i["i"] % NTMP; tmp_i["i"] += 1
            return TMP[i], f"TMP{i}"

        sq_i = {"i": 0}

        def next_sq():
            i = sq_i["i"] % 3; sq_i["i"] += 1
            return SQ[i], f"SQ{i}"

        def colp(l, name, c=0):
            o, n = lay[(name, l)]
            return PV[:, o + c:o + c + 1]

        trk.dma("sp", PV[:], pvd, W=("PV",), sem="c_pv")
        trk.dma("sp", IDN[:], idnd, W=("IDN",), sem="c_idn")
        trk.dma("sp", MSKF[:], mskd, W=("MSKF",), sem="c_msk")
        op("dve", lambda e: e.tensor_copy(out=IDNB[:], in_=IDN[:]), R=("IDN",), W=("IDNB",))
        op("dve", lambda e: e.memset(ONES[:], 1.0), W=("ONES",))
        for m in range(2):
            for j in range(4):
                op("dve", lambda e, m=m, j=j: e.tensor_copy(out=MSK[:, m, j, :], in_=MSKF[:, m, :]), R=("MSKF",), W=("MSK",))
        for t_, nm in ((HC, "HC"), (CUX, "CUX"), (CSC, "CSC"), (KH, "KH"), (VH, "VH")):
            op("dve", lambda e, t_=t_: e.memset(t_[:], 0.0), W=(nm,))
        for l in range(3):
            o, n = lay[("lam", l)]
            tt, tn = next_tmp()
            op("act", lambda e: e.activation(out=tt[:, 0:4], in_=PV[:, o:o + 4], func=AF.Exp, scale=-1.0), R=("PV",), W=(tn,))
            op("act", lambda e: e.activation(out=tt[:, 4:8], in_=tt[:, 0:4], func=AF.Ln, bias=1.0), R=(tn,), W=(tn,))
            op("dve", lambda e: e.tensor_scalar(out=SCL[:, l, 0, :], in0=tt[:, 4:8], scalar1=-8.0, scalar2=None, op0=ALU.mult), R=(tn,), W=("SCL",))
            op("dve", lambda e: e.tensor_scalar(out=SCL[:, l, 1, :], in0=tt[:, 4:8], scalar1=-16.0, scalar2=None, op0=ALU.mult), R=(tn,), W=("SCL",))
            o, n = lay[("sink", l)]
            op("act", lambda e: e.activation(out=EXS[:, l, :], in_=PV[:, o:o + 8], func=AF.Exp), R=("PV",), W=("EXS",))
            o, n = lay[("sinkall", l)]
            op("act", lambda e: e.activation(out=tt[:, 16:32], in_=PV[:, o:o + 16], func=AF.Exp), R=("PV",), W=(tn,))
            for n_ in range(NS):
                if l < 2:
                    op("dve", lambda e, n_=n_: e.tensor_copy(out=SNKS[:, l, n_, :], in_=tt[:, 16:32]), R=(tn,), W=("SNKS",))

        def rmsnorm_to_xn(l, gname, N):
            pb, pbn = PB[0], "PB0"
            for c in range(KC):
                sq, sqn = next_sq()
                op("act", lambda e: e.activation(out=sq[:, 0:N], in_=X[:, c, 0:N], func=AF.Square), R=(f"X{c}",), W=(sqn,))
                op("pe", lambda e: e.matmul(pb[:, 0:N], lhsT=ONES[:, :], rhs=sq[:, 0:N], start=(c == 0), stop=(c == KC - 1)),
                   R=(sqn, "ONES"), W=(pbn,), inc=True)
            rs = RS[0]
            op("dve", lambda e: e.tensor_scalar(out=rs[:, 0:N], in0=pb[:, 0:N], scalar1=1.0 / D, scalar2=EPS, op0=ALU.mult, op1=ALU.add), R=(pbn,), W=("RS0",))
            op("act", lambda e: e.activation(out=rs[:, 0:N], in_=rs[:, 0:N], func=AF.Sqrt), R=("RS0",), W=("RS0",))
            op("dve", lambda e: e.reciprocal(out=rs[:, 0:N], in_=rs[:, 0:N]), R=("RS0",), W=("RS0",))
            for c in range(KC):
                op("dve", lambda e: e.scalar_tensor_tensor(out=XN[:, c, 0:N], in0=X[:, c, 0:N], scalar=colp(l, gname, c), in1=rs[:, 0:N], op0=ALU.mult, op1=ALU.mult),
                   R=(f"X{c}", "RS0", "PV"), W=(f"XN{c}", f"STG{c // 8}"))

        def gemm_chunk(lhsT_of_k, wtok, nk, rhs_of_k, rtok_of_k, N, last_in_block=False):
            pa, pan = next_pa()
            for k in range(nk):
                op("pe", lambda e: e.matmul(pa[:, 0:N], lhsT=lhsT_of_k(k), rhs=rhs_of_k(k), start=(k == 0), stop=(k == nk - 1)),
                   R=(wtok, rtok_of_k(k)), W=(pan,), inc=(k == nk - 1))
            return pa, pan

        def gemm_chunk_ct(lhsT_of_kg, wtok, nk, rhs_of_k, rtok_of_k, N):
            pa, pan = next_pa()
            for k in range(nk):
                for g in range(4):
                    op("pe", lambda e: e.matmul(pa[32 * g:32 * g + 32, 0:N], lhsT=lhsT_of_kg(k, g), rhs=rhs_of_k(k), start=(k == 0), stop=(k == nk - 1), tile_position=(0, 32 * g)),
                       R=(wtok, rtok_of_k(k)), W=(pan,), inc=(k == nk - 1 and g == 3))
            return pa, pan

        def group_norm_finish(pb, pbn, l, c0, nch, N, rsi):
            rs = RS[rsi]; rsn = f"RS{rsi}"
            op("dve", lambda e: e.tensor_scalar(out=rs[:, 0:N], in0=pb[:, 0:N], scalar1=1.0 / (nch * 128), scalar2=EPS, op0=ALU.mult, op1=ALU.add), R=(pbn,), W=(rsn,))
            op("act", lambda e: e.activation(out=rs[:, 0:N], in_=rs[:, 0:N], func=AF.Sqrt), R=(rsn,), W=(rsn,))
            op("dve", lambda e: e.reciprocal(out=rs[:, 0:N], in_=rs[:, 0:N]), R=(rsn,), W=(rsn,))
            for c in range(c0, c0 + nch):
                op("dve", lambda e: e.scalar_tensor_tensor(out=MX[:, c, 0:N], in0=MX[:, c, 0:N], scalar=colp(l, "ngrp", c), in1=rs[:, 0:N], op0=ALU.mult, op1=ALU.mult),
                   R=(f"MX{c}", rsn, "PV"), W=(f"MX{c}",))

        def sq_accum(pb, pbn, c, first, last, N):
            sq, sqn = next_sq()
            op("act", lambda e: e.activation(out=sq[:, 0:N], in_=MX[:, c, 0:N], func=AF.Square), R=(f"MX{c}",), W=(sqn,))
            op("pe", lambda e: e.matmul(pb[:, 0:N], lhsT=ONES[:, :], rhs=sq[:, 0:N], start=first, stop=last), R=(sqn, "ONES"), W=(pbn,), inc=True)

        def transpose_out(src_ap, rows, cols, dst_dram, rtoks, tag, view=None, multi=None):
            op("pe", lambda e: e.transpose(PB[3][0:cols, 0:rows], src_ap, IDN[0:rows, 0:rows]), R=tuple(rtoks) + ("IDN",), W=("PB3",))
            op("act", lambda e: e.activation(out=ROW[0:cols, 0:rows], in_=PB[3][0:cols, 0:rows], func=AF.Copy), R=("PB3",), W=("ROW",))
            if multi is not None:
                trk.reserve("oROW", len(multi))
                for (d_ap, r0, r1) in multi:
                    trk.dma("sp", d_ap, ROW[r0:r1, 0:rows], R=("ROW",), W=(tag,), sem="oROW")
                return
            srcv = ROW[0:cols, 0:rows]
            if view is not None:
                srcv = view(srcv)
            trk.dma("sp", dst_dram, srcv, R=("ROW",), W=(tag,), sem="oROW")

        def layer(l, pi, kind, pout, ffi, xfinal_cb=None):
            N = T if kind == "p" else NS
            last_pass = pout is not None
            o_fl, _ = lay[("flags", 0)]
            if kind == "p":
                for kc in range(2):
                    for j in range(4):
                        op("dve", lambda e: e.tensor_scalar(out=SNK[:, kc, j, :], in0=IDN[:, :], scalar1=0.0, scalar2=EXS[:, l, kc * 4 + j:kc * 4 + j + 1], op0=ALU.mult, op1=ALU.add), R=("EXS", "IDN"), W=("SNK",))
            rmsnorm_to_xn(l, "nmix", N)
            xr = lambda k: XN[:, k, 0:N]
            xt = lambda k: f"XN{k}"
            if kind == "s":
                trk.dma("sp", STS[0:48, :], st_conv[l].rearrange("n k f -> (n k) f"), W=("STS",), sem="st")
                for c in range(4):
                    op("pe", lambda e: e.transpose(PB[3][:, 0:48], STS[0:48, c * 128:(c + 1) * 128], IDN[0:48, 0:48]), R=("STS", "IDN"), W=("PB3",))
                    op("act", lambda e: e.activation(out=UXS[:, c, :, 0:3], in_=PB[3][:, 0:48].rearrange("p (n k) -> p n k", k=3), func=AF.Copy), R=("PB3",), W=("UXS",))
                trk.dma("sp", STS[0:32, :], st_sc[l].rearrange("n k f -> (n k) f"), R=(), W=("STS",), sem="st")
                for c in range(4):
                    op("pe", lambda e: e.transpose(PB[3][:, 0:32], STS[0:32, c * 128:(c + 1) * 128], IDN[0:32, 0:32]), R=("STS", "IDN"), W=("PB3",))
                    op("act", lambda e: e.activation(out=GCS[:, c, :, 0:2], in_=PB[3][:, 0:32].rearrange("p (n k) -> p n k", k=2), func=AF.Copy), R=("PB3",), W=("GCS",))
                trk.dma("sp", STS[0:16, :], st_h[l], W=("STS",), sem="st")
                for c in range(4):
                    op("pe", lambda e: e.transpose(PB[3][:, 0:16], STS[0:16, c * 128:(c + 1) * 128], IDN[0:16, 0:16]), R=("STS", "IDN"), W=("PB3",))
                    op("act", lambda e: e.activation(out=H0S[:, c, :], in_=PB[3][:, 0:16], func=AF.Copy), R=("PB3",), W=("H0S",))
                trk.dma("sp", s_k[l][:, 0:127, :], ck[l][:, 1:128, :], W=(f"skA{l}",), sem=f"o2a{l}")
                trk.dma("sp", s_v[l][:, 0:127, :], cv[l][:, 1:128, :], W=(f"svA{l}",), sem=f"o2b{l}")
                trk.dma("sp", s_conv[l][:, 0:2, :], st_conv[l][:, 1:3, :], W=(f"scvA{l}",), sem=f"o2c{l}")
                trk.dma("sp", s_sc[l][:, 0:1, :], st_sc[l][:, 1:2, :], W=(f"sscA{l}",), sem=f"o2d{l}")

            if (_DBG.get("stop") == "norm" and kind == _DBG.get("stop_kind", "p") and l == _DBG.get("stop_layer", 0)):
                raise _Stop()
            trk.dma("sp", BD[:], bd[l], W=("BD",), sem="bd")
            pbl, pbln = PB[1], "PB1"
            pbs, pbsn = PB[2], "PB2"
            deferred = []

            def flush_deferred():
                while deferred:
                    deferred.pop(0)()

            def lru_chunk(c):
                cc = 0
                slot, wtok = wload([
                    (lambda s: wv(s, KC, 512)[:, :, 0:128], WIN[l][:, 1536 + c * 128:1536 + (c + 1) * 128].rearrange("(k p) m -> p k m", p=128)),
                    (lambda s: wv(s, KC, 512)[:, :, 256:384], WIN[l][:, 2048 + c * 128:2048 + (c + 1) * 128].rearrange("(k p) m -> p k m", p=128)),
                ], "lru")
                sv = wv(slot, KC, 512)
                pa, pan = gemm_chunk(lambda k: sv[:, k, cc * 128:(cc + 1) * 128], wtok, KC, xr, xt, N)
                ux, uxn = next_tmp()
                xc, xcn = next_tmp()
                if kind == "p":
                    op("dve", lambda e: e.tensor_copy(out=ux[:, 0:3], in_=CUX[:, l, c, :]), R=("CUX",), W=(uxn,))
                    op("act", lambda e: e.activation(out=ux[:, 3:3 + T], in_=pa[:, 0:T], func=AF.Copy), R=(pan,), W=(uxn,))
                    op("dve", lambda e: e.tensor_copy(out=CUX[:, l, c, :], in_=ux[:, T:T + 3]), R=(uxn,), W=("CUX",))
                    tap = lambda k: ux[:, k:k + T]
                else:
                    op("act", lambda e: e.activation(out=UXS[:, c, :, 3], in_=pa[:, 0:NS], func=AF.Copy), R=(pan,), W=("UXS",))
                    uxn = "UXS"
                    tap = lambda k: UXS[:, c, :, k]
                o_w, _ = lay[("lcw", l)]
                op("dve", lambda e: e.tensor_scalar(out=xc[:, 0:N], in0=tap(3), scalar1=PV[:, o_w + 12 + c:o_w + 13 + c], scalar2=colp(l, "lcb", c), op0=ALU.mult, op1=ALU.add), R=(uxn, "PV"), W=(xcn,))
                for k in (2, 1, 0):
                    op("dve", lambda e, k=k: e.scalar_tensor_tensor(out=xc[:, 0:N], in0=tap(k), scalar=PV[:, o_w + k * 4 + c:o_w + k * 4 + c + 1], in1=xc[:, 0:N], op0=ALU.mult, op1=ALU.add), R=(uxn, "PV", xcn), W=(xcn,))
                if kind == "s":
                    transpose_out(UXS[:, c, :, 3], 128, NS, s_conv[l][:, 2, c * 128:(c + 1) * 128], ("UXS",), f"scvB{l}")
                pa2, pa2n = gemm_chunk(lambda k: sv[:, k, 256:384], wtok, KC, xr, xt, N)
                ug, ugn = next_tmp()
                op("act", lambda e: e.activation(out=ug[:, 0:N], in_=pa2[:, 0:N], func=AF.Copy), R=(pa2n,), W=(ugn,))
                gates = []
                for gi, bname in ((0, "ba"), (1, "bi")):
                    pg, pgn = next_pa()
                    op("pe", lambda e: e.matmul(pg[:, 0:N], lhsT=BD[:, gi * 4 + c, :], rhs=xc[:, 0:N], start=True, stop=True), R=("BD", xcn), W=(pgn,))
                    gt, gtn = next_tmp()
                    op("act", lambda e: e.activation(out=gt[:, 0:N], in_=pg[:, 0:N], func=AF.Sigmoid, bias=colp(l, bname, c)), R=(pgn, "PV"), W=(gtn,))
                    gates.append((gt, gtn))
                (rg, rgn), (ig, ign) = gates
                a_, an = next_tmp()
                m_, mn = next_tmp()
                op("act", lambda e: e.activation(out=a_[:, 0:N], in_=rg[:, 0:N], func=AF.Exp, scale=SCL[:, l, 0, c:c + 1]), R=(rgn, "SCL"), W=(an,))
                op("act", lambda e: e.activation(out=m_[:, 0:N], in_=rg[:, 0:N], func=AF.Exp, scale=SCL[:, l, 1, c:c + 1]), R=(rgn, "SCL"), W=(mn,))
                op("dve", lambda e: e.tensor_scalar(out=m_[:, 0:N], in0=m_[:, 0:N], scalar1=-1.0, scalar2=1.0, op0=ALU.mult, op1=ALU.add), R=(mn,), W=(mn,))
                op("act", lambda e: e.activation(out=m_[:, 0:N], in_=m_[:, 0:N], func=AF.Sqrt), R=(mn,), W=(mn,))
                if ffi is not None:
                    op("dve", lambda e: e.tensor_scalar(out=m_[:, 0:1], in0=m_[:, 0:1], scalar1=PV[:, o_fl + 7 + ffi:o_fl + 8 + ffi], scalar2=PV[:, o_fl + 2 + ffi:o_fl + 3 + ffi], op0=ALU.mult, op1=ALU.add), R=(mn, "PV"), W=(mn,))
                op("dve", lambda e: e.tensor_tensor(out=ig[:, 0:N], in0=ig[:, 0:N], in1=xc[:, 0:N], op=ALU.mult), R=(ign, xcn), W=(ign,))
                op("dve", lambda e: e.tensor_tensor(out=ig[:, 0:N], in0=ig[:, 0:N], in1=m_[:, 0:N], op=ALU.mult), R=(ign, mn), W=(ign,))
                hs, hsn = rg, rgn
                if kind == "p":
                    op("dve", lambda e: e.tensor_tensor_scan(out=hs[:, 0:T], data0=a_[:, 0:T], data1=ig[:, 0:T], initial=HC[:, l, c:c + 1], op0=ALU.mult, op1=ALU.add), R=(an, ign, "HC", rgn), W=(hsn,))
                    op("dve", lambda e: e.tensor_copy(out=HC[:, l, c:c + 1], in_=hs[:, T - 1:T]), R=(hsn,), W=("HC",))
                else:
                    op("dve", lambda e: e.tensor_tensor(out=hs[:, 0:N], in0=a_[:, 0:N], in1=H0S[:, c, :], op=ALU.mult), R=(an, "H0S", rgn), W=(hsn,))
                    op("dve", lambda e: e.tensor_tensor(out=hs[:, 0:N], in0=hs[:, 0:N], in1=ig[:, 0:N], op=ALU.add), R=(hsn, ign), W=(hsn,))
                    transpose_out(hs[:, 0:NS], 128, NS, s_h[l][:, c * 128:(c + 1) * 128], (hsn,), f"shB{l}")
                flush_deferred()
                g2, g2n = next_tmp()
                op("dve", lambda e: e.tensor_tensor(out=g2[:, 0:N], in0=ug[:, 0:N], in1=ug[:, 0:N], op=ALU.mult), R=(ugn,), W=(g2n,))
                op("dve", lambda e: e.tensor_scalar(out=g2[:, 0:N], in0=g2[:, 0:N], scalar1=0.044715, scalar2=1.0, op0=ALU.mult, op1=ALU.add), R=(g2n,), W=(g2n,))
                op("dve", lambda e: e.tensor_tensor(out=g2[:, 0:N], in0=g2[:, 0:N], in1=ug[:, 0:N], op=ALU.mult), R=(g2n, ugn), W=(g2n,))
                op("act", lambda e: e.activation(out=g2[:, 0:N], in_=g2[:, 0:N], func=AF.Sigmoid, scale=1.5957691216057308), R=(g2n,), W=(g2n,))
                op("dve", lambda e: e.tensor_tensor(out=g2[:, 0:N], in0=g2[:, 0:N], in1=ug[:, 0:N], op=ALU.mult), R=(g2n, ugn), W=(g2n,))
                op("dve", lambda e: e.tensor_tensor(out=MX[:, 8 + c, 0:N], in0=g2[:, 0:N], in1=hs[:, 0:N], op=ALU.mult), R=(g2n, hsn), W=(f"MX{8 + c}",))
                deferred.append(lambda: sq_accum(pbl, pbln, 8 + c, c == 0, c == 3, N))

            def sc_chunk(c):
                slot, wtok = wload([
                    (lambda s: wv(s, KC, 384)[:, :, 0:128], WIN[l][:, 3072 + c * 128:3072 + (c + 1) * 128].rearrange("(k p) m -> p k m", p=128)),
                    (lambda s: wv(s, KC, 384)[:, :, 128:256], WIN[l][:, 3584 + c * 128:3584 + (c + 1) * 128].rearrange("(k p) m -> p k m", p=128)),
                    (lambda s: wv(s, KC, 384)[:, :, 256:384], WIN[l][:, 2560 + c * 128:2560 + (c + 1) * 128].rearrange("(k p) m -> p k m", p=128)),
                ], "sc")
                sv = wv(slot, KC, 384)
                pa, pan = gemm_chunk(lambda k: sv[:, k, 0:128], wtok, KC, xr, xt, N)
                uc, ucn = next_tmp()
                op("act", lambda e: e.activation(out=uc[:, 0:N], in_=pa[:, 0:N], func=AF.Copy), R=(pan,), W=(ucn,))
                flush_deferred()
                pa, pan = gemm_chunk(lambda k: sv[:, k, 128:256], wtok, KC, xr, xt, N)
                gc, gcn = next_tmp()
                if kind == "p":
                    op("dve", lambda e: e.tensor_copy(out=gc[:, 0:2], in_=CSC[:, l, c, :]), R=("CSC",), W=(gcn,))
                    op("dve", lambda e: e.tensor_tensor(out=gc[:, 2:2 + T], in0=pa[:, 0:T], in1=uc[:, 0:T], op=ALU.mult), R=(pan, ucn), W=(gcn,))
                    op("dve", lambda e: e.tensor_copy(out=CSC[:, l, c, :], in_=gc[:, T:T + 2]), R=(gcn,), W=("CSC",))
                    tap = lambda k: gc[:, k:k + T]
                else:
                    op("dve", lambda e: e.tensor_tensor(out=GCS[:, c, :, 2], in0=pa[:, 0:NS], in1=uc[:, 0:NS], op=ALU.mult), R=(pan, ucn), W=("GCS",))
                    gcn = "GCS"
                    tap = lambda k: GCS[:, c, :, k]
                    transpose_out(GCS[:, c, :, 2], 128, NS, s_sc[l][:, 1, c * 128:(c + 1) * 128], ("GCS",), f"sscB{l}")
                y_, yn = next_tmp()
                o_w, _ = lay[("scw", l)]
                op("dve", lambda e: e.tensor_scalar(out=y_[:, 0:N], in0=tap(2), scalar1=PV[:, o_w + 8 + c:o_w + 9 + c], scalar2=None, op0=ALU.mult), R=(gcn, "PV"), W=(yn,))
                for k in (1, 0):
                    op("dve", lambda e, k=k: e.scalar_tensor_tensor(out=y_[:, 0:N], in0=tap(k), scalar=PV[:, o_w + k * 4 + c:o_w + k * 4 + c + 1], in1=y_[:, 0:N], op0=ALU.mult, op1=ALU.add), R=(gcn, "PV", yn), W=(yn,))
                pa, pan = gemm_chunk(lambda k: sv[:, k, 256:384], wtok, KC, xr, xt, N)
                op("dve", lambda e: e.tensor_tensor(out=MX[:, 12 + c, 0:N], in0=pa[:, 0:N], in1=y_[:, 0:N], op=ALU.mult), R=(pan, yn), W=(f"MX{12 + c}",))
                deferred.append(lambda: sq_accum(pbs, pbsn, 12 + c, c == 0, c == 3, N))

            def kv_blk():
                pieces = [(lambda s: wv(s, KC, 512), WIN[l][:, 1024:1536].rearrange("(k p) m -> p k m", p=128))]
                slot, wtok = wload(pieces, "kv")
                sv = wv(slot, KC, 512)
                if kind == "p":
                    for pl in range(2):
                        op("dve", lambda e, pl=pl: e.tensor_copy(out=KH[:, 0, pl, 0:128], in_=KH[:, 0, pl, T:T + 128]), R=("KH",), W=("KH",))
                    op("dve", lambda e: e.tensor_copy(out=VH[:, 0, 0, :], in_=VH[:, 0, 4, :]), R=("VH",), W=("VH",))
                kf = []
                for pl in range(2):
                    pa, pan = gemm_chunk_ct(lambda k, g: sv[:, k, g * 64 + pl * 32:g * 64 + pl * 32 + 32], wtok, KC, xr, xt, N)
                    t_, tn = next_tmp()
                    op("act", lambda e: e.activation(out=t_[:, 0:N], in_=pa[:, 0:N], func=AF.Copy), R=(pan,), W=(tn,))
                    kf.append((t_, tn))
                rope(kf, [(KF[0], "KF0"), (KF[1], "KF1")], N)
                if kind == "p":
                    for pl in range(2):
                        op("act", lambda e, pl=pl: e.activation(out=KH[:, 0, pl, 128:128 + T], in_=KF[pl][:, 0:T], func=AF.Copy), R=(f"KF{pl}",), W=("KH",))
                    if last_pass:
                        for pl in range(2):
                            transpose_out(KF[pl][:, T - 128:T], 128, 128, pout["k"].rearrange("t (g h i) -> t h g i", h=2, i=32)[:, pl, :, :], (f"KF{pl}",), "pk", view=lambda a: a.rearrange("t (g i) -> t g i", i=32))
                else:
                    for pl in range(2):
                        op("act", lambda e, pl=pl: e.activation(out=QK[:, 8 + pl, 0:NS], in_=KF[pl][:, 0:NS], func=AF.Copy), R=(f"KF{pl}",), W=(f"QK{8 + pl}",))
                        transpose_out(KF[pl][:, 0:NS], 128, NS, s_k[l].rearrange("n s (g h i) -> n s h g i", h=2, i=32)[:, 127, pl, :, :], (f"KF{pl}",), f"skB{l}", view=lambda a: a.rearrange("t (g i) -> t g i", i=32))
                ntb = 4 if kind == "p" else 1
                for tb in range(ntb):
                    M = 128 if kind == "p" else NS
                    pa, pan = next_pa()
                    for k in range(KC):
                        op("pe", lambda e: e.matmul(pa[0:M, 0:256], lhsT=XN[:, k, tb * 128:tb * 128 + M], rhs=sv[:, k, 256:512], start=(k == 0), stop=(k == KC - 1)),
                           R=(wtok, f"XN{k}"), W=(pan,), inc=(k == KC - 1))
                    if kind == "p":
                        op("act", lambda e: e.activation(out=VH[:, 0, 1 + tb, :], in_=pa[:, 0:256], func=AF.Copy), R=(pan,), W=("VH",))
                        if last_pass and tb == 3:
                            op("act", lambda e: e.activation(out=ROW[:, 0:256], in_=pa[:, 0:256], func=AF.Copy), R=(pan,), W=("ROW",))
                            trk.dma("sp", pout["v"], ROW[:, 0:256], R=("ROW",), W=("pv",), sem="oROW")
                    else:
                        op("act", lambda e: e.activation(out=ROW[0:NS, 0:256], in_=pa[0:NS, 0:256], func=AF.Copy), R=(pan,), W=("ROW",))
                        trk.dma("sp", s_v[l][:, 127, :], ROW[0:NS, 0:256], R=("ROW",), W=(f"svB{l}",), sem="oROW")


            def q_blk(jb):
                pieces = [(lambda s, g_=g_: wv(s, KC, 512)[:, :, g_ * 128:(g_ + 1) * 128], WIN[l][:, g_ * 256 + jb * 128:g_ * 256 + (jb + 1) * 128].rearrange("(k p) m -> p k m", p=128)) for g_ in range(4)]
                slot, wtok = wload(pieces, "q")
                sv = wv(slot, KC, 512)
                for jj in range(2):
                    j = jb * 2 + jj
                    qf = []
                    for pl in range(2):
                        pa, pan = gemm_chunk_ct(lambda k, g: sv[:, k, g * 128 + jj * 64 + pl * 32:g * 128 + jj * 64 + pl * 32 + 32], wtok, KC, xr, xt, N)
                        t_, tn = next_tmp()
                        op("act", lambda e: e.activation(out=t_[:, 0:N], in_=pa[:, 0:N], func=AF.Copy), R=(pan,), W=(tn,))
                        qf.append((t_, tn))
                    rope(qf, [(QK[:, j * 2, :], f"QK{j * 2}"), (QK[:, j * 2 + 1, :], f"QK{j * 2 + 1}")], N)


            for c in range(4):
                lru_chunk(c)
                sc_chunk(c)
                if c == 0:
                    q_blk(0)
                elif c == 1:
                    q_blk(1)
                elif c == 2:
                    kv_blk()
            flush_deferred()
            group_norm_finish(pbl, pbln, l, 8, 4, N, 1)
            group_norm_finish(pbs, pbsn, l, 12, 4, N, 0)
            if kind == "p" and last_pass:
                transpose_out(HC[:, l, :], 128, 4, pout["h"].rearrange("(c f) -> c f", f=128), ("HC",), "ph")
                transpose_out(CUX[:, l, :, :], 128, 12, None, ("CUX",), "pconv", multi=[(pout["conv"][:, c * 128:(c + 1) * 128], c * 3, c * 3 + 3) for c in range(4)])
                transpose_out(CSC[:, l, :, :], 128, 8, None, ("CSC",), "psc", multi=[(pout["sc"][:, c * 128:(c + 1) * 128], c * 2, c * 2 + 2) for c in range(4)])

            if (_DBG.get("stop") == "sc" and kind == _DBG.get("stop_kind", "p") and l == _DBG.get("stop_layer", 0)):
                raise _Stop()
            pba, pban = PB[1], "PB1"
            if kind == "p":
                attention_prompt(l, ffi)
            else:
                attention_sample(l)
            for c in range(8):
                sq_accum(pba, pban, c, c == 0, c == 7, N)
            group_norm_finish(pba, pban, l, 0, 8, N, 1)

            if (_DBG.get("stop") == "attn" and kind == _DBG.get("stop_kind", "p") and l == _DBG.get("stop_layer", 0)):
                raise _Stop()
            for cb in range(4):
                cs_ = slice(cb * 512, (cb + 1) * 512)
                pieces = []
                for half in range(2):
                    for kc2 in range(2):
                        r0 = kc2 * 512 + half * 256
                        pieces.append((lambda s, half=half, kc2=kc2: wv(s, KC, 512)[half * 64:(half + 1) * 64, kc2 * 4:(kc2 + 1) * 4, :],
                                       WOUT[l][r0:r0 + 256, cs_].rearrange("(j d) m -> d j m", d=64)))
                pieces.append((lambda s: wv(s, KC, 512)[:, 8:16, :], WOUT[l][1024:2048, cs_].rearrange("(k p) m -> p k m", p=128)))
                slot, wtok = wload(pieces, "out")
                sv = wv(slot, KC, 512)
                for mm in range(4):
                    m = cb * 4 + mm
                    ko = lambda k: (k + 8) % 16
                    pa, pan = gemm_chunk(lambda k: sv[:, ko(k), mm * 128:(mm + 1) * 128], wtok, KC, lambda k: MX[:, ko(k), 0:N], lambda k: f"MX{ko(k)}", N)
                    op("dve", lambda e: e.tensor_tensor(out=X[:, m, 0:N], in0=pa[:, 0:N], in1=X[:, m, 0:N], op=ALU.add), R=(pan, f"X{m}"), W=(f"X{m}",))

            if (_DBG.get("stop") == "out" and kind == _DBG.get("stop_kind", "p") and l == _DBG.get("stop_layer", 0)):
                raise _Stop()
            rmsnorm_to_xn(l, "nffn", N)
            trk.fence("dve", ["pe", "act"])
            trk.fence("act", ["pe", "dve"])
            for fb in range(11):
                for part in range(2):
                    c0_ = part * DFF + fb * 512
                    slot, wtok = wload([(lambda s: wv(s, KC, 512), WGU[l][:, c0_:c0_ + 512].rearrange("(k p) m -> p k m", p=128))], "gu")
                    sv = wv(slot, KC, 512)
                    for cc in range(4):
                        fc = fb * 4 + cc
                        pa, pan = gemm_chunk(lambda k: sv[:, k, cc * 128:(cc + 1) * 128], wtok, KC, xr, xt, N)
                        if part == 0:
                            op("act", lambda e: e.activation(out=H[:, fc, 0:N], in_=pa[:, 0:N], func=AF.Silu), R=(pan,), W=(f"H{fc}",))
                        else:
                            op("dve", lambda e: e.tensor_tensor(out=H[:, fc, 0:N], in0=pa[:, 0:N], in1=H[:, fc, 0:N], op=ALU.mult), R=(pan, f"H{fc}"), W=(f"H{fc}",))
            dn_issued = {}

            def dn_issue(cg_, kh_):
                if cg_ < 8 and (cg_, kh_) not in dn_issued:
                    dn_issued[(cg_, kh_)] = wload([(lambda s: wv(s, 22, 256), WDN[l][kh_ * 2816:(kh_ + 1) * 2816, cg_ * 256:(cg_ + 1) * 256].rearrange("(k p) m -> p k m", p=128))], "dn")
            for cg in range(8):
                accs = None
                for kh in range(2):
                    dn_issue(cg, kh)
                    slot, wtok = dn_issued[(cg, kh)]
                    sv = wv(slot, 22, 256)
                    if kh == 0:
                        accs = [next_pa(), next_pa()]
                    for mm in range(2):
                        pa, pan = accs[mm]
                        for k in range(22):
                            kk = kh * 22 + k
                            op("pe", lambda e: e.matmul(pa[:, 0:N], lhsT=sv[:, k, mm * 128:(mm + 1) * 128], rhs=H[:, kk, 0:N], start=(kk == 0), stop=(kk == FC - 1)),
                               R=(wtok, f"H{kk}"), W=(pan,), inc=(k == 21))
                for mm in range(2):
                    m = cg * 2 + mm
                    pa, pan = accs[mm]
                    op("dve", lambda e: e.tensor_tensor(out=X[:, m, 0:N], in0=pa[:, 0:N], in1=X[:, m, 0:N], op=ALU.add), R=(pan, f"X{m}"), W=(f"X{m}",))
                if xfinal_cb is not None and cg % 2 == 1:
                    dn_issue(cg + 1, 0); dn_issue(cg + 1, 1)
                    xfinal_cb(cg // 2)

        def rope(src, dst, N):
            (a_, an), (b_, bn) = src
            (da, dan), (db, dbn) = dst
            t1, t1n = next_tmp(); t2, t2n = next_tmp()
            cos = CS[:, 0, 0:N]; sin = CS[:, 1, 0:N]
            op("dve", lambda e: e.tensor_tensor(out=t1[:, 0:N], in0=a_[:, 0:N], in1=cos, op=ALU.mult), R=(an, "CS"), W=(t1n,))
            op("dve", lambda e: e.tensor_tensor(out=t2[:, 0:N], in0=b_[:, 0:N], in1=sin, op=ALU.mult), R=(bn, "CS"), W=(t2n,))
            op("dve", lambda e: e.tensor_tensor(out=da[:, 0:N], in0=t1[:, 0:N], in1=t2[:, 0:N], op=ALU.subtract), R=(t1n, t2n), W=(dan,))
            op("dve", lambda e: e.tensor_tensor(out=t1[:, 0:N], in0=b_[:, 0:N], in1=cos, op=ALU.mult), R=(bn, "CS"), W=(t1n,))
            op("dve", lambda e: e.tensor_tensor(out=t2[:, 0:N], in0=a_[:, 0:N], in1=sin, op=ALU.mult), R=(an, "CS"), W=(t2n,))
            op("dve", lambda e: e.tensor_tensor(out=db[:, 0:N], in0=t1[:, 0:N], in1=t2[:, 0:N], op=ALU.add), R=(t1n, t2n), W=(dbn,))

        pt_i = {"i": 0}

        def attention_prompt(l, ffi):
            QKv = QK[:, 0:8, :].rearrange("p (j pl) t -> p pl j t", pl=2)
            num, numn = PB[2], "PB2"
            den, denn = PB[3], "PB3"
            units = [(qb, kc, half) for qb in range(4) for kc in range(2) for half in range(2)]

            def qk_phase(u):
                qb, kc, half = u
                g = kc * 2 + half
                blocks = [(qb, 1), (qb + 1, 0)]
                pts = []
                for bi, (kb, mk) in enumerate(blocks):
                    sps = PB[bi]; spsn = f"PB{bi}"
                    for pl in range(2):
                        op("pe", lambda e: e.matmul(sps[:, :].rearrange("p (j q) -> p j q", q=128), lhsT=KH[32 * g:32 * g + 32, 0, pl, kb * 128:(kb + 1) * 128],
                                                    rhs=QKv[32 * g:32 * g + 32, pl, :, qb * 128:(qb + 1) * 128], start=(pl == 0), stop=(pl == 1), tile_position=(32 * g, 0)),
                           R=("KH",) + tuple(f"QK{j * 2 + pl}" for j in range(4)), W=(spsn,), inc=(pl == 1))
                    i = pt_i["i"] % 4; pt_i["i"] += 1
                    pt, ptn = PT[i], f"PT{i}"
                    op("act", lambda e: e.activation(out=pt[:, :], in_=sps[:, :], func=AF.Exp, scale=0.125), R=(spsn,), W=(ptn,))
                    mk_ap = MSK0[:, :, :] if (mk == 1 and qb == 0 and ffi is not None) else MSK[:, mk, :, :]
                    op("dve", lambda e: e.tensor_tensor(out=pt[:, :], in0=pt[:, :], in1=mk_ap.rearrange("p j q -> p (j q)"), op=ALU.mult), R=(ptn, "MSK", "MSK0"), W=(ptn,))
                    pts.append((pt, ptn, kb))
                return pts

            def pv_phase(u, pts):
                qb, kc, half = u
                g = kc * 2 + half
                for bi, (pt, ptn, kb) in enumerate(pts):
                    st = (bi == 0); sp_ = (bi == len(pts) - 1)
                    op("pe", lambda e: e.matmul(num[half * 64:(half + 1) * 64, :], lhsT=VH[:, 0, kb, g * 64:(g + 1) * 64], rhs=pt[:, :], start=st, stop=sp_),
                       R=(ptn, "VH"), W=(numn,), inc=False)
                    op("pe", lambda e: e.matmul(den[half * 64:(half + 1) * 64, :], lhsT=ONES[:, 0:64], rhs=pt[:, :], start=st, stop=sp_),
                       R=(ptn, "ONES"), W=(denn,), inc=True)
                if half == 1:
                    dn, dnn = next_tmp()
                    op("dve", lambda e: e.tensor_tensor(out=dn[:, 0:512], in0=den[:, :], in1=SNK[:, kc, :, :].rearrange("p j q -> p (j q)"), op=ALU.add), R=(denn, "SNK"), W=(dnn,))
                    op("dve", lambda e: e.reciprocal(out=dn[:, 0:512], in_=dn[:, 0:512]), R=(dnn,), W=(dnn,))
                    op("dve", lambda e: e.tensor_tensor(out=MX[:, kc * 4:kc * 4 + 4, qb * 128:(qb + 1) * 128], in0=num[:, :].rearrange("p (j q) -> p j q", q=128),
                                                        in1=dn[:, 0:512].rearrange("p (j q) -> p j q", q=128), op=ALU.mult),
                       R=(numn, dnn), W=tuple(f"MX{kc * 4 + j}" for j in range(4)))

            prev = None
            for u in units:
                pts = qk_phase(u)
                if prev is not None:
                    pv_phase(*prev)
                prev = (u, pts)
            pv_phase(*prev)

        def attention_sample(l):
            QKv = QK[:, 0:8, :].rearrange("p (j pl) t -> p pl j t", pl=2)
            first = True
            trk.reserve("ks", 8)
            for h_ in range(2):
                for g_ in range(4):
                    trk.dma("pool", KS[:, :, h_ * 128 + g_ * 32:h_ * 128 + g_ * 32 + 32], s_k[l][:, :, g_ * 64 + h_ * 32:g_ * 64 + h_ * 32 + 32].rearrange("n s i -> s n i"),
                            R=(f"skA{l}", f"skB{l}"), W=("KS",), sem="ks", skip_deps=not first)
                    first = False
            trk.dma("pool", VS2[:], s_v[l].rearrange("n s f -> s n f"), R=(f"svA{l}", f"svB{l}"), W=("VS2",), sem="vs")
            for n in range(NS):
                kt, ktn = KTS[n % 2], f"KTS{n % 2}"
                ptb = PB[3][:, 0:128].bitcast(BF16).rearrange("p (pl s) -> p pl s", pl=2)
                for pl in range(2):
                    op("pe", lambda e: e.transpose(ptb[:, pl, :], KS[:, n, pl * 128:(pl + 1) * 128], IDNB[:, :]), R=("KS", "IDNB"), W=("PB3",))
                op("act", lambda e: e.activation(out=kt[:, :, :], in_=ptb, func=AF.Copy), R=("PB3",), W=(ktn,))
                for g in range(4):
                    for pl in range(2):
                        op("pe", lambda e: e.matmul(PA[g][:, n * 4:n * 4 + 4], lhsT=kt[32 * g:32 * g + 32, pl, :], rhs=QKv[32 * g:32 * g + 32, pl, :, n],
                                                    start=(pl == 0), stop=(pl == 1), tile_position=(32 * g, 0)),
                           R=(ktn,) + tuple(f"QK{j * 2 + pl}" for j in range(4)), W=(f"PA{g}",), inc=(pl == 1))
            pssv = PSS[:, :].rearrange("p (n g j) -> p n g j", g=4, j=4)
            for g in range(4):
                op("act", lambda e: e.activation(out=pssv[:, :, g, :], in_=PA[g][:, 0:64].rearrange("p (n j) -> p n j", j=4), func=AF.Exp, scale=0.125), R=(f"PA{g}",), W=("PSS",))
            den, denn = PB[1], "PB1"
            op("pe", lambda e: e.matmul(den[:, 0:256], lhsT=ONES[:, :], rhs=PSS[:, :], start=True, stop=True), R=("PSS", "ONES"), W=(denn,))
            op("dve", lambda e: e.tensor_tensor(out=RD[:, :], in0=den[:, 0:256], in1=SNKS[:, l, :, :].rearrange("p n h -> p (n h)"), op=ALU.add), R=(denn, "SNKS"), W=("RD",))
            op("dve", lambda e: e.reciprocal(out=RD[:, :], in_=RD[:, :]), R=("RD",), W=("RD",))
            num, numn = PB[2], "PB2"
            numv = num[:, 0:128].rearrange("p (kc j n) -> p kc j n", kc=2, j=4)
            for n in range(NS):
                for g in range(4):
                    hf = g % 2
                    op("pe", lambda e: e.matmul(numv[hf * 64:(hf + 1) * 64, g // 2, :, n], lhsT=VS2[:, n, g * 64:(g + 1) * 64], rhs=PSS[:, n * 16 + g * 4:n * 16 + g * 4 + 4], start=True, stop=True),
                       R=("VS2", "PSS"), W=(numn,), inc=(n == NS - 1 and g == 3))
            rdv = RD[:, :].rearrange("p (n g j) -> p g j n", g=4, j=4)
            for hf in range(2):
                op("dve", lambda e: e.tensor_tensor(out=MX[hf * 64:(hf + 1) * 64, 0:8, 0:NS].rearrange("p (kc j) n -> p kc j n", kc=2),
                                                    in0=numv[hf * 64:(hf + 1) * 64, :, :, :],
                                                    in1=rdv[hf * 64:(hf + 1) * 64, hf::2, :, :], op=ALU.mult),
                   R=(numn, "RD"), W=tuple(f"MX{c}" for c in range(8)))

        def run_pass(pi, kind):
            N = T if kind == "p" else NS
            ci = pi if kind == "p" else NSLOT
            o_fl, _ = lay[("flags", 0)]
            trk.dma("sp", CS[:], csd[ci], W=("CS",), sem="cs")
            ntb = 4 if kind == "p" else 1
            for tb in range(ntb):
                rows = 128 if kind == "p" else NS
                src = xp[pi * T + tb * 128: pi * T + tb * 128 + 128, :] if kind == "p" else xs
                so = (tb % 2) * D; stn = f"STG{tb % 2}"
                trk.dma("sp", STG[0:rows, so:so + D], src, W=(stn,), sem=f"xin{tb % 2}")
                for c4 in range(4):
                    pb = PB[c4 % 2]; pbn = f"PB{c4 % 2}"
                    for cc in range(4):
                        c = c4 * 4 + cc
                        op("pe", lambda e: e.transpose(pb[:, cc * 128:cc * 128 + rows], STG[0:rows, so + c * 128:so + (c + 1) * 128], IDN[0:rows, 0:rows]), R=(stn, "IDN"), W=(pbn,), inc=(cc == 3))
                    op("act", lambda e: e.activation(out=X[:, c4 * 4:c4 * 4 + 4, tb * 128:tb * 128 + rows], in_=pb[:, :].rearrange("p (c t) -> p c t", t=128)[:, :, 0:rows], func=AF.Copy),
                       R=(pbn,), W=tuple(f"X{c4 * 4 + cc}" for cc in range(4)))
            if kind == "p" and pi >= 1:
                for j in range(4):
                    so = (j % 2) * D; stn = f"STG{j % 2}"
                    trk.dma("sp", STG[:, so:so + 2048], obs[j][0:128, :], R=(f"ob{j}",), W=(stn,), sem=f"xin{j % 2}")
                    for cc in range(4):
                        c = j * 4 + cc
                        op("dve", lambda e: e.scalar_tensor_tensor(out=X[:, c, :], in0=STG[:, so + cc * 512:so + (cc + 1) * 512], scalar=PV[:, o_fl + 1:o_fl + 2], in1=X[:, c, :], op0=ALU.mult, op1=ALU.add),
                           R=(stn, "PV", f"X{c}"), W=(f"X{c}",))
            if (_DBG.get("stop") == "load" and kind == _DBG.get("stop_kind", "p")):
                raise _Stop()
            if kind == "p":
                ffi = pi if pi <= 1 else None
                if ffi is not None:
                    for j in range(4):
                        op("dve", lambda e: e.tensor_scalar(out=MSK0[:, j, :], in0=MSK[:, 1, j, :], scalar1=PV[:, o_fl + 7 + ffi:o_fl + 8 + ffi], scalar2=None, op0=ALU.mult), R=("MSK", "PV"), W=("MSK0",))
                pout = None
                if pi >= NPP - 1:
                    w_ = pi - (NPP - 1)
                    pout = {"h": p_h[w_], "conv": p_conv[w_], "k": p_k[w_], "v": p_v[w_], "sc": p_sc[w_]}
                def exchange(j):
                    trk.dma("sp", ibs[j][:, :], X[:, j * 4:(j + 1) * 4, :].rearrange("p c t -> p (c t)"), R=tuple(f"X{j * 4 + cc}" for cc in range(4)), W=(f"ib{j}",), sem=f"xex{j}")
                    trk.coll(lambda e: e.collective_compute("AllGather", ALU.bypass, replica_groups=[[2 * i_, 2 * i_ + 1] for i_ in range(_DBG.get('ncores', NCORES) // 2)],
                                                            ins=[ibs[j].ap().opt()], outs=[obs[j].ap().opt()]), R=(f"ib{j}",), W=(f"ob{j}",))
                layer(2, pi, kind, pout, ffi, xfinal_cb=(exchange if pi < NSLOT - 1 else None))
                if pi == 0:
                    fa = PV[:, o_fl:o_fl + 1]
                    for t_, nm in ((HC[:, 2, :], "HC"), (CUX[:, 2, :, :], "CUX"), (CSC[:, 2, :, :], "CSC"), (KH[:, 0, :, :], "KH"), (VH[:, 0, :, :], "VH")):
                        op("dve", lambda e, t_=t_: e.tensor_scalar(out=t_, in0=t_, scalar1=fa, scalar2=None, op0=ALU.mult), R=(nm, "PV"), W=(nm,))
            else:
                for l in range(2):
                    layer(l, pi, kind, None, None)
            if (_DBG.get("stop") == "final" and kind == _DBG.get("stop_kind", "p")):
                raise _Stop()
            pb, pbn = PB[0], "PB0"
            for c in range(KC):
                sq, sqn = next_sq()
                op("act", lambda e: e.activation(out=sq[:, 0:N], in_=X[:, c, 0:N], func=AF.Square), R=(f"X{c}",), W=(sqn,))
                op("pe", lambda e: e.matmul(pb[:, 0:N], lhsT=ONES[:, :], rhs=sq[:, 0:N], start=(c == 0), stop=(c == KC - 1)), R=(sqn, "ONES"), W=(pbn,), inc=True)
            rs = RS[0]
            op("dve", lambda e: e.tensor_scalar(out=rs[:, 0:N], in0=pb[:, 0:N], scalar1=1.0 / D, scalar2=EPS, op0=ALU.mult, op1=ALU.add), R=(pbn,), W=("RS0",))
            op("act", lambda e: e.activation(out=rs[:, 0:N], in_=rs[:, 0:N], func=AF.Sqrt), R=("RS0",), W=("RS0",))
            op("dve", lambda e: e.reciprocal(out=rs[:, 0:N], in_=rs[:, 0:N]), R=("RS0",), W=("RS0",))
            o_f, _ = lay[("nfin", 0)]
            for c in range(KC):
                op("dve", lambda e: e.scalar_tensor_tensor(out=X[:, c, 0:N], in0=X[:, c, 0:N], scalar=PV[:, o_f + c:o_f + c + 1], in1=rs[:, 0:N], op0=ALU.mult, op1=ALU.mult),
                   R=(f"X{c}", "RS0", "PV"), W=(f"X{c}",))
            for tb in range(ntb):
                rows = 128 if kind == "p" else NS
                so = (tb % 2) * D; stn = f"STG{tb % 2}"
                for c4 in range(4):
                    pb = PB[c4 % 2]; pbn = f"PB{c4 % 2}"
                    for cc in range(4):
                        c = c4 * 4 + cc
                        op("pe", lambda e: e.transpose(pb[0:rows, cc * 128:(cc + 1) * 128], X[:, c, tb * 128:tb * 128 + rows], IDN[:, :]), R=(f"X{c}", "IDN"), W=(pbn,), inc=(cc == 3))
                    op("act", lambda e: e.activation(out=STG[0:rows, so + c4 * 512:so + (c4 + 1) * 512], in_=pb[0:rows, :], func=AF.Copy), R=(pbn,), W=(stn,))
                dst = yp[pi * T + tb * 128: pi * T + tb * 128 + 128, :] if kind == "p" else ys
                trk.dma("sp", dst, STG[0:rows, so:so + D], R=(stn,), W=("yout",), sem=f"oSTG{tb % 2}")

        try:
            for pi in range(NSLOT):
                run_pass(pi, "p")
            if not _DBG.get("nosample"):
                run_pass(0, "s")
        except _Stop:
            pass
        trk.wait_all_dma("sp")
    return nc


_CACHE = {}
_DBG = {}


def kernel(x_prompt, x_sample, state_lru_h, state_lru_conv, cache_swa_k, cache_swa_v, state_sconv,
           norm_mix, w_in, norm_grp, w_out, lru_conv_w, lru_conv_b, lru_w_a, lru_b_a, lru_w_i, lru_b_i,
           lru_lambda, sc_conv_w, attn_sinks, norm_ffn, ffn_w_gu, ffn_w_down, norm_final):
    f = lambda a: np.ascontiguousarray(np.asarray(a, dtype=np.float32))
    x_prompt = f(x_prompt); x_sample = f(x_sample)
    NPP = _DBG.get('npp', SEQ // T)
    lay, NV = pv_layout()
    pv = np.zeros((128, NV), np.float32)

    def put(name, l, arr2d):
        o, n = lay[(name, l)]
        assert arr2d.shape == (128, n), (name, arr2d.shape)
        pv[:, o:o + n] = arr2d
    fm = lambda v: f(v).reshape(-1, 128).T
    def put_layer(l, ls):
        put("nmix", l, fm(norm_mix[ls])); put("nffn", l, fm(norm_ffn[ls]))
        g = f(norm_grp[ls])
        ga = g[:1024].reshape(2, 2, 4, 64)
        ga = ga.transpose(1, 3, 0, 2).reshape(128, 8)
        put("ngrp", l, np.concatenate([ga, fm(g[1024:1536]), fm(g[1536:2048])], axis=1))
        cw = f(lru_conv_w[ls])
        put("lcw", l, np.concatenate([fm(cw[k]) for k in range(4)], axis=1))
        put("lcb", l, fm(lru_conv_b[ls])); put("ba", l, fm(lru_b_a[ls])); put("bi", l, fm(lru_b_i[ls])); put("lam", l, fm(lru_lambda[ls]))
        sw = f(sc_conv_w[ls])
        put("scw", l, np.concatenate([fm(sw[k]) for k in range(3)], axis=1))
        sk = f(attn_sinks[ls]).reshape(2, 2, 4)
        sk2 = np.repeat(sk.transpose(1, 0, 2).reshape(2, 1, 8), 64, axis=1).reshape(128, 8)
        put("sink", l, sk2)
        put("sinkall", l, np.repeat(f(attn_sinks[ls])[None, :], 128, axis=0))
    put_layer(0, 0); put_layer(1, 1)
    put("nfin", 0, fm(norm_final))
    bdh = np.zeros((3, 128, 8, 128), np.float32)
    for l in range(2):
        for gi, wsrc in enumerate((lru_w_a, lru_w_i)):
            wl = f(wsrc[l])
            for c in range(4):
                bdh[l, 0:64, gi * 4 + c, 0:64] = wl[2 * c]
                bdh[l, 64:128, gi * 4 + c, 64:128] = wl[2 * c + 1]
    half = 32
    inv = (10000.0 ** (-np.arange(half, dtype=np.float32) / half)).astype(np.float32)
    NSLOT = NPP + 1

    def cs_tables(lc):
        cs = np.zeros((NSLOT + 1, 128, 2, T), np.float32)
        for si in range(NSLOT + 1):
            if si < NSLOT:
                pss = min(max(si - lc, 0), NPP - 1)
                pos = (np.arange(T) + pss * T).astype(np.float32)
            else:
                pos = np.full(T, float(PAST), np.float32)
            ang = pos[None, :] * inv[:, None]
            cs[si, :, 0, :] = np.tile(np.cos(ang).astype(np.float32), (4, 1))
            cs[si, :, 1, :] = np.tile(np.sin(ang).astype(np.float32), (4, 1))
        return cs
    cs_by_lc = [cs_tables(0), cs_tables(1)]
    sidx = np.arange(128)[:, None]; qidx = np.arange(128)[None, :]
    msk = np.stack([(sidx <= qidx), (sidx > qidx)], axis=1).astype(np.float32)
    idn = np.eye(128, dtype=np.float32)
    w_in = f(w_in); w_out = f(w_out); ffn_w_gu = f(ffn_w_gu); ffn_w_down = f(ffn_w_down)
    sh = f(state_lru_h); scv = f(state_lru_conv); ssc = f(state_sconv)
    ckk = f(cache_swa_k).reshape(2, 128, 128, 256); cvv = f(cache_swa_v).reshape(2, 128, 128, 256)
    in_maps = []
    o_fl, _ = lay[("flags", 0)]
    zeros_xp = np.zeros((NSLOT * T, D), np.float32)
    for c in range(NCORES):
        b = c // 2
        lc = c % 2
        ns = slice(c * NS, (c + 1) * NS)
        put_layer(2, lc)
        pvc = pv.copy()
        pvc[:, o_fl + 0] = 1.0 - lc; pvc[:, o_fl + 1] = float(lc)
        for si in range(5):
            ff = 1.0 if si == lc else 0.0
            pvc[:, o_fl + 2 + si] = ff; pvc[:, o_fl + 7 + si] = 1.0 - ff
        bdc = bdh.copy(); bdc[2] = bdh[lc]
        if lc == 0:
            xpc = np.concatenate([x_prompt[b][:NPP * T], np.zeros((T, D), np.float32)], axis=0)
        else:
            xpc = zeros_xp
        in_maps.append({
            "xp": xpc, "xs": np.ascontiguousarray(x_sample[ns, 0, :]),
            "st_h": np.ascontiguousarray(sh[:, ns]), "st_conv": np.ascontiguousarray(scv[:, ns]), "st_sc": np.ascontiguousarray(ssc[:, ns]),
            "ck": np.ascontiguousarray(ckk[:, ns]), "cv": np.ascontiguousarray(cvv[:, ns]),
            "w_in": w_in, "w_out": w_out, "w_gu": ffn_w_gu, "w_dn": ffn_w_down,
            "wo_in": w_in[lc:lc + 1], "wo_out": w_out[lc:lc + 1], "wo_gu": ffn_w_gu[lc:lc + 1], "wo_dn": ffn_w_down[lc:lc + 1],
            "bd": bdc, "pv": pvc, "cs": cs_by_lc[lc], "msk": msk, "idn": idn,
        })
    if NPP not in _CACHE:
        _CACHE[NPP] = build(NPP)
    nc = _CACHE[NPP]
    if 'ncores' in _DBG:
        n_ = _DBG['ncores']
        res = run_bass_kernel_spmd(nc, in_maps[:n_], core_ids=list(range(n_)), trace=_DBG.get('trace', False))
        _DBG['res'] = res
        return res.results
    res = run_bass_kernel_spmd(nc, in_maps, core_ids=list(range(NCORES)))
    R = res.results
    y_prompt = np.stack([R[2 * b + 1]["yp"][T:T + SEQ] for b in range(4)]).reshape(4, SEQ, D)
    y_sample = np.concatenate([R[c]["ys"] for c in range(NCORES)], axis=0).reshape(128, 1, D)
    stk = lambda name, shp: np.stack([np.stack([R[2 * b][name][0], R[2 * b + 1][name][1]]) for b in range(4)], axis=1).reshape(shp)
    p_lru_h = stk("p_h", (2, 4, 512))
    p_lru_conv = stk("p_conv", (2, 4, 3, 512))
    p_swa_k = stk("p_k", (2, 4, 128, 4, 64))
    p_swa_v = stk("p_v", (2, 4, 128, 4, 64))
    p_sconv = stk("p_sc", (2, 4, 2, 512))
    cat = lambda name, shp: np.concatenate([R[c][name] for c in range(NCORES)], axis=1).reshape(shp)
    s_lru_h = cat("s_h", (2, 128, 512))
    s_lru_conv = cat("s_conv", (2, 128, 3, 512))
    s_swa_k = cat("s_k", (2, 128, 128, 4, 64))
    s_swa_v = cat("s_v", (2, 128, 128, 4, 64))
    s_sconv = cat("s_sc", (2, 128, 2, 512))
    outs = (y_prompt, y_sample, p_lru_h, p_lru_conv, p_swa_k, p_swa_v, p_sconv, s_lru_h, s_lru_conv, s_swa_k, s_swa_v, s_sconv)
    return tuple(np.ascontiguousarray(o, dtype=np.float32) for o in outs)
```
